# Optimizing a Trainium2 kernel written in Bass

```python
import jax, jax.numpy as jnp
from jax import lax
import numpy as np

D_MODEL = 1024
BATCH = 8
SEQ = 4096
DEPTH = 1

GRID_W = 64
CTX_LEN = 256
RET_HEADS = 8
RET_DK = 64
RET_DV = 64
RET_W = RET_HEADS * RET_DK
RET_VW = RET_HEADS * RET_DV
RET_CHUNK = 128
NA_HEADS = 8
NA_DH = 64
NA_W = NA_HEADS * NA_DH
NA_WIN_H = 8
NA_WIN_W = 16
D_FF = 4 * D_MODEL
ROPE_BASE = 10000.0
EPS = 1e-6
NEG_INF = -1e30
IN_COLS = 2 * RET_W + 2 * RET_VW + 3 * NA_W + 2 * D_MODEL
SPLIT_AT = (RET_W, 2 * RET_W, 2 * RET_W + RET_VW, 2 * RET_W + 2 * RET_VW,
            2 * RET_W + 2 * RET_VW + NA_W, 2 * RET_W + 2 * RET_VW + 2 * NA_W,
            2 * RET_W + 2 * RET_VW + 3 * NA_W)

kernel_name = 'hybrid_retention_natten_dit_block'


def rmsnorm(x, g):
    xf = x.astype(jnp.float32)
    y = xf * lax.rsqrt(jnp.mean(xf * xf, axis=-1, keepdims=True) + EPS)
    return (y * g.astype(jnp.float32)).astype(x.dtype)


def modulate(x, g, shift, scale):
    return rmsnorm(x, g) * (1 + scale) + shift


def to_heads(t, n_heads):
    b, n, _ = t.shape
    return t.reshape(b, n, n_heads, -1).transpose(0, 2, 1, 3)


def from_heads(t):
    b, h, n, d = t.shape
    return t.transpose(0, 2, 1, 3).reshape(b, n, h * d)


def flip_seq(t):
    return jnp.flip(t, axis=2)


def axial_rope_tables(n, dh):
    pos = jnp.arange(n)
    row = (pos // GRID_W).astype(jnp.float32)
    col = (pos % GRID_W).astype(jnp.float32)
    d_axis = dh // 2
    inv = ROPE_BASE ** (-jnp.arange(0, d_axis, 2, dtype=jnp.float32) / d_axis)
    ang = jnp.concatenate([row[:, None] * inv, col[:, None] * inv], axis=-1)
    return jnp.cos(ang), jnp.sin(ang)


def apply_rope(t, cos, sin):
    half = t.shape[-1] // 2
    t1 = t[..., :half].astype(jnp.float32)
    t2 = t[..., half:].astype(jnp.float32)
    return jnp.concatenate([t1 * cos - t2 * sin, t1 * sin + t2 * cos], axis=-1).astype(t.dtype)


def retention_chunkwise(q, k, v, log_g, init_state, inclusive):
    b, h, n, dk = q.shape
    dv = v.shape[-1]
    L = RET_CHUNK
    nc = n // L
    qc = q.astype(jnp.float32).reshape(b, h, nc, L, dk)
    kc = k.astype(jnp.float32).reshape(b, h, nc, L, dk)
    vc = v.astype(jnp.float32).reshape(b, h, nc, L, dv)
    idx = jnp.arange(L, dtype=jnp.float32)
    diff = idx[:, None] - idx[None, :]
    mask = (diff >= 0) if inclusive else (diff > 0)
    decay = jnp.where(mask, jnp.exp(log_g[:, None, None] * jnp.maximum(diff, 0.0)), 0.0)
    scores = jnp.einsum('bhcid,bhcjd->bhcij', qc, kc) * decay[None, :, None]
    o_intra = jnp.einsum('bhcij,bhcje->bhcie', scores, vc)
    k_w = jnp.exp(log_g[:, None] * (L - 1 - idx)[None, :])
    s_chunk = jnp.einsum('bhcjd,hj,bhcje->cbhde', kc, k_w, vc)
    g_chunk = jnp.exp(log_g * L)[None, :, None, None]

    def step(state, s_c):
        return g_chunk * state + s_c, state

    _, r_before = lax.scan(step, init_state, s_chunk)
    q_w = jnp.exp(log_g[:, None] * (idx + 1)[None, :])
    o_cross = jnp.einsum('bhcid,hi,cbhde->bhcie', qc, q_w, r_before)
    return (o_intra + o_cross).reshape(b, h, n, dv)


def retention_final_state(k, v, log_g):
    n = k.shape[2]
    w = jnp.exp(log_g[:, None] * (n - 1 - jnp.arange(n, dtype=jnp.float32))[None, :])
    return jnp.einsum('bhnd,hn,bhne->bhde', k.astype(jnp.float32), w, v.astype(jnp.float32))


def bidirectional_retention(q, k, v, log_g_fwd, log_g_bwd, state_fwd, state_bwd):
    o_f = retention_chunkwise(q, k, v, log_g_fwd, state_fwd, inclusive=True)
    o_b = retention_chunkwise(flip_seq(q), flip_seq(k), flip_seq(v), log_g_bwd, state_bwd, inclusive=False)
    return o_f + flip_seq(o_b)


def retention_readout(o, g):
    o = o * lax.rsqrt(jnp.mean(o * o, axis=-1, keepdims=True) + EPS)
    return from_heads(o).astype(g.dtype) * jax.nn.silu(g)


def neighborhood_attention(q, k, v, k_ctx, v_ctx, rpb):
    b, h, n, d = q.shape
    rows = n // GRID_W
    kh = min(NA_WIN_H, rows)
    qg = (q.astype(jnp.float32) * d ** -0.5).reshape(b, h, rows, GRID_W, d)
    kg = k.astype(jnp.float32).reshape(b, h, rows, GRID_W, d)
    vg = v.astype(jnp.float32).reshape(b, h, rows, GRID_W, d)
    kc = k_ctx.astype(jnp.float32)
    vc = v_ctx.astype(jnp.float32)
    rpb = rpb.astype(jnp.float32)
    col = jnp.arange(GRID_W)
    c0 = jnp.clip(col - NA_WIN_W // 2, 0, GRID_W - NA_WIN_W)
    col_mask = (col[None, :] >= c0[:, None]) & (col[None, :] < c0[:, None] + NA_WIN_W)
    col_idx = jnp.clip(col[None, :] - col[:, None] + NA_WIN_W - 1, 0, 2 * NA_WIN_W - 2)

    def row_block(r):
        r0 = jnp.clip(r - kh // 2, 0, rows - kh)
        q_r = lax.dynamic_index_in_dim(qg, r, axis=2, keepdims=False)
        k_s = lax.dynamic_slice_in_dim(kg, r0, kh, axis=2)
        v_s = lax.dynamic_slice_in_dim(vg, r0, kh, axis=2)
        row_idx = r0 + jnp.arange(kh) - r + NA_WIN_H - 1
        bias = rpb[:, row_idx][:, :, col_idx].transpose(0, 2, 1, 3)
        s_loc = jnp.einsum('bhqd,bhkwd->bhqkw', q_r, k_s) + bias
        s_loc = jnp.where(col_mask[:, None, :], s_loc, NEG_INF)
        s_ctx = jnp.einsum('bhqd,bhkd->bhqk', q_r, kc)
        s = jnp.concatenate([s_loc.reshape(b, h, GRID_W, kh * GRID_W), s_ctx], axis=-1)
        p = jax.nn.softmax(s, axis=-1)
        p_loc = p[..., :kh * GRID_W].reshape(b, h, GRID_W, kh, GRID_W)
        p_ctx = p[..., kh * GRID_W:]
        return (jnp.einsum('bhqkw,bhkwd->bhqd', p_loc, v_s)
                + jnp.einsum('bhqk,bhkd->bhqd', p_ctx, vc))

    out = lax.map(row_block, jnp.arange(rows))
    return out.transpose(1, 2, 0, 3, 4).reshape(b, h, n, d).astype(q.dtype)


def context_attention(q, k, v):
    d = q.shape[-1]
    s = jnp.einsum('bhqd,bhkd->bhqk', q.astype(jnp.float32), k.astype(jnp.float32)) * d ** -0.5
    p = jax.nn.softmax(s, axis=-1)
    return jnp.einsum('bhqk,bhkd->bhqd', p, v.astype(jnp.float32)).astype(q.dtype)


def merge_branches(y_ret, y_na, gates, w_ret_out, w_na_out, w_o):
    g_ret, g_na = jnp.split(jax.nn.sigmoid(gates), 2, axis=-1)
    return (g_ret * (y_ret @ w_ret_out) + g_na * (y_na @ w_na_out)) @ w_o


def squared_relu_mlp(h, w1, w2):
    return jnp.square(jax.nn.relu(h @ w1)) @ w2


def setup_inputs(seed: int = 0) -> dict:
    key = jax.random.key(seed)
    ks = jax.random.split(key, 18)

    def nrm(k, shape, s):
        return jax.random.normal(k, shape, jnp.float32) * s

    base_logit = jnp.asarray(np.log(2.0 ** (5 + np.arange(RET_HEADS)) - 1.0), jnp.float32)
    return {
        'x': nrm(ks[0], (BATCH, SEQ, D_MODEL), 1.0),
        'c': nrm(ks[1], (BATCH, D_MODEL), 1.0),
        'ctx': nrm(ks[2], (BATCH, CTX_LEN, D_MODEL), 1.0),
        'c_ctx': nrm(ks[3], (D_MODEL,), 1.0),
        'w_ada': nrm(ks[4], (DEPTH, D_MODEL, 6 * D_MODEL), 0.5 * D_MODEL ** -0.5),
        'b_ada': nrm(ks[5], (DEPTH, 6 * D_MODEL), 0.02),
        'norm_pre_mix': 1.0 + nrm(ks[6], (DEPTH, D_MODEL), 0.02),
        'norm_post_mix': 1.0 + nrm(ks[7], (DEPTH, D_MODEL), 0.02),
        'norm_pre_ffn': 1.0 + nrm(ks[8], (DEPTH, D_MODEL), 0.02),
        'norm_post_ffn': 1.0 + nrm(ks[9], (DEPTH, D_MODEL), 0.02),
        'w_in': nrm(ks[10], (DEPTH, D_MODEL, IN_COLS), D_MODEL ** -0.5),
        'ret_decay_logit': base_logit[None, None, :] + nrm(ks[11], (DEPTH, 2, RET_HEADS), 0.05),
        'w_ret_out': nrm(ks[12], (DEPTH, RET_VW, D_MODEL), RET_VW ** -0.5),
        'na_rpb': nrm(ks[13], (DEPTH, NA_HEADS, 2 * NA_WIN_H - 1, 2 * NA_WIN_W - 1), 0.02),
        'w_na_out': nrm(ks[14], (DEPTH, NA_W, D_MODEL), NA_W ** -0.5),
        'w_o': nrm(ks[15], (DEPTH, D_MODEL, D_MODEL), D_MODEL ** -0.5),
        'w_ff1': nrm(ks[16], (DEPTH, D_MODEL, D_FF), D_MODEL ** -0.5),
        'w_ff2': nrm(ks[17], (DEPTH, D_FF, D_MODEL), D_FF ** -0.5),
    }


def reference(x, c, ctx, c_ctx, w_ada, b_ada, norm_pre_mix, norm_post_mix, norm_pre_ffn,
              norm_post_ffn, w_in, ret_decay_logit, w_ret_out, na_rpb, w_na_out, w_o, w_ff1, w_ff2):
    n = x.shape[1]
    cos, sin = axial_rope_tables(n, RET_DK)
    k_scale = RET_DK ** -0.5
    x_lat, x_ctx = x, ctx
    for l in range(DEPTH):
        last = l == DEPTH - 1
        mod_lat = (jax.nn.silu(c) @ w_ada[l] + b_ada[l])[:, None, :]
        mod_ctx = jax.nn.silu(c_ctx) @ w_ada[l] + b_ada[l]
        sh1, sc1, gt1, sh2, sc2, gt2 = jnp.split(mod_lat, 6, axis=-1)
        csh1, csc1, cgt1, csh2, csc2, cgt2 = jnp.split(mod_ctx, 6, axis=-1)
        log_g = jax.nn.log_sigmoid(ret_decay_logit[l].astype(jnp.float32))
        lg_f, lg_b = log_g[0], log_g[1]

        h_lat = modulate(x_lat, norm_pre_mix[l], sh1, sc1)
        h_ctx = modulate(x_ctx, norm_pre_mix[l], csh1, csc1)
        rq, rk, rv, rg, nq, nk, nv, gates = jnp.split(h_lat @ w_in[l], SPLIT_AT, axis=-1)
        crq, crk, crv, crg, cnq, cnk, cnv, cgates = jnp.split(h_ctx @ w_in[l], SPLIT_AT, axis=-1)

        crk_h = to_heads(crk, RET_HEADS) * k_scale
        crv_h = to_heads(crv, RET_HEADS)
        s_f = retention_final_state(crk_h, crv_h, lg_f)
        s_b = retention_final_state(flip_seq(crk_h), flip_seq(crv_h), lg_b)
        q_r = apply_rope(to_heads(rq, RET_HEADS), cos, sin)
        k_r = apply_rope(to_heads(rk, RET_HEADS), cos, sin) * k_scale
        o_ret = bidirectional_retention(q_r, k_r, to_heads(rv, RET_HEADS), lg_f, lg_b, s_f, s_b)
        y_ret = retention_readout(o_ret, rg)

        cnk_h = to_heads(cnk, NA_HEADS)
        cnv_h = to_heads(cnv, NA_HEADS)
        o_na = neighborhood_attention(to_heads(nq, NA_HEADS), to_heads(nk, NA_HEADS),
                                      to_heads(nv, NA_HEADS), cnk_h, cnv_h, na_rpb[l])
        y_na = from_heads(o_na)

        y = merge_branches(y_ret, y_na, gates, w_ret_out[l], w_na_out[l], w_o[l])
        x_lat_next = x_lat + gt1 * rmsnorm(y, norm_post_mix[l])

        if not last:
            zero_state = jnp.zeros((x_ctx.shape[0], RET_HEADS, RET_DK, RET_DV), jnp.float32)
            co_ret = bidirectional_retention(to_heads(crq, RET_HEADS), crk_h, crv_h, lg_f, lg_b,
                                             zero_state, zero_state)
            cy_ret = retention_readout(co_ret, crg)
            cy_na = from_heads(context_attention(to_heads(cnq, NA_HEADS), cnk_h, cnv_h))
            cy = merge_branches(cy_ret, cy_na, cgates, w_ret_out[l], w_na_out[l], w_o[l])
            x_ctx = x_ctx + cgt1 * rmsnorm(cy, norm_post_mix[l])
            ch = modulate(x_ctx, norm_pre_ffn[l], csh2, csc2)
            x_ctx = x_ctx + cgt2 * rmsnorm(squared_relu_mlp(ch, w_ff1[l], w_ff2[l]), norm_post_ffn[l])

        x_lat = x_lat_next
        h2 = modulate(x_lat, norm_pre_ffn[l], sh2, sc2)
        x_lat = x_lat + gt2 * rmsnorm(squared_relu_mlp(h2, w_ff1[l], w_ff2[l]), norm_post_ffn[l])
    return x_lat
```

```python
import numpy as np
import ml_dtypes
import concourse.bass as bass
import concourse.mybir as mybir
from concourse.bass_utils import run_bass_kernel_spmd

F32 = mybir.dt.float32
BF16 = mybir.dt.bfloat16
U8 = mybir.dt.uint8
AF = mybir.ActivationFunctionType
ALU = mybir.AluOpType
AX = mybir.AxisListType

D = 1024
KC = 8
CTX = 256
GRID_W = 64
EPS = 1e-6
CH = 4096


class Prog:
    ENGS = ("pe", "act", "dve", "pool", "sp")

    def __init__(self, nc, n_dma_sems=16):
        self.nc = nc
        self.ops = []
        self.last_w = {}
        self.readers = {}
        self.n_dma_sems = n_dma_sems
        self.pending = {e: set() for e in self.ENGS}
        self.bar_start = 0
        self.nbar = 0

    def op(self, eng, fn, reads=(), writes=(), dma=False):
        oid = len(self.ops)
        deps = set()
        for k in list(reads) + list(writes):
            if k in self.last_w:
                deps.add(self.last_w[k])
        for k in writes:
            for r in self.readers.get(k, ()):
                deps.add(r)
        deps |= self.pending[eng]
        self.pending[eng] = set()
        deps.discard(oid)
        self.ops.append(dict(eng=eng, fn=fn, deps=deps, dma=dma, has_dep=False))
        for k in reads:
            self.readers.setdefault(k, []).append(oid)
        for k in writes:
            self.last_w[k] = oid
            self.readers[k] = []
        return oid

    def dma(self, q, out, in_, reads=(), writes=(), **kw):
        def fn(e):
            return e.dma_start(out=out, in_=in_, **kw)
        return self.op(q, fn, reads, writes, dma=True)

    def barrier(self, scratch):
        n = self.nbar
        self.nbar += 1
        dmas = [i for i in range(self.bar_start, len(self.ops)) if self.ops[i]["dma"]]
        marks = []
        marks.append(self.op("act", lambda e: e.copy(out=scratch["act"], in_=scratch["act"]), writes=[("bar", n, "act")]))
        marks.append(self.op("dve", lambda e: e.memset(scratch["dve"], 0.0), writes=[("bar", n, "dve")]))
        marks.append(self.op("pool", lambda e: e.memset(scratch["pool"], 0.0), writes=[("bar", n, "pool")]))
        for e in self.ENGS:
            self.pending[e] = set(marks) | set(dmas)
        self.last_w = {}
        self.readers = {}
        self.bar_start = len(self.ops)

    def emit(self, final_wait_eng="sp"):
        nc = self.nc
        ops = self.ops
        for i, o in enumerate(ops):
            keep = set()
            for d in o["deps"]:
                od = ops[d]
                if (not od["dma"]) and od["eng"] == o["eng"] and o["eng"] == "pe" and not o["dma"]:
                    continue
                keep.add(d)
            o["deps"] = keep
            for d in keep:
                ops[d]["has_dep"] = True
        tail = [i for i, o in enumerate(ops) if o["dma"] and not o["has_dep"]]
        for i in tail:
            ops[i]["has_dep"] = True
        cnt = {e: 0 for e in self.ENGS}
        for o in ops:
            if not o["dma"] and o["has_dep"]:
                o["seq"] = cnt[o["eng"]]
                cnt[o["eng"]] += 1
        sems = {}
        for e in self.ENGS:
            n = (cnt[e] + CH - 1) // CH
            sems[e] = [nc.alloc_semaphore(name=f"s_{e}_{j}") for j in range(n)]
        dsems = [nc.alloc_semaphore(name=f"s_dma_{j}") for j in range(self.n_dma_sems)]
        dcount = [0] * self.n_dma_sems
        dnext = 0
        waited = {e: {} for e in self.ENGS}

        def plan_wait(o, e, sem, val):
            key = id(sem)
            if waited[e].get(key, 0) >= val:
                return
            waited[e][key] = val
            o["waits"].append((sem, val))

        for i, o in enumerate(ops):
            e = o["eng"]
            o["waits"] = []
            for d in sorted(o["deps"]):
                od = ops[d]
                if od["dma"]:
                    plan_wait(o, e, od["dsem"], od["dval"])
                else:
                    s = od["seq"]
                    plan_wait(o, e, sems[od["eng"]][s // CH], s % CH + 1)
            if o["dma"] and e == "pool":
                sw = nc.alloc_semaphore(name=f"s_swdma_{i}")
                o["dsem"] = sw
                o["dval"] = 16
                o["inc"] = (sw, 16)
            elif o["dma"]:
                j = dnext
                dnext = (dnext + 1) % self.n_dma_sems
                if dcount[j] > 0:
                    plan_wait(o, e, dsems[j], dcount[j])
                dcount[j] += 16
                o["dsem"] = dsems[j]
                o["dval"] = dcount[j]
                o["inc"] = (dsems[j], 16)
            elif o["has_dep"]:
                s = o["seq"]
                o["inc"] = (sems[e][s // CH], 1)
            else:
                o["inc"] = None
        final_waits = []
        fo = dict(waits=final_waits)
        for i in tail:
            plan_wait(fo, final_wait_eng, ops[i]["dsem"], ops[i]["dval"])

        def run_engine(ename, eng):
            for o in ops:
                if o["eng"] != ename:
                    continue
                for (sem, val) in o["waits"]:
                    eng.wait_ge(sem, val)
                ins = o["fn"](eng)
                if o["inc"] is not None:
                    ins.then_inc(o["inc"][0], o["inc"][1])
            if ename == final_wait_eng:
                for (sem, val) in final_waits:
                    eng.wait_ge(sem, val)

        with nc.Block() as block:
            @block.sync
            def _(eng):
                run_engine("sp", eng)

            @block.tensor
            def _(eng):
                run_engine("pe", eng)

            @block.scalar
            def _(eng):
                run_engine("act", eng)

            @block.vector
            def _(eng):
                run_engine("dve", eng)

            @block.gpsimd
            def _(eng):
                run_engine("pool", eng)
        return dict(n_ops=len(ops), cnt=cnt)


class Arena:
    def __init__(self, ap_u8, size):
        self.ap = ap_u8
        self.size = size
        self.off = 0
        self.peak = 0

    def alloc(self, shape, dtype):
        esz = {F32: 4, BF16: 2}[dtype]
        n = int(np.prod(shape))
        nbytes = (n * esz + 63) // 64 * 64
        assert self.off + nbytes <= self.size, f"arena overflow {self.off}+{nbytes}>{self.size}"
        v = self.ap[:, self.off:self.off + n * esz].bitcast(dtype)
        self.off += nbytes
        self.peak = max(self.peak, self.off)
        if len(shape) == 1:
            return v
        names = " ".join(f"d{i}" for i in range(len(shape)))
        kw = {f"d{i}": int(s) for i, s in enumerate(shape)}
        return v.rearrange(f"p ({names}) -> p {names}", **kw)

    def mark(self):
        return self.off

    def release(self, m):
        self.off = m


def na_structure(rows):
    T = rows // 2
    types = {}
    per_t = []
    for t in range(T):
        lst = []
        for u in range(T):
            vis = []
            anyv = False
            for kr in range(2):
                for qr in range(2):
                    r = 2 * t + qr
                    r0 = min(max(r - 4, 0), rows - 8)
                    v = r0 <= 2 * u + kr < r0 + 8
                    vis.append(v)
                    anyv = anyv or v
            if not anyv:
                continue
            key = (u - t, tuple(vis))
            if key not in types:
                types[key] = len(types)
            lst.append((u, types[key]))
        per_t.append(lst)
    return per_t, types


def na_consts(types):
    nt = len(types)
    mask = np.zeros((nt, 128, 128), np.float32)
    idr = np.zeros((nt, 128, 128), np.int64)
    idc = np.zeros((nt, 128, 128), np.int64)
    kc = np.arange(64)[:, None]
    qc = np.arange(64)[None, :]
    c0 = np.clip(qc - 8, 0, 48)
    colok = (kc >= c0) & (kc < c0 + 16)
    dc = np.clip(kc - qc + 15, 0, 30)
    for (delta, vis), ti in types.items():
        for kr in range(2):
            for qr in range(2):
                v = vis[kr * 2 + qr]
                dr = int(np.clip(2 * delta + kr - qr + 7, 0, 14))
                blk = np.where(colok & v, 0.0, -30000.0).astype(np.float32)
                mask[ti, kr * 64:(kr + 1) * 64, qr * 64:(qr + 1) * 64] = blk
                idr[ti, kr * 64:(kr + 1) * 64, qr * 64:(qr + 1) * 64] = dr
                idc[ti, kr * 64:(kr + 1) * 64, qr * 64:(qr + 1) * 64] = dc
    return mask, idr, idc


def rope_tables(n):
    pos = np.arange(n)
    row = (pos // GRID_W).astype(np.float32)
    col = (pos % GRID_W).astype(np.float32)
    inv = (10000.0 ** (-np.arange(0, 32, 2, dtype=np.float32) / 32)).astype(np.float32)
    ang = np.concatenate([row[:, None] * inv, col[:, None] * inv], axis=-1).astype(np.float32)
    return np.cos(ang).astype(np.float32), np.sin(ang).astype(np.float32)


class _Stop(Exception):
    pass


def build(SEQ, debug=(), upto=None):
    NT = SEQ // 128
    ROWS = SEQ // 64
    NB = SEQ // 512
    per_t, types = na_structure(ROWS)
    NTYPE = len(types)
    nc = bass.Bass("TRN2", target_bir_lowering=False)

    def din(name, shape, dt=F32):
        return nc.dram_tensor(name, list(shape), dt, kind="ExternalInput").ap()

    x = din("x", [SEQ, D])
    ctx = din("ctx", [CTX, D])
    c_fm = din("c_fm", [128, KC, 2])
    w_ada = din("w_ada", [D, 6 * D])
    bada_fm = din("bada_fm", [128, 48])
    b_ada = din("b_ada", [1, 6 * D])
    gpre_fm = din("gpre_fm", [128, 2, KC])
    gpost = din("gpost", [2, D])
    w_in = din("w_in", [D, 5632])
    w_ro = din("w_ro", [512, D])
    w_no = din("w_no", [512, D])
    w_o = din("w_o", [D, D])
    w_ff1 = din("w_ff1", [D, 4 * D])
    w_ff2 = din("w_ff2", [4 * D, D])
    lgt_pair = din("lgt_pair", [128, 8])
    lgt_bc = din("lgt_bc", [128, 16])
    ident_d = din("ident", [128, 128])
    cmat = din("cmat", [128, 4, 128])
    colc_d = din("colc", [128, 6])
    rowc_d = din("rowc", [128, 2, 128])
    cos_d = din("cos_tm", [128, NT, 32])
    sin_d = din("sin_tm", [128, NT, 32])
    rpbB = din("rpbB", [8, 128, NTYPE, 128])
    maskB_d = din("maskB", [128, NTYPE, 128])
    out = nc.dram_tensor("out", [SEQ, D], F32, kind="ExternalOutput").ap()
    yT_d = nc.dram_tensor("yT_scratch", [8, 128, SEQ], BF16, kind="Internal").ap()
    wgb = nc.dram_tensor("wg_bf", [D, 2048], BF16, kind="Internal").ap()
    wrob = nc.dram_tensor("wro_bf", [512, D], BF16, kind="Internal").ap()
    wnob = nc.dram_tensor("wno_bf", [512, D], BF16, kind="Internal").ap()
    wob = nc.dram_tensor("wo_bf", [D, D], BF16, kind="Internal").ap()
    w1b = nc.dram_tensor("w1_bf", [D, 4 * D], BF16, kind="Internal").ap()
    w2b = nc.dram_tensor("w2_bf", [4 * D, D], BF16, kind="Internal").ap()
    wadab = nc.dram_tensor("wada_bf", [2, D, D], BF16, kind="Internal").ap()
    dbg = {}
    for name, shape, dt in debug:
        dbg[name] = nc.dram_tensor(name, list(shape), dt, kind="ExternalOutput").ap()

    P = Prog(nc)
    ARENA_BYTES = 207 * 1024
    cm = nc.sbuf_tensor("arena", [128, ARENA_BYTES], U8)
    arena_h = cm.__enter__()
    A = Arena(arena_h, ARENA_BYTES)
    cmp_ = nc.psum_tensor("ps", [128, 8, 512], F32)
    ps = cmp_.__enter__()

    def PB(b):
        return ps[:, b, :]

    def PK(*bs):
        return [("ps", b) for b in bs]

    def body():
        ident = A.alloc([128], F32)
        identb = A.alloc([128], BF16)
        scr = {e: A.alloc([16], F32) for e in ("act", "dve", "pool")}
        S1 = A.alloc([KC, 2], F32)
        SH1 = A.alloc([KC, 2], F32)
        S2 = A.alloc([KC, 2], F32)
        SH2 = A.alloc([KC, 2], F32)
        scb = A.alloc([KC, 2], BF16)
        sc_rep = A.alloc([KC, 128], BF16)
        gpre = A.alloc([2, KC], F32)
        badafm = A.alloc([48], F32)

        P.dma("sp", ident, ident_d, writes=["ident"])
        P.op("dve", lambda e: e.tensor_copy(out=identb, in_=ident), reads=["ident"], writes=["identb"])
        for e_ in ("act", "dve", "pool"):
            pass
        P.op("dve", lambda e: e.memset(scr["dve"], 0.0), writes=["scr_dve"])
        P.op("pool", lambda e: e.memset(scr["pool"], 0.0), writes=["scr_pool"])
        P.op("dve", lambda e: e.memset(scr["act"], 0.0), writes=["scr_act"])
        P.dma("sp", gpre, gpre_fm, writes=["gpre"])
        P.dma("sp", badafm, bada_fm, writes=["badafm"])

        m_phase2 = None

        def norm_p1(xt_ap, xt_key, tmp, inplace=False):
            junk, ss, rstd, xn = tmp["junk"], tmp["ss"], tmp["rstd"], tmp["xn"]
            tk = tmp["key"]
            jkey = tmp.get("junk_key", (tk, "junk"))
            P.op("act", lambda e: e.activation(out=junk, in_=xt_ap, func=AF.Square, accum_out=ss),
                 reads=[xt_key], writes=[jkey, (tk, "ss")])
            P.op("act", lambda e: e.activation(out=rstd, in_=ss, func=AF.Sqrt, scale=1.0 / D, bias=EPS),
                 reads=[(tk, "ss")], writes=[(tk, "rstd")])
            P.op("dve", lambda e: e.reciprocal(out=rstd, in_=rstd), writes=[(tk, "rstd")])
            xnk = xt_key if inplace else (tk, "xn")
            P.op("dve", lambda e: e.tensor_scalar(out=xn, in0=xt_ap, scalar1=rstd[:, 0:1], scalar2=None, op0=ALU.mult),
                 reads=[(tk, "rstd")] + ([] if inplace else [xt_key]), writes=[xnk])
            return xnk

        def norm_p2(xnk, dst_fn, dst_keys, Sc, Sh, col, banks, tmp):
            xn = tmp["xn"]
            for half in range(2):
                b = banks[half]

                def tr(e, half=half, b=b):
                    for cc in range(4):
                        c = half * 4 + cc
                        ins = e.transpose(out=PB(b)[:, cc * 128:(cc + 1) * 128], in_=xn[:, c * 128:(c + 1) * 128], identity=ident)
                    return ins
                P.op("pe", tr, reads=[xnk, "ident"], writes=PK(b))
                for cc in range(4):
                    c = half * 4 + cc
                    if c % 2 == 0:
                        P.op("act", lambda e, c=c, cc=cc, b=b: e.activation(
                            out=dst_fn(c), in_=PB(b)[:, cc * 128:(cc + 1) * 128], func=AF.Identity,
                            scale=Sc[:, c, col:col + 1], bias=Sh[:, c, col:col + 1]),
                            reads=["mod"], writes=PK(b) + [dst_keys[c]])
                    else:
                        P.op("dve", lambda e, c=c, cc=cc, b=b: e.tensor_scalar(
                            out=dst_fn(c), in0=PB(b)[:, cc * 128:(cc + 1) * 128],
                            scalar1=Sc[:, c, col:col + 1], scalar2=Sh[:, c, col:col + 1], op0=ALU.mult, op1=ALU.add),
                            reads=["mod"], writes=PK(b) + [dst_keys[c]])

        def norm_transpose(xt_ap, xt_key, dst_fn, dst_keys, Sc, Sh, col, banks, tmp, tag, inplace=False):
            xnk = norm_p1(xt_ap, xt_key, tmp, inplace)
            norm_p2(xnk, dst_fn, dst_keys, Sc, Sh, col, banks, tmp)

        m0 = A.mark()
        cm_t = A.alloc([4, 128], F32)
        colc = A.alloc([6], F32)
        rowc = A.alloc([2, 128], F32)
        lgp = A.alloc([8], F32)
        lgb = A.alloc([16], F32)
        cfm = A.alloc([KC, 2], F32)
        A.release(m0)
        DT = A.alloc([8, 128], F32)
        kw = A.alloc([8, 2], F32)
        ckw = A.alloc([2, 8, 2], F32)
        QW = A.alloc([2, 128], F32)
        GL = A.alloc([8], F32)
        rowc = A.alloc([2, 128], F32)
        lgp = A.alloc([8], F32)
        hT = A.alloc([KC, SEQ], BF16)
        hcT = A.alloc([KC, CTX], BF16)
        m_after_persist = A.mark()
        cm_t = A.alloc([4, 128], F32)
        colc = A.alloc([6], F32)
        lgb = A.alloc([16], F32)
        cfm = A.alloc([KC, 2], F32)
        tmpA = A.alloc([128], F32)
        tmpB = A.alloc([128], F32)
        arg16 = A.alloc([16], F32)
        argc = A.alloc([2, 8, 2], F32)
        wbuf0 = A.alloc([KC, 1024], BF16)
        modfm = A.alloc([4, KC, 2], F32)

        P.dma("sp", cm_t, cmat, writes=["cmat"])
        P.dma("sp", colc, colc_d, writes=["colc"])
        P.dma("sp", rowc, rowc_d, writes=["rowc"])
        P.dma("sp", lgp, lgt_pair, writes=["lgp"])
        P.dma("sp", lgb, lgt_bc, writes=["lgb"])
        P.dma("sp", cfm, c_fm, writes=["cfm"])

        for t_, k_ in ((lgp, "lgp"), (lgb, "lgb")):
            P.op("act", lambda e, t_=t_: e.activation(out=t_, in_=t_, func=AF.Exp, scale=-1.0), writes=[k_])
            P.op("act", lambda e, t_=t_: e.activation(out=t_, in_=t_, func=AF.Ln, bias=1.0), writes=[k_])
            P.op("dve", lambda e, t_=t_: e.tensor_scalar(out=t_, in0=t_, scalar1=-1.0, scalar2=None, op0=ALU.mult), writes=[k_])
        for h in range(8):
            P.op("act", lambda e, h=h: e.activation(out=tmpA, in_=cm_t[:, 0, :], func=AF.Exp, scale=lgb[:, 2 * h:2 * h + 1]),
                 reads=["cmat", "lgb"], writes=["tmpA"])
            P.op("act", lambda e, h=h: e.activation(out=tmpB, in_=cm_t[:, 1, :], func=AF.Exp, scale=lgb[:, 2 * h + 1:2 * h + 2]),
                 reads=["cmat", "lgb"], writes=["tmpB"])
            P.op("dve", lambda e: e.tensor_tensor(out=tmpA, in0=tmpA, in1=cm_t[:, 2, :], op=ALU.mult), reads=["cmat"], writes=["tmpA"])
            P.op("dve", lambda e: e.tensor_tensor(out=tmpB, in0=tmpB, in1=cm_t[:, 3, :], op=ALU.mult), reads=["cmat"], writes=["tmpB"])
            P.op("dve", lambda e, h=h: e.tensor_tensor(out=DT[:, h, :], in0=tmpA, in1=tmpB, op=ALU.add),
                 reads=["tmpA", "tmpB"], writes=["DT"])
        lgb3 = lgb.rearrange("p (h d) -> p h d", d=2)
        arg3 = arg16.rearrange("p (h d) -> p h d", d=2)
        for d_ in range(2):
            P.op("dve", lambda e, d_=d_: e.tensor_scalar(out=arg3[:, :, d_], in0=lgb3[:, :, d_], scalar1=colc[:, d_:d_ + 1], scalar2=None, op0=ALU.mult),
                 reads=["lgb", "colc"], writes=["arg16"])
        P.op("act", lambda e: e.activation(out=arg16, in_=arg16, func=AF.Exp), writes=["arg16"])
        P.op("dve", lambda e: e.tensor_scalar(out=kw.rearrange("p h d -> p (h d)"), in0=arg16, scalar1=0.125, scalar2=None, op0=ALU.mult),
             reads=["arg16"], writes=["kw"])
        for ct in range(2):
            for d_ in range(2):
                cc_ = 2 + ct if d_ == 0 else 4 + ct
                P.op("dve", lambda e, ct=ct, d_=d_, cc_=cc_: e.tensor_scalar(out=argc[:, ct, :, d_], in0=lgb3[:, :, d_], scalar1=colc[:, cc_:cc_ + 1], scalar2=None, op0=ALU.mult),
                     reads=["lgb", "colc"], writes=["argc"])
        P.op("act", lambda e: e.activation(out=argc, in_=argc, func=AF.Exp), writes=["argc"])
        P.op("dve", lambda e: e.tensor_scalar(out=ckw, in0=argc, scalar1=0.125, scalar2=None, op0=ALU.mult), reads=["argc"], writes=["ckw"])
        P.op("act", lambda e: e.activation(out=GL, in_=lgp, func=AF.Exp, scale=128.0), reads=["lgp"], writes=["GL"])

        P.op("act", lambda e: e.activation(out=scb, in_=cfm, func=AF.Silu), reads=["cfm"], writes=["scb"])
        P.op("dve", lambda e: e.tensor_copy(out=sc_rep, in_=scb[:, :, 0:1].to_broadcast([128, KC, 128])), reads=["scb"], writes=["sc_rep"])

        def load_w(dst, src_rows_cols, key, nk=KC):
            P.dma("pool", dst, src_rows_cols.rearrange("(k p) n -> p k n", p=128), writes=[key])

        wbuf1 = A.alloc([KC, 1024], BF16)
        wbufs = {0: (wbuf1, "wbuf1"), 1: (wbuf0, "wbuf0"), 3: (wbuf1, "wbuf1"), 4: (wbuf0, "wbuf0")}

        def ada_load(j):
            wb_, key_ = wbufs[j]
            load_w(wb_, w_ada[:, j * D:(j + 1) * D], key_)

        def ada_mm(mi, j):
            wb_, key_ = wbufs[j]

            def mm(e):
                for cc in range(8):
                    for k in range(KC):
                        ins = e.matmul(PB(0)[:, cc * 2:cc * 2 + 2], lhsT=wb_[:, k, cc * 128:(cc + 1) * 128], rhs=scb[:, k, :],
                                       start=(k == 0), stop=(k == KC - 1))
                return ins
            P.op("pe", mm, reads=[key_, "scb"], writes=PK(0))
            P.op("dve", lambda e: e.tensor_tensor(
                out=modfm[:, mi, :, :], in0=PB(0)[:, 0:16].rearrange("p (c t) -> p c t", t=2),
                in1=badafm[:, j * 8:(j + 1) * 8].unsqueeze(2).to_broadcast([128, KC, 2]), op=ALU.add),
                reads=["badafm"], writes=PK(0) + [("modfm", mi)])

        def ada_fin(Sx, SHx, mi_sh, mi_sc, gi):
            P.op("dve", lambda e: e.scalar_tensor_tensor(
                out=Sx, in0=modfm[:, mi_sc, :, :], scalar=1.0, in1=gpre[:, gi, :].unsqueeze(2).to_broadcast([128, KC, 2]),
                op0=ALU.add, op1=ALU.mult), reads=[("modfm", mi_sc), "gpre"], writes=["mod"])
            P.op("dve", lambda e: e.tensor_copy(out=SHx, in_=modfm[:, mi_sh, :, :]), reads=[("modfm", mi_sh)], writes=["mod"])

        ada_load(1)
        ada_load(0)
        ada_mm(1, 1)
        ada_mm(0, 0)
        ada_fin(S1, SH1, 0, 1, 0)
        ada_load(4)
        ada_load(3)

        if upto == "p0":
            raise _Stop()
        xts = [A.alloc([D], F32) for _ in range(2)]
        tmps = []
        for i in range(2):
            tmps.append(dict(junk=A.alloc([D], BF16), ss=A.alloc([1], F32), rstd=A.alloc([1], F32), xn=A.alloc([D], F32), key=("nt", i)))
        for i in range(NT + 2):
            bi = i % 2
            src = x[i * 128:(i + 1) * 128, :] if i < NT else ctx[(i - NT) * 128:(i - NT + 1) * 128, :]
            P.dma("sp", xts[bi], src, writes=[("xt", bi)])
            if i < NT:
                dst_fn = (lambda c, i=i: hT[:, c, i * 128:(i + 1) * 128])
                dkeys = [("hT", c, i) for c in range(KC)]
                col = 0
            else:
                dst_fn = (lambda c, i=i: hcT[:, c, (i - NT) * 128:(i - NT + 1) * 128])
                dkeys = [("hcT", c, i - NT) for c in range(KC)]
                col = 1
            norm_transpose(xts[bi], ("xt", bi), dst_fn, dkeys, S1, SH1, col, (2 * bi, 2 * bi + 1), tmps[bi], "p1")
        ada_mm(3, 4)
        ada_mm(2, 3)
        ada_fin(S2, SH2, 2, 3, 1)
        if "hT" in dbg:
            P.dma("sp", dbg["hT"], hT, reads=[("hT", c, i) for c in range(KC) for i in range(NT)])
        P.barrier(scr)
        A.release(m_after_persist)
        if upto == "p1":
            raise _Stop()

        maskB = A.alloc([NTYPE, 128], BF16)
        P.dma("pool", maskB, maskB_d, writes=["maskB"])
        wb = A.alloc([KC, 7, 128], BF16)
        BT = A.alloc([2, NTYPE, 128], BF16)
        rpst = A.alloc([NTYPE, 128], BF16)
        slabQ = A.alloc([SEQ], BF16)
        slabK = A.alloc([SEQ], BF16)
        rv = A.alloc([NT, 128], BF16)
        nva = A.alloc([NT, 2, 65], BF16)
        srg = A.alloc([NT, 128], BF16)
        DS = A.alloc([NT, 2, 64], F32)
        Rb = A.alloc([NT, 2, 64], BF16)
        BLK = 4
        rtmp = [A.alloc([BLK, 4, 32], F32) for _ in range(2)] * 2
        cs_t = [A.alloc([2, BLK, 32], F32) for _ in range(2)]
        qk_tm = [A.alloc([BLK, 256], BF16) for _ in range(2)]
        Vfb = [A.alloc([BLK, 2, 2, 64], BF16) for _ in range(2)]
        crk = A.alloc([2, 128], BF16)
        cVfb = A.alloc([2, 2, 2, 64], BF16)
        cnva = A.alloc([2, 2, 65], BF16)
        cnkT = A.alloc([CTX], BF16)
        PT = [A.alloc([7, 128], BF16) for _ in range(2)]
        SDT = [A.alloc([2, 4, 128], BF16) for _ in range(2)]
        QfbT = [A.alloc([2, 4, 128], BF16) for _ in range(2)]
        sq = A.alloc([512], F32)
        ms = A.alloc([8], F32)
        on = A.alloc([512], F32)
        ytile = [A.alloc([4, 128], BF16) for _ in range(2)]
        ystage = [A.alloc([512], BF16) for _ in range(2)]
        rc = A.alloc([2], F32)

        qm = [[A.alloc([128], BF16) for _ in range(2)] for _ in range(2)]
        for hh_ in range(2):
            for par_ in range(2):
                P.op("pool", lambda e, hh_=hh_, par_=par_: e.memset(qm[hh_][par_], 0.0), writes=[("qm", hh_, par_)])
        P.op("pool", lambda e: e.memset(nva, 1.0), writes=["nva_init"])
        P.op("pool", lambda e: e.memset(cnva, 1.0), writes=["cnva_init"])

        def pair_body(hp):
            hk = ("hp", hp)
            for d_ in range(2):
                P.op("act", lambda e, d_=d_: e.activation(out=QW[:, d_, :], in_=rowc[:, d_, :], func=AF.Exp,
                                                           scale=lgp[:, hp * 2 + d_:hp * 2 + d_ + 1]),
                     writes=["QW"])
            for s in range(7):
                c0 = s * 512 + hp * 128
                P.dma("pool", wb[:, :, s, :], w_in[:, c0:c0 + 128].rearrange("(k p) n -> p k n", p=128), writes=[("wb", s)])
            wbk = [("wb", s) for s in range(7)]
            if hp == 0:
                P.dma("pool", wgb, w_in[:, 3584:5632], writes=["wgb"])
                P.dma("pool", wrob, w_ro, writes=["wrob"])
                P.dma("pool", wnob, w_no, writes=["wnob"])
                P.dma("pool", wob, w_o, writes=["wob"])
                P.dma("pool", wadab[0], w_ada[:, 2 * D:3 * D], writes=["wadab0"])
                P.dma("pool", wadab[1], w_ada[:, 5 * D:6 * D], writes=["wadab1"])
                P.dma("pool", w1b.rearrange("r (a c) -> (r a) c", a=2), w_ff1.rearrange("r (a c) -> (r a) c", a=2), writes=["w1b"])
                P.dma("pool", w2b, w_ff2, writes=["w2b"])
            for hh in range(2):
                P.dma("pool", rpst, rpbB[2 * hp + hh], writes=["rpst"])
                P.op("dve", lambda e, hh=hh: e.tensor_tensor(out=BT[:, hh, :, :], in0=rpst, in1=maskB, op=ALU.add),
                     reads=["rpst", "maskB"], writes=[("BT", hh)])
            for ct in range(2):
                def mm(e, ct=ct):
                    for k in range(KC):
                        ins = e.matmul(PB(6)[:, 0:256], lhsT=hcT[:, k, ct * 128:(ct + 1) * 128], rhs=wb[:, k, 1:3, :].rearrange("p s n -> p (s n)"),
                                       start=(k == 0), stop=(k == KC - 1))
                    for k in range(KC):
                        ins = e.matmul(PB(6)[:, 256:384], lhsT=hcT[:, k, ct * 128:(ct + 1) * 128], rhs=wb[:, k, 6, :],
                                       start=(k == 0), stop=(k == KC - 1))
                    return ins
                P.op("pe", mm, reads=wbk + ["hcT"], writes=PK(6))
                P.op("act", lambda e, ct=ct: e.copy(out=crk[:, ct, :], in_=PB(6)[:, 0:128]), writes=PK(6) + [("crk", ct)])
                for hh in range(2):
                    for d_ in range(2):
                        P.op("dve", lambda e, ct=ct, hh=hh, d_=d_: e.tensor_scalar(
                            out=cVfb[:, ct, hh, d_, :], in0=PB(6)[:, 128 + hh * 64:128 + (hh + 1) * 64],
                            scalar1=ckw[:, ct, 2 * hp + hh, d_:d_ + 1], scalar2=None, op0=ALU.mult),
                            reads=["ckw"], writes=PK(6) + [("cVfb", ct)])
                P.op("dve", lambda e, ct=ct: e.tensor_copy(out=cnva[:, ct, :, 0:64], in_=PB(6)[:, 256:384].rearrange("p (h d) -> p h d", d=64)),
                     reads=["cnva_init"], writes=PK(6) + [("cnva", ct)])

            def mm(e):
                for k in range(KC):
                    ins = e.matmul(PB(7)[:, 0:CTX], lhsT=wb[:, k, 5, :], rhs=hcT[:, k, :], start=(k == 0), stop=(k == KC - 1))
                return ins
            P.op("pe", mm, reads=wbk + ["hcT"], writes=PK(7))
            P.op("act", lambda e: e.copy(out=cnkT, in_=PB(7)[:, 0:CTX]), writes=PK(7) + ["cnkT"])

            def mm(e):
                for hh in range(2):
                    for ct in range(2):
                        ins = e.matmul(PB(6)[hh * 64:(hh + 1) * 64, 0:128], lhsT=crk[:, ct, hh * 64:(hh + 1) * 64],
                                       rhs=cVfb[:, ct, hh, :, :].rearrange("p a b -> p (a b)"), start=(ct == 0), stop=(ct == 1))
                return ins
            P.op("pe", mm, reads=[("crk", 0), ("crk", 1), ("cVfb", 0), ("cVfb", 1)], writes=PK(6))
            P.op("dve", lambda e: e.tensor_copy(out=DS[:, 0, 0, :], in_=PB(6)[:, 0:64]), writes=PK(6) + [("DS", 0, 0)])
            P.op("dve", lambda e: e.tensor_copy(out=DS[:, NT - 1, 1, :], in_=PB(6)[:, 64:128]), writes=PK(6) + [("DS", NT - 1, 1)])

            if upto == "p2ctx":
                raise _Stop()
            def blockA(b0):
                bi = (b0 // BLK) % 2
                qk_, vf_ = qk_tm[bi], Vfb[bi]
                pbanks = [2, 3, 4, 5]
                for ii in range(BLK):
                    i = b0 + ii
                    pb = pbanks[ii]

                    def mm(e, i=i, pb=pb):
                        for k in range(KC):
                            ins = e.matmul(PB(pb)[:, 0:512], lhsT=hT[:, k, i * 128:(i + 1) * 128], rhs=wb[:, k, 0:4, :].rearrange("p s n -> p (s n)"),
                                           start=(k == 0), stop=(k == KC - 1))
                        return ins
                    P.op("pe", mm, reads=wbk, writes=PK(pb))
                    P.op("dve", lambda e, i=i, pb=pb: e.tensor_copy(out=rv[:, i, :], in_=PB(pb)[:, 256:384]),
                         writes=PK(pb) + [("rv", i)])
                    P.op("act", lambda e, i=i, pb=pb: e.activation(out=srg[:, i, :], in_=PB(pb)[:, 384:512], func=AF.Silu),
                         writes=PK(pb) + [("srg", i)])
                s5 = ps[:, 2:6, 0:256].rearrange("p b (g t f) -> p b g t f", g=4, t=2)
                q5 = qk_.rearrange("p b (g t f) -> p b g t f", g=4, t=2)
                cst = cs_t[bi]
                P.dma("sp", cst[:, 0, :, :], cos_d[:, b0:b0 + BLK, :], writes=[("cs", bi, 0)])
                P.dma("sp", cst[:, 1, :, :], sin_d[:, b0:b0 + BLK, :], writes=[("cs", bi, 1)])
                cosb = cst[:, 0, :, :].unsqueeze(2).to_broadcast([128, BLK, 4, 32])
                sinb = cst[:, 1, :, :].unsqueeze(2).to_broadcast([128, BLK, 4, 32])
                PKA = PK(2, 3, 4, 5)
                P.op("dve", lambda e, s5=s5, cosb=cosb: e.tensor_tensor(out=rtmp[0], in0=s5[:, :, :, 0, :], in1=cosb, op=ALU.mult),
                     reads=[("cs", bi, 0)], writes=PKA + [("rtmp", 0)])
                P.op("dve", lambda e, s5=s5, sinb=sinb: e.tensor_tensor(out=rtmp[1], in0=s5[:, :, :, 1, :], in1=sinb, op=ALU.mult),
                     reads=[("cs", bi, 1)], writes=PKA + [("rtmp", 1)])
                P.op("pool", lambda e, q5=q5: e.tensor_tensor(out=q5[:, :, :, 0, :], in0=rtmp[0], in1=rtmp[1], op=ALU.subtract),
                     reads=[("rtmp", 0), ("rtmp", 1)], writes=[("qk", bi, 0)])
                P.op("dve", lambda e, s5=s5, sinb=sinb: e.tensor_tensor(out=rtmp[0], in0=s5[:, :, :, 0, :], in1=sinb, op=ALU.mult),
                     reads=[("cs", bi, 1)], writes=PKA + [("rtmp", 0)])
                P.op("dve", lambda e, s5=s5, cosb=cosb: e.tensor_tensor(out=rtmp[1], in0=s5[:, :, :, 1, :], in1=cosb, op=ALU.mult),
                     reads=[("cs", bi, 0)], writes=PKA + [("rtmp", 1)])
                P.op("pool", lambda e, q5=q5: e.tensor_tensor(out=q5[:, :, :, 1, :], in0=rtmp[0], in1=rtmp[1], op=ALU.add),
                     reads=[("rtmp", 0), ("rtmp", 1)], writes=[("qk", bi, 1)])
                qkk = [("qk", bi, 0), ("qk", bi, 1)]
                for hh in range(2):
                    for d_ in range(2):
                        P.op("act", lambda e, hh=hh, d_=d_, vf_=vf_: e.activation(
                            out=vf_[:, :, hh, d_, :], in_=rv[:, b0:b0 + BLK, hh * 64:(hh + 1) * 64], func=AF.Copy,
                            scale=kw[:, 2 * hp + hh, d_:d_ + 1]),
                            reads=[("rv", b0 + ii) for ii in range(BLK)] + ["kw"], writes=[("Vfb", bi, hh, d_)])
                vfk = [("Vfb", bi, hh, d_) for hh in range(2) for d_ in range(2)]
                for which, slab, sk in ((0, slabQ, "slabQ"), (1, slabK, "slabK")):
                    pbt = 6 + which
                    pbv = PB(pbt).bitcast(BF16)

                    def tr(e, which=which, pbv=pbv, qk_=qk_):
                        for ii in range(BLK):
                            ins = e.transpose(out=pbv[:, ii * 128:(ii + 1) * 128], in_=qk_[:, ii, which * 128:(which + 1) * 128], identity=identb)
                        return ins
                    P.op("pe", tr, reads=qkk + ["identb"], writes=PK(pbt))
                    if which == 0:
                        P.op("act", lambda e, pbv=pbv, slab=slab: e.copy(out=slab[:, b0 * 128:(b0 + BLK) * 128], in_=pbv[:, 0:BLK * 128]),
                             writes=PK(pbt) + [(sk, b0 // BLK)])
                    else:
                        P.op("dve", lambda e, pbv=pbv, slab=slab: e.tensor_copy(out=slab[:, b0 * 128:(b0 + BLK) * 128], in_=pbv[:, 0:BLK * 128]),
                             writes=PK(pbt) + [(sk, b0 // BLK)])
                pbd = (b0 // BLK) % 2

                def mm(e, pbd=pbd, qk_=qk_, vf_=vf_):
                    for ii in range(BLK):
                        for hh in range(2):
                            ins = e.matmul(PB(pbd)[hh * 64:(hh + 1) * 64, ii * 128:(ii + 1) * 128],
                                           lhsT=qk_[:, ii, 128 + hh * 64:128 + (hh + 1) * 64],
                                           rhs=vf_[:, ii, hh, :, :].rearrange("p a b -> p (a b)"), start=True, stop=True)
                    return ins
                P.op("pe", mm, reads=qkk + vfk, writes=PK(pbd))
                pv = PB(pbd).rearrange("p (b d f) -> p b d f", d=2, f=64)
                lo, hi = b0, min(b0 + BLK, NT - 1)
                if hi > lo:
                    P.op("act", lambda e, lo=lo, hi=hi, pv=pv: e.copy(out=DS[:, lo + 1:hi + 1, 0, :], in_=pv[:, lo - b0:hi - b0, 0, :]),
                         writes=PK(pbd) + [("DS", c + 1, 0) for c in range(lo, hi)])
                lo2, hi2 = max(b0, 1), b0 + BLK
                if hi2 > lo2:
                    P.op("dve", lambda e, lo2=lo2, hi2=hi2, pv=pv: e.tensor_copy(out=DS[:, lo2 - 1:hi2 - 1, 1, :], in_=pv[:, lo2 - b0:hi2 - b0, 1, :]),
                         writes=PK(pbd) + [("DS", c - 1, 1) for c in range(lo2, hi2)])
            for b0 in range(0, NT, BLK):
                blockA(b0)
            if upto == "p2a":
                raise _Stop()
            for c in range(NT - 1):
                P.op("dve", lambda e, c=c: e.scalar_tensor_tensor(out=DS[:, c + 1, 0, :], in0=DS[:, c, 0, :], scalar=GL[:, 2 * hp:2 * hp + 1],
                                                                  in1=DS[:, c + 1, 0, :], op0=ALU.mult, op1=ALU.add),
                     reads=[("DS", c, 0), "GL"], writes=[("DS", c + 1, 0)])
            for c in range(NT - 1, 0, -1):
                P.op("dve", lambda e, c=c: e.scalar_tensor_tensor(out=DS[:, c - 1, 1, :], in0=DS[:, c, 1, :], scalar=GL[:, 2 * hp + 1:2 * hp + 2],
                                                                  in1=DS[:, c - 1, 1, :], op0=ALU.mult, op1=ALU.add),
                     reads=[("DS", c, 1), "GL"], writes=[("DS", c - 1, 1)])
            P.op("dve", lambda e: e.tensor_copy(out=Rb, in_=DS), reads=[("DS", c, d_) for c in range(NT) for d_ in range(2)], writes=["Rb"])
            if f"Rb{hp}" in dbg:
                P.dma("sp", dbg[f"Rb{hp}"], Rb, reads=["Rb"])

            if upto == "p2scan":
                raise _Stop()
            def blockB(g0):
                gi = (g0 // 4) % 2
                pbo = [2 + gi, 4 + gi]
                yt = ytile[gi]
                qf = QfbT[gi]
                sd = SDT[gi]
                P.op("pool", lambda e, qf=qf: e.tensor_tensor(
                    out=qf, in0=slabQ[:, g0 * 128:(g0 + 4) * 128].rearrange("p (c i) -> p c i", c=4).unsqueeze(1).to_broadcast([128, 2, 4, 128]),
                    in1=QW.unsqueeze(2).to_broadcast([128, 2, 4, 128]), op=ALU.mult),
                    reads=[("slabQ", g0 // BLK), "QW"], writes=[("QfbT", gi)])
                for hh in range(2):
                    def mm(e, hh=hh):
                        for cc in range(4):
                            c = g0 + cc
                            ins = e.matmul(PB(hh)[:, cc * 128:(cc + 1) * 128], lhsT=slabK[hh * 64:(hh + 1) * 64, c * 128:(c + 1) * 128],
                                           rhs=slabQ[hh * 64:(hh + 1) * 64, c * 128:(c + 1) * 128], start=True, stop=True)
                        return ins
                    P.op("pe", mm, reads=[("slabQ", g0 // BLK), ("slabK", g0 // BLK)], writes=PK(hh))
                    P.op("dve", lambda e, hh=hh, sd=sd: e.tensor_tensor(
                        out=sd[:, hh, :, :], in0=PB(hh).rearrange("p (c i) -> p c i", c=4),
                        in1=DT[:, 2 * hp + hh, :].unsqueeze(1).to_broadcast([128, 4, 128]), op=ALU.mult),
                        reads=["DT"], writes=PK(hh) + [("SDT", gi, hh)])
                for hh in range(2):
                    def mm(e, hh=hh, sd=sd, qf=qf):
                        for cc in range(4):
                            c = g0 + cc
                            o_ = PB(pbo[hh])[:, cc * 64:(cc + 1) * 64]
                            e.matmul(o_, lhsT=sd[:, hh, cc, :], rhs=rv[:, c, hh * 64:(hh + 1) * 64], start=True, stop=False)
                            e.matmul(o_, lhsT=qf[hh * 64:(hh + 1) * 64, 0, cc, :], rhs=Rb[hh * 64:(hh + 1) * 64, c, 0, :], start=False, stop=False)
                            ins = e.matmul(o_, lhsT=qf[hh * 64:(hh + 1) * 64, 1, cc, :], rhs=Rb[hh * 64:(hh + 1) * 64, c, 1, :], start=False, stop=True)
                        return ins
                    P.op("pe", mm, reads=[("SDT", gi, hh), ("QfbT", gi), "Rb"] + [("rv", g0 + cc) for cc in range(4)], writes=PK(pbo[hh]))
                for hh in range(2):
                    P.op("act", lambda e, hh=hh: e.activation(out=sq[:, hh * 256:(hh + 1) * 256], in_=PB(pbo[hh])[:, 0:256], func=AF.Square),
                         writes=PK(pbo[hh]) + [("sq", hh)])
                P.op("dve", lambda e: e.tensor_reduce(out=ms, in_=sq.rearrange("p (g f) -> p g f", f=64), axis=AX.X, op=ALU.add),
                     reads=[("sq", 0), ("sq", 1)], writes=["ms"])
                P.op("act", lambda e: e.activation(out=ms, in_=ms, func=AF.Sqrt, scale=1.0 / 64, bias=EPS), writes=["ms"])
                P.op("dve", lambda e: e.reciprocal(out=ms, in_=ms), writes=["ms"])
                for hh in range(2):
                    P.op("dve", lambda e, hh=hh: e.tensor_tensor(
                        out=on[:, hh * 256:(hh + 1) * 256].rearrange("p (g f) -> p g f", f=64),
                        in0=PB(pbo[hh])[:, 0:256].rearrange("p (g f) -> p g f", f=64),
                        in1=ms[:, hh * 4:(hh + 1) * 4].unsqueeze(2).to_broadcast([128, 4, 64]), op=ALU.mult),
                        reads=["ms"], writes=PK(pbo[hh]) + [("on", hh)])
                    P.op("pool", lambda e, hh=hh: e.tensor_tensor(
                        out=yt[:, :, hh * 64:(hh + 1) * 64], in0=on[:, hh * 256:(hh + 1) * 256].rearrange("p (c f) -> p c f", f=64),
                        in1=srg[:, g0:g0 + 4, hh * 64:(hh + 1) * 64], op=ALU.mult),
                        reads=[("on", hh)] + [("srg", g0 + cc) for cc in range(4)],
                        writes=[("ytile", gi, "h", hh)] + ([("ytile", gi)] + [("ytile", gi, cc) for cc in range(4)] if hh == 1 else []))
                pbt = 6 + gi
                pbv = PB(pbt).bitcast(BF16)

                def tr(e, pbv=pbv, yt=yt):
                    for cc in range(4):
                        ins = e.transpose(out=pbv[:, cc * 128:(cc + 1) * 128], in_=yt[:, cc, :], identity=identb)
                    return ins
                P.op("pe", tr, reads=[("ytile", gi), "identb", ("ytile", gi, "h", 0), ("ytile", gi, "h", 1)] + [("ytile", gi, cc) for cc in range(4)], writes=PK(pbt))
                P.op("act", lambda e, pbv=pbv, gi=gi: e.copy(out=ystage[gi], in_=pbv[:, 0:512]), writes=PK(pbt) + [("ystage", gi)])
                P.dma("sp", yT_d[hp, :, g0 * 128:(g0 + 4) * 128], ystage[gi], reads=[("ystage", gi)], writes=[("yT", 0, hp, g0 // 4)])
            for g0 in range(0, NT, 4):
                blockB(g0)

            if upto == "p2b":
                raise _Stop()
            def naproj(nb):
                for which, slab, sk, slot in ((0, slabQ, "slabQ", 4), (1, slabK, "slabK", 5)):
                    pbp = 4 + which

                    def mm(e, nb=nb, slot=slot, pbp=pbp):
                        for k in range(KC):
                            ins = e.matmul(PB(pbp)[:, 0:512], lhsT=wb[:, k, slot, :], rhs=hT[:, k, nb * 512:(nb + 1) * 512],
                                           start=(k == 0), stop=(k == KC - 1))
                        return ins
                    P.op("pe", mm, reads=wbk, writes=PK(pbp))
                    if which == 0:
                        P.op("act", lambda e, nb=nb, pbp=pbp: e.activation(out=slabQ[:, nb * 512:(nb + 1) * 512], in_=PB(pbp), func=AF.Copy, scale=0.125),
                             writes=PK(pbp) + [("slabQ", nb)])
                    else:
                        P.op("dve", lambda e, nb=nb, pbp=pbp: e.tensor_copy(out=slabK[:, nb * 512:(nb + 1) * 512], in_=PB(pbp)),
                             writes=PK(pbp) + [("slabK", nb)])
                pbp = 6 + (nb % 2)

                def mm(e, nb=nb, pbp=pbp):
                    for ii in range(4):
                        i = nb * 4 + ii
                        for k in range(KC):
                            ins = e.matmul(PB(pbp)[:, ii * 128:(ii + 1) * 128], lhsT=hT[:, k, i * 128:(i + 1) * 128], rhs=wb[:, k, 6, :],
                                           start=(k == 0), stop=(k == KC - 1))
                    return ins
                P.op("pe", mm, reads=wbk, writes=PK(pbp))
                P.op("pool" if False else "dve", lambda e, nb=nb, pbp=pbp: e.tensor_copy(
                    out=nva[:, nb * 4:(nb + 1) * 4, :, 0:64], in_=PB(pbp).rearrange("p (i h d) -> p i h d", h=2, d=64)),
                    reads=["nva_init"], writes=PK(pbp) + [("nva", nb)])
            for nb in range(NB):
                naproj(nb)
            if upto == "p2np":
                raise _Stop()
            def na_front(t, hh):
                lst = per_t[t]
                pi = hh
                pA, pB_ = 2 * pi, 2 * pi + 1
                pt_ = PT[pi]
                nloc = len(lst)
                assert nloc <= 5
                qmb = qm[hh][t % 2]
                P.op("dve", lambda e: e.tensor_copy(out=qmb[hh * 64:(hh + 1) * 64, :], in_=slabQ[hh * 64:(hh + 1) * 64, t * 128:(t + 1) * 128]),
                     reads=[("slabQ", t // 4)], writes=[("qm", hh, t % 2)])

                def mm(e):
                    for m, (u, ty) in enumerate(lst):
                        o_ = (PB(pA)[:, m * 128:(m + 1) * 128] if m < 4 else PB(pB_)[:, 0:128])
                        e.matmul(o_, lhsT=slabK[:, u * 128:(u + 1) * 128], rhs=qmb, start=True, stop=False)
                        ins = e.matmul(o_, lhsT=identb, rhs=BT[:, hh, ty, :], start=False, stop=True)
                    for ct in range(2):
                        ins = e.matmul(PB(pB_)[:, (1 + ct) * 128:(2 + ct) * 128], lhsT=cnkT[:, ct * 128:(ct + 1) * 128],
                                       rhs=qmb, start=True, stop=True)
                    return ins
                kblocks = sorted(set(u // 4 for (u, _) in lst))
                P.op("pe", mm, reads=[("qm", hh, t % 2), "cnkT", ("BT", hh), "identb"] + [("slabK", kb) for kb in kblocks],
                     writes=PK(pA, pB_))
                na4 = min(nloc, 4)
                P.op("act", lambda e: e.activation(out=pt_[:, 0:na4, :], in_=PB(pA)[:, 0:na4 * 128].rearrange("p (m q) -> p m q", q=128), func=AF.Exp),
                     writes=PK(pA) + [("PT", pi, 0)])
                lo_ = 0 if nloc == 5 else 1
                P.op("act", lambda e: e.activation(out=pt_[:, 4 + lo_:7, :], in_=PB(pB_)[:, lo_ * 128:3 * 128].rearrange("p (m q) -> p m q", q=128), func=AF.Exp),
                     writes=PK(pB_) + [("PT", pi, 1)])

            def na_back(t, hh):
                lst = per_t[t]
                pi = hh
                pt_ = PT[pi]
                gi = (t // 4) % 2
                yt = ytile[gi]
                pbo = 4 + (t % 2)
                kblocks = sorted(set(u // 4 for (u, _) in lst))

                def mm(e):
                    o_ = PB(pbo)[:, hh * 66:hh * 66 + 65]
                    for m, (u, ty) in enumerate(lst):
                        slot = m if m < 4 else 4
                        e.matmul(o_, lhsT=pt_[:, slot, :], rhs=nva[:, u, hh, :], start=(m == 0), stop=False)
                    for ct in range(2):
                        ins = e.matmul(o_, lhsT=pt_[:, 5 + ct, :], rhs=cnva[:, ct, hh, :], start=False, stop=(ct == 1))
                    return ins
                P.op("pe", mm, reads=[("PT", pi, 0), ("PT", pi, 1), ("cnva", 0), ("cnva", 1)] + [("nva", kb) for kb in kblocks],
                     writes=PK(pbo))
                if hh == 0:
                    return
                ov = PB(pbo)[:, 0:132].rearrange("p (h f) -> p h f", f=66)
                P.op("dve", lambda e: e.reciprocal(out=rc, in_=ov[:, :, 64]), writes=PK(pbo) + ["rc"])
                P.op("dve", lambda e: e.tensor_tensor(
                    out=yt[:, t % 4, :].rearrange("p (h d) -> p h d", d=64), in0=ov[:, :, 0:64],
                    in1=rc.unsqueeze(2).to_broadcast([128, 2, 64]), op=ALU.mult),
                    reads=["rc"], writes=PK(pbo) + [("ytile", gi, t % 4)])
                if t % 4 == 3:
                    g0 = t - 3
                    pbt = 6 + gi
                    pbv = PB(pbt).bitcast(BF16)

                    def tr(e):
                        for cc in range(4):
                            ins = e.transpose(out=pbv[:, cc * 128:(cc + 1) * 128], in_=yt[:, cc, :], identity=identb)
                        return ins
                    P.op("pe", tr, reads=[("ytile", gi, cc) for cc in range(4)] + [("ytile", gi), "identb"], writes=PK(pbt))
                    P.op("act", lambda e: e.copy(out=ystage[gi], in_=pbv[:, 0:512]), writes=PK(pbt) + [("ystage", gi)])
                    P.dma("sp", yT_d[4 + hp, :, g0 * 128:(g0 + 4) * 128], ystage[gi], reads=[("ystage", gi)], writes=[("yT", 1, hp, g0 // 4)])
            units = [(t, hh) for t in range(NT) for hh in range(2)]
            for k_, (t_, hh_) in enumerate(units):
                na_front(t_, hh_)
                if k_ >= 1:
                    na_back(*units[k_ - 1])
            na_back(*units[-1])
        for hp in range(4):
            pair_body(hp)
        P.barrier(scr)
        A.release(m_after_persist)
        A.release(m0)

        if upto == "p2":
            raise _Stop()
        GT = [A.alloc([D], F32) for _ in range(2)]
        m3 = A.mark()
        wbufg = A.alloc([KC, 1024], BF16)
        bb = A.alloc([D], F32)
        gb = A.alloc([D], F32)
        for gi_, j in enumerate((2, 5)):
            P.dma("sp", wbufg, wadab[gi_].rearrange("(k p) n -> p k n", p=128), writes=["wbufg"])
            P.dma("sp", bb, b_ada[0:1, j * D:(j + 1) * D].partition_broadcast(128), writes=["bb"])
            P.dma("sp", gb, gpost[gi_:gi_ + 1, :].partition_broadcast(128), writes=["gb"])

            def mm(e):
                for half in range(2):
                    for k in range(KC):
                        ins = e.matmul(PB(half)[:, 0:512], lhsT=sc_rep[:, k, :], rhs=wbufg[:, k, half * 512:(half + 1) * 512],
                                       start=(k == 0), stop=(k == KC - 1))
                return ins
            P.op("pe", mm, reads=["wbufg", "sc_rep"], writes=PK(0, 1))
            P.op("dve", lambda e, gi_=gi_: e.tensor_tensor(out=GT[gi_].rearrange("p (b n) -> p b n", b=2), in0=ps[:, 0:2, :], in1=bb.rearrange("p (b n) -> p b n", b=2), op=ALU.add),
                 reads=["bb"], writes=PK(0, 1) + [("GT", gi_)])
            P.op("dve", lambda e, gi_=gi_: e.tensor_tensor(out=GT[gi_], in0=GT[gi_], in1=gb, op=ALU.mult), reads=["gb"], writes=[("GT", gi_)])
        P.barrier(scr)
        A.release(m3)

        if upto == "p3p":
            raise _Stop()
        Wg = A.alloc([KC, 2048], BF16)
        Wro = A.alloc([4, D], BF16)
        Wno = A.alloc([4, D], BF16)
        Wo = A.alloc([KC, D], BF16)
        def load3a_weights():
            for q4 in range(4):
                P.dma("sp", Wg[:, :, q4 * 512:(q4 + 1) * 512], wgb[:, q4 * 512:(q4 + 1) * 512].rearrange("(k p) n -> p k n", p=128), writes=[("Wg", q4)])
            P.dma("sp", Wro, wrob.rearrange("(k p) n -> p k n", p=128), writes=["Wro"])
            P.dma("sp", Wno, wnob.rearrange("(k p) n -> p k n", p=128), writes=["Wno"])
            for q2 in range(2):
                P.dma("sp", Wo[:, :, q2 * 512:(q2 + 1) * 512], wob[:, q2 * 512:(q2 + 1) * 512].rearrange("(k p) n -> p k n", p=128), writes=[("Wo", q2)])
        xbA = [A.alloc([4, D], F32) for _ in range(2)]
        junkA = A.alloc([D], BF16)
        tmpA3 = [dict(junk=junkA, junk_key="junkA", ss=A.alloc([1], F32), rstd=A.alloc([1], F32), xn=A.alloc([D], F32), key=("nt3", i_)) for i_ in range(2)]
        hTb = A.alloc([KC, 512], BF16)
        yTbA = [A.alloc([8, 512], BF16) for _ in range(2)]
        sgT = A.alloc([16, 512], F32)
        z1A = [A.alloc([512], F32) for _ in range(2)]
        z2A = [A.alloc([512], F32) for _ in range(2)]
        zT = A.alloc([KC, 512], BF16)
        ssyA = [A.alloc([1], F32) for _ in range(2)]
        rsyA = [A.alloc([1], F32) for _ in range(2)]
        tyA = [A.alloc([D], F32) for _ in range(2)]
        Wgk = [("Wg", q4) for q4 in range(4)]

        def load3a(nb):
            xb = xbA[nb % 2]
            for tt in range(4):
                P.dma("sp", xb[:, tt, :], x[(nb * 4 + tt) * 128:(nb * 4 + tt + 1) * 128, :], writes=[("xbA", nb % 2, tt)])
            P.dma("sp", yTbA[nb % 2], yT_d[:, :, nb * 512:(nb + 1) * 512].rearrange("a p n -> p a n"), writes=[("yTb", nb % 2)])

        def n3a_p1(nb, tt):
            xb = xbA[nb % 2]
            return norm_p1(xb[:, tt, :], ("xbA", nb % 2, tt), tmpA3[tt % 2])

        def n3a_p2(nb, tt, xnk):
            norm_p2(xnk, (lambda c: hTb[:, c, tt * 128:(tt + 1) * 128]), [("hTb", c, tt) for c in range(KC)], S1, SH1, 0, (6, 7), tmpA3[tt % 2])

        def norm3a(nb):
            for tt in range(4):
                n3a_p2(nb, tt, n3a_p1(nb, tt))

        def blk3a(nb):
            xb = xbA[nb % 2]
            yTb = yTbA[nb % 2]
            if nb + 1 < NB:
                load3a(nb + 1)
            hkeys = [("hTb", c, tt) for c in range(KC) for tt in range(4)]
            xnks = {}
            for g in range(16):
                pb = g % 2

                def mm(e, g=g, pb=pb):
                    for k in range(KC):
                        ins = e.matmul(PB(pb), lhsT=Wg[:, k, g * 128:(g + 1) * 128], rhs=hTb[:, k, :], start=(k == 0), stop=(k == KC - 1))
                    return ins
                P.op("pe", mm, reads=hkeys + [("Wg", g // 4)], writes=PK(pb))
                P.op("act", lambda e, g=g, pb=pb: e.activation(out=sgT[:, g, :], in_=PB(pb), func=AF.Sigmoid), writes=PK(pb) + [("sgT", g)])
                if nb + 1 < NB and g in (2, 6):
                    xnks[g // 4] = n3a_p1(nb + 1, g // 4)
            for fc in range(KC):
                pa, pbb = 2 + fc % 2, 4 + fc % 2
                z1, z2 = z1A[fc % 2], z2A[fc % 2]

                def mm(e, fc=fc, pa=pa):
                    for k in range(4):
                        ins = e.matmul(PB(pa), lhsT=Wro[:, k, fc * 128:(fc + 1) * 128], rhs=yTb[:, k, :], start=(k == 0), stop=(k == 3))
                    return ins
                P.op("pe", mm, reads=[("yTb", nb % 2), "Wro"], writes=PK(pa))

                def mm(e, fc=fc, pbb=pbb):
                    for k in range(4):
                        ins = e.matmul(PB(pbb), lhsT=Wno[:, k, fc * 128:(fc + 1) * 128], rhs=yTb[:, 4 + k, :], start=(k == 0), stop=(k == 3))
                    return ins
                P.op("pe", mm, reads=[("yTb", nb % 2), "Wno"], writes=PK(pbb))
                P.op("dve", lambda e, fc=fc, pa=pa, z1=z1: e.tensor_tensor(out=z1, in0=PB(pa), in1=sgT[:, fc, :], op=ALU.mult),
                     reads=[("sgT", fc)], writes=PK(pa) + [("z1", fc % 2)])
                P.op("dve", lambda e, fc=fc, pbb=pbb, z2=z2: e.tensor_tensor(out=z2, in0=PB(pbb), in1=sgT[:, 8 + fc, :], op=ALU.mult),
                     reads=[("sgT", 8 + fc)], writes=PK(pbb) + [("z2", fc % 2)])
                P.op("pool", lambda e, fc=fc, z1=z1, z2=z2: e.tensor_tensor(out=zT[:, fc, :], in0=z1, in1=z2, op=ALU.add),
                     reads=[("z1", fc % 2), ("z2", fc % 2)], writes=[("zT", fc)])
                if nb + 1 < NB and fc % 2 == 1:
                    tt_ = fc // 2
                    n3a_p2(nb + 1, tt_, xnks[tt_])
                    if tt_ + 2 < 4:
                        xnks[tt_ + 2] = n3a_p1(nb + 1, tt_ + 2)
            for tt in range(4):
                i = nb * 4 + tt
                py0 = 2 * (tt % 2)
                ssy, rsy, ty = ssyA[tt % 2], rsyA[tt % 2], tyA[tt % 2]

                def mm(e, tt=tt, py0=py0):
                    for half in range(2):
                        for k in range(KC):
                            ins = e.matmul(PB(py0 + half), lhsT=zT[:, k, tt * 128:(tt + 1) * 128], rhs=Wo[:, k, half * 512:(half + 1) * 512],
                                           start=(k == 0), stop=(k == KC - 1))
                    return ins
                P.op("pe", mm, reads=[("zT", fc) for fc in range(KC)] + [("Wo", 0), ("Wo", 1)], writes=PK(py0, py0 + 1))
                jk = tmpA3[tt % 2]
                P.op("act", lambda e, py0=py0, jk=jk, ssy=ssy: e.activation(out=jk["junk"].rearrange("p (b n) -> p b n", b=2), in_=ps[:, py0:py0 + 2, :], func=AF.Square, accum_out=ssy),
                     writes=PK(py0, py0 + 1) + ["junkA", ("ssy", tt % 2)])
                P.op("act", lambda e, ssy=ssy, rsy=rsy: e.activation(out=rsy, in_=ssy, func=AF.Sqrt, scale=1.0 / D, bias=EPS),
                     reads=[("ssy", tt % 2)], writes=[("rsy", tt % 2)])
                P.op("dve", lambda e, rsy=rsy: e.reciprocal(out=rsy, in_=rsy), writes=[("rsy", tt % 2)])
                P.op("dve", lambda e, py0=py0, rsy=rsy, ty=ty: e.scalar_tensor_tensor(
                    out=ty.rearrange("p (b n) -> p b n", b=2), in0=ps[:, py0:py0 + 2, :], scalar=rsy[:, 0:1],
                    in1=GT[0].rearrange("p (b n) -> p b n", b=2), op0=ALU.mult, op1=ALU.mult),
                    reads=[("rsy", tt % 2), ("GT", 0)], writes=PK(py0, py0 + 1) + [("ty", tt % 2)])
                P.op("pool", lambda e, tt=tt, ty=ty, xb=xb: e.tensor_tensor(out=ty, in0=ty, in1=xb[:, tt, :], op=ALU.add),
                     reads=[("xbA", nb % 2, tt)], writes=[("ty", tt % 2)])
                P.dma("sp", out[i * 128:(i + 1) * 128, :], ty, reads=[("ty", tt % 2)], writes=[("x1d", i)])
        load3a(0)
        load3a_weights()
        norm3a(0)
        for nb in range(NB):
            blk3a(nb)
        P.barrier(scr)
        A.release(m3)

        if upto == "p3a":
            raise _Stop()
        W1 = A.alloc([KC, 4 * D], BF16)
        W2 = A.alloc([32, D], BF16)
        def load3b_weights():
            for q8 in range(8):
                P.dma("sp", W1[:, :, q8 * 512:(q8 + 1) * 512], w1b[:, q8 * 512:(q8 + 1) * 512].rearrange("(k p) n -> p k n", p=128), writes=[("W1", q8)])
            for q8 in range(8):
                P.dma("sp", W2[:, q8 * 4:(q8 + 1) * 4, :], w2b[q8 * 512:(q8 + 1) * 512, :].rearrange("(k p) n -> p k n", p=128), writes=[("W2", q8)])
        xtB = [A.alloc([D], F32) for _ in range(2)]
        xrB = A.alloc([D], F32)
        rl = [A.alloc([512], F32) for _ in range(2)]
        h2TB = [A.alloc([KC, 512], BF16) for _ in range(2)]
        uT = A.alloc([32, 512], BF16)
        ssB = [A.alloc([1], F32) for _ in range(2)]
        rstdB = [A.alloc([1], F32) for _ in range(2)]
        ssyB = [A.alloc([1], F32) for _ in range(2)]
        rsyB = [A.alloc([1], F32) for _ in range(2)]
        tmpB3 = [dict(junk=rl[i_].bitcast(BF16), junk_key=("rl", i_), ss=ssB[i_], rstd=rstdB[i_], xn=xtB[i_], key=("nt4", i_)) for i_ in range(2)]
        W2k = [("W2", q8) for q8 in range(8)]

        def n3b_p1(nb, tt):
            i = nb * 4 + tt
            bi = i % 2
            P.dma("sp", xtB[bi], out[i * 128:(i + 1) * 128, :], reads=[("x1d", i)], writes=[("xtB", bi)])
            return norm_p1(xtB[bi], ("xtB", bi), tmpB3[bi], inplace=True)

        def n3b_p2(nb, tt, xnk):
            i = nb * 4 + tt
            h2T = h2TB[nb % 2]
            norm_p2(xnk, (lambda c: h2T[:, c, tt * 128:(tt + 1) * 128]), [("h2T", nb % 2, c, tt) for c in range(KC)], S2, SH2, 0, (6, 7), tmpB3[i % 2])

        def blk3b(nb):
            h2T = h2TB[nb % 2]
            hkeys = [("h2T", nb % 2, c, tt) for c in range(KC) for tt in range(4)]
            nxt = nb + 1 < NB
            xnks = {}
            for j in range(32):
                pb = j % 2

                def mm(e, j=j, pb=pb):
                    for k in range(KC):
                        ins = e.matmul(PB(pb), lhsT=W1[:, k, j * 128:(j + 1) * 128], rhs=h2T[:, k, :], start=(k == 0), stop=(k == KC - 1))
                    return ins
                P.op("pe", mm, reads=hkeys + [("W1", j // 4)], writes=PK(pb))
                P.op("act", lambda e, pb=pb: e.activation(out=rl[pb], in_=PB(pb), func=AF.Relu), writes=PK(pb) + [("rl", pb)])
                P.op("dve" if j % 2 == 0 else "pool", lambda e, j=j, pb=pb: e.tensor_tensor(out=uT[:, j, :], in0=rl[pb], in1=rl[pb], op=ALU.mult),
                     reads=[("rl", pb)], writes=[("uT", j)])
                if nxt:
                    if j == 3:
                        xnks[0] = n3b_p1(nb + 1, 0)
                    elif j == 7:
                        xnks[1] = n3b_p1(nb + 1, 1)
                    elif j == 15:
                        n3b_p2(nb + 1, 0, xnks[0])
                        xnks[2] = n3b_p1(nb + 1, 2)
                    elif j == 21:
                        n3b_p2(nb + 1, 1, xnks[1])
                        xnks[3] = n3b_p1(nb + 1, 3)
                    elif j == 27:
                        n3b_p2(nb + 1, 2, xnks[2])
                    elif j == 31:
                        n3b_p2(nb + 1, 3, xnks[3])
            for tt in range(4):
                i = nb * 4 + tt
                pbm = 2 + 2 * (tt % 2)
                ssy, rsy = ssyB[tt % 2], rsyB[tt % 2]
                P.dma("sp", xrB, out[i * 128:(i + 1) * 128, :], reads=[("x1d", i)], writes=["xrB"])

                def mm(e, tt=tt, pbm=pbm):
                    for half in range(2):
                        for j in range(32):
                            ins = e.matmul(PB(pbm + half), lhsT=uT[:, j, tt * 128:(tt + 1) * 128], rhs=W2[:, j, half * 512:(half + 1) * 512],
                                           start=(j == 0), stop=(j == 31))
                    return ins
                P.op("pe", mm, reads=[("uT", j) for j in range(32)] + W2k, writes=PK(pbm, pbm + 1))
                jb = tt % 2
                P.op("act", lambda e, pbm=pbm, jb=jb, ssy=ssy: e.activation(out=rl[jb].bitcast(BF16).rearrange("p (b n) -> p b n", b=2), in_=ps[:, pbm:pbm + 2, :], func=AF.Square, accum_out=ssy),
                     writes=PK(pbm, pbm + 1) + [("rl", jb), ("ssyB", tt % 2)])
                P.op("act", lambda e, ssy=ssy, rsy=rsy: e.activation(out=rsy, in_=ssy, func=AF.Sqrt, scale=1.0 / D, bias=EPS),
                     reads=[("ssyB", tt % 2)], writes=[("rsyB", tt % 2)])
                P.op("dve", lambda e, rsy=rsy: e.reciprocal(out=rsy, in_=rsy), writes=[("rsyB", tt % 2)])
                P.op("dve", lambda e, pbm=pbm, rsy=rsy: e.scalar_tensor_tensor(
                    out=ps[:, pbm:pbm + 2, :], in0=ps[:, pbm:pbm + 2, :], scalar=rsy[:, 0:1],
                    in1=GT[1].rearrange("p (b n) -> p b n", b=2), op0=ALU.mult, op1=ALU.mult),
                    reads=[("rsyB", tt % 2), ("GT", 1)], writes=PK(pbm, pbm + 1))
                P.op("dve", lambda e, pbm=pbm: e.tensor_tensor(out=xrB.rearrange("p (b n) -> p b n", b=2), in0=ps[:, pbm:pbm + 2, :],
                                                               in1=xrB.rearrange("p (b n) -> p b n", b=2), op=ALU.add),
                     writes=PK(pbm, pbm + 1) + ["xrB"])
                P.dma("sp", out[i * 128:(i + 1) * 128, :], xrB, reads=["xrB"], writes=[("outd", i)])
        for tt in range(4):
            n3b_p2(0, tt, n3b_p1(0, tt))
        load3b_weights()
        for nb in range(NB):
            blk3b(nb)
    try:
        body()
    except _Stop:
        pass
    info = P.emit()
    info["arena_peak"] = A.peak
    cmp_.__exit__(None, None, None)
    cm.__exit__(None, None, None)
    return nc, info, types


def prep_inputs(inputs, SEQ, types):
    NT = SEQ // 128
    f = lambda a: np.ascontiguousarray(np.asarray(a, dtype=np.float32))
    x = f(inputs["x"]); c = f(inputs["c"]); ctx = f(inputs["ctx"]); c_ctx = f(inputs["c_ctx"])
    B = x.shape[0]
    w_ada = f(inputs["w_ada"][0]); b_ada = f(inputs["b_ada"][0])
    shared = dict(
        w_ada=w_ada,
        bada_fm=np.ascontiguousarray(b_ada.reshape(48, 128).T),
        b_ada=b_ada.reshape(1, -1),
        gpre_fm=np.ascontiguousarray(np.stack([f(inputs["norm_pre_mix"][0]).reshape(KC, 128).T,
                                               f(inputs["norm_pre_ffn"][0]).reshape(KC, 128).T], axis=1)),
        gpost=np.ascontiguousarray(np.stack([f(inputs["norm_post_mix"][0]), f(inputs["norm_post_ffn"][0])], axis=0)),
        w_in=f(inputs["w_in"][0]), w_ro=f(inputs["w_ret_out"][0]), w_no=f(inputs["w_na_out"][0]),
        w_o=f(inputs["w_o"][0]), w_ff1=f(inputs["w_ff1"][0]), w_ff2=f(inputs["w_ff2"][0]),
    )
    lg = f(inputs["ret_decay_logit"][0])
    lgt_pair = np.zeros((128, 8), np.float32)
    def pair_body(hp):
        for d_ in range(2):
            lgt_pair[0:64, hp * 2 + d_] = lg[d_, 2 * hp]
            lgt_pair[64:128, hp * 2 + d_] = lg[d_, 2 * hp + 1]
    for hp in range(4):
        pair_body(hp)
    lgt_bc = np.zeros((128, 16), np.float32)
    for h in range(8):
        for d_ in range(2):
            lgt_bc[:, 2 * h + d_] = lg[d_, h]
    shared["lgt_pair"] = lgt_pair
    shared["lgt_bc"] = lgt_bc
    shared["ident"] = np.eye(128, dtype=np.float32)
    j = np.arange(128)[:, None].astype(np.float32)
    i = np.arange(128)[None, :].astype(np.float32)
    cmat = np.stack([np.maximum(i - j, 0), np.maximum(j - i, 0), (i >= j) * 0.125, (j > i) * 0.125], axis=1).astype(np.float32)
    shared["cmat"] = np.ascontiguousarray(cmat)
    jj = np.arange(128, dtype=np.float32)
    shared["colc"] = np.ascontiguousarray(np.stack([127 - jj, jj, 255 - jj, 127 - jj, jj, 128 + jj], axis=1))
    ii = np.arange(128, dtype=np.float32)
    shared["rowc"] = np.ascontiguousarray(np.broadcast_to(np.stack([ii + 1, 128 - ii], axis=0)[None], (128, 2, 128)).astype(np.float32))
    cos, sin = rope_tables(SEQ)
    shared["cos_tm"] = np.ascontiguousarray(cos.reshape(NT, 128, 32).transpose(1, 0, 2))
    shared["sin_tm"] = np.ascontiguousarray(sin.reshape(NT, 128, 32).transpose(1, 0, 2))
    mask, idr, idc = na_consts(types)
    rpb = f(inputs["na_rpb"][0])
    rpbB = rpb[:, idr, idc]
    shared["rpbB"] = np.ascontiguousarray(rpbB.transpose(0, 2, 1, 3))
    shared["maskB"] = np.ascontiguousarray(mask.transpose(1, 0, 2))
    in_maps = []
    for b in range(B):
        m = dict(shared)
        m["x"] = x[b]
        m["ctx"] = ctx[b]
        m["c_fm"] = np.ascontiguousarray(np.stack([c[b].reshape(KC, 128).T, c_ctx.reshape(KC, 128).T], axis=2))
        in_maps.append(m)
    return in_maps


_CACHE = {}


def kernel(**inputs):
    x = inputs["x"]
    B, SEQ, _ = x.shape
    if SEQ not in _CACHE:
        _CACHE[SEQ] = build(SEQ)
    nc, info, types = _CACHE[SEQ]
    in_maps = prep_inputs(inputs, SEQ, types)
    res = run_bass_kernel_spmd(nc, in_maps, core_ids=list(range(B)))
    return np.stack([np.asarray(r["out"], dtype=np.float32) for r in res.results], axis=0)
```

```python
import numpy as np
import ml_dtypes
import concourse.bass as bass
import concourse.mybir as mybir
from concourse.bass_utils import run_bass_kernel_spmd

F32 = mybir.dt.float32
BF16 = mybir.dt.bfloat16
U8 = mybir.dt.uint8
AF = mybir.ActivationFunctionType
ALU = mybir.AluOpType
AX = mybir.AxisListType

D = 1024
KC = 8
CTX = 256
GRID_W = 64
EPS = 1e-6
CH = 4096


class Prog:
    ENGS = ("pe", "act", "dve", "pool", "sp")

    def __init__(self, nc, n_dma_sems=16):
        self.nc = nc
        self.ops = []
        self.last_w = {}
        self.readers = {}
        self.n_dma_sems = n_dma_sems
        self.pending = {e: set() for e in self.ENGS}
        self.bar_start = 0
        self.nbar = 0

    def op(self, eng, fn, reads=(), writes=(), dma=False):
        oid = len(self.ops)
        deps = set()
        for k in list(reads) + list(writes):
            if k in self.last_w:
                deps.add(self.last_w[k])
        for k in writes:
            for r in self.readers.get(k, ()):
                deps.add(r)
        deps |= self.pending[eng]
        self.pending[eng] = set()
        deps.discard(oid)
        self.ops.append(dict(eng=eng, fn=fn, deps=deps, dma=dma, has_dep=False))
        for k in reads:
            self.readers.setdefault(k, []).append(oid)
        for k in writes:
            self.last_w[k] = oid
            self.readers[k] = []
        return oid

    def dma(self, q, out, in_, reads=(), writes=(), **kw):
        def fn(e):
            return e.dma_start(out=out, in_=in_, **kw)
        return self.op(q, fn, reads, writes, dma=True)

    def barrier(self, scratch):
        n = self.nbar
        self.nbar += 1
        dmas = [i for i in range(self.bar_start, len(self.ops)) if self.ops[i]["dma"]]
        marks = []
        marks.append(self.op("act", lambda e: e.copy(out=scratch["act"], in_=scratch["act"]), writes=[("bar", n, "act")]))
        marks.append(self.op("dve", lambda e: e.memset(scratch["dve"], 0.0), writes=[("bar", n, "dve")]))
        marks.append(self.op("pool", lambda e: e.memset(scratch["pool"], 0.0), writes=[("bar", n, "pool")]))
        for e in self.ENGS:
            self.pending[e] = set(marks) | set(dmas)
        self.last_w = {}
        self.readers = {}
        self.bar_start = len(self.ops)

    def emit(self, final_wait_eng="sp"):
        nc = self.nc
        ops = self.ops
        for i, o in enumerate(ops):
            keep = set()
            for d in o["deps"]:
                od = ops[d]
                if (not od["dma"]) and od["eng"] == o["eng"] and o["eng"] == "pe" and not o["dma"]:
                    continue
                keep.add(d)
            o["deps"] = keep
            for d in keep:
                ops[d]["has_dep"] = True
        tail = [i for i, o in enumerate(ops) if o["dma"] and not o["has_dep"]]
        for i in tail:
            ops[i]["has_dep"] = True
        cnt = {e: 0 for e in self.ENGS}
        for o in ops:
            if not o["dma"] and o["has_dep"]:
                o["seq"] = cnt[o["eng"]]
                cnt[o["eng"]] += 1
        sems = {}
        for e in self.ENGS:
            n = (cnt[e] + CH - 1) // CH
            sems[e] = [nc.alloc_semaphore(name=f"s_{e}_{j}") for j in range(n)]
        dsems = [nc.alloc_semaphore(name=f"s_dma_{j}") for j in range(self.n_dma_sems)]
        dcount = [0] * self.n_dma_sems
        dnext = 0
        waited = {e: {} for e in self.ENGS}

        def plan_wait(o, e, sem, val):
            key = id(sem)
            if waited[e].get(key, 0) >= val:
                return
            waited[e][key] = val
            o["waits"].append((sem, val))

        for i, o in enumerate(ops):
            e = o["eng"]
            o["waits"] = []
            for d in sorted(o["deps"]):
                od = ops[d]
                if od["dma"]:
                    plan_wait(o, e, od["dsem"], od["dval"])
                else:
                    s = od["seq"]
                    plan_wait(o, e, sems[od["eng"]][s // CH], s % CH + 1)
            if o["dma"] and e == "pool":
                sw = nc.alloc_semaphore(name=f"s_swdma_{i}")
                o["dsem"] = sw
                o["dval"] = 16
                o["inc"] = (sw, 16)
            elif o["dma"]:
                j = dnext
                dnext = (dnext + 1) % self.n_dma_sems
                if dcount[j] > 0:
                    plan_wait(o, e, dsems[j], dcount[j])
                dcount[j] += 16
                o["dsem"] = dsems[j]
                o["dval"] = dcount[j]
                o["inc"] = (dsems[j], 16)
            elif o["has_dep"]:
                s = o["seq"]
                o["inc"] = (sems[e][s // CH], 1)
            else:
                o["inc"] = None
        final_waits = []
        fo = dict(waits=final_waits)
        for i in tail:
            plan_wait(fo, final_wait_eng, ops[i]["dsem"], ops[i]["dval"])

        def run_engine(ename, eng):
            for o in ops:
                if o["eng"] != ename:
                    continue
                for (sem, val) in o["waits"]:
                    eng.wait_ge(sem, val)
                ins = o["fn"](eng)
                if o["inc"] is not None:
                    ins.then_inc(o["inc"][0], o["inc"][1])
            if ename == final_wait_eng:
                for (sem, val) in final_waits:
                    eng.wait_ge(sem, val)

        with nc.Block() as block:
            @block.sync
            def _(eng):
                run_engine("sp", eng)

            @block.tensor
            def _(eng):
                run_engine("pe", eng)

            @block.scalar
            def _(eng):
                run_engine("act", eng)

            @block.vector
            def _(eng):
                run_engine("dve", eng)

            @block.gpsimd
            def _(eng):
                run_engine("pool", eng)
        return dict(n_ops=len(ops), cnt=cnt)


class Arena:
    def __init__(self, ap_u8, size):
        self.ap = ap_u8
        self.size = size
        self.off = 0
        self.peak = 0

    def alloc(self, shape, dtype):
        esz = {F32: 4, BF16: 2}[dtype]
        n = int(np.prod(shape))
        nbytes = (n * esz + 63) // 64 * 64
        assert self.off + nbytes <= self.size, f"arena overflow {self.off}+{nbytes}>{self.size}"
        v = self.ap[:, self.off:self.off + n * esz].bitcast(dtype)
        self.off += nbytes
        self.peak = max(self.peak, self.off)
        if len(shape) == 1:
            return v
        names = " ".join(f"d{i}" for i in range(len(shape)))
        kw = {f"d{i}": int(s) for i, s in enumerate(shape)}
        return v.rearrange(f"p ({names}) -> p {names}", **kw)

    def mark(self):
        return self.off

    def release(self, m):
        self.off = m


def na_structure(rows):
    T = rows // 2
    types = {}
    per_t = []
    for t in range(T):
        lst = []
        for u in range(T):
            vis = []
            anyv = False
            for kr in range(2):
                for qr in range(2):
                    r = 2 * t + qr
                    r0 = min(max(r - 4, 0), rows - 8)
                    v = r0 <= 2 * u + kr < r0 + 8
                    vis.append(v)
                    anyv = anyv or v
            if not anyv:
                continue
            key = (u - t, tuple(vis))
            if key not in types:
                types[key] = len(types)
            lst.append((u, types[key]))
        per_t.append(lst)
    return per_t, types


def na_consts(types):
    nt = len(types)
    mask = np.zeros((nt, 128, 128), np.float32)
    idr = np.zeros((nt, 128, 128), np.int64)
    idc = np.zeros((nt, 128, 128), np.int64)
    kc = np.arange(64)[:, None]
    qc = np.arange(64)[None, :]
    c0 = np.clip(qc - 8, 0, 48)
    colok = (kc >= c0) & (kc < c0 + 16)
    dc = np.clip(kc - qc + 15, 0, 30)
    for (delta, vis), ti in types.items():
        for kr in range(2):
            for qr in range(2):
                v = vis[kr * 2 + qr]
                dr = int(np.clip(2 * delta + kr - qr + 7, 0, 14))
                blk = np.where(colok & v, 0.0, -30000.0).astype(np.float32)
                mask[ti, kr * 64:(kr + 1) * 64, qr * 64:(qr + 1) * 64] = blk
                idr[ti, kr * 64:(kr + 1) * 64, qr * 64:(qr + 1) * 64] = dr
                idc[ti, kr * 64:(kr + 1) * 64, qr * 64:(qr + 1) * 64] = dc
    return mask, idr, idc


def rope_tables(n):
    pos = np.arange(n)
    row = (pos // GRID_W).astype(np.float32)
    col = (pos % GRID_W).astype(np.float32)
    inv = (10000.0 ** (-np.arange(0, 32, 2, dtype=np.float32) / 32)).astype(np.float32)
    ang = np.concatenate([row[:, None] * inv, col[:, None] * inv], axis=-1).astype(np.float32)
    return np.cos(ang).astype(np.float32), np.sin(ang).astype(np.float32)


class _Stop(Exception):
    pass


def build(SEQ, debug=(), upto=None):
    NT = SEQ // 128
    ROWS = SEQ // 64
    NB = SEQ // 512
    per_t, types = na_structure(ROWS)
    NTYPE = len(types)
    nc = bass.Bass("TRN2", target_bir_lowering=False)

    def din(name, shape, dt=F32):
        return nc.dram_tensor(name, list(shape), dt, kind="ExternalInput").ap()

    x = din("x", [SEQ, D])
    ctx = din("ctx", [CTX, D])
    c_fm = din("c_fm", [128, KC, 2])
    w_ada = din("w_ada", [D, 6 * D])
    bada_fm = din("bada_fm", [128, 48])
    b_ada = din("b_ada", [1, 6 * D])
    gpre_fm = din("gpre_fm", [128, 2, KC])
    gpost = din("gpost", [2, D])
    w_in = din("w_in", [D, 5632])
    w_ro = din("w_ro", [512, D])
    w_no = din("w_no", [512, D])
    w_o = din("w_o", [D, D])
    w_ff1 = din("w_ff1", [D, 4 * D])
    w_ff2 = din("w_ff2", [4 * D, D])
    lgt_pair = din("lgt_pair", [128, 8])
    lgt_bc = din("lgt_bc", [128, 16])
    ident_d = din("ident", [128, 128])
    cmat = din("cmat", [128, 4, 128])
    colc_d = din("colc", [128, 6])
    rowc_d = din("rowc", [128, 2, 128])
    cos_d = din("cos_tm", [128, NT, 32])
    sin_d = din("sin_tm", [128, NT, 32])
    rpbB = din("rpbB", [8, 128, NTYPE, 128])
    maskB_d = din("maskB", [128, NTYPE, 128])
    out = nc.dram_tensor("out", [SEQ, D], F32, kind="ExternalOutput").ap()
    yT_d = nc.dram_tensor("yT_scratch", [8, 128, SEQ], BF16, kind="Internal").ap()
    wgb = nc.dram_tensor("wg_bf", [D, 2048], BF16, kind="Internal").ap()
    wrob = nc.dram_tensor("wro_bf", [512, D], BF16, kind="Internal").ap()
    wnob = nc.dram_tensor("wno_bf", [512, D], BF16, kind="Internal").ap()
    wob = nc.dram_tensor("wo_bf", [D, D], BF16, kind="Internal").ap()
    w1b = nc.dram_tensor("w1_bf", [D, 4 * D], BF16, kind="Internal").ap()
    w2b = nc.dram_tensor("w2_bf", [4 * D, D], BF16, kind="Internal").ap()
    wadab = nc.dram_tensor("wada_bf", [2, D, D], BF16, kind="Internal").ap()
    dbg = {}
    for name, shape, dt in debug:
        dbg[name] = nc.dram_tensor(name, list(shape), dt, kind="ExternalOutput").ap()

    P = Prog(nc)
    ARENA_BYTES = 207 * 1024
    cm = nc.sbuf_tensor("arena", [128, ARENA_BYTES], U8)
    arena_h = cm.__enter__()
    A = Arena(arena_h, ARENA_BYTES)
    cmp_ = nc.psum_tensor("ps", [128, 8, 512], F32)
    ps = cmp_.__enter__()

    def PB(b):
        return ps[:, b, :]

    def PK(*bs):
        return [("ps", b) for b in bs]

    def body():
        ident = A.alloc([128], F32)
        identb = A.alloc([128], BF16)
        scr = {e: A.alloc([16], F32) for e in ("act", "dve", "pool")}
        S1 = A.alloc([KC, 2], F32)
        SH1 = A.alloc([KC, 2], F32)
        S2 = A.alloc([KC, 2], F32)
        SH2 = A.alloc([KC, 2], F32)
        scb = A.alloc([KC, 2], BF16)
        sc_rep = A.alloc([KC, 128], BF16)
        gpre = A.alloc([2, KC], F32)
        badafm = A.alloc([48], F32)

        P.dma("sp", ident, ident_d, writes=["ident"])
        P.op("dve", lambda e: e.tensor_copy(out=identb, in_=ident), reads=["ident"], writes=["identb"])
        for e_ in ("act", "dve", "pool"):
            pass
        P.op("dve", lambda e: e.memset(scr["dve"], 0.0), writes=["scr_dve"])
        P.op("pool", lambda e: e.memset(scr["pool"], 0.0), writes=["scr_pool"])
        P.op("dve", lambda e: e.memset(scr["act"], 0.0), writes=["scr_act"])
        P.dma("sp", gpre, gpre_fm, writes=["gpre"])
        P.dma("sp", badafm, bada_fm, writes=["badafm"])

        m_phase2 = None

        def norm_p1(xt_ap, xt_key, tmp, inplace=False):
            junk, ss, rstd, xn = tmp["junk"], tmp["ss"], tmp["rstd"], tmp["xn"]
            tk = tmp["key"]
            jkey = tmp.get("junk_key", (tk, "junk"))
            P.op("act", lambda e: e.activation(out=junk, in_=xt_ap, func=AF.Square, accum_out=ss),
                 reads=[xt_key], writes=[jkey, (tk, "ss")])
            P.op("act", lambda e: e.activation(out=rstd, in_=ss, func=AF.Sqrt, scale=1.0 / D, bias=EPS),
                 reads=[(tk, "ss")], writes=[(tk, "rstd")])
            P.op("dve", lambda e: e.reciprocal(out=rstd, in_=rstd), writes=[(tk, "rstd")])
            xnk = xt_key if inplace else (tk, "xn")
            P.op("dve", lambda e: e.tensor_scalar(out=xn, in0=xt_ap, scalar1=rstd[:, 0:1], scalar2=None, op0=ALU.mult),
                 reads=[(tk, "rstd")] + ([] if inplace else [xt_key]), writes=[xnk])
            return xnk

        def norm_p2(xnk, dst_fn, dst_keys, Sc, Sh, col, banks, tmp):
            xn = tmp["xn"]
            for half in range(2):
                b = banks[half]

                def tr(e, half=half, b=b):
                    for cc in range(4):
                        c = half * 4 + cc
                        ins = e.transpose(out=PB(b)[:, cc * 128:(cc + 1) * 128], in_=xn[:, c * 128:(c + 1) * 128], identity=ident)
                    return ins
                P.op("pe", tr, reads=[xnk, "ident"], writes=PK(b))
                for cc in range(4):
                    c = half * 4 + cc
                    if c % 2 == 0:
                        P.op("act", lambda e, c=c, cc=cc, b=b: e.activation(
                            out=dst_fn(c), in_=PB(b)[:, cc * 128:(cc + 1) * 128], func=AF.Identity,
                            scale=Sc[:, c, col:col + 1], bias=Sh[:, c, col:col + 1]),
                            reads=["mod"], writes=PK(b) + [dst_keys[c]])
                    else:
                        P.op("dve", lambda e, c=c, cc=cc, b=b: e.tensor_scalar(
                            out=dst_fn(c), in0=PB(b)[:, cc * 128:(cc + 1) * 128],
                            scalar1=Sc[:, c, col:col + 1], scalar2=Sh[:, c, col:col + 1], op0=ALU.mult, op1=ALU.add),
                            reads=["mod"], writes=PK(b) + [dst_keys[c]])

        def norm_transpose(xt_ap, xt_key, dst_fn, dst_keys, Sc, Sh, col, banks, tmp, tag, inplace=False):
            xnk = norm_p1(xt_ap, xt_key, tmp, inplace)
            norm_p2(xnk, dst_fn, dst_keys, Sc, Sh, col, banks, tmp)

        m0 = A.mark()
        cm_t = A.alloc([4, 128], F32)
        colc = A.alloc([6], F32)
        rowc = A.alloc([2, 128], F32)
        lgp = A.alloc([8], F32)
        lgb = A.alloc([16], F32)
        cfm = A.alloc([KC, 2], F32)
        A.release(m0)
        DT = A.alloc([8, 128], F32)
        kw = A.alloc([8, 2], F32)
        ckw = A.alloc([2, 8, 2], F32)
        QW = A.alloc([2, 128], F32)
        GL = A.alloc([8], F32)
        rowc = A.alloc([2, 128], F32)
        lgp = A.alloc([8], F32)
        hT = A.alloc([KC, SEQ], BF16)
        hcT = A.alloc([KC, CTX], BF16)
        m_after_persist = A.mark()
        cm_t = A.alloc([4, 128], F32)
        colc = A.alloc([6], F32)
        lgb = A.alloc([16], F32)
        cfm = A.alloc([KC, 2], F32)
        tmpA = A.alloc([128], F32)
        tmpB = A.alloc([128], F32)
        arg16 = A.alloc([16], F32)
        argc = A.alloc([2, 8, 2], F32)
        wbuf0 = A.alloc([KC, 1024], BF16)
        modfm = A.alloc([4, KC, 2], F32)

        P.dma("sp", cm_t, cmat, writes=["cmat"])
        P.dma("sp", colc, colc_d, writes=["colc"])
        P.dma("sp", rowc, rowc_d, writes=["rowc"])
        P.dma("sp", lgp, lgt_pair, writes=["lgp"])
        P.dma("sp", lgb, lgt_bc, writes=["lgb"])
        P.dma("sp", cfm, c_fm, writes=["cfm"])

        for t_, k_ in ((lgp, "lgp"), (lgb, "lgb")):
            P.op("act", lambda e, t_=t_: e.activation(out=t_, in_=t_, func=AF.Exp, scale=-1.0), writes=[k_])
            P.op("act", lambda e, t_=t_: e.activation(out=t_, in_=t_, func=AF.Ln, bias=1.0), writes=[k_])
            P.op("dve", lambda e, t_=t_: e.tensor_scalar(out=t_, in0=t_, scalar1=-1.0, scalar2=None, op0=ALU.mult), writes=[k_])
        for h in range(8):
            P.op("act", lambda e, h=h: e.activation(out=tmpA, in_=cm_t[:, 0, :], func=AF.Exp, scale=lgb[:, 2 * h:2 * h + 1]),
                 reads=["cmat", "lgb"], writes=["tmpA"])
            P.op("act", lambda e, h=h: e.activation(out=tmpB, in_=cm_t[:, 1, :], func=AF.Exp, scale=lgb[:, 2 * h + 1:2 * h + 2]),
                 reads=["cmat", "lgb"], writes=["tmpB"])
            P.op("dve", lambda e: e.tensor_tensor(out=tmpA, in0=tmpA, in1=cm_t[:, 2, :], op=ALU.mult), reads=["cmat"], writes=["tmpA"])
            P.op("dve", lambda e: e.tensor_tensor(out=tmpB, in0=tmpB, in1=cm_t[:, 3, :], op=ALU.mult), reads=["cmat"], writes=["tmpB"])
            P.op("dve", lambda e, h=h: e.tensor_tensor(out=DT[:, h, :], in0=tmpA, in1=tmpB, op=ALU.add),
                 reads=["tmpA", "tmpB"], writes=["DT"])
        lgb3 = lgb.rearrange("p (h d) -> p h d", d=2)
        arg3 = arg16.rearrange("p (h d) -> p h d", d=2)
        for d_ in range(2):
            P.op("dve", lambda e, d_=d_: e.tensor_scalar(out=arg3[:, :, d_], in0=lgb3[:, :, d_], scalar1=colc[:, d_:d_ + 1], scalar2=None, op0=ALU.mult),
                 reads=["lgb", "colc"], writes=["arg16"])
        P.op("act", lambda e: e.activation(out=arg16, in_=arg16, func=AF.Exp), writes=["arg16"])
        P.op("dve", lambda e: e.tensor_scalar(out=kw.rearrange("p h d -> p (h d)"), in0=arg16, scalar1=0.125, scalar2=None, op0=ALU.mult),
             reads=["arg16"], writes=["kw"])
        for ct in range(2):
            for d_ in range(2):
                cc_ = 2 + ct if d_ == 0 else 4 + ct
                P.op("dve", lambda e, ct=ct, d_=d_, cc_=cc_: e.tensor_scalar(out=argc[:, ct, :, d_], in0=lgb3[:, :, d_], scalar1=colc[:, cc_:cc_ + 1], scalar2=None, op0=ALU.mult),
                     reads=["lgb", "colc"], writes=["argc"])
        P.op("act", lambda e: e.activation(out=argc, in_=argc, func=AF.Exp), writes=["argc"])
        P.op("dve", lambda e: e.tensor_scalar(out=ckw, in0=argc, scalar1=0.125, scalar2=None, op0=ALU.mult), reads=["argc"], writes=["ckw"])
        P.op("act", lambda e: e.activation(out=GL, in_=lgp, func=AF.Exp, scale=128.0), reads=["lgp"], writes=["GL"])

        P.op("act", lambda e: e.activation(out=scb, in_=cfm, func=AF.Silu), reads=["cfm"], writes=["scb"])
        P.op("dve", lambda e: e.tensor_copy(out=sc_rep, in_=scb[:, :, 0:1].to_broadcast([128, KC, 128])), reads=["scb"], writes=["sc_rep"])

        def load_w(dst, src_rows_cols, key, nk=KC):
            P.dma("pool", dst, src_rows_cols.rearrange("(k p) n -> p k n", p=128), writes=[key])

        wbuf1 = A.alloc([KC, 1024], BF16)
        wbufs = {0: (wbuf1, "wbuf1"), 1: (wbuf0, "wbuf0"), 3: (wbuf1, "wbuf1"), 4: (wbuf0, "wbuf0")}

        def ada_load(j):
            wb_, key_ = wbufs[j]
            load_w(wb_, w_ada[:, j * D:(j + 1) * D], key_)

        def ada_mm(mi, j):
            wb_, key_ = wbufs[j]

            def mm(e):
                for cc in range(8):
                    for k in range(KC):
                        ins = e.matmul(PB(0)[:, cc * 2:cc * 2 + 2], lhsT=wb_[:, k, cc * 128:(cc + 1) * 128], rhs=scb[:, k, :],
                                       start=(k == 0), stop=(k == KC - 1))
                return ins
            P.op("pe", mm, reads=[key_, "scb"], writes=PK(0))
            P.op("dve", lambda e: e.tensor_tensor(
                out=modfm[:, mi, :, :], in0=PB(0)[:, 0:16].rearrange("p (c t) -> p c t", t=2),
                in1=badafm[:, j * 8:(j + 1) * 8].unsqueeze(2).to_broadcast([128, KC, 2]), op=ALU.add),
                reads=["badafm"], writes=PK(0) + [("modfm", mi)])

        def ada_fin(Sx, SHx, mi_sh, mi_sc, gi):
            P.op("dve", lambda e: e.scalar_tensor_tensor(
                out=Sx, in0=modfm[:, mi_sc, :, :], scalar=1.0, in1=gpre[:, gi, :].unsqueeze(2).to_broadcast([128, KC, 2]),
                op0=ALU.add, op1=ALU.mult), reads=[("modfm", mi_sc), "gpre"], writes=["mod"])
            P.op("dve", lambda e: e.tensor_copy(out=SHx, in_=modfm[:, mi_sh, :, :]), reads=[("modfm", mi_sh)], writes=["mod"])

        ada_load(1)
        ada_load(0)
        ada_mm(1, 1)
        ada_mm(0, 0)
        ada_fin(S1, SH1, 0, 1, 0)
        ada_load(4)
        ada_load(3)

        if upto == "p0":
            raise _Stop()
        xts = [A.alloc([D], F32) for _ in range(2)]
        tmps = []
        for i in range(2):
            tmps.append(dict(junk=A.alloc([D], BF16), ss=A.alloc([1], F32), rstd=A.alloc([1], F32), xn=A.alloc([D], F32), key=("nt", i)))
        for i in range(NT + 2):
            bi = i % 2
            src = x[i * 128:(i + 1) * 128, :] if i < NT else ctx[(i - NT) * 128:(i - NT + 1) * 128, :]
            P.dma("sp", xts[bi], src, writes=[("xt", bi)])
            if i < NT:
                dst_fn = (lambda c, i=i: hT[:, c, i * 128:(i + 1) * 128])
                dkeys = [("hT", c, i) for c in range(KC)]
                col = 0
            else:
                dst_fn = (lambda c, i=i: hcT[:, c, (i - NT) * 128:(i - NT + 1) * 128])
                dkeys = [("hcT", c, i - NT) for c in range(KC)]
                col = 1
            norm_transpose(xts[bi], ("xt", bi), dst_fn, dkeys, S1, SH1, col, (2 * bi, 2 * bi + 1), tmps[bi], "p1")
        ada_mm(3, 4)
        ada_mm(2, 3)
        ada_fin(S2, SH2, 2, 3, 1)
        if "hT" in dbg:
            P.dma("sp", dbg["hT"], hT, reads=[("hT", c, i) for c in range(KC) for i in range(NT)])
        P.barrier(scr)
        A.release(m_after_persist)
        if upto == "p1":
            raise _Stop()

        maskB = A.alloc([NTYPE, 128], BF16)
        P.dma("pool", maskB, maskB_d, writes=["maskB"])
        wb = A.alloc([KC, 7, 128], BF16)
        BT = A.alloc([2, NTYPE, 128], BF16)
        rpst = A.alloc([NTYPE, 128], BF16)
        slabQ = A.alloc([SEQ], BF16)
        slabK = A.alloc([SEQ], BF16)
        rv = A.alloc([NT, 128], BF16)
        nva = A.alloc([NT, 2, 65], BF16)
        srg = A.alloc([NT, 128], BF16)
        DS = A.alloc([NT, 2, 64], F32)
        Rb = A.alloc([NT, 2, 64], BF16)
        BLK = 4
        rtmp = [A.alloc([BLK, 4, 32], F32) for _ in range(2)] * 2
        cs_t = [A.alloc([2, BLK, 32], F32) for _ in range(2)]
        qk_tm = [A.alloc([BLK, 256], BF16) for _ in range(2)]
        Vfb = [A.alloc([BLK, 2, 2, 64], BF16) for _ in range(2)]
        crk = A.alloc([2, 128], BF16)
        cVfb = A.alloc([2, 2, 2, 64], BF16)
        cnva = A.alloc([2, 2, 65], BF16)
        cnkT = A.alloc([CTX], BF16)
        PT = [A.alloc([7, 128], BF16) for _ in range(2)]
        SDT = [A.alloc([2, 4, 128], BF16) for _ in range(2)]
        QfbT = [A.alloc([2, 4, 128], BF16) for _ in range(2)]
        sq = A.alloc([512], F32)
        ms = A.alloc([8], F32)
        on = A.alloc([512], F32)
        ytile = [A.alloc([4, 128], BF16) for _ in range(2)]
        ystage = [A.alloc([512], BF16) for _ in range(2)]
        rc = A.alloc([2], F32)

        qm = [[A.alloc([128], BF16) for _ in range(2)] for _ in range(2)]
        for hh_ in range(2):
            for par_ in range(2):
                P.op("pool", lambda e, hh_=hh_, par_=par_: e.memset(qm[hh_][par_], 0.0), writes=[("qm", hh_, par_)])
        P.op("pool", lambda e: e.memset(nva, 1.0), writes=["nva_init"])
        P.op("pool", lambda e: e.memset(cnva, 1.0), writes=["cnva_init"])

        def pair_body(hp):
            hk = ("hp", hp)
            for d_ in range(2):
                P.op("act", lambda e, d_=d_: e.activation(out=QW[:, d_, :], in_=rowc[:, d_, :], func=AF.Exp,
                                                           scale=lgp[:, hp * 2 + d_:hp * 2 + d_ + 1]),
                     writes=["QW"])
            for s in range(7):
                c0 = s * 512 + hp * 128
                P.dma("pool", wb[:, :, s, :], w_in[:, c0:c0 + 128].rearrange("(k p) n -> p k n", p=128), writes=[("wb", s)])
            wbk = [("wb", s) for s in range(7)]
            for hh in range(2):
                P.dma("pool", rpst, rpbB[2 * hp + hh], writes=["rpst"])
                P.op("dve", lambda e, hh=hh: e.tensor_tensor(out=BT[:, hh, :, :], in0=rpst, in1=maskB, op=ALU.add),
                     reads=["rpst", "maskB"], writes=[("BT", hh)])
            for ct in range(2):
                def mm(e, ct=ct):
                    for k in range(KC):
                        ins = e.matmul(PB(6)[:, 0:256], lhsT=hcT[:, k, ct * 128:(ct + 1) * 128], rhs=wb[:, k, 1:3, :].rearrange("p s n -> p (s n)"),
                                       start=(k == 0), stop=(k == KC - 1))
                    for k in range(KC):
                        ins = e.matmul(PB(6)[:, 256:384], lhsT=hcT[:, k, ct * 128:(ct + 1) * 128], rhs=wb[:, k, 6, :],
                                       start=(k == 0), stop=(k == KC - 1))
                    return ins
                P.op("pe", mm, reads=wbk + ["hcT"], writes=PK(6))
                P.op("act", lambda e, ct=ct: e.copy(out=crk[:, ct, :], in_=PB(6)[:, 0:128]), writes=PK(6) + [("crk", ct)])
                for hh in range(2):
                    for d_ in range(2):
                        P.op("dve", lambda e, ct=ct, hh=hh, d_=d_: e.tensor_scalar(
                            out=cVfb[:, ct, hh, d_, :], in0=PB(6)[:, 128 + hh * 64:128 + (hh + 1) * 64],
                            scalar1=ckw[:, ct, 2 * hp + hh, d_:d_ + 1], scalar2=None, op0=ALU.mult),
                            reads=["ckw"], writes=PK(6) + [("cVfb", ct)])
                P.op("dve", lambda e, ct=ct: e.tensor_copy(out=cnva[:, ct, :, 0:64], in_=PB(6)[:, 256:384].rearrange("p (h d) -> p h d", d=64)),
                     reads=["cnva_init"], writes=PK(6) + [("cnva", ct)])

            def mm(e):
                for k in range(KC):
                    ins = e.matmul(PB(7)[:, 0:CTX], lhsT=wb[:, k, 5, :], rhs=hcT[:, k, :], start=(k == 0), stop=(k == KC - 1))
                return ins
            P.op("pe", mm, reads=wbk + ["hcT"], writes=PK(7))
            P.op("act", lambda e: e.copy(out=cnkT, in_=PB(7)[:, 0:CTX]), writes=PK(7) + ["cnkT"])

            def mm(e):
                for hh in range(2):
                    for ct in range(2):
                        ins = e.matmul(PB(6)[hh * 64:(hh + 1) * 64, 0:128], lhsT=crk[:, ct, hh * 64:(hh + 1) * 64],
                                       rhs=cVfb[:, ct, hh, :, :].rearrange("p a b -> p (a b)"), start=(ct == 0), stop=(ct == 1))
                return ins
            P.op("pe", mm, reads=[("crk", 0), ("crk", 1), ("cVfb", 0), ("cVfb", 1)], writes=PK(6))
            P.op("dve", lambda e: e.tensor_copy(out=DS[:, 0, 0, :], in_=PB(6)[:, 0:64]), writes=PK(6) + [("DS", 0, 0)])
            P.op("dve", lambda e: e.tensor_copy(out=DS[:, NT - 1, 1, :], in_=PB(6)[:, 64:128]), writes=PK(6) + [("DS", NT - 1, 1)])

            if upto == "p2ctx":
                raise _Stop()
            def blockA(b0):
                bi = (b0 // BLK) % 2
                qk_, vf_ = qk_tm[bi], Vfb[bi]
                pbanks = [2, 3, 4, 5]
                for ii in range(BLK):
                    i = b0 + ii
                    pb = pbanks[ii]

                    def mm(e, i=i, pb=pb):
                        for k in range(KC):
                            ins = e.matmul(PB(pb)[:, 0:512], lhsT=hT[:, k, i * 128:(i + 1) * 128], rhs=wb[:, k, 0:4, :].rearrange("p s n -> p (s n)"),
                                           start=(k == 0), stop=(k == KC - 1))
                        return ins
                    P.op("pe", mm, reads=wbk, writes=PK(pb))
                    P.op("dve", lambda e, i=i, pb=pb: e.tensor_copy(out=rv[:, i, :], in_=PB(pb)[:, 256:384]),
                         writes=PK(pb) + [("rv", i)])
                    P.op("act", lambda e, i=i, pb=pb: e.activation(out=srg[:, i, :], in_=PB(pb)[:, 384:512], func=AF.Silu),
                         writes=PK(pb) + [("srg", i)])
                s5 = ps[:, 2:6, 0:256].rearrange("p b (g t f) -> p b g t f", g=4, t=2)
                q5 = qk_.rearrange("p b (g t f) -> p b g t f", g=4, t=2)
                cst = cs_t[bi]
                P.dma("sp", cst[:, 0, :, :], cos_d[:, b0:b0 + BLK, :], writes=[("cs", bi, 0)])
                P.dma("sp", cst[:, 1, :, :], sin_d[:, b0:b0 + BLK, :], writes=[("cs", bi, 1)])
                cosb = cst[:, 0, :, :].unsqueeze(2).to_broadcast([128, BLK, 4, 32])
                sinb = cst[:, 1, :, :].unsqueeze(2).to_broadcast([128, BLK, 4, 32])
                PKA = PK(2, 3, 4, 5)
                P.op("dve", lambda e, s5=s5, cosb=cosb: e.tensor_tensor(out=rtmp[0], in0=s5[:, :, :, 0, :], in1=cosb, op=ALU.mult),
                     reads=[("cs", bi, 0)], writes=PKA + [("rtmp", 0)])
                P.op("dve", lambda e, s5=s5, sinb=sinb: e.tensor_tensor(out=rtmp[1], in0=s5[:, :, :, 1, :], in1=sinb, op=ALU.mult),
                     reads=[("cs", bi, 1)], writes=PKA + [("rtmp", 1)])
                P.op("pool", lambda e, q5=q5: e.tensor_tensor(out=q5[:, :, :, 0, :], in0=rtmp[0], in1=rtmp[1], op=ALU.subtract),
                     reads=[("rtmp", 0), ("rtmp", 1)], writes=[("qk", bi, 0)])
                P.op("dve", lambda e, s5=s5, sinb=sinb: e.tensor_tensor(out=rtmp[0], in0=s5[:, :, :, 0, :], in1=sinb, op=ALU.mult),
                     reads=[("cs", bi, 1)], writes=PKA + [("rtmp", 0)])
                P.op("dve", lambda e, s5=s5, cosb=cosb: e.tensor_tensor(out=rtmp[1], in0=s5[:, :, :, 1, :], in1=cosb, op=ALU.mult),
                     reads=[("cs", bi, 0)], writes=PKA + [("rtmp", 1)])
                P.op("pool", lambda e, q5=q5: e.tensor_tensor(out=q5[:, :, :, 1, :], in0=rtmp[0], in1=rtmp[1], op=ALU.add),
                     reads=[("rtmp", 0), ("rtmp", 1)], writes=[("qk", bi, 1)])
                qkk = [("qk", bi, 0), ("qk", bi, 1)]
                for hh in range(2):
                    for d_ in range(2):
                        P.op("act", lambda e, hh=hh, d_=d_, vf_=vf_: e.activation(
                            out=vf_[:, :, hh, d_, :], in_=rv[:, b0:b0 + BLK, hh * 64:(hh + 1) * 64], func=AF.Copy,
                            scale=kw[:, 2 * hp + hh, d_:d_ + 1]),
                            reads=[("rv", b0 + ii) for ii in range(BLK)] + ["kw"], writes=[("Vfb", bi, hh, d_)])
                vfk = [("Vfb", bi, hh, d_) for hh in range(2) for d_ in range(2)]
                for which, slab, sk in ((0, slabQ, "slabQ"), (1, slabK, "slabK")):
                    pbt = 6 + which
                    pbv = PB(pbt).bitcast(BF16)

                    def tr(e, which=which, pbv=pbv, qk_=qk_):
                        for ii in range(BLK):
                            ins = e.transpose(out=pbv[:, ii * 128:(ii + 1) * 128], in_=qk_[:, ii, which * 128:(which + 1) * 128], identity=identb)
                        return ins
                    P.op("pe", tr, reads=qkk + ["identb"], writes=PK(pbt))
                    if which == 0:
                        P.op("act", lambda e, pbv=pbv, slab=slab: e.copy(out=slab[:, b0 * 128:(b0 + BLK) * 128], in_=pbv[:, 0:BLK * 128]),
                             writes=PK(pbt) + [(sk, b0 // BLK)])
                    else:
                        P.op("dve", lambda e, pbv=pbv, slab=slab: e.tensor_copy(out=slab[:, b0 * 128:(b0 + BLK) * 128], in_=pbv[:, 0:BLK * 128]),
                             writes=PK(pbt) + [(sk, b0 // BLK)])
                pbd = (b0 // BLK) % 2

                def mm(e, pbd=pbd, qk_=qk_, vf_=vf_):
                    for ii in range(BLK):
                        for hh in range(2):
                            ins = e.matmul(PB(pbd)[hh * 64:(hh + 1) * 64, ii * 128:(ii + 1) * 128],
                                           lhsT=qk_[:, ii, 128 + hh * 64:128 + (hh + 1) * 64],
                                           rhs=vf_[:, ii, hh, :, :].rearrange("p a b -> p (a b)"), start=True, stop=True)
                    return ins
                P.op("pe", mm, reads=qkk + vfk, writes=PK(pbd))
                pv = PB(pbd).rearrange("p (b d f) -> p b d f", d=2, f=64)
                lo, hi = b0, min(b0 + BLK, NT - 1)
                if hi > lo:
                    P.op("act", lambda e, lo=lo, hi=hi, pv=pv: e.copy(out=DS[:, lo + 1:hi + 1, 0, :], in_=pv[:, lo - b0:hi - b0, 0, :]),
                         writes=PK(pbd) + [("DS", c + 1, 0) for c in range(lo, hi)])
                lo2, hi2 = max(b0, 1), b0 + BLK
                if hi2 > lo2:
                    P.op("dve", lambda e, lo2=lo2, hi2=hi2, pv=pv: e.tensor_copy(out=DS[:, lo2 - 1:hi2 - 1, 1, :], in_=pv[:, lo2 - b0:hi2 - b0, 1, :]),
                         writes=PK(pbd) + [("DS", c - 1, 1) for c in range(lo2, hi2)])
            for b0 in range(0, NT, BLK):
                blockA(b0)
            if upto == "p2a":
                raise _Stop()
            for c in range(NT - 1):
                P.op("dve", lambda e, c=c: e.scalar_tensor_tensor(out=DS[:, c + 1, 0, :], in0=DS[:, c, 0, :], scalar=GL[:, 2 * hp:2 * hp + 1],
                                                                  in1=DS[:, c + 1, 0, :], op0=ALU.mult, op1=ALU.add),
                     reads=[("DS", c, 0), "GL"], writes=[("DS", c + 1, 0)])
            for c in range(NT - 1, 0, -1):
                P.op("dve", lambda e, c=c: e.scalar_tensor_tensor(out=DS[:, c - 1, 1, :], in0=DS[:, c, 1, :], scalar=GL[:, 2 * hp + 1:2 * hp + 2],
                                                                  in1=DS[:, c - 1, 1, :], op0=ALU.mult, op1=ALU.add),
                     reads=[("DS", c, 1), "GL"], writes=[("DS", c - 1, 1)])
            P.op("dve", lambda e: e.tensor_copy(out=Rb, in_=DS), reads=[("DS", c, d_) for c in range(NT) for d_ in range(2)], writes=["Rb"])
            if f"Rb{hp}" in dbg:
                P.dma("sp", dbg[f"Rb{hp}"], Rb, reads=["Rb"])

            if upto == "p2scan":
                raise _Stop()
            def blockB(g0):
                gi = (g0 // 4) % 2
                pbo = [2 + gi, 4 + gi]
                yt = ytile[gi]
                qf = QfbT[gi]
                sd = SDT[gi]
                P.op("pool", lambda e, qf=qf: e.tensor_tensor(
                    out=qf, in0=slabQ[:, g0 * 128:(g0 + 4) * 128].rearrange("p (c i) -> p c i", c=4).unsqueeze(1).to_broadcast([128, 2, 4, 128]),
                    in1=QW.unsqueeze(2).to_broadcast([128, 2, 4, 128]), op=ALU.mult),
                    reads=[("slabQ", g0 // BLK), "QW"], writes=[("QfbT", gi)])
                for hh in range(2):
                    def mm(e, hh=hh):
                        for cc in range(4):
                            c = g0 + cc
                            ins = e.matmul(PB(hh)[:, cc * 128:(cc + 1) * 128], lhsT=slabK[hh * 64:(hh + 1) * 64, c * 128:(c + 1) * 128],
                                           rhs=slabQ[hh * 64:(hh + 1) * 64, c * 128:(c + 1) * 128], start=True, stop=True)
                        return ins
                    P.op("pe", mm, reads=[("slabQ", g0 // BLK), ("slabK", g0 // BLK)], writes=PK(hh))
                    P.op("dve", lambda e, hh=hh, sd=sd: e.tensor_tensor(
                        out=sd[:, hh, :, :], in0=PB(hh).rearrange("p (c i) -> p c i", c=4),
                        in1=DT[:, 2 * hp + hh, :].unsqueeze(1).to_broadcast([128, 4, 128]), op=ALU.mult),
                        reads=["DT"], writes=PK(hh) + [("SDT", gi, hh)])
                for hh in range(2):
                    def mm(e, hh=hh, sd=sd, qf=qf):
                        for cc in range(4):
                            c = g0 + cc
                            o_ = PB(pbo[hh])[:, cc * 64:(cc + 1) * 64]
                            e.matmul(o_, lhsT=sd[:, hh, cc, :], rhs=rv[:, c, hh * 64:(hh + 1) * 64], start=True, stop=False)
                            e.matmul(o_, lhsT=qf[hh * 64:(hh + 1) * 64, 0, cc, :], rhs=Rb[hh * 64:(hh + 1) * 64, c, 0, :], start=False, stop=False)
                            ins = e.matmul(o_, lhsT=qf[hh * 64:(hh + 1) * 64, 1, cc, :], rhs=Rb[hh * 64:(hh + 1) * 64, c, 1, :], start=False, stop=True)
                        return ins
                    P.op("pe", mm, reads=[("SDT", gi, hh), ("QfbT", gi), "Rb"] + [("rv", g0 + cc) for cc in range(4)], writes=PK(pbo[hh]))
                for hh in range(2):
                    P.op("act", lambda e, hh=hh: e.activation(out=sq[:, hh * 256:(hh + 1) * 256], in_=PB(pbo[hh])[:, 0:256], func=AF.Square),
                         writes=PK(pbo[hh]) + [("sq", hh)])
                P.op("dve", lambda e: e.tensor_reduce(out=ms, in_=sq.rearrange("p (g f) -> p g f", f=64), axis=AX.X, op=ALU.add),
                     reads=[("sq", 0), ("sq", 1)], writes=["ms"])
                P.op("act", lambda e: e.activation(out=ms, in_=ms, func=AF.Sqrt, scale=1.0 / 64, bias=EPS), writes=["ms"])
                P.op("dve", lambda e: e.reciprocal(out=ms, in_=ms), writes=["ms"])
                for hh in range(2):
                    P.op("dve", lambda e, hh=hh: e.tensor_tensor(
                        out=on[:, hh * 256:(hh + 1) * 256].rearrange("p (g f) -> p g f", f=64),
                        in0=PB(pbo[hh])[:, 0:256].rearrange("p (g f) -> p g f", f=64),
                        in1=ms[:, hh * 4:(hh + 1) * 4].unsqueeze(2).to_broadcast([128, 4, 64]), op=ALU.mult),
                        reads=["ms"], writes=PK(pbo[hh]) + [("on", hh)])
                    P.op("pool", lambda e, hh=hh: e.tensor_tensor(
                        out=yt[:, :, hh * 64:(hh + 1) * 64], in0=on[:, hh * 256:(hh + 1) * 256].rearrange("p (c f) -> p c f", f=64),
                        in1=srg[:, g0:g0 + 4, hh * 64:(hh + 1) * 64], op=ALU.mult),
                        reads=[("on", hh)] + [("srg", g0 + cc) for cc in range(4)],
                        writes=[("ytile", gi, "h", hh)] + ([("ytile", gi)] + [("ytile", gi, cc) for cc in range(4)] if hh == 1 else []))
                pbt = 6 + gi
                pbv = PB(pbt).bitcast(BF16)

                def tr(e, pbv=pbv, yt=yt):
                    for cc in range(4):
                        ins = e.transpose(out=pbv[:, cc * 128:(cc + 1) * 128], in_=yt[:, cc, :], identity=identb)
                    return ins
                P.op("pe", tr, reads=[("ytile", gi), "identb", ("ytile", gi, "h", 0), ("ytile", gi, "h", 1)] + [("ytile", gi, cc) for cc in range(4)], writes=PK(pbt))
                P.op("act", lambda e, pbv=pbv, gi=gi: e.copy(out=ystage[gi], in_=pbv[:, 0:512]), writes=PK(pbt) + [("ystage", gi)])
                P.dma("sp", yT_d[hp, :, g0 * 128:(g0 + 4) * 128], ystage[gi], reads=[("ystage", gi)], writes=[("yT", 0, hp, g0 // 4)])
            for g0 in range(0, NT, 4):
                blockB(g0)

            if upto == "p2b":
                raise _Stop()
            def naproj(nb):
                for which, slab, sk, slot in ((0, slabQ, "slabQ", 4), (1, slabK, "slabK", 5)):
                    pbp = 4 + which

                    def mm(e, nb=nb, slot=slot, pbp=pbp):
                        for k in range(KC):
                            ins = e.matmul(PB(pbp)[:, 0:512], lhsT=wb[:, k, slot, :], rhs=hT[:, k, nb * 512:(nb + 1) * 512],
                                           start=(k == 0), stop=(k == KC - 1))
                        return ins
                    P.op("pe", mm, reads=wbk, writes=PK(pbp))
                    if which == 0:
                        P.op("act", lambda e, nb=nb, pbp=pbp: e.activation(out=slabQ[:, nb * 512:(nb + 1) * 512], in_=PB(pbp), func=AF.Copy, scale=0.125),
                             writes=PK(pbp) + [("slabQ", nb)])
                    else:
                        P.op("dve", lambda e, nb=nb, pbp=pbp: e.tensor_copy(out=slabK[:, nb * 512:(nb + 1) * 512], in_=PB(pbp)),
                             writes=PK(pbp) + [("slabK", nb)])
                pbp = 6 + (nb % 2)

                def mm(e, nb=nb, pbp=pbp):
                    for ii in range(4):
                        i = nb * 4 + ii
                        for k in range(KC):
                            ins = e.matmul(PB(pbp)[:, ii * 128:(ii + 1) * 128], lhsT=hT[:, k, i * 128:(i + 1) * 128], rhs=wb[:, k, 6, :],
                                           start=(k == 0), stop=(k == KC - 1))
                    return ins
                P.op("pe", mm, reads=wbk, writes=PK(pbp))
                P.op("pool" if False else "dve", lambda e, nb=nb, pbp=pbp: e.tensor_copy(
                    out=nva[:, nb * 4:(nb + 1) * 4, :, 0:64], in_=PB(pbp).rearrange("p (i h d) -> p i h d", h=2, d=64)),
                    reads=["nva_init"], writes=PK(pbp) + [("nva", nb)])
            for nb in range(NB):
                naproj(nb)
            if upto == "p2np":
                raise _Stop()
            def na_front(t, hh):
                lst = per_t[t]
                pi = hh
                pA, pB_ = 2 * pi, 2 * pi + 1
                pt_ = PT[pi]
                nloc = len(lst)
                assert nloc <= 5
                qmb = qm[hh][t % 2]
                P.op("dve", lambda e: e.tensor_copy(out=qmb[hh * 64:(hh + 1) * 64, :], in_=slabQ[hh * 64:(hh + 1) * 64, t * 128:(t + 1) * 128]),
                     reads=[("slabQ", t // 4)], writes=[("qm", hh, t % 2)])

                def mm(e):
                    for m, (u, ty) in enumerate(lst):
                        o_ = (PB(pA)[:, m * 128:(m + 1) * 128] if m < 4 else PB(pB_)[:, 0:128])
                        e.matmul(o_, lhsT=slabK[:, u * 128:(u + 1) * 128], rhs=qmb, start=True, stop=False)
                        ins = e.matmul(o_, lhsT=identb, rhs=BT[:, hh, ty, :], start=False, stop=True)
                    for ct in range(2):
                        ins = e.matmul(PB(pB_)[:, (1 + ct) * 128:(2 + ct) * 128], lhsT=cnkT[:, ct * 128:(ct + 1) * 128],
                                       rhs=qmb, start=True, stop=True)
                    return ins
                kblocks = sorted(set(u // 4 for (u, _) in lst))
                P.op("pe", mm, reads=[("qm", hh, t % 2), "cnkT", ("BT", hh), "identb"] + [("slabK", kb) for kb in kblocks],
                     writes=PK(pA, pB_))
                na4 = min(nloc, 4)
                P.op("act", lambda e: e.activation(out=pt_[:, 0:na4, :], in_=PB(pA)[:, 0:na4 * 128].rearrange("p (m q) -> p m q", q=128), func=AF.Exp),
                     writes=PK(pA) + [("PT", pi, 0)])
                lo_ = 0 if nloc == 5 else 1
                P.op("act", lambda e: e.activation(out=pt_[:, 4 + lo_:7, :], in_=PB(pB_)[:, lo_ * 128:3 * 128].rearrange("p (m q) -> p m q", q=128), func=AF.Exp),
                     writes=PK(pB_) + [("PT", pi, 1)])

            def na_back(t, hh):
                lst = per_t[t]
                pi = hh
                pt_ = PT[pi]
                gi = (t // 4) % 2
                yt = ytile[gi]
                pbo = 4 + (t % 2)
                kblocks = sorted(set(u // 4 for (u, _) in lst))

                def mm(e):
                    o_ = PB(pbo)[:, hh * 66:hh * 66 + 65]
                    for m, (u, ty) in enumerate(lst):
                        slot = m if m < 4 else 4
                        e.matmul(o_, lhsT=pt_[:, slot, :], rhs=nva[:, u, hh, :], start=(m == 0), stop=False)
                    for ct in range(2):
                        ins = e.matmul(o_, lhsT=pt_[:, 5 + ct, :], rhs=cnva[:, ct, hh, :], start=False, stop=(ct == 1))
                    return ins
                P.op("pe", mm, reads=[("PT", pi, 0), ("PT", pi, 1), ("cnva", 0), ("cnva", 1)] + [("nva", kb) for kb in kblocks],
                     writes=PK(pbo))
                if hh == 0:
                    return
                ov = PB(pbo)[:, 0:132].rearrange("p (h f) -> p h f", f=66)
                P.op("dve", lambda e: e.reciprocal(out=rc, in_=ov[:, :, 64]), writes=PK(pbo) + ["rc"])
                P.op("dve", lambda e: e.tensor_tensor(
                    out=yt[:, t % 4, :].rearrange("p (h d) -> p h d", d=64), in0=ov[:, :, 0:64],
                    in1=rc.unsqueeze(2).to_broadcast([128, 2, 64]), op=ALU.mult),
                    reads=["rc"], writes=PK(pbo) + [("ytile", gi, t % 4)])
                if t % 4 == 3:
                    g0 = t - 3
                    pbt = 6 + gi
                    pbv = PB(pbt).bitcast(BF16)

                    def tr(e):
                        for cc in range(4):
                            ins = e.transpose(out=pbv[:, cc * 128:(cc + 1) * 128], in_=yt[:, cc, :], identity=identb)
                        return ins
                    P.op("pe", tr, reads=[("ytile", gi, cc) for cc in range(4)] + [("ytile", gi), "identb"], writes=PK(pbt))
                    P.op("act", lambda e: e.copy(out=ystage[gi], in_=pbv[:, 0:512]), writes=PK(pbt) + [("ystage", gi)])
                    P.dma("sp", yT_d[4 + hp, :, g0 * 128:(g0 + 4) * 128], ystage[gi], reads=[("ystage", gi)], writes=[("yT", 1, hp, g0 // 4)])
            if hp == 0:
                P.dma("pool", wgb, w_in[:, 3584:5632], writes=["wgb"])
                P.dma("pool", wrob, w_ro, writes=["wrob"])
                P.dma("pool", wnob, w_no, writes=["wnob"])
                P.dma("pool", wob, w_o, writes=["wob"])
                P.dma("pool", wadab[0], w_ada[:, 2 * D:3 * D], writes=["wadab0"])
                P.dma("pool", wadab[1], w_ada[:, 5 * D:6 * D], writes=["wadab1"])
            if hp == 1:
                P.dma("pool", w1b.rearrange("r (a c) -> (r a) c", a=2), w_ff1.rearrange("r (a c) -> (r a) c", a=2), writes=["w1b"])
            if hp == 2:
                P.dma("pool", w2b, w_ff2, writes=["w2b"])
            units = [(t, hh) for t in range(NT) for hh in range(2)]
            for k_, (t_, hh_) in enumerate(units):
                na_front(t_, hh_)
                if k_ >= 1:
                    na_back(*units[k_ - 1])
            na_back(*units[-1])
        for hp in range(4):
            pair_body(hp)
        P.barrier(scr)
        A.release(m_after_persist)
        A.release(m0)

        if upto == "p2":
            raise _Stop()
        GT = [A.alloc([D], F32) for _ in range(2)]
        m3 = A.mark()
        wbufg = A.alloc([KC, 1024], BF16)
        bb = A.alloc([D], F32)
        gb = A.alloc([D], F32)
        for gi_, j in enumerate((2, 5)):
            P.dma("sp", wbufg, wadab[gi_].rearrange("(k p) n -> p k n", p=128), writes=["wbufg"])
            P.dma("sp", bb, b_ada[0:1, j * D:(j + 1) * D].partition_broadcast(128), writes=["bb"])
            P.dma("sp", gb, gpost[gi_:gi_ + 1, :].partition_broadcast(128), writes=["gb"])

            def mm(e):
                for half in range(2):
                    for k in range(KC):
                        ins = e.matmul(PB(half)[:, 0:512], lhsT=sc_rep[:, k, :], rhs=wbufg[:, k, half * 512:(half + 1) * 512],
                                       start=(k == 0), stop=(k == KC - 1))
                return ins
            P.op("pe", mm, reads=["wbufg", "sc_rep"], writes=PK(0, 1))
            P.op("dve", lambda e, gi_=gi_: e.tensor_tensor(out=GT[gi_].rearrange("p (b n) -> p b n", b=2), in0=ps[:, 0:2, :], in1=bb.rearrange("p (b n) -> p b n", b=2), op=ALU.add),
                 reads=["bb"], writes=PK(0, 1) + [("GT", gi_)])
            P.op("dve", lambda e, gi_=gi_: e.tensor_tensor(out=GT[gi_], in0=GT[gi_], in1=gb, op=ALU.mult), reads=["gb"], writes=[("GT", gi_)])
        P.barrier(scr)
        A.release(m3)

        if upto == "p3p":
            raise _Stop()
        Wg = A.alloc([KC, 2048], BF16)
        Wro = A.alloc([4, D], BF16)
        Wno = A.alloc([4, D], BF16)
        Wo = A.alloc([KC, D], BF16)
        def load3a_weights():
            for q4 in range(4):
                P.dma("sp", Wg[:, :, q4 * 512:(q4 + 1) * 512], wgb[:, q4 * 512:(q4 + 1) * 512].rearrange("(k p) n -> p k n", p=128), writes=[("Wg", q4)])
            P.dma("sp", Wro, wrob.rearrange("(k p) n -> p k n", p=128), writes=["Wro"])
            P.dma("sp", Wno, wnob.rearrange("(k p) n -> p k n", p=128), writes=["Wno"])
            for q2 in range(2):
                P.dma("sp", Wo[:, :, q2 * 512:(q2 + 1) * 512], wob[:, q2 * 512:(q2 + 1) * 512].rearrange("(k p) n -> p k n", p=128), writes=[("Wo", q2)])
        xbA = [A.alloc([4, D], F32) for _ in range(2)]
        junkA = A.alloc([D], BF16)
        tmpA3 = [dict(junk=junkA, junk_key="junkA", ss=A.alloc([1], F32), rstd=A.alloc([1], F32), xn=A.alloc([D], F32), key=("nt3", i_)) for i_ in range(2)]
        hTb = A.alloc([KC, 512], BF16)
        yTbA = [A.alloc([8, 512], BF16) for _ in range(2)]
        sgT = A.alloc([16, 512], F32)
        z1A = [A.alloc([512], F32) for _ in range(2)]
        z2A = [A.alloc([512], F32) for _ in range(2)]
        zT = A.alloc([KC, 512], BF16)
        ssyA = [A.alloc([1], F32) for _ in range(2)]
        rsyA = [A.alloc([1], F32) for _ in range(2)]
        tyA = [A.alloc([D], F32) for _ in range(2)]
        Wgk = [("Wg", q4) for q4 in range(4)]

        def load3a(nb):
            xb = xbA[nb % 2]
            for tt in range(4):
                P.dma("sp", xb[:, tt, :], x[(nb * 4 + tt) * 128:(nb * 4 + tt + 1) * 128, :], writes=[("xbA", nb % 2, tt)])
            P.dma("sp", yTbA[nb % 2], yT_d[:, :, nb * 512:(nb + 1) * 512].rearrange("a p n -> p a n"), writes=[("yTb", nb % 2)])

        def n3a_p1(nb, tt):
            xb = xbA[nb % 2]
            return norm_p1(xb[:, tt, :], ("xbA", nb % 2, tt), tmpA3[tt % 2])

        def n3a_p2(nb, tt, xnk):
            norm_p2(xnk, (lambda c: hTb[:, c, tt * 128:(tt + 1) * 128]), [("hTb", c, tt) for c in range(KC)], S1, SH1, 0, (6, 7), tmpA3[tt % 2])

        def norm3a(nb):
            for tt in range(4):
                n3a_p2(nb, tt, n3a_p1(nb, tt))

        def blk3a(nb):
            xb = xbA[nb % 2]
            yTb = yTbA[nb % 2]
            if nb + 1 < NB:
                load3a(nb + 1)
            hkeys = [("hTb", c, tt) for c in range(KC) for tt in range(4)]
            xnks = {}
            for g in range(16):
                pb = g % 2

                def mm(e, g=g, pb=pb):
                    for k in range(KC):
                        ins = e.matmul(PB(pb), lhsT=Wg[:, k, g * 128:(g + 1) * 128], rhs=hTb[:, k, :], start=(k == 0), stop=(k == KC - 1))
                    return ins
                P.op("pe", mm, reads=hkeys + [("Wg", g // 4)], writes=PK(pb))
                P.op("act", lambda e, g=g, pb=pb: e.activation(out=sgT[:, g, :], in_=PB(pb), func=AF.Sigmoid), writes=PK(pb) + [("sgT", g)])
                if nb + 1 < NB and g in (2, 6):
                    xnks[g // 4] = n3a_p1(nb + 1, g // 4)
            for fc in range(KC):
                pa, pbb = 2 + fc % 2, 4 + fc % 2
                z1, z2 = z1A[fc % 2], z2A[fc % 2]

                def mm(e, fc=fc, pa=pa):
                    for k in range(4):
                        ins = e.matmul(PB(pa), lhsT=Wro[:, k, fc * 128:(fc + 1) * 128], rhs=yTb[:, k, :], start=(k == 0), stop=(k == 3))
                    return ins
                P.op("pe", mm, reads=[("yTb", nb % 2), "Wro"], writes=PK(pa))

                def mm(e, fc=fc, pbb=pbb):
                    for k in range(4):
                        ins = e.matmul(PB(pbb), lhsT=Wno[:, k, fc * 128:(fc + 1) * 128], rhs=yTb[:, 4 + k, :], start=(k == 0), stop=(k == 3))
                    return ins
                P.op("pe", mm, reads=[("yTb", nb % 2), "Wno"], writes=PK(pbb))
                P.op("dve", lambda e, fc=fc, pa=pa, z1=z1: e.tensor_tensor(out=z1, in0=PB(pa), in1=sgT[:, fc, :], op=ALU.mult),
                     reads=[("sgT", fc)], writes=PK(pa) + [("z1", fc % 2)])
                P.op("dve", lambda e, fc=fc, pbb=pbb, z2=z2: e.tensor_tensor(out=z2, in0=PB(pbb), in1=sgT[:, 8 + fc, :], op=ALU.mult),
                     reads=[("sgT", 8 + fc)], writes=PK(pbb) + [("z2", fc % 2)])
                P.op("pool", lambda e, fc=fc, z1=z1, z2=z2: e.tensor_tensor(out=zT[:, fc, :], in0=z1, in1=z2, op=ALU.add),
                     reads=[("z1", fc % 2), ("z2", fc % 2)], writes=[("zT", fc)])
                if nb + 1 < NB and fc % 2 == 1:
                    tt_ = fc // 2
                    n3a_p2(nb + 1, tt_, xnks[tt_])
                    if tt_ + 2 < 4:
                        xnks[tt_ + 2] = n3a_p1(nb + 1, tt_ + 2)
            for tt in range(4):
                i = nb * 4 + tt
                py0 = 2 * (tt % 2)
                ssy, rsy, ty = ssyA[tt % 2], rsyA[tt % 2], tyA[tt % 2]

                def mm(e, tt=tt, py0=py0):
                    for half in range(2):
                        for k in range(KC):
                            ins = e.matmul(PB(py0 + half), lhsT=zT[:, k, tt * 128:(tt + 1) * 128], rhs=Wo[:, k, half * 512:(half + 1) * 512],
                                           start=(k == 0), stop=(k == KC - 1))
                    return ins
                P.op("pe", mm, reads=[("zT", fc) for fc in range(KC)] + [("Wo", 0), ("Wo", 1)], writes=PK(py0, py0 + 1))
                jk = tmpA3[tt % 2]
                P.op("act", lambda e, py0=py0, jk=jk, ssy=ssy: e.activation(out=jk["junk"].rearrange("p (b n) -> p b n", b=2), in_=ps[:, py0:py0 + 2, :], func=AF.Square, accum_out=ssy),
                     writes=PK(py0, py0 + 1) + ["junkA", ("ssy", tt % 2)])
                P.op("act", lambda e, ssy=ssy, rsy=rsy: e.activation(out=rsy, in_=ssy, func=AF.Sqrt, scale=1.0 / D, bias=EPS),
                     reads=[("ssy", tt % 2)], writes=[("rsy", tt % 2)])
                P.op("dve", lambda e, rsy=rsy: e.reciprocal(out=rsy, in_=rsy), writes=[("rsy", tt % 2)])
                P.op("dve", lambda e, py0=py0, rsy=rsy, ty=ty: e.scalar_tensor_tensor(
                    out=ty.rearrange("p (b n) -> p b n", b=2), in0=ps[:, py0:py0 + 2, :], scalar=rsy[:, 0:1],
                    in1=GT[0].rearrange("p (b n) -> p b n", b=2), op0=ALU.mult, op1=ALU.mult),
                    reads=[("rsy", tt % 2), ("GT", 0)], writes=PK(py0, py0 + 1) + [("ty", tt % 2)])
                P.op("pool", lambda e, tt=tt, ty=ty, xb=xb: e.tensor_tensor(out=ty, in0=ty, in1=xb[:, tt, :], op=ALU.add),
                     reads=[("xbA", nb % 2, tt)], writes=[("ty", tt % 2)])
                P.dma("sp", out[i * 128:(i + 1) * 128, :], ty, reads=[("ty", tt % 2)], writes=[("x1d", i)])
        load3a(0)
        load3a_weights()
        norm3a(0)
        for nb in range(NB):
            blk3a(nb)
        P.barrier(scr)
        A.release(m3)

        if upto == "p3a":
            raise _Stop()
        W1 = A.alloc([KC, 4 * D], BF16)
        W2 = A.alloc([32, D], BF16)
        def load3b_weights():
            for q8 in range(8):
                P.dma("sp", W1[:, :, q8 * 512:(q8 + 1) * 512], w1b[:, q8 * 512:(q8 + 1) * 512].rearrange("(k p) n -> p k n", p=128), writes=[("W1", q8)])
            for q8 in range(8):
                P.dma("sp", W2[:, q8 * 4:(q8 + 1) * 4, :], w2b[q8 * 512:(q8 + 1) * 512, :].rearrange("(k p) n -> p k n", p=128), writes=[("W2", q8)])
        xtB = [A.alloc([D], F32) for _ in range(2)]
        xrB = A.alloc([D], F32)
        rl = [A.alloc([512], F32) for _ in range(2)]
        h2TB = [A.alloc([KC, 512], BF16) for _ in range(2)]
        uT = A.alloc([32, 512], BF16)
        ssB = [A.alloc([1], F32) for _ in range(2)]
        rstdB = [A.alloc([1], F32) for _ in range(2)]
        ssyB = [A.alloc([1], F32) for _ in range(2)]
        rsyB = [A.alloc([1], F32) for _ in range(2)]
        tmpB3 = [dict(junk=rl[i_].bitcast(BF16), junk_key=("rl", i_), ss=ssB[i_], rstd=rstdB[i_], xn=xtB[i_], key=("nt4", i_)) for i_ in range(2)]
        W2k = [("W2", q8) for q8 in range(8)]

        def n3b_p1(nb, tt):
            i = nb * 4 + tt
            bi = i % 2
            P.dma("sp", xtB[bi], out[i * 128:(i + 1) * 128, :], reads=[("x1d", i)], writes=[("xtB", bi)])
            return norm_p1(xtB[bi], ("xtB", bi), tmpB3[bi], inplace=True)

        def n3b_p2(nb, tt, xnk):
            i = nb * 4 + tt
            h2T = h2TB[nb % 2]
            norm_p2(xnk, (lambda c: h2T[:, c, tt * 128:(tt + 1) * 128]), [("h2T", nb % 2, c, tt) for c in range(KC)], S2, SH2, 0, (6, 7), tmpB3[i % 2])

        def blk3b(nb):
            h2T = h2TB[nb % 2]
            hkeys = [("h2T", nb % 2, c, tt) for c in range(KC) for tt in range(4)]
            nxt = nb + 1 < NB
            xnks = {}
            for j in range(32):
                pb = j % 2

                def mm(e, j=j, pb=pb):
                    for k in range(KC):
                        ins = e.matmul(PB(pb), lhsT=W1[:, k, j * 128:(j + 1) * 128], rhs=h2T[:, k, :], start=(k == 0), stop=(k == KC - 1))
                    return ins
                P.op("pe", mm, reads=hkeys + [("W1", j // 4)], writes=PK(pb))
                P.op("act", lambda e, pb=pb: e.activation(out=rl[pb], in_=PB(pb), func=AF.Relu), writes=PK(pb) + [("rl", pb)])
                P.op("dve" if j % 2 == 0 else "pool", lambda e, j=j, pb=pb: e.tensor_tensor(out=uT[:, j, :], in0=rl[pb], in1=rl[pb], op=ALU.mult),
                     reads=[("rl", pb)], writes=[("uT", j)])
                if nxt:
                    if j == 3:
                        xnks[0] = n3b_p1(nb + 1, 0)
                    elif j == 7:
                        xnks[1] = n3b_p1(nb + 1, 1)
                    elif j == 15:
                        n3b_p2(nb + 1, 0, xnks[0])
                        xnks[2] = n3b_p1(nb + 1, 2)
                    elif j == 21:
                        n3b_p2(nb + 1, 1, xnks[1])
                        xnks[3] = n3b_p1(nb + 1, 3)
                    elif j == 27:
                        n3b_p2(nb + 1, 2, xnks[2])
                    elif j == 31:
                        n3b_p2(nb + 1, 3, xnks[3])
            for tt in range(4):
                i = nb * 4 + tt
                pbm = 2 + 2 * (tt % 2)
                ssy, rsy = ssyB[tt % 2], rsyB[tt % 2]
                P.dma("sp", xrB, out[i * 128:(i + 1) * 128, :], reads=[("x1d", i)], writes=["xrB"])

                def mm(e, tt=tt, pbm=pbm):
                    for half in range(2):
                        for j in range(32):
                            ins = e.matmul(PB(pbm + half), lhsT=uT[:, j, tt * 128:(tt + 1) * 128], rhs=W2[:, j, half * 512:(half + 1) * 512],
                                           start=(j == 0), stop=(j == 31))
                    return ins
                P.op("pe", mm, reads=[("uT", j) for j in range(32)] + W2k, writes=PK(pbm, pbm + 1))
                jb = tt % 2
                P.op("act", lambda e, pbm=pbm, jb=jb, ssy=ssy: e.activation(out=rl[jb].bitcast(BF16).rearrange("p (b n) -> p b n", b=2), in_=ps[:, pbm:pbm + 2, :], func=AF.Square, accum_out=ssy),
                     writes=PK(pbm, pbm + 1) + [("rl", jb), ("ssyB", tt % 2)])
                P.op("act", lambda e, ssy=ssy, rsy=rsy: e.activation(out=rsy, in_=ssy, func=AF.Sqrt, scale=1.0 / D, bias=EPS),
                     reads=[("ssyB", tt % 2)], writes=[("rsyB", tt % 2)])
                P.op("dve", lambda e, rsy=rsy: e.reciprocal(out=rsy, in_=rsy), writes=[("rsyB", tt % 2)])
                P.op("dve", lambda e, pbm=pbm, rsy=rsy: e.scalar_tensor_tensor(
                    out=ps[:, pbm:pbm + 2, :], in0=ps[:, pbm:pbm + 2, :], scalar=rsy[:, 0:1],
                    in1=GT[1].rearrange("p (b n) -> p b n", b=2), op0=ALU.mult, op1=ALU.mult),
                    reads=[("rsyB", tt % 2), ("GT", 1)], writes=PK(pbm, pbm + 1))
                P.op("dve", lambda e, pbm=pbm: e.tensor_tensor(out=xrB.rearrange("p (b n) -> p b n", b=2), in0=ps[:, pbm:pbm + 2, :],
                                                               in1=xrB.rearrange("p (b n) -> p b n", b=2), op=ALU.add),
                     writes=PK(pbm, pbm + 1) + ["xrB"])
                P.dma("sp", out[i * 128:(i + 1) * 128, :], xrB, reads=["xrB"], writes=[("outd", i)])
        for tt in range(4):
            n3b_p2(0, tt, n3b_p1(0, tt))
        load3b_weights()
        for nb in range(NB):
            blk3b(nb)
    try:
        body()
    except _Stop:
        pass
    info = P.emit()
    info["arena_peak"] = A.peak
    cmp_.__exit__(None, None, None)
    cm.__exit__(None, None, None)
    return nc, info, types


def prep_inputs(inputs, SEQ, types):
    NT = SEQ // 128
    f = lambda a: np.ascontiguousarray(np.asarray(a, dtype=np.float32))
    x = f(inputs["x"]); c = f(inputs["c"]); ctx = f(inputs["ctx"]); c_ctx = f(inputs["c_ctx"])
    B = x.shape[0]
    w_ada = f(inputs["w_ada"][0]); b_ada = f(inputs["b_ada"][0])
    shared = dict(
        w_ada=w_ada,
        bada_fm=np.ascontiguousarray(b_ada.reshape(48, 128).T),
        b_ada=b_ada.reshape(1, -1),
        gpre_fm=np.ascontiguousarray(np.stack([f(inputs["norm_pre_mix"][0]).reshape(KC, 128).T,
                                               f(inputs["norm_pre_ffn"][0]).reshape(KC, 128).T], axis=1)),
        gpost=np.ascontiguousarray(np.stack([f(inputs["norm_post_mix"][0]), f(inputs["norm_post_ffn"][0])], axis=0)),
        w_in=f(inputs["w_in"][0]), w_ro=f(inputs["w_ret_out"][0]), w_no=f(inputs["w_na_out"][0]),
        w_o=f(inputs["w_o"][0]), w_ff1=f(inputs["w_ff1"][0]), w_ff2=f(inputs["w_ff2"][0]),
    )
    lg = f(inputs["ret_decay_logit"][0])
    lgt_pair = np.zeros((128, 8), np.float32)
    def pair_body(hp):
        for d_ in range(2):
            lgt_pair[0:64, hp * 2 + d_] = lg[d_, 2 * hp]
            lgt_pair[64:128, hp * 2 + d_] = lg[d_, 2 * hp + 1]
    for hp in range(4):
        pair_body(hp)
    lgt_bc = np.zeros((128, 16), np.float32)
    for h in range(8):
        for d_ in range(2):
            lgt_bc[:, 2 * h + d_] = lg[d_, h]
    shared["lgt_pair"] = lgt_pair
    shared["lgt_bc"] = lgt_bc
    shared["ident"] = np.eye(128, dtype=np.float32)
    j = np.arange(128)[:, None].astype(np.float32)
    i = np.arange(128)[None, :].astype(np.float32)
    cmat = np.stack([np.maximum(i - j, 0), np.maximum(j - i, 0), (i >= j) * 0.125, (j > i) * 0.125], axis=1).astype(np.float32)
    shared["cmat"] = np.ascontiguousarray(cmat)
    jj = np.arange(128, dtype=np.float32)
    shared["colc"] = np.ascontiguousarray(np.stack([127 - jj, jj, 255 - jj, 127 - jj, jj, 128 + jj], axis=1))
    ii = np.arange(128, dtype=np.float32)
    shared["rowc"] = np.ascontiguousarray(np.broadcast_to(np.stack([ii + 1, 128 - ii], axis=0)[None], (128, 2, 128)).astype(np.float32))
    cos, sin = rope_tables(SEQ)
    shared["cos_tm"] = np.ascontiguousarray(cos.reshape(NT, 128, 32).transpose(1, 0, 2))
    shared["sin_tm"] = np.ascontiguousarray(sin.reshape(NT, 128, 32).transpose(1, 0, 2))
    mask, idr, idc = na_consts(types)
    rpb = f(inputs["na_rpb"][0])
    rpbB = rpb[:, idr, idc]
    shared["rpbB"] = np.ascontiguousarray(rpbB.transpose(0, 2, 1, 3))
    shared["maskB"] = np.ascontiguousarray(mask.transpose(1, 0, 2))
    in_maps = []
    for b in range(B):
        m = dict(shared)
        m["x"] = x[b]
        m["ctx"] = ctx[b]
        m["c_fm"] = np.ascontiguousarray(np.stack([c[b].reshape(KC, 128).T, c_ctx.reshape(KC, 128).T], axis=2))
        in_maps.append(m)
    return in_maps


_CACHE = {}


def kernel(**inputs):
    x = inputs["x"]
    B, SEQ, _ = x.shape
    if SEQ not in _CACHE:
        _CACHE[SEQ] = build(SEQ)
    nc, info, types = _CACHE[SEQ]
    in_maps = prep_inputs(inputs, SEQ, types)
    res = run_bass_kernel_spmd(nc, in_maps, core_ids=list(range(B)))
    return np.stack([np.asarray(r["out"], dtype=np.float32) for r in res.results], axis=0)
```

```python
import numpy as np
import ml_dtypes
import concourse.bass as bass
import concourse.mybir as mybir
from concourse.bass_utils import run_bass_kernel_spmd

F32 = mybir.dt.float32
BF16 = mybir.dt.bfloat16
U8 = mybir.dt.uint8
AF = mybir.ActivationFunctionType
ALU = mybir.AluOpType
AX = mybir.AxisListType

D = 1024
KC = 8
CTX = 256
GRID_W = 64
EPS = 1e-6
CH = 4096


class Prog:
    ENGS = ("pe", "act", "dve", "pool", "sp")

    def __init__(self, nc, n_dma_sems=16):
        self.nc = nc
        self.ops = []
        self.last_w = {}
        self.readers = {}
        self.n_dma_sems = n_dma_sems
        self.pending = {e: set() for e in self.ENGS}
        self.bar_start = 0
        self.nbar = 0

    def op(self, eng, fn, reads=(), writes=(), dma=False):
        oid = len(self.ops)
        deps = set()
        for k in list(reads) + list(writes):
            if k in self.last_w:
                deps.add(self.last_w[k])
        for k in writes:
            for r in self.readers.get(k, ()):
                deps.add(r)
        deps |= self.pending[eng]
        self.pending[eng] = set()
        deps.discard(oid)
        self.ops.append(dict(eng=eng, fn=fn, deps=deps, dma=dma, has_dep=False))
        for k in reads:
            self.readers.setdefault(k, []).append(oid)
        for k in writes:
            self.last_w[k] = oid
            self.readers[k] = []
        return oid

    def dma(self, q, out, in_, reads=(), writes=(), **kw):
        def fn(e):
            return e.dma_start(out=out, in_=in_, **kw)
        return self.op(q, fn, reads, writes, dma=True)

    def barrier(self, scratch):
        n = self.nbar
        self.nbar += 1
        dmas = [i for i in range(self.bar_start, len(self.ops)) if self.ops[i]["dma"]]
        marks = []
        marks.append(self.op("act", lambda e: e.copy(out=scratch["act"], in_=scratch["act"]), writes=[("bar", n, "act")]))
        marks.append(self.op("dve", lambda e: e.memset(scratch["dve"], 0.0), writes=[("bar", n, "dve")]))
        marks.append(self.op("pool", lambda e: e.memset(scratch["pool"], 0.0), writes=[("bar", n, "pool")]))
        for e in self.ENGS:
            self.pending[e] = set(marks) | set(dmas)
        self.last_w = {}
        self.readers = {}
        self.bar_start = len(self.ops)

    def emit(self, final_wait_eng="sp"):
        nc = self.nc
        ops = self.ops
        for i, o in enumerate(ops):
            keep = set()
            for d in o["deps"]:
                od = ops[d]
                if (not od["dma"]) and od["eng"] == o["eng"] and o["eng"] == "pe" and not o["dma"]:
                    continue
                keep.add(d)
            o["deps"] = keep
            for d in keep:
                ops[d]["has_dep"] = True
        tail = [i for i, o in enumerate(ops) if o["dma"] and not o["has_dep"]]
        for i in tail:
            ops[i]["has_dep"] = True
        cnt = {e: 0 for e in self.ENGS}
        for o in ops:
            if not o["dma"] and o["has_dep"]:
                o["seq"] = cnt[o["eng"]]
                cnt[o["eng"]] += 1
        sems = {}
        for e in self.ENGS:
            n = (cnt[e] + CH - 1) // CH
            sems[e] = [nc.alloc_semaphore(name=f"s_{e}_{j}") for j in range(n)]
        dsems = [nc.alloc_semaphore(name=f"s_dma_{j}") for j in range(self.n_dma_sems)]
        dcount = [0] * self.n_dma_sems
        dnext = 0
        waited = {e: {} for e in self.ENGS}

        def plan_wait(o, e, sem, val):
            key = id(sem)
            if waited[e].get(key, 0) >= val:
                return
            waited[e][key] = val
            o["waits"].append((sem, val))

        for i, o in enumerate(ops):
            e = o["eng"]
            o["waits"] = []
            for d in sorted(o["deps"]):
                od = ops[d]
                if od["dma"]:
                    plan_wait(o, e, od["dsem"], od["dval"])
                else:
                    s = od["seq"]
                    plan_wait(o, e, sems[od["eng"]][s // CH], s % CH + 1)
            if o["dma"] and e == "pool":
                sw = nc.alloc_semaphore(name=f"s_swdma_{i}")
                o["dsem"] = sw
                o["dval"] = 16
                o["inc"] = (sw, 16)
            elif o["dma"]:
                j = dnext
                dnext = (dnext + 1) % self.n_dma_sems
                if dcount[j] > 0:
                    plan_wait(o, e, dsems[j], dcount[j])
                dcount[j] += 16
                o["dsem"] = dsems[j]
                o["dval"] = dcount[j]
                o["inc"] = (dsems[j], 16)
            elif o["has_dep"]:
                s = o["seq"]
                o["inc"] = (sems[e][s // CH], 1)
            else:
                o["inc"] = None
        final_waits = []
        fo = dict(waits=final_waits)
        for i in tail:
            plan_wait(fo, final_wait_eng, ops[i]["dsem"], ops[i]["dval"])

        def run_engine(ename, eng):
            for o in ops:
                if o["eng"] != ename:
                    continue
                for (sem, val) in o["waits"]:
                    eng.wait_ge(sem, val)
                ins = o["fn"](eng)
                if o["inc"] is not None:
                    ins.then_inc(o["inc"][0], o["inc"][1])
            if ename == final_wait_eng:
                for (sem, val) in final_waits:
                    eng.wait_ge(sem, val)

        with nc.Block() as block:
            @block.sync
            def _(eng):
                run_engine("sp", eng)

            @block.tensor
            def _(eng):
                run_engine("pe", eng)

            @block.scalar
            def _(eng):
                run_engine("act", eng)

            @block.vector
            def _(eng):
                run_engine("dve", eng)

            @block.gpsimd
            def _(eng):
                run_engine("pool", eng)
        return dict(n_ops=len(ops), cnt=cnt)


class Arena:
    def __init__(self, ap_u8, size):
        self.ap = ap_u8
        self.size = size
        self.off = 0
        self.peak = 0

    def alloc(self, shape, dtype):
        esz = {F32: 4, BF16: 2}[dtype]
        n = int(np.prod(shape))
        nbytes = (n * esz + 63) // 64 * 64
        assert self.off + nbytes <= self.size, f"arena overflow {self.off}+{nbytes}>{self.size}"
        v = self.ap[:, self.off:self.off + n * esz].bitcast(dtype)
        self.off += nbytes
        self.peak = max(self.peak, self.off)
        if len(shape) == 1:
            return v
        names = " ".join(f"d{i}" for i in range(len(shape)))
        kw = {f"d{i}": int(s) for i, s in enumerate(shape)}
        return v.rearrange(f"p ({names}) -> p {names}", **kw)

    def mark(self):
        return self.off

    def release(self, m):
        self.off = m


def na_structure(rows):
    T = rows // 2
    types = {}
    per_t = []
    for t in range(T):
        lst = []
        for u in range(T):
            vis = []
            anyv = False
            for kr in range(2):
                for qr in range(2):
                    r = 2 * t + qr
                    r0 = min(max(r - 4, 0), rows - 8)
                    v = r0 <= 2 * u + kr < r0 + 8
                    vis.append(v)
                    anyv = anyv or v
            if not anyv:
                continue
            key = (u - t, tuple(vis))
            if key not in types:
                types[key] = len(types)
            lst.append((u, types[key]))
        per_t.append(lst)
    return per_t, types


def na_consts(types):
    nt = len(types)
    mask = np.zeros((nt, 128, 128), np.float32)
    idr = np.zeros((nt, 128, 128), np.int64)
    idc = np.zeros((nt, 128, 128), np.int64)
    kc = np.arange(64)[:, None]
    qc = np.arange(64)[None, :]
    c0 = np.clip(qc - 8, 0, 48)
    colok = (kc >= c0) & (kc < c0 + 16)
    dc = np.clip(kc - qc + 15, 0, 30)
    for (delta, vis), ti in types.items():
        for kr in range(2):
            for qr in range(2):
                v = vis[kr * 2 + qr]
                dr = int(np.clip(2 * delta + kr - qr + 7, 0, 14))
                blk = np.where(colok & v, 0.0, -30000.0).astype(np.float32)
                mask[ti, kr * 64:(kr + 1) * 64, qr * 64:(qr + 1) * 64] = blk
                idr[ti, kr * 64:(kr + 1) * 64, qr * 64:(qr + 1) * 64] = dr
                idc[ti, kr * 64:(kr + 1) * 64, qr * 64:(qr + 1) * 64] = dc
    return mask, idr, idc


def rope_tables(n):
    pos = np.arange(n)
    row = (pos // GRID_W).astype(np.float32)
    col = (pos % GRID_W).astype(np.float32)
    inv = (10000.0 ** (-np.arange(0, 32, 2, dtype=np.float32) / 32)).astype(np.float32)
    ang = np.concatenate([row[:, None] * inv, col[:, None] * inv], axis=-1).astype(np.float32)
    return np.cos(ang).astype(np.float32), np.sin(ang).astype(np.float32)


class _Stop(Exception):
    pass


def build(SEQ, debug=(), upto=None):
    NT = SEQ // 128
    ROWS = SEQ // 64
    NB = SEQ // 512
    per_t, types = na_structure(ROWS)
    NTYPE = len(types)
    nc = bass.Bass("TRN2", target_bir_lowering=False)

    def din(name, shape, dt=F32):
        return nc.dram_tensor(name, list(shape), dt, kind="ExternalInput").ap()

    x = din("x", [SEQ, D])
    ctx = din("ctx", [CTX, D])
    c_fm = din("c_fm", [128, KC, 2])
    w_ada = din("w_ada", [D, 6 * D])
    bada_fm = din("bada_fm", [128, 48])
    b_ada = din("b_ada", [1, 6 * D])
    gpre_fm = din("gpre_fm", [128, 2, KC])
    gpost = din("gpost", [2, D])
    w_in = din("w_in", [D, 5632])
    w_ro = din("w_ro", [512, D])
    w_no = din("w_no", [512, D])
    w_o = din("w_o", [D, D])
    w_ff1 = din("w_ff1", [D, 4 * D])
    w_ff2 = din("w_ff2", [4 * D, D])
    lgt_pair = din("lgt_pair", [128, 8])
    lgt_bc = din("lgt_bc", [128, 16])
    ident_d = din("ident", [128, 128])
    cmat = din("cmat", [128, 4, 128])
    colc_d = din("colc", [128, 6])
    rowc_d = din("rowc", [128, 2, 128])
    cos_d = din("cos_tm", [128, NT, 32])
    sin_d = din("sin_tm", [128, NT, 32])
    rpbB = din("rpbB", [8, 128, NTYPE, 128])
    maskB_d = din("maskB", [128, NTYPE, 128])
    out = nc.dram_tensor("out", [SEQ, D], F32, kind="ExternalOutput").ap()
    yT_d = nc.dram_tensor("yT_scratch", [8, 128, SEQ], BF16, kind="Internal").ap()
    wgb = nc.dram_tensor("wg_bf", [D, 2048], BF16, kind="Internal").ap()
    wrob = nc.dram_tensor("wro_bf", [512, D], BF16, kind="Internal").ap()
    wnob = nc.dram_tensor("wno_bf", [512, D], BF16, kind="Internal").ap()
    wob = nc.dram_tensor("wo_bf", [D, D], BF16, kind="Internal").ap()
    w1b = nc.dram_tensor("w1_bf", [D, 4 * D], BF16, kind="Internal").ap()
    w2b = nc.dram_tensor("w2_bf", [4 * D, D], BF16, kind="Internal").ap()
    wadab = nc.dram_tensor("wada_bf", [2, D, D], BF16, kind="Internal").ap()
    dbg = {}
    for name, shape, dt in debug:
        dbg[name] = nc.dram_tensor(name, list(shape), dt, kind="ExternalOutput").ap()

    P = Prog(nc)
    ARENA_BYTES = 207 * 1024
    cm = nc.sbuf_tensor("arena", [128, ARENA_BYTES], U8)
    arena_h = cm.__enter__()
    A = Arena(arena_h, ARENA_BYTES)
    cmp_ = nc.psum_tensor("ps", [128, 8, 512], F32)
    ps = cmp_.__enter__()

    def PB(b):
        return ps[:, b, :]

    def PK(*bs):
        return [("ps", b) for b in bs]

    def body():
        ident = A.alloc([128], F32)
        identb = A.alloc([128], BF16)
        scr = {e: A.alloc([16], F32) for e in ("act", "dve", "pool")}
        S1 = A.alloc([KC, 2], F32)
        SH1 = A.alloc([KC, 2], F32)
        S2 = A.alloc([KC, 2], F32)
        SH2 = A.alloc([KC, 2], F32)
        scb = A.alloc([KC, 2], BF16)
        sc_rep = A.alloc([KC, 128], BF16)
        gpre = A.alloc([2, KC], F32)
        badafm = A.alloc([48], F32)

        P.dma("sp", ident, ident_d, writes=["ident"])
        P.op("dve", lambda e: e.tensor_copy(out=identb, in_=ident), reads=["ident"], writes=["identb"])
        for e_ in ("act", "dve", "pool"):
            pass
        P.op("dve", lambda e: e.memset(scr["dve"], 0.0), writes=["scr_dve"])
        P.op("pool", lambda e: e.memset(scr["pool"], 0.0), writes=["scr_pool"])
        P.op("dve", lambda e: e.memset(scr["act"], 0.0), writes=["scr_act"])
        P.dma("sp", gpre, gpre_fm, writes=["gpre"])
        P.dma("sp", badafm, bada_fm, writes=["badafm"])

        m_phase2 = None

        def norm_p1(xt_ap, xt_key, tmp, inplace=False):
            junk, ss, rstd, xn = tmp["junk"], tmp["ss"], tmp["rstd"], tmp["xn"]
            tk = tmp["key"]
            jkey = tmp.get("junk_key", (tk, "junk"))
            P.op("act", lambda e: e.activation(out=junk, in_=xt_ap, func=AF.Square, accum_out=ss),
                 reads=[xt_key], writes=[jkey, (tk, "ss")])
            P.op("act", lambda e: e.activation(out=rstd, in_=ss, func=AF.Sqrt, scale=1.0 / D, bias=EPS),
                 reads=[(tk, "ss")], writes=[(tk, "rstd")])
            P.op("dve", lambda e: e.reciprocal(out=rstd, in_=rstd), writes=[(tk, "rstd")])
            xnk = xt_key if inplace else (tk, "xn")
            P.op("dve", lambda e: e.tensor_scalar(out=xn, in0=xt_ap, scalar1=rstd[:, 0:1], scalar2=None, op0=ALU.mult),
                 reads=[(tk, "rstd")] + ([] if inplace else [xt_key]), writes=[xnk])
            return xnk

        def norm_p2(xnk, dst_fn, dst_keys, Sc, Sh, col, banks, tmp):
            xn = tmp["xn"]
            for half in range(2):
                b = banks[half]

                def tr(e, half=half, b=b):
                    for cc in range(4):
                        c = half * 4 + cc
                        ins = e.transpose(out=PB(b)[:, cc * 128:(cc + 1) * 128], in_=xn[:, c * 128:(c + 1) * 128], identity=ident)
                    return ins
                P.op("pe", tr, reads=[xnk, "ident"], writes=PK(b))
                for cc in range(4):
                    c = half * 4 + cc
                    if c % 2 == 0:
                        P.op("act", lambda e, c=c, cc=cc, b=b: e.activation(
                            out=dst_fn(c), in_=PB(b)[:, cc * 128:(cc + 1) * 128], func=AF.Identity,
                            scale=Sc[:, c, col:col + 1], bias=Sh[:, c, col:col + 1]),
                            reads=["mod"], writes=PK(b) + [dst_keys[c]])
                    else:
                        P.op("dve", lambda e, c=c, cc=cc, b=b: e.tensor_scalar(
                            out=dst_fn(c), in0=PB(b)[:, cc * 128:(cc + 1) * 128],
                            scalar1=Sc[:, c, col:col + 1], scalar2=Sh[:, c, col:col + 1], op0=ALU.mult, op1=ALU.add),
                            reads=["mod"], writes=PK(b) + [dst_keys[c]])

        def norm_transpose(xt_ap, xt_key, dst_fn, dst_keys, Sc, Sh, col, banks, tmp, tag, inplace=False):
            xnk = norm_p1(xt_ap, xt_key, tmp, inplace)
            norm_p2(xnk, dst_fn, dst_keys, Sc, Sh, col, banks, tmp)

        m0 = A.mark()
        cm_t = A.alloc([4, 128], F32)
        colc = A.alloc([6], F32)
        rowc = A.alloc([2, 128], F32)
        lgp = A.alloc([8], F32)
        lgb = A.alloc([16], F32)
        cfm = A.alloc([KC, 2], F32)
        A.release(m0)
        DT = A.alloc([8, 128], F32)
        kw = A.alloc([8, 2], F32)
        ckw = A.alloc([2, 8, 2], F32)
        QW = A.alloc([2, 128], F32)
        GL = A.alloc([8], F32)
        rowc = A.alloc([2, 128], F32)
        lgp = A.alloc([8], F32)
        hT = A.alloc([KC, SEQ], BF16)
        hcT = A.alloc([KC, CTX], BF16)
        m_after_persist = A.mark()
        cm_t = A.alloc([4, 128], F32)
        colc = A.alloc([6], F32)
        lgb = A.alloc([16], F32)
        cfm = A.alloc([KC, 2], F32)
        tmpA = A.alloc([128], F32)
        tmpB = A.alloc([128], F32)
        arg16 = A.alloc([16], F32)
        argc = A.alloc([2, 8, 2], F32)
        wbuf0 = A.alloc([KC, 1024], BF16)
        modfm = A.alloc([4, KC, 2], F32)

        P.dma("sp", cm_t, cmat, writes=["cmat"])
        P.dma("sp", colc, colc_d, writes=["colc"])
        P.dma("sp", rowc, rowc_d, writes=["rowc"])
        P.dma("sp", lgp, lgt_pair, writes=["lgp"])
        P.dma("sp", lgb, lgt_bc, writes=["lgb"])
        P.dma("sp", cfm, c_fm, writes=["cfm"])

        for t_, k_ in ((lgp, "lgp"), (lgb, "lgb")):
            P.op("act", lambda e, t_=t_: e.activation(out=t_, in_=t_, func=AF.Exp, scale=-1.0), writes=[k_])
            P.op("act", lambda e, t_=t_: e.activation(out=t_, in_=t_, func=AF.Ln, bias=1.0), writes=[k_])
            P.op("dve", lambda e, t_=t_: e.tensor_scalar(out=t_, in0=t_, scalar1=-1.0, scalar2=None, op0=ALU.mult), writes=[k_])
        for h in range(8):
            P.op("act", lambda e, h=h: e.activation(out=tmpA, in_=cm_t[:, 0, :], func=AF.Exp, scale=lgb[:, 2 * h:2 * h + 1]),
                 reads=["cmat", "lgb"], writes=["tmpA"])
            P.op("act", lambda e, h=h: e.activation(out=tmpB, in_=cm_t[:, 1, :], func=AF.Exp, scale=lgb[:, 2 * h + 1:2 * h + 2]),
                 reads=["cmat", "lgb"], writes=["tmpB"])
            P.op("dve", lambda e: e.tensor_tensor(out=tmpA, in0=tmpA, in1=cm_t[:, 2, :], op=ALU.mult), reads=["cmat"], writes=["tmpA"])
            P.op("dve", lambda e: e.tensor_tensor(out=tmpB, in0=tmpB, in1=cm_t[:, 3, :], op=ALU.mult), reads=["cmat"], writes=["tmpB"])
            P.op("dve", lambda e, h=h: e.tensor_tensor(out=DT[:, h, :], in0=tmpA, in1=tmpB, op=ALU.add),
                 reads=["tmpA", "tmpB"], writes=["DT"])
        lgb3 = lgb.rearrange("p (h d) -> p h d", d=2)
        arg3 = arg16.rearrange("p (h d) -> p h d", d=2)
        for d_ in range(2):
            P.op("dve", lambda e, d_=d_: e.tensor_scalar(out=arg3[:, :, d_], in0=lgb3[:, :, d_], scalar1=colc[:, d_:d_ + 1], scalar2=None, op0=ALU.mult),
                 reads=["lgb", "colc"], writes=["arg16"])
        P.op("act", lambda e: e.activation(out=arg16, in_=arg16, func=AF.Exp), writes=["arg16"])
        P.op("dve", lambda e: e.tensor_scalar(out=kw.rearrange("p h d -> p (h d)"), in0=arg16, scalar1=0.125, scalar2=None, op0=ALU.mult),
             reads=["arg16"], writes=["kw"])
        for ct in range(2):
            for d_ in range(2):
                cc_ = 2 + ct if d_ == 0 else 4 + ct
                P.op("dve", lambda e, ct=ct, d_=d_, cc_=cc_: e.tensor_scalar(out=argc[:, ct, :, d_], in0=lgb3[:, :, d_], scalar1=colc[:, cc_:cc_ + 1], scalar2=None, op0=ALU.mult),
                     reads=["lgb", "colc"], writes=["argc"])
        P.op("act", lambda e: e.activation(out=argc, in_=argc, func=AF.Exp), writes=["argc"])
        P.op("dve", lambda e: e.tensor_scalar(out=ckw, in0=argc, scalar1=0.125, scalar2=None, op0=ALU.mult), reads=["argc"], writes=["ckw"])
        P.op("act", lambda e: e.activation(out=GL, in_=lgp, func=AF.Exp, scale=128.0), reads=["lgp"], writes=["GL"])

        P.op("act", lambda e: e.activation(out=scb, in_=cfm, func=AF.Silu), reads=["cfm"], writes=["scb"])
        P.op("dve", lambda e: e.tensor_copy(out=sc_rep, in_=scb[:, :, 0:1].to_broadcast([128, KC, 128])), reads=["scb"], writes=["sc_rep"])

        def load_w(dst, src_rows_cols, key, nk=KC):
            P.dma("pool", dst, src_rows_cols.rearrange("(k p) n -> p k n", p=128), writes=[key])

        wbuf1 = A.alloc([KC, 1024], BF16)
        wbufs = {0: (wbuf1, "wbuf1"), 1: (wbuf0, "wbuf0"), 3: (wbuf1, "wbuf1"), 4: (wbuf0, "wbuf0")}

        def ada_load(j):
            wb_, key_ = wbufs[j]
            load_w(wb_, w_ada[:, j * D:(j + 1) * D], key_)

        def ada_mm(mi, j):
            wb_, key_ = wbufs[j]

            def mm(e):
                for cc in range(8):
                    for k in range(KC):
                        ins = e.matmul(PB(0)[:, cc * 2:cc * 2 + 2], lhsT=wb_[:, k, cc * 128:(cc + 1) * 128], rhs=scb[:, k, :],
                                       start=(k == 0), stop=(k == KC - 1))
                return ins
            P.op("pe", mm, reads=[key_, "scb"], writes=PK(0))
            P.op("dve", lambda e: e.tensor_tensor(
                out=modfm[:, mi, :, :], in0=PB(0)[:, 0:16].rearrange("p (c t) -> p c t", t=2),
                in1=badafm[:, j * 8:(j + 1) * 8].unsqueeze(2).to_broadcast([128, KC, 2]), op=ALU.add),
                reads=["badafm"], writes=PK(0) + [("modfm", mi)])

        def ada_fin(Sx, SHx, mi_sh, mi_sc, gi):
            P.op("dve", lambda e: e.scalar_tensor_tensor(
                out=Sx, in0=modfm[:, mi_sc, :, :], scalar=1.0, in1=gpre[:, gi, :].unsqueeze(2).to_broadcast([128, KC, 2]),
                op0=ALU.add, op1=ALU.mult), reads=[("modfm", mi_sc), "gpre"], writes=["mod"])
            P.op("dve", lambda e: e.tensor_copy(out=SHx, in_=modfm[:, mi_sh, :, :]), reads=[("modfm", mi_sh)], writes=["mod"])

        ada_load(1)
        ada_load(0)
        ada_mm(1, 1)
        ada_mm(0, 0)
        ada_fin(S1, SH1, 0, 1, 0)
        ada_load(4)
        ada_load(3)

        if upto == "p0":
            raise _Stop()
        NBUF1 = 3
        xts = [A.alloc([D], F32) for _ in range(NBUF1)]
        tmps = []
        for i in range(NBUF1):
            tmps.append(dict(junk=A.alloc([D], BF16), ss=A.alloc([1], F32), rstd=A.alloc([1], F32), xn=A.alloc([D], F32), key=("nt", i)))

        def p1_load(i):
            bi = i % NBUF1
            src = x[i * 128:(i + 1) * 128, :] if i < NT else ctx[(i - NT) * 128:(i - NT + 1) * 128, :]
            P.dma("sp", xts[bi], src, writes=[("xt", bi)])

        def p1_front(i):
            bi = i % NBUF1
            return norm_p1(xts[bi], ("xt", bi), tmps[bi])

        def p1_back(i, xnk):
            bi = i % NBUF1
            if i < NT:
                dst_fn = (lambda c: hT[:, c, i * 128:(i + 1) * 128])
                dkeys = [("hT", c, i) for c in range(KC)]
                col = 0
            else:
                dst_fn = (lambda c: hcT[:, c, (i - NT) * 128:(i - NT + 1) * 128])
                dkeys = [("hcT", c, i - NT) for c in range(KC)]
                col = 1
            pbk = (2 * (i % 2), 2 * (i % 2) + 1)
            norm_p2(xnk, dst_fn, dkeys, S1, SH1, col, pbk, tmps[bi])
        NTT = NT + 2
        p1_load(0)
        p1_load(1)
        xk_prev = p1_front(0)
        for i in range(NTT):
            if i + 2 < NTT:
                p1_load(i + 2)
            xk_next = p1_front(i + 1) if i + 1 < NTT else None
            p1_back(i, xk_prev)
            xk_prev = xk_next
        ada_mm(3, 4)
        ada_mm(2, 3)
        ada_fin(S2, SH2, 2, 3, 1)
        if "hT" in dbg:
            P.dma("sp", dbg["hT"], hT, reads=[("hT", c, i) for c in range(KC) for i in range(NT)])
        P.barrier(scr)
        A.release(m_after_persist)
        if upto == "p1":
            raise _Stop()

        maskB = A.alloc([NTYPE, 128], BF16)
        P.dma("pool", maskB, maskB_d, writes=["maskB"])
        wb = A.alloc([KC, 7, 128], BF16)
        BT = A.alloc([2, NTYPE, 128], BF16)
        rpst = A.alloc([NTYPE, 128], BF16)
        slabQ = A.alloc([SEQ], BF16)
        slabK = A.alloc([SEQ], BF16)
        rv = A.alloc([NT, 128], BF16)
        nva = A.alloc([NT, 2, 65], BF16)
        srg = A.alloc([NT, 128], BF16)
        DS = A.alloc([NT, 2, 64], F32)
        Rb = A.alloc([NT, 2, 64], BF16)
        BLK = 4
        rtmp = [A.alloc([BLK, 4, 32], F32) for _ in range(2)] * 2
        cs_t = [A.alloc([2, BLK, 32], F32) for _ in range(2)]
        qk_tm = [A.alloc([BLK, 256], BF16) for _ in range(2)]
        Vfb = [A.alloc([BLK, 2, 2, 64], BF16) for _ in range(2)]
        crk = A.alloc([2, 128], BF16)
        cVfb = A.alloc([2, 2, 2, 64], BF16)
        cnva = A.alloc([2, 2, 65], BF16)
        cnkT = A.alloc([CTX], BF16)
        PT = [A.alloc([7, 128], BF16) for _ in range(2)]
        SDT = [A.alloc([2, 4, 128], BF16) for _ in range(2)]
        QfbT = [A.alloc([2, 4, 128], BF16) for _ in range(2)]
        sqA = [A.alloc([512], F32) for _ in range(2)]
        msA = [A.alloc([8], F32) for _ in range(2)]
        onA = [A.alloc([512], F32) for _ in range(2)]
        ytile = [A.alloc([4, 128], BF16) for _ in range(2)]
        ystage = [A.alloc([512], BF16) for _ in range(2)]
        rc = A.alloc([2], F32)

        qm = [[A.alloc([128], BF16) for _ in range(2)] for _ in range(2)]
        for hh_ in range(2):
            for par_ in range(2):
                P.op("pool", lambda e, hh_=hh_, par_=par_: e.memset(qm[hh_][par_], 0.0), writes=[("qm", hh_, par_)])
        P.op("pool", lambda e: e.memset(nva, 1.0), writes=["nva_init"])
        P.op("pool", lambda e: e.memset(cnva, 1.0), writes=["cnva_init"])

        def pair_body(hp):
            hk = ("hp", hp)
            for d_ in range(2):
                P.op("act", lambda e, d_=d_: e.activation(out=QW[:, d_, :], in_=rowc[:, d_, :], func=AF.Exp,
                                                           scale=lgp[:, hp * 2 + d_:hp * 2 + d_ + 1]),
                     writes=["QW"])
            for s in range(7):
                c0 = s * 512 + hp * 128
                P.dma("pool", wb[:, :, s, :], w_in[:, c0:c0 + 128].rearrange("(k p) n -> p k n", p=128), writes=[("wb", s)])
            wbk = [("wb", s) for s in range(7)]
            for hh in range(2):
                P.dma("pool", rpst, rpbB[2 * hp + hh], writes=["rpst"])
                P.op("dve", lambda e, hh=hh: e.tensor_tensor(out=BT[:, hh, :, :], in0=rpst, in1=maskB, op=ALU.add),
                     reads=["rpst", "maskB"], writes=[("BT", hh)])
            for ct in range(2):
                def mm(e, ct=ct):
                    for k in range(KC):
                        ins = e.matmul(PB(6)[:, 0:256], lhsT=hcT[:, k, ct * 128:(ct + 1) * 128], rhs=wb[:, k, 1:3, :].rearrange("p s n -> p (s n)"),
                                       start=(k == 0), stop=(k == KC - 1))
                    for k in range(KC):
                        ins = e.matmul(PB(6)[:, 256:384], lhsT=hcT[:, k, ct * 128:(ct + 1) * 128], rhs=wb[:, k, 6, :],
                                       start=(k == 0), stop=(k == KC - 1))
                    return ins
                P.op("pe", mm, reads=wbk + ["hcT"], writes=PK(6))
                P.op("act", lambda e, ct=ct: e.copy(out=crk[:, ct, :], in_=PB(6)[:, 0:128]), writes=PK(6) + [("crk", ct)])
                for hh in range(2):
                    for d_ in range(2):
                        P.op("dve", lambda e, ct=ct, hh=hh, d_=d_: e.tensor_scalar(
                            out=cVfb[:, ct, hh, d_, :], in0=PB(6)[:, 128 + hh * 64:128 + (hh + 1) * 64],
                            scalar1=ckw[:, ct, 2 * hp + hh, d_:d_ + 1], scalar2=None, op0=ALU.mult),
                            reads=["ckw"], writes=PK(6) + [("cVfb", ct)])
                P.op("dve", lambda e, ct=ct: e.tensor_copy(out=cnva[:, ct, :, 0:64], in_=PB(6)[:, 256:384].rearrange("p (h d) -> p h d", d=64)),
                     reads=["cnva_init"], writes=PK(6) + [("cnva", ct)])

            def mm(e):
                for k in range(KC):
                    ins = e.matmul(PB(7)[:, 0:CTX], lhsT=wb[:, k, 5, :], rhs=hcT[:, k, :], start=(k == 0), stop=(k == KC - 1))
                return ins
            P.op("pe", mm, reads=wbk + ["hcT"], writes=PK(7))
            P.op("act", lambda e: e.copy(out=cnkT, in_=PB(7)[:, 0:CTX]), writes=PK(7) + ["cnkT"])

            def mm(e):
                for hh in range(2):
                    for ct in range(2):
                        ins = e.matmul(PB(6)[hh * 64:(hh + 1) * 64, 0:128], lhsT=crk[:, ct, hh * 64:(hh + 1) * 64],
                                       rhs=cVfb[:, ct, hh, :, :].rearrange("p a b -> p (a b)"), start=(ct == 0), stop=(ct == 1))
                return ins
            P.op("pe", mm, reads=[("crk", 0), ("crk", 1), ("cVfb", 0), ("cVfb", 1)], writes=PK(6))
            P.op("dve", lambda e: e.tensor_copy(out=DS[:, 0, 0, :], in_=PB(6)[:, 0:64]), writes=PK(6) + [("DS", 0, 0)])
            P.op("dve", lambda e: e.tensor_copy(out=DS[:, NT - 1, 1, :], in_=PB(6)[:, 64:128]), writes=PK(6) + [("DS", NT - 1, 1)])

            if upto == "p2ctx":
                raise _Stop()
            def blockA(b0):
                bi = (b0 // BLK) % 2
                qk_, vf_ = qk_tm[bi], Vfb[bi]
                pbanks = [2, 3, 4, 5]
                for ii in range(BLK):
                    i = b0 + ii
                    pb = pbanks[ii]

                    def mm(e, i=i, pb=pb):
                        for k in range(KC):
                            ins = e.matmul(PB(pb)[:, 0:512], lhsT=hT[:, k, i * 128:(i + 1) * 128], rhs=wb[:, k, 0:4, :].rearrange("p s n -> p (s n)"),
                                           start=(k == 0), stop=(k == KC - 1))
                        return ins
                    P.op("pe", mm, reads=wbk, writes=PK(pb))
                    P.op("dve", lambda e, i=i, pb=pb: e.tensor_copy(out=rv[:, i, :], in_=PB(pb)[:, 256:384]),
                         writes=PK(pb) + [("rv", i)])
                    P.op("act", lambda e, i=i, pb=pb: e.activation(out=srg[:, i, :], in_=PB(pb)[:, 384:512], func=AF.Silu),
                         writes=PK(pb) + [("srg", i)])
                s5 = ps[:, 2:6, 0:256].rearrange("p b (g t f) -> p b g t f", g=4, t=2)
                q5 = qk_.rearrange("p b (g t f) -> p b g t f", g=4, t=2)
                cst = cs_t[bi]
                P.dma("sp", cst[:, 0, :, :], cos_d[:, b0:b0 + BLK, :], writes=[("cs", bi, 0)])
                P.dma("sp", cst[:, 1, :, :], sin_d[:, b0:b0 + BLK, :], writes=[("cs", bi, 1)])
                cosb = cst[:, 0, :, :].unsqueeze(2).to_broadcast([128, BLK, 4, 32])
                sinb = cst[:, 1, :, :].unsqueeze(2).to_broadcast([128, BLK, 4, 32])
                PKA = PK(2, 3, 4, 5)
                P.op("dve", lambda e, s5=s5, cosb=cosb: e.tensor_tensor(out=rtmp[0], in0=s5[:, :, :, 0, :], in1=cosb, op=ALU.mult),
                     reads=[("cs", bi, 0)], writes=PKA + [("rtmp", 0)])
                P.op("dve", lambda e, s5=s5, sinb=sinb: e.tensor_tensor(out=rtmp[1], in0=s5[:, :, :, 1, :], in1=sinb, op=ALU.mult),
                     reads=[("cs", bi, 1)], writes=PKA + [("rtmp", 1)])
                P.op("pool", lambda e, q5=q5: e.tensor_tensor(out=q5[:, :, :, 0, :], in0=rtmp[0], in1=rtmp[1], op=ALU.subtract),
                     reads=[("rtmp", 0), ("rtmp", 1)], writes=[("qk", bi, 0)])
                P.op("dve", lambda e, s5=s5, sinb=sinb: e.tensor_tensor(out=rtmp[0], in0=s5[:, :, :, 0, :], in1=sinb, op=ALU.mult),
                     reads=[("cs", bi, 1)], writes=PKA + [("rtmp", 0)])
                P.op("dve", lambda e, s5=s5, cosb=cosb: e.tensor_tensor(out=rtmp[1], in0=s5[:, :, :, 1, :], in1=cosb, op=ALU.mult),
                     reads=[("cs", bi, 0)], writes=PKA + [("rtmp", 1)])
                P.op("pool", lambda e, q5=q5: e.tensor_tensor(out=q5[:, :, :, 1, :], in0=rtmp[0], in1=rtmp[1], op=ALU.add),
                     reads=[("rtmp", 0), ("rtmp", 1)], writes=[("qk", bi, 1)])
                qkk = [("qk", bi, 0), ("qk", bi, 1)]
                for hh in range(2):
                    for d_ in range(2):
                        P.op("act", lambda e, hh=hh, d_=d_, vf_=vf_: e.activation(
                            out=vf_[:, :, hh, d_, :], in_=rv[:, b0:b0 + BLK, hh * 64:(hh + 1) * 64], func=AF.Copy,
                            scale=kw[:, 2 * hp + hh, d_:d_ + 1]),
                            reads=[("rv", b0 + ii) for ii in range(BLK)] + ["kw"], writes=[("Vfb", bi, hh, d_)])
                vfk = [("Vfb", bi, hh, d_) for hh in range(2) for d_ in range(2)]
                for which, slab, sk in ((0, slabQ, "slabQ"), (1, slabK, "slabK")):
                    pbt = 6 + which
                    pbv = PB(pbt).bitcast(BF16)

                    def tr(e, which=which, pbv=pbv, qk_=qk_):
                        for ii in range(BLK):
                            ins = e.transpose(out=pbv[:, ii * 128:(ii + 1) * 128], in_=qk_[:, ii, which * 128:(which + 1) * 128], identity=identb)
                        return ins
                    P.op("pe", tr, reads=qkk + ["identb"], writes=PK(pbt))
                    if which == 0:
                        P.op("act", lambda e, pbv=pbv, slab=slab: e.copy(out=slab[:, b0 * 128:(b0 + BLK) * 128], in_=pbv[:, 0:BLK * 128]),
                             writes=PK(pbt) + [(sk, b0 // BLK)])
                    else:
                        P.op("dve", lambda e, pbv=pbv, slab=slab: e.tensor_copy(out=slab[:, b0 * 128:(b0 + BLK) * 128], in_=pbv[:, 0:BLK * 128]),
                             writes=PK(pbt) + [(sk, b0 // BLK)])
                pbd = (b0 // BLK) % 2

                def mm(e, pbd=pbd, qk_=qk_, vf_=vf_):
                    for ii in range(BLK):
                        for hh in range(2):
                            ins = e.matmul(PB(pbd)[hh * 64:(hh + 1) * 64, ii * 128:(ii + 1) * 128],
                                           lhsT=qk_[:, ii, 128 + hh * 64:128 + (hh + 1) * 64],
                                           rhs=vf_[:, ii, hh, :, :].rearrange("p a b -> p (a b)"), start=True, stop=True)
                    return ins
                P.op("pe", mm, reads=qkk + vfk, writes=PK(pbd))
                pv = PB(pbd).rearrange("p (b d f) -> p b d f", d=2, f=64)
                lo, hi = b0, min(b0 + BLK, NT - 1)
                if hi > lo:
                    P.op("act", lambda e, lo=lo, hi=hi, pv=pv: e.copy(out=DS[:, lo + 1:hi + 1, 0, :], in_=pv[:, lo - b0:hi - b0, 0, :]),
                         writes=PK(pbd) + [("DS", c + 1, 0) for c in range(lo, hi)])
                lo2, hi2 = max(b0, 1), b0 + BLK
                if hi2 > lo2:
                    P.op("dve", lambda e, lo2=lo2, hi2=hi2, pv=pv: e.tensor_copy(out=DS[:, lo2 - 1:hi2 - 1, 1, :], in_=pv[:, lo2 - b0:hi2 - b0, 1, :]),
                         writes=PK(pbd) + [("DS", c - 1, 1) for c in range(lo2, hi2)])
            for b0 in range(0, NT, BLK):
                blockA(b0)
            if upto == "p2a":
                raise _Stop()
            def nvproj(nb):
                pbp = 6 + (nb % 2)

                def mm(e):
                    for ii in range(4):
                        i = nb * 4 + ii
                        for k in range(KC):
                            ins = e.matmul(PB(pbp)[:, ii * 128:(ii + 1) * 128], lhsT=hT[:, k, i * 128:(i + 1) * 128], rhs=wb[:, k, 6, :],
                                           start=(k == 0), stop=(k == KC - 1))
                    return ins
                P.op("pe", mm, reads=wbk, writes=PK(pbp))
                P.op("act", lambda e: e.copy(out=nva[:, nb * 4:(nb + 1) * 4, :, 0:64], in_=PB(pbp).rearrange("p (i h d) -> p i h d", h=2, d=64)),
                     reads=["nva_init"], writes=PK(pbp) + [("nva", nb)])
            for nb in range(NB):
                nvproj(nb)
            for c in range(NT - 1):
                P.op("dve", lambda e, c=c: e.scalar_tensor_tensor(out=DS[:, c + 1, 0, :], in0=DS[:, c, 0, :], scalar=GL[:, 2 * hp:2 * hp + 1],
                                                                  in1=DS[:, c + 1, 0, :], op0=ALU.mult, op1=ALU.add),
                     reads=[("DS", c, 0), "GL"], writes=[("DS", c + 1, 0)])
            for c in range(NT - 1, 0, -1):
                P.op("dve", lambda e, c=c: e.scalar_tensor_tensor(out=DS[:, c - 1, 1, :], in0=DS[:, c, 1, :], scalar=GL[:, 2 * hp + 1:2 * hp + 2],
                                                                  in1=DS[:, c - 1, 1, :], op0=ALU.mult, op1=ALU.add),
                     reads=[("DS", c, 1), "GL"], writes=[("DS", c - 1, 1)])
            P.op("dve", lambda e: e.tensor_copy(out=Rb, in_=DS), reads=[("DS", c, d_) for c in range(NT) for d_ in range(2)], writes=["Rb"])
            if f"Rb{hp}" in dbg:
                P.dma("sp", dbg[f"Rb{hp}"], Rb, reads=["Rb"])

            if upto == "p2scan":
                raise _Stop()
            def blockB(g0):
                gi = (g0 // 4) % 2
                pbo = [2 + gi, 4 + gi]
                yt = ytile[gi]
                qf = QfbT[gi]
                sd = SDT[gi]
                sq, ms, on = sqA[gi], msA[gi], onA[gi]
                P.op("pool", lambda e, qf=qf: e.tensor_tensor(
                    out=qf, in0=slabQ[:, g0 * 128:(g0 + 4) * 128].rearrange("p (c i) -> p c i", c=4).unsqueeze(1).to_broadcast([128, 2, 4, 128]),
                    in1=QW.unsqueeze(2).to_broadcast([128, 2, 4, 128]), op=ALU.mult),
                    reads=[("slabQ", g0 // BLK), "QW"], writes=[("QfbT", gi)])
                for hh in range(2):
                    def mm(e, hh=hh):
                        for cc in range(4):
                            c = g0 + cc
                            ins = e.matmul(PB(hh)[:, cc * 128:(cc + 1) * 128], lhsT=slabK[hh * 64:(hh + 1) * 64, c * 128:(c + 1) * 128],
                                           rhs=slabQ[hh * 64:(hh + 1) * 64, c * 128:(c + 1) * 128], start=True, stop=True)
                        return ins
                    P.op("pe", mm, reads=[("slabQ", g0 // BLK), ("slabK", g0 // BLK)], writes=PK(hh))
                    P.op("dve", lambda e, hh=hh, sd=sd: e.tensor_tensor(
                        out=sd[:, hh, :, :], in0=PB(hh).rearrange("p (c i) -> p c i", c=4),
                        in1=DT[:, 2 * hp + hh, :].unsqueeze(1).to_broadcast([128, 4, 128]), op=ALU.mult),
                        reads=["DT"], writes=PK(hh) + [("SDT", gi, hh)])
                for hh in range(2):
                    def mm(e, hh=hh, sd=sd, qf=qf):
                        for cc in range(4):
                            c = g0 + cc
                            o_ = PB(pbo[hh])[:, cc * 64:(cc + 1) * 64]
                            e.matmul(o_, lhsT=sd[:, hh, cc, :], rhs=rv[:, c, hh * 64:(hh + 1) * 64], start=True, stop=False)
                            e.matmul(o_, lhsT=qf[hh * 64:(hh + 1) * 64, 0, cc, :], rhs=Rb[hh * 64:(hh + 1) * 64, c, 0, :], start=False, stop=False)
                            ins = e.matmul(o_, lhsT=qf[hh * 64:(hh + 1) * 64, 1, cc, :], rhs=Rb[hh * 64:(hh + 1) * 64, c, 1, :], start=False, stop=True)
                        return ins
                    P.op("pe", mm, reads=[("SDT", gi, hh), ("QfbT", gi), "Rb"] + [("rv", g0 + cc) for cc in range(4)], writes=PK(pbo[hh]))
                for hh in range(2):
                    P.op("act", lambda e, hh=hh: e.activation(out=sq[:, hh * 256:(hh + 1) * 256], in_=PB(pbo[hh])[:, 0:256], func=AF.Square),
                         writes=PK(pbo[hh]) + [("sq", gi, hh)])
                P.op("dve", lambda e: e.tensor_reduce(out=ms, in_=sq.rearrange("p (g f) -> p g f", f=64), axis=AX.X, op=ALU.add),
                     reads=[("sq", gi, 0), ("sq", gi, 1)], writes=[("ms", gi)])
                P.op("act", lambda e: e.activation(out=ms, in_=ms, func=AF.Sqrt, scale=1.0 / 64, bias=EPS), writes=[("ms", gi)])
                P.op("dve", lambda e: e.reciprocal(out=ms, in_=ms), writes=[("ms", gi)])
                for hh in range(2):
                    P.op("dve", lambda e, hh=hh: e.tensor_tensor(
                        out=on[:, hh * 256:(hh + 1) * 256].rearrange("p (g f) -> p g f", f=64),
                        in0=PB(pbo[hh])[:, 0:256].rearrange("p (g f) -> p g f", f=64),
                        in1=ms[:, hh * 4:(hh + 1) * 4].unsqueeze(2).to_broadcast([128, 4, 64]), op=ALU.mult),
                        reads=[("ms", gi)], writes=PK(pbo[hh]) + [("on", gi, hh)])
                    P.op("pool", lambda e, hh=hh: e.tensor_tensor(
                        out=yt[:, :, hh * 64:(hh + 1) * 64], in0=on[:, hh * 256:(hh + 1) * 256].rearrange("p (c f) -> p c f", f=64),
                        in1=srg[:, g0:g0 + 4, hh * 64:(hh + 1) * 64], op=ALU.mult),
                        reads=[("on", gi, hh)] + [("srg", g0 + cc) for cc in range(4)],
                        writes=[("ytile", gi, "h", hh)] + ([("ytile", gi)] + [("ytile", gi, cc) for cc in range(4)] if hh == 1 else []))
                pbt = 6 + gi
                pbv = PB(pbt).bitcast(BF16)

                def tr(e, pbv=pbv, yt=yt):
                    for cc in range(4):
                        ins = e.transpose(out=pbv[:, cc * 128:(cc + 1) * 128], in_=yt[:, cc, :], identity=identb)
                    return ins
                P.op("pe", tr, reads=[("ytile", gi), "identb", ("ytile", gi, "h", 0), ("ytile", gi, "h", 1)] + [("ytile", gi, cc) for cc in range(4)], writes=PK(pbt))
                P.op("act", lambda e, pbv=pbv, gi=gi: e.copy(out=ystage[gi], in_=pbv[:, 0:512]), writes=PK(pbt) + [("ystage", gi)])
                P.dma("sp", yT_d[hp, :, g0 * 128:(g0 + 4) * 128], ystage[gi], reads=[("ystage", gi)], writes=[("yT", 0, hp, g0 // 4)])
            for g0 in range(0, NT, 4):
                blockB(g0)

            if upto == "p2b":
                raise _Stop()
            def naproj(nb):
                for which, slab, sk, slot in ((0, slabQ, "slabQ", 4), (1, slabK, "slabK", 5)):
                    pbp = 4 + which

                    def mm(e, nb=nb, slot=slot, pbp=pbp):
                        for k in range(KC):
                            ins = e.matmul(PB(pbp)[:, 0:512], lhsT=wb[:, k, slot, :], rhs=hT[:, k, nb * 512:(nb + 1) * 512],
                                           start=(k == 0), stop=(k == KC - 1))
                        return ins
                    P.op("pe", mm, reads=wbk, writes=PK(pbp))
                    if which == 0:
                        P.op("act", lambda e, nb=nb, pbp=pbp: e.activation(out=slabQ[:, nb * 512:(nb + 1) * 512], in_=PB(pbp), func=AF.Copy, scale=0.125),
                             writes=PK(pbp) + [("slabQ", nb)])
                    else:
                        P.op("dve", lambda e, nb=nb, pbp=pbp: e.tensor_copy(out=slabK[:, nb * 512:(nb + 1) * 512], in_=PB(pbp)),
                             writes=PK(pbp) + [("slabK", nb)])
            for nb in range(NB):
                naproj(nb)
            if upto == "p2np":
                raise _Stop()
            def na_front(t, hh):
                lst = per_t[t]
                pi = hh
                pA, pB_ = 2 * pi, 2 * pi + 1
                pt_ = PT[pi]
                nloc = len(lst)
                assert nloc <= 5
                qmb = qm[hh][t % 2]
                P.op("dve", lambda e: e.tensor_copy(out=qmb[hh * 64:(hh + 1) * 64, :], in_=slabQ[hh * 64:(hh + 1) * 64, t * 128:(t + 1) * 128]),
                     reads=[("slabQ", t // 4)], writes=[("qm", hh, t % 2)])

                def mm(e):
                    for m, (u, ty) in enumerate(lst):
                        o_ = (PB(pA)[:, m * 128:(m + 1) * 128] if m < 4 else PB(pB_)[:, 0:128])
                        e.matmul(o_, lhsT=slabK[:, u * 128:(u + 1) * 128], rhs=qmb, start=True, stop=False)
                        ins = e.matmul(o_, lhsT=identb, rhs=BT[:, hh, ty, :], start=False, stop=True)
                    for ct in range(2):
                        ins = e.matmul(PB(pB_)[:, (1 + ct) * 128:(2 + ct) * 128], lhsT=cnkT[:, ct * 128:(ct + 1) * 128],
                                       rhs=qmb, start=True, stop=True)
                    return ins
                kblocks = sorted(set(u // 4 for (u, _) in lst))
                P.op("pe", mm, reads=[("qm", hh, t % 2), "cnkT", ("BT", hh), "identb"] + [("slabK", kb) for kb in kblocks],
                     writes=PK(pA, pB_))
                na4 = min(nloc, 4)
                P.op("act", lambda e: e.activation(out=pt_[:, 0:na4, :], in_=PB(pA)[:, 0:na4 * 128].rearrange("p (m q) -> p m q", q=128), func=AF.Exp),
                     writes=PK(pA) + [("PT", pi, 0)])
                lo_ = 0 if nloc == 5 else 1
                P.op("act", lambda e: e.activation(out=pt_[:, 4 + lo_:7, :], in_=PB(pB_)[:, lo_ * 128:3 * 128].rearrange("p (m q) -> p m q", q=128), func=AF.Exp),
                     writes=PK(pB_) + [("PT", pi, 1)])

            def na_back(t, hh):
                lst = per_t[t]
                pi = hh
                pt_ = PT[pi]
                gi = (t // 4) % 2
                yt = ytile[gi]
                pbo = 4 + (t % 2)
                kblocks = sorted(set(u // 4 for (u, _) in lst))

                def mm(e):
                    o_ = PB(pbo)[:, hh * 66:hh * 66 + 65]
                    for m, (u, ty) in enumerate(lst):
                        slot = m if m < 4 else 4
                        e.matmul(o_, lhsT=pt_[:, slot, :], rhs=nva[:, u, hh, :], start=(m == 0), stop=False)
                    for ct in range(2):
                        ins = e.matmul(o_, lhsT=pt_[:, 5 + ct, :], rhs=cnva[:, ct, hh, :], start=False, stop=(ct == 1))
                    return ins
                P.op("pe", mm, reads=[("PT", pi, 0), ("PT", pi, 1), ("cnva", 0), ("cnva", 1)] + [("nva", kb) for kb in kblocks],
                     writes=PK(pbo))
                if hh == 0:
                    return
                ov = PB(pbo)[:, 0:132].rearrange("p (h f) -> p h f", f=66)
                P.op("dve", lambda e: e.reciprocal(out=rc, in_=ov[:, :, 64]), writes=PK(pbo) + ["rc"])
                P.op("dve", lambda e: e.tensor_tensor(
                    out=yt[:, t % 4, :].rearrange("p (h d) -> p h d", d=64), in0=ov[:, :, 0:64],
                    in1=rc.unsqueeze(2).to_broadcast([128, 2, 64]), op=ALU.mult),
                    reads=["rc"], writes=PK(pbo) + [("ytile", gi, t % 4)])
                if t % 4 == 3:
                    g0 = t - 3
                    pbt = 6 + gi
                    pbv = PB(pbt).bitcast(BF16)

                    def tr(e):
                        for cc in range(4):
                            ins = e.transpose(out=pbv[:, cc * 128:(cc + 1) * 128], in_=yt[:, cc, :], identity=identb)
                        return ins
                    P.op("pe", tr, reads=[("ytile", gi, cc) for cc in range(4)] + [("ytile", gi), "identb"], writes=PK(pbt))
                    P.op("act", lambda e: e.copy(out=ystage[gi], in_=pbv[:, 0:512]), writes=PK(pbt) + [("ystage", gi)])
                    P.dma("sp", yT_d[4 + hp, :, g0 * 128:(g0 + 4) * 128], ystage[gi], reads=[("ystage", gi)], writes=[("yT", 1, hp, g0 // 4)])
            if hp == 0:
                P.dma("pool", wgb, w_in[:, 3584:5632], writes=["wgb"])
                P.dma("pool", wrob, w_ro, writes=["wrob"])
                P.dma("pool", wnob, w_no, writes=["wnob"])
                P.dma("pool", wob, w_o, writes=["wob"])
                P.dma("pool", wadab[0], w_ada[:, 2 * D:3 * D], writes=["wadab0"])
                P.dma("pool", wadab[1], w_ada[:, 5 * D:6 * D], writes=["wadab1"])
            if hp == 1:
                P.dma("pool", w1b.rearrange("r (a c) -> (r a) c", a=2), w_ff1.rearrange("r (a c) -> (r a) c", a=2), writes=["w1b"])
            if hp == 2:
                P.dma("pool", w2b, w_ff2, writes=["w2b"])
            units = [(t, hh) for t in range(NT) for hh in range(2)]
            for k_, (t_, hh_) in enumerate(units):
                na_front(t_, hh_)
                if k_ >= 1:
                    na_back(*units[k_ - 1])
            na_back(*units[-1])
        for hp in range(4):
            pair_body(hp)
        P.barrier(scr)
        A.release(m_after_persist)
        A.release(m0)

        if upto == "p2":
            raise _Stop()
        GT = [A.alloc([D], F32) for _ in range(2)]
        m3 = A.mark()
        wbufg = A.alloc([KC, 1024], BF16)
        bb = A.alloc([D], F32)
        gb = A.alloc([D], F32)
        for gi_, j in enumerate((2, 5)):
            P.dma("sp", wbufg, wadab[gi_].rearrange("(k p) n -> p k n", p=128), writes=["wbufg"])
            P.dma("sp", bb, b_ada[0:1, j * D:(j + 1) * D].partition_broadcast(128), writes=["bb"])
            P.dma("sp", gb, gpost[gi_:gi_ + 1, :].partition_broadcast(128), writes=["gb"])

            def mm(e):
                for half in range(2):
                    for k in range(KC):
                        ins = e.matmul(PB(half)[:, 0:512], lhsT=sc_rep[:, k, :], rhs=wbufg[:, k, half * 512:(half + 1) * 512],
                                       start=(k == 0), stop=(k == KC - 1))
                return ins
            P.op("pe", mm, reads=["wbufg", "sc_rep"], writes=PK(0, 1))
            P.op("dve", lambda e, gi_=gi_: e.tensor_tensor(out=GT[gi_].rearrange("p (b n) -> p b n", b=2), in0=ps[:, 0:2, :], in1=bb.rearrange("p (b n) -> p b n", b=2), op=ALU.add),
                 reads=["bb"], writes=PK(0, 1) + [("GT", gi_)])
            P.op("dve", lambda e, gi_=gi_: e.tensor_tensor(out=GT[gi_], in0=GT[gi_], in1=gb, op=ALU.mult), reads=["gb"], writes=[("GT", gi_)])
        P.barrier(scr)
        A.release(m3)

        if upto == "p3p":
            raise _Stop()
        Wg = A.alloc([KC, 2048], BF16)
        Wro = A.alloc([4, D], BF16)
        Wno = A.alloc([4, D], BF16)
        Wo = A.alloc([KC, D], BF16)
        def load3a_weights():
            for q4 in range(4):
                P.dma("sp", Wg[:, :, q4 * 512:(q4 + 1) * 512], wgb[:, q4 * 512:(q4 + 1) * 512].rearrange("(k p) n -> p k n", p=128), writes=[("Wg", q4)])
            P.dma("sp", Wro, wrob.rearrange("(k p) n -> p k n", p=128), writes=["Wro"])
            P.dma("sp", Wno, wnob.rearrange("(k p) n -> p k n", p=128), writes=["Wno"])
            for q2 in range(2):
                P.dma("sp", Wo[:, :, q2 * 512:(q2 + 1) * 512], wob[:, q2 * 512:(q2 + 1) * 512].rearrange("(k p) n -> p k n", p=128), writes=[("Wo", q2)])
        xbA = [A.alloc([4, D], F32) for _ in range(2)]
        junkA = A.alloc([D], BF16)
        tmpA3 = [dict(junk=junkA, junk_key="junkA", ss=A.alloc([1], F32), rstd=A.alloc([1], F32), xn=A.alloc([D], F32), key=("nt3", i_)) for i_ in range(2)]
        hTb = A.alloc([KC, 512], BF16)
        yTbA = [A.alloc([8, 512], BF16) for _ in range(2)]
        sgT = A.alloc([16, 512], F32)
        z1A = [A.alloc([512], F32) for _ in range(2)]
        z2A = [A.alloc([512], F32) for _ in range(2)]
        zT = A.alloc([KC, 512], BF16)
        ssyA = [A.alloc([1], F32) for _ in range(2)]
        rsyA = [A.alloc([1], F32) for _ in range(2)]
        tyA = [A.alloc([D], F32) for _ in range(2)]
        Wgk = [("Wg", q4) for q4 in range(4)]

        def load3a(nb):
            xb = xbA[nb % 2]
            for tt in range(4):
                P.dma("sp", xb[:, tt, :], x[(nb * 4 + tt) * 128:(nb * 4 + tt + 1) * 128, :], writes=[("xbA", nb % 2, tt)])
            P.dma("sp", yTbA[nb % 2], yT_d[:, :, nb * 512:(nb + 1) * 512].rearrange("a p n -> p a n"), writes=[("yTb", nb % 2)])

        def n3a_p1(nb, tt):
            xb = xbA[nb % 2]
            return norm_p1(xb[:, tt, :], ("xbA", nb % 2, tt), tmpA3[tt % 2])

        def n3a_p2(nb, tt, xnk):
            norm_p2(xnk, (lambda c: hTb[:, c, tt * 128:(tt + 1) * 128]), [("hTb", c, tt) for c in range(KC)], S1, SH1, 0, (6, 7), tmpA3[tt % 2])

        def norm3a(nb):
            for tt in range(4):
                n3a_p2(nb, tt, n3a_p1(nb, tt))

        def blk3a(nb):
            xb = xbA[nb % 2]
            yTb = yTbA[nb % 2]
            if nb + 1 < NB:
                load3a(nb + 1)
            hkeys = [("hTb", c, tt) for c in range(KC) for tt in range(4)]
            xnks = {}
            for g in range(16):
                pb = g % 2

                def mm(e, g=g, pb=pb):
                    for k in range(KC):
                        ins = e.matmul(PB(pb), lhsT=Wg[:, k, g * 128:(g + 1) * 128], rhs=hTb[:, k, :], start=(k == 0), stop=(k == KC - 1))
                    return ins
                P.op("pe", mm, reads=hkeys + [("Wg", g // 4)], writes=PK(pb))
                P.op("act", lambda e, g=g, pb=pb: e.activation(out=sgT[:, g, :], in_=PB(pb), func=AF.Sigmoid), writes=PK(pb) + [("sgT", g)])
                if nb + 1 < NB and g in (2, 6):
                    xnks[g // 4] = n3a_p1(nb + 1, g // 4)
            for fc in range(KC):
                pa, pbb = 2 + fc % 2, 4 + fc % 2
                z1, z2 = z1A[fc % 2], z2A[fc % 2]

                def mm(e, fc=fc, pa=pa):
                    for k in range(4):
                        ins = e.matmul(PB(pa), lhsT=Wro[:, k, fc * 128:(fc + 1) * 128], rhs=yTb[:, k, :], start=(k == 0), stop=(k == 3))
                    return ins
                P.op("pe", mm, reads=[("yTb", nb % 2), "Wro"], writes=PK(pa))

                def mm(e, fc=fc, pbb=pbb):
                    for k in range(4):
                        ins = e.matmul(PB(pbb), lhsT=Wno[:, k, fc * 128:(fc + 1) * 128], rhs=yTb[:, 4 + k, :], start=(k == 0), stop=(k == 3))
                    return ins
                P.op("pe", mm, reads=[("yTb", nb % 2), "Wno"], writes=PK(pbb))
                P.op("dve", lambda e, fc=fc, pa=pa, z1=z1: e.tensor_tensor(out=z1, in0=PB(pa), in1=sgT[:, fc, :], op=ALU.mult),
                     reads=[("sgT", fc)], writes=PK(pa) + [("z1", fc % 2)])
                P.op("dve", lambda e, fc=fc, pbb=pbb, z2=z2: e.tensor_tensor(out=z2, in0=PB(pbb), in1=sgT[:, 8 + fc, :], op=ALU.mult),
                     reads=[("sgT", 8 + fc)], writes=PK(pbb) + [("z2", fc % 2)])
                P.op("pool", lambda e, fc=fc, z1=z1, z2=z2: e.tensor_tensor(out=zT[:, fc, :], in0=z1, in1=z2, op=ALU.add),
                     reads=[("z1", fc % 2), ("z2", fc % 2)], writes=[("zT", fc)])
                if nb + 1 < NB and fc % 2 == 1:
                    tt_ = fc // 2
                    n3a_p2(nb + 1, tt_, xnks[tt_])
                    if tt_ + 2 < 4:
                        xnks[tt_ + 2] = n3a_p1(nb + 1, tt_ + 2)
            for tt in range(4):
                i = nb * 4 + tt
                py0 = 2 * (tt % 2)
                ssy, rsy, ty = ssyA[tt % 2], rsyA[tt % 2], tyA[tt % 2]

                def mm(e, tt=tt, py0=py0):
                    for half in range(2):
                        for k in range(KC):
                            ins = e.matmul(PB(py0 + half), lhsT=zT[:, k, tt * 128:(tt + 1) * 128], rhs=Wo[:, k, half * 512:(half + 1) * 512],
                                           start=(k == 0), stop=(k == KC - 1))
                    return ins
                P.op("pe", mm, reads=[("zT", fc) for fc in range(KC)] + [("Wo", 0), ("Wo", 1)], writes=PK(py0, py0 + 1))
                jk = tmpA3[tt % 2]
                P.op("act", lambda e, py0=py0, jk=jk, ssy=ssy: e.activation(out=jk["junk"].rearrange("p (b n) -> p b n", b=2), in_=ps[:, py0:py0 + 2, :], func=AF.Square, accum_out=ssy),
                     writes=PK(py0, py0 + 1) + ["junkA", ("ssy", tt % 2)])
                P.op("act", lambda e, ssy=ssy, rsy=rsy: e.activation(out=rsy, in_=ssy, func=AF.Sqrt, scale=1.0 / D, bias=EPS),
                     reads=[("ssy", tt % 2)], writes=[("rsy", tt % 2)])
                P.op("dve", lambda e, rsy=rsy: e.reciprocal(out=rsy, in_=rsy), writes=[("rsy", tt % 2)])
                P.op("dve", lambda e, py0=py0, rsy=rsy, ty=ty: e.scalar_tensor_tensor(
                    out=ty.rearrange("p (b n) -> p b n", b=2), in0=ps[:, py0:py0 + 2, :], scalar=rsy[:, 0:1],
                    in1=GT[0].rearrange("p (b n) -> p b n", b=2), op0=ALU.mult, op1=ALU.mult),
                    reads=[("rsy", tt % 2), ("GT", 0)], writes=PK(py0, py0 + 1) + [("ty", tt % 2)])
                P.op("pool", lambda e, tt=tt, ty=ty, xb=xb: e.tensor_tensor(out=ty, in0=ty, in1=xb[:, tt, :], op=ALU.add),
                     reads=[("xbA", nb % 2, tt)], writes=[("ty", tt % 2)])
                P.dma("sp", out[i * 128:(i + 1) * 128, :], ty, reads=[("ty", tt % 2)], writes=[("x1d", i)])
        load3a(0)
        load3a_weights()
        norm3a(0)
        for nb in range(NB):
            blk3a(nb)
        P.barrier(scr)
        A.release(m3)

        if upto == "p3a":
            raise _Stop()
        W1 = A.alloc([KC, 4 * D], BF16)
        W2 = A.alloc([32, D], BF16)
        def load3b_weights():
            for q8 in range(8):
                P.dma("sp", W1[:, :, q8 * 512:(q8 + 1) * 512], w1b[:, q8 * 512:(q8 + 1) * 512].rearrange("(k p) n -> p k n", p=128), writes=[("W1", q8)])
            for q8 in range(8):
                P.dma("sp", W2[:, q8 * 4:(q8 + 1) * 4, :], w2b[q8 * 512:(q8 + 1) * 512, :].rearrange("(k p) n -> p k n", p=128), writes=[("W2", q8)])
        xtB = [A.alloc([D], F32) for _ in range(2)]
        xrB = A.alloc([D], F32)
        rl = [A.alloc([512], F32) for _ in range(2)]
        h2TB = [A.alloc([KC, 512], BF16) for _ in range(2)]
        uT = A.alloc([32, 512], BF16)
        ssB = [A.alloc([1], F32) for _ in range(2)]
        rstdB = [A.alloc([1], F32) for _ in range(2)]
        ssyB = [A.alloc([1], F32) for _ in range(2)]
        rsyB = [A.alloc([1], F32) for _ in range(2)]
        tmpB3 = [dict(junk=rl[i_].bitcast(BF16), junk_key=("rl", i_), ss=ssB[i_], rstd=rstdB[i_], xn=xtB[i_], key=("nt4", i_)) for i_ in range(2)]
        W2k = [("W2", q8) for q8 in range(8)]

        def n3b_p1(nb, tt):
            i = nb * 4 + tt
            bi = i % 2
            P.dma("sp", xtB[bi], out[i * 128:(i + 1) * 128, :], reads=[("x1d", i)], writes=[("xtB", bi)])
            return norm_p1(xtB[bi], ("xtB", bi), tmpB3[bi], inplace=True)

        def n3b_p2(nb, tt, xnk):
            i = nb * 4 + tt
            h2T = h2TB[nb % 2]
            norm_p2(xnk, (lambda c: h2T[:, c, tt * 128:(tt + 1) * 128]), [("h2T", nb % 2, c, tt) for c in range(KC)], S2, SH2, 0, (6, 7), tmpB3[i % 2])

        def blk3b(nb):
            h2T = h2TB[nb % 2]
            hkeys = [("h2T", nb % 2, c, tt) for c in range(KC) for tt in range(4)]
            nxt = nb + 1 < NB
            xnks = {}
            for j in range(32):
                pb = j % 2

                def mm(e, j=j, pb=pb):
                    for k in range(KC):
                        ins = e.matmul(PB(pb), lhsT=W1[:, k, j * 128:(j + 1) * 128], rhs=h2T[:, k, :], start=(k == 0), stop=(k == KC - 1))
                    return ins
                P.op("pe", mm, reads=hkeys + [("W1", j // 4)], writes=PK(pb))
                P.op("act", lambda e, pb=pb: e.activation(out=rl[pb], in_=PB(pb), func=AF.Relu), writes=PK(pb) + [("rl", pb)])
                P.op("dve" if j % 2 == 0 else "pool", lambda e, j=j, pb=pb: e.tensor_tensor(out=uT[:, j, :], in0=rl[pb], in1=rl[pb], op=ALU.mult),
                     reads=[("rl", pb)], writes=[("uT", j)])
                if nxt:
                    if j == 3:
                        xnks[0] = n3b_p1(nb + 1, 0)
                    elif j == 7:
                        xnks[1] = n3b_p1(nb + 1, 1)
                    elif j == 15:
                        n3b_p2(nb + 1, 0, xnks[0])
                        xnks[2] = n3b_p1(nb + 1, 2)
                    elif j == 21:
                        n3b_p2(nb + 1, 1, xnks[1])
                        xnks[3] = n3b_p1(nb + 1, 3)
                    elif j == 27:
                        n3b_p2(nb + 1, 2, xnks[2])
                    elif j == 31:
                        n3b_p2(nb + 1, 3, xnks[3])
            for tt in range(4):
                i = nb * 4 + tt
                pbm = 2 + 2 * (tt % 2)
                ssy, rsy = ssyB[tt % 2], rsyB[tt % 2]
                P.dma("sp", xrB, out[i * 128:(i + 1) * 128, :], reads=[("x1d", i)], writes=["xrB"])

                def mm(e, tt=tt, pbm=pbm):
                    for half in range(2):
                        for j in range(32):
                            ins = e.matmul(PB(pbm + half), lhsT=uT[:, j, tt * 128:(tt + 1) * 128], rhs=W2[:, j, half * 512:(half + 1) * 512],
                                           start=(j == 0), stop=(j == 31))
                    return ins
                P.op("pe", mm, reads=[("uT", j) for j in range(32)] + W2k, writes=PK(pbm, pbm + 1))
                jb = tt % 2
                P.op("act", lambda e, pbm=pbm, jb=jb, ssy=ssy: e.activation(out=rl[jb].bitcast(BF16).rearrange("p (b n) -> p b n", b=2), in_=ps[:, pbm:pbm + 2, :], func=AF.Square, accum_out=ssy),
                     writes=PK(pbm, pbm + 1) + [("rl", jb), ("ssyB", tt % 2)])
                P.op("act", lambda e, ssy=ssy, rsy=rsy: e.activation(out=rsy, in_=ssy, func=AF.Sqrt, scale=1.0 / D, bias=EPS),
                     reads=[("ssyB", tt % 2)], writes=[("rsyB", tt % 2)])
                P.op("dve", lambda e, rsy=rsy: e.reciprocal(out=rsy, in_=rsy), writes=[("rsyB", tt % 2)])
                P.op("dve", lambda e, pbm=pbm, rsy=rsy: e.scalar_tensor_tensor(
                    out=ps[:, pbm:pbm + 2, :], in0=ps[:, pbm:pbm + 2, :], scalar=rsy[:, 0:1],
                    in1=GT[1].rearrange("p (b n) -> p b n", b=2), op0=ALU.mult, op1=ALU.mult),
                    reads=[("rsyB", tt % 2), ("GT", 1)], writes=PK(pbm, pbm + 1))
                P.op("dve", lambda e, pbm=pbm: e.tensor_tensor(out=xrB.rearrange("p (b n) -> p b n", b=2), in0=ps[:, pbm:pbm + 2, :],
                                                               in1=xrB.rearrange("p (b n) -> p b n", b=2), op=ALU.add),
                     writes=PK(pbm, pbm + 1) + ["xrB"])
                P.dma("sp", out[i * 128:(i + 1) * 128, :], xrB, reads=["xrB"], writes=[("outd", i)])
        for tt in range(4):
            n3b_p2(0, tt, n3b_p1(0, tt))
        load3b_weights()
        for nb in range(NB):
            blk3b(nb)
    try:
        body()
    except _Stop:
        pass
    info = P.emit()
    info["arena_peak"] = A.peak
    cmp_.__exit__(None, None, None)
    cm.__exit__(None, None, None)
    return nc, info, types


def prep_inputs(inputs, SEQ, types):
    NT = SEQ // 128
    f = lambda a: np.ascontiguousarray(np.asarray(a, dtype=np.float32))
    x = f(inputs["x"]); c = f(inputs["c"]); ctx = f(inputs["ctx"]); c_ctx = f(inputs["c_ctx"])
    B = x.shape[0]
    w_ada = f(inputs["w_ada"][0]); b_ada = f(inputs["b_ada"][0])
    shared = dict(
        w_ada=w_ada,
        bada_fm=np.ascontiguousarray(b_ada.reshape(48, 128).T),
        b_ada=b_ada.reshape(1, -1),
        gpre_fm=np.ascontiguousarray(np.stack([f(inputs["norm_pre_mix"][0]).reshape(KC, 128).T,
                                               f(inputs["norm_pre_ffn"][0]).reshape(KC, 128).T], axis=1)),
        gpost=np.ascontiguousarray(np.stack([f(inputs["norm_post_mix"][0]), f(inputs["norm_post_ffn"][0])], axis=0)),
        w_in=f(inputs["w_in"][0]), w_ro=f(inputs["w_ret_out"][0]), w_no=f(inputs["w_na_out"][0]),
        w_o=f(inputs["w_o"][0]), w_ff1=f(inputs["w_ff1"][0]), w_ff2=f(inputs["w_ff2"][0]),
    )
    lg = f(inputs["ret_decay_logit"][0])
    lgt_pair = np.zeros((128, 8), np.float32)
    def pair_body(hp):
        for d_ in range(2):
            lgt_pair[0:64, hp * 2 + d_] = lg[d_, 2 * hp]
            lgt_pair[64:128, hp * 2 + d_] = lg[d_, 2 * hp + 1]
    for hp in range(4):
        pair_body(hp)
    lgt_bc = np.zeros((128, 16), np.float32)
    for h in range(8):
        for d_ in range(2):
            lgt_bc[:, 2 * h + d_] = lg[d_, h]
    shared["lgt_pair"] = lgt_pair
    shared["lgt_bc"] = lgt_bc
    shared["ident"] = np.eye(128, dtype=np.float32)
    j = np.arange(128)[:, None].astype(np.float32)
    i = np.arange(128)[None, :].astype(np.float32)
    cmat = np.stack([np.maximum(i - j, 0), np.maximum(j - i, 0), (i >= j) * 0.125, (j > i) * 0.125], axis=1).astype(np.float32)
    shared["cmat"] = np.ascontiguousarray(cmat)
    jj = np.arange(128, dtype=np.float32)
    shared["colc"] = np.ascontiguousarray(np.stack([127 - jj, jj, 255 - jj, 127 - jj, jj, 128 + jj], axis=1))
    ii = np.arange(128, dtype=np.float32)
    shared["rowc"] = np.ascontiguousarray(np.broadcast_to(np.stack([ii + 1, 128 - ii], axis=0)[None], (128, 2, 128)).astype(np.float32))
    cos, sin = rope_tables(SEQ)
    shared["cos_tm"] = np.ascontiguousarray(cos.reshape(NT, 128, 32).transpose(1, 0, 2))
    shared["sin_tm"] = np.ascontiguousarray(sin.reshape(NT, 128, 32).transpose(1, 0, 2))
    mask, idr, idc = na_consts(types)
    rpb = f(inputs["na_rpb"][0])
    rpbB = rpb[:, idr, idc]
    shared["rpbB"] = np.ascontiguousarray(rpbB.transpose(0, 2, 1, 3))
    shared["maskB"] = np.ascontiguousarray(mask.transpose(1, 0, 2))
    in_maps = []
    for b in range(B):
        m = dict(shared)
        m["x"] = x[b]
        m["ctx"] = ctx[b]
        m["c_fm"] = np.ascontiguousarray(np.stack([c[b].reshape(KC, 128).T, c_ctx.reshape(KC, 128).T], axis=2))
        in_maps.append(m)
    return in_maps


_CACHE = {}


def kernel(**inputs):
    x = inputs["x"]
    B, SEQ, _ = x.shape
    if SEQ not in _CACHE:
        _CACHE[SEQ] = build(SEQ)
    nc, info, types = _CACHE[SEQ]
    in_maps = prep_inputs(inputs, SEQ, types)
    res = run_bass_kernel_spmd(nc, in_maps, core_ids=list(range(B)))
    return np.stack([np.asarray(r["out"], dtype=np.float32) for r in res.results], axis=0)
```

```python
import numpy as np
import ml_dtypes
import concourse.bass as bass
import concourse.mybir as mybir
from concourse.bass_utils import run_bass_kernel_spmd

F32 = mybir.dt.float32
BF16 = mybir.dt.bfloat16
U8 = mybir.dt.uint8
AF = mybir.ActivationFunctionType
ALU = mybir.AluOpType
AX = mybir.AxisListType

D = 1024
KC = 8
CTX = 256
GRID_W = 64
EPS = 1e-6
CH = 4096


class Prog:
    ENGS = ("pe", "act", "dve", "pool", "sp")

    def __init__(self, nc, n_dma_sems=16):
        self.nc = nc
        self.ops = []
        self.last_w = {}
        self.readers = {}
        self.n_dma_sems = n_dma_sems
        self.pending = {e: set() for e in self.ENGS}
        self.bar_start = 0
        self.nbar = 0

    def op(self, eng, fn, reads=(), writes=(), dma=False):
        oid = len(self.ops)
        deps = set()
        for k in list(reads) + list(writes):
            if k in self.last_w:
                deps.add(self.last_w[k])
        for k in writes:
            for r in self.readers.get(k, ()):
                deps.add(r)
        deps |= self.pending[eng]
        self.pending[eng] = set()
        deps.discard(oid)
        self.ops.append(dict(eng=eng, fn=fn, deps=deps, dma=dma, has_dep=False))
        for k in reads:
            self.readers.setdefault(k, []).append(oid)
        for k in writes:
            self.last_w[k] = oid
            self.readers[k] = []
        return oid

    def dma(self, q, out, in_, reads=(), writes=(), **kw):
        def fn(e):
            return e.dma_start(out=out, in_=in_, **kw)
        return self.op(q, fn, reads, writes, dma=True)

    def barrier(self, scratch):
        n = self.nbar
        self.nbar += 1
        dmas = [i for i in range(self.bar_start, len(self.ops)) if self.ops[i]["dma"]]
        marks = []
        marks.append(self.op("act", lambda e: e.copy(out=scratch["act"], in_=scratch["act"]), writes=[("bar", n, "act")]))
        marks.append(self.op("dve", lambda e: e.memset(scratch["dve"], 0.0), writes=[("bar", n, "dve")]))
        marks.append(self.op("pool", lambda e: e.memset(scratch["pool"], 0.0), writes=[("bar", n, "pool")]))
        for e in self.ENGS:
            self.pending[e] = set(marks) | set(dmas)
        self.last_w = {}
        self.readers = {}
        self.bar_start = len(self.ops)

    def emit(self, final_wait_eng="sp"):
        nc = self.nc
        ops = self.ops
        for i, o in enumerate(ops):
            keep = set()
            for d in o["deps"]:
                od = ops[d]
                if (not od["dma"]) and od["eng"] == o["eng"] and o["eng"] == "pe" and not o["dma"]:
                    continue
                keep.add(d)
            o["deps"] = keep
            for d in keep:
                ops[d]["has_dep"] = True
        tail = [i for i, o in enumerate(ops) if o["dma"] and not o["has_dep"]]
        for i in tail:
            ops[i]["has_dep"] = True
        cnt = {e: 0 for e in self.ENGS}
        for o in ops:
            if not o["dma"] and o["has_dep"]:
                o["seq"] = cnt[o["eng"]]
                cnt[o["eng"]] += 1
        sems = {}
        for e in self.ENGS:
            n = (cnt[e] + CH - 1) // CH
            sems[e] = [nc.alloc_semaphore(name=f"s_{e}_{j}") for j in range(n)]
        dsems = [nc.alloc_semaphore(name=f"s_dma_{j}") for j in range(self.n_dma_sems)]
        dcount = [0] * self.n_dma_sems
        dnext = 0
        waited = {e: {} for e in self.ENGS}

        def plan_wait(o, e, sem, val):
            key = id(sem)
            if waited[e].get(key, 0) >= val:
                return
            waited[e][key] = val
            o["waits"].append((sem, val))

        for i, o in enumerate(ops):
            e = o["eng"]
            o["waits"] = []
            for d in sorted(o["deps"]):
                od = ops[d]
                if od["dma"]:
                    plan_wait(o, e, od["dsem"], od["dval"])
                else:
                    s = od["seq"]
                    plan_wait(o, e, sems[od["eng"]][s // CH], s % CH + 1)
            if o["dma"] and e == "pool":
                sw = nc.alloc_semaphore(name=f"s_swdma_{i}")
                o["dsem"] = sw
                o["dval"] = 16
                o["inc"] = (sw, 16)
            elif o["dma"]:
                j = dnext
                dnext = (dnext + 1) % self.n_dma_sems
                if dcount[j] > 0:
                    plan_wait(o, e, dsems[j], dcount[j])
                dcount[j] += 16
                o["dsem"] = dsems[j]
                o["dval"] = dcount[j]
                o["inc"] = (dsems[j], 16)
            elif o["has_dep"]:
                s = o["seq"]
                o["inc"] = (sems[e][s // CH], 1)
            else:
                o["inc"] = None
        final_waits = []
        fo = dict(waits=final_waits)
        for i in tail:
            plan_wait(fo, final_wait_eng, ops[i]["dsem"], ops[i]["dval"])

        def run_engine(ename, eng):
            for o in ops:
                if o["eng"] != ename:
                    continue
                for (sem, val) in o["waits"]:
                    eng.wait_ge(sem, val)
                ins = o["fn"](eng)
                if o["inc"] is not None:
                    ins.then_inc(o["inc"][0], o["inc"][1])
            if ename == final_wait_eng:
                for (sem, val) in final_waits:
                    eng.wait_ge(sem, val)

        with nc.Block() as block:
            @block.sync
            def _(eng):
                run_engine("sp", eng)

            @block.tensor
            def _(eng):
                run_engine("pe", eng)

            @block.scalar
            def _(eng):
                run_engine("act", eng)

            @block.vector
            def _(eng):
                run_engine("dve", eng)

            @block.gpsimd
            def _(eng):
                run_engine("pool", eng)
        return dict(n_ops=len(ops), cnt=cnt)


class Arena:
    def __init__(self, ap_u8, size):
        self.ap = ap_u8
        self.size = size
        self.off = 0
        self.peak = 0

    def alloc(self, shape, dtype):
        esz = {F32: 4, BF16: 2}[dtype]
        n = int(np.prod(shape))
        nbytes = (n * esz + 63) // 64 * 64
        assert self.off + nbytes <= self.size, f"arena overflow {self.off}+{nbytes}>{self.size}"
        v = self.ap[:, self.off:self.off + n * esz].bitcast(dtype)
        self.off += nbytes
        self.peak = max(self.peak, self.off)
        if len(shape) == 1:
            return v
        names = " ".join(f"d{i}" for i in range(len(shape)))
        kw = {f"d{i}": int(s) for i, s in enumerate(shape)}
        return v.rearrange(f"p ({names}) -> p {names}", **kw)

    def mark(self):
        return self.off

    def release(self, m):
        self.off = m


def na_structure(rows):
    T = rows // 2
    types = {}
    per_t = []
    for t in range(T):
        lst = []
        for u in range(T):
            vis = []
            anyv = False
            for kr in range(2):
                for qr in range(2):
                    r = 2 * t + qr
                    r0 = min(max(r - 4, 0), rows - 8)
                    v = r0 <= 2 * u + kr < r0 + 8
                    vis.append(v)
                    anyv = anyv or v
            if not anyv:
                continue
            key = (u - t, tuple(vis))
            if key not in types:
                types[key] = len(types)
            lst.append((u, types[key]))
        per_t.append(lst)
    return per_t, types


def na_consts(types):
    nt = len(types)
    mask = np.zeros((nt, 128, 128), np.float32)
    idr = np.zeros((nt, 128, 128), np.int64)
    idc = np.zeros((nt, 128, 128), np.int64)
    kc = np.arange(64)[:, None]
    qc = np.arange(64)[None, :]
    c0 = np.clip(qc - 8, 0, 48)
    colok = (kc >= c0) & (kc < c0 + 16)
    dc = np.clip(kc - qc + 15, 0, 30)
    for (delta, vis), ti in types.items():
        for kr in range(2):
            for qr in range(2):
                v = vis[kr * 2 + qr]
                dr = int(np.clip(2 * delta + kr - qr + 7, 0, 14))
                blk = np.where(colok & v, 0.0, -30000.0).astype(np.float32)
                mask[ti, kr * 64:(kr + 1) * 64, qr * 64:(qr + 1) * 64] = blk
                idr[ti, kr * 64:(kr + 1) * 64, qr * 64:(qr + 1) * 64] = dr
                idc[ti, kr * 64:(kr + 1) * 64, qr * 64:(qr + 1) * 64] = dc
    return mask, idr, idc


def rope_tables(n):
    pos = np.arange(n)
    row = (pos // GRID_W).astype(np.float32)
    col = (pos % GRID_W).astype(np.float32)
    inv = (10000.0 ** (-np.arange(0, 32, 2, dtype=np.float32) / 32)).astype(np.float32)
    ang = np.concatenate([row[:, None] * inv, col[:, None] * inv], axis=-1).astype(np.float32)
    return np.cos(ang).astype(np.float32), np.sin(ang).astype(np.float32)


class _Stop(Exception):
    pass


def build(SEQ, debug=(), upto=None):
    NT = SEQ // 128
    ROWS = SEQ // 64
    NB = SEQ // 512
    per_t, types = na_structure(ROWS)
    NTYPE = len(types)
    nc = bass.Bass("TRN2", target_bir_lowering=False)

    def din(name, shape, dt=F32):
        return nc.dram_tensor(name, list(shape), dt, kind="ExternalInput").ap()

    x = din("x", [SEQ, D])
    ctx = din("ctx", [CTX, D])
    c_fm = din("c_fm", [128, KC, 2])
    w_ada = din("w_ada", [D, 6 * D])
    bada_fm = din("bada_fm", [128, 48])
    b_ada = din("b_ada", [1, 6 * D])
    gpre_fm = din("gpre_fm", [128, 2, KC])
    gpost = din("gpost", [2, D])
    w_in = din("w_in", [D, 5632])
    w_ro = din("w_ro", [512, D])
    w_no = din("w_no", [512, D])
    w_o = din("w_o", [D, D])
    w_ff1 = din("w_ff1", [D, 4 * D])
    w_ff2 = din("w_ff2", [4 * D, D])
    lgt_pair = din("lgt_pair", [128, 8])
    lgt_bc = din("lgt_bc", [128, 16])
    ident_d = din("ident", [128, 128])
    cmat = din("cmat", [128, 4, 128])
    colc_d = din("colc", [128, 6])
    rowc_d = din("rowc", [128, 2, 128])
    cos_d = din("cos_tm", [128, NT, 32])
    sin_d = din("sin_tm", [128, NT, 32])
    rpbB = din("rpbB", [8, 128, NTYPE, 128])
    maskB_d = din("maskB", [128, NTYPE, 128])
    out = nc.dram_tensor("out", [SEQ, D], F32, kind="ExternalOutput").ap()
    yT_d = nc.dram_tensor("yT_scratch", [8, 128, SEQ], BF16, kind="Internal").ap()
    wgb = nc.dram_tensor("wg_bf", [D, 2048], BF16, kind="Internal").ap()
    wrob = nc.dram_tensor("wro_bf", [512, D], BF16, kind="Internal").ap()
    wnob = nc.dram_tensor("wno_bf", [512, D], BF16, kind="Internal").ap()
    wob = nc.dram_tensor("wo_bf", [D, D], BF16, kind="Internal").ap()
    w1b = nc.dram_tensor("w1_bf", [D, 4 * D], BF16, kind="Internal").ap()
    w2b = nc.dram_tensor("w2_bf", [4 * D, D], BF16, kind="Internal").ap()
    wadab = nc.dram_tensor("wada_bf", [2, D, D], BF16, kind="Internal").ap()
    dbg = {}
    for name, shape, dt in debug:
        dbg[name] = nc.dram_tensor(name, list(shape), dt, kind="ExternalOutput").ap()

    P = Prog(nc)
    ARENA_BYTES = 207 * 1024
    cm = nc.sbuf_tensor("arena", [128, ARENA_BYTES], U8)
    arena_h = cm.__enter__()
    A = Arena(arena_h, ARENA_BYTES)
    cmp_ = nc.psum_tensor("ps", [128, 8, 512], F32)
    ps = cmp_.__enter__()

    def PB(b):
        return ps[:, b, :]

    def PK(*bs):
        return [("ps", b) for b in bs]

    def body():
        ident = A.alloc([128], F32)
        identb = A.alloc([128], BF16)
        scr = {e: A.alloc([16], F32) for e in ("act", "dve", "pool")}
        S1 = A.alloc([KC, 2], F32)
        SH1 = A.alloc([KC, 2], F32)
        S2 = A.alloc([KC, 2], F32)
        SH2 = A.alloc([KC, 2], F32)
        scb = A.alloc([KC, 2], BF16)
        sc_rep = A.alloc([KC, 128], BF16)
        gpre = A.alloc([2, KC], F32)
        badafm = A.alloc([48], F32)

        P.dma("sp", ident, ident_d, writes=["ident"])
        P.op("dve", lambda e: e.tensor_copy(out=identb, in_=ident), reads=["ident"], writes=["identb"])
        for e_ in ("act", "dve", "pool"):
            pass
        P.op("dve", lambda e: e.memset(scr["dve"], 0.0), writes=["scr_dve"])
        P.op("pool", lambda e: e.memset(scr["pool"], 0.0), writes=["scr_pool"])
        P.op("dve", lambda e: e.memset(scr["act"], 0.0), writes=["scr_act"])
        P.dma("sp", gpre, gpre_fm, writes=["gpre"])
        P.dma("sp", badafm, bada_fm, writes=["badafm"])

        m_phase2 = None

        def norm_p1(xt_ap, xt_key, tmp, inplace=False):
            junk, ss, rstd, xn = tmp["junk"], tmp["ss"], tmp["rstd"], tmp["xn"]
            tk = tmp["key"]
            jkey = tmp.get("junk_key", (tk, "junk"))
            P.op("act", lambda e: e.activation(out=junk, in_=xt_ap, func=AF.Square, accum_out=ss),
                 reads=[xt_key], writes=[jkey, (tk, "ss")])
            P.op("act", lambda e: e.activation(out=rstd, in_=ss, func=AF.Sqrt, scale=1.0 / D, bias=EPS),
                 reads=[(tk, "ss")], writes=[(tk, "rstd")])
            P.op("dve", lambda e: e.reciprocal(out=rstd, in_=rstd), writes=[(tk, "rstd")])
            xnk = xt_key if inplace else (tk, "xn")
            P.op("dve", lambda e: e.tensor_scalar(out=xn, in0=xt_ap, scalar1=rstd[:, 0:1], scalar2=None, op0=ALU.mult),
                 reads=[(tk, "rstd")] + ([] if inplace else [xt_key]), writes=[xnk])
            return xnk

        def norm_p2(xnk, dst_fn, dst_keys, Sc, Sh, col, banks, tmp):
            xn = tmp["xn"]
            for half in range(2):
                b = banks[half]

                def tr(e, half=half, b=b):
                    for cc in range(4):
                        c = half * 4 + cc
                        ins = e.transpose(out=PB(b)[:, cc * 128:(cc + 1) * 128], in_=xn[:, c * 128:(c + 1) * 128], identity=ident)
                    return ins
                P.op("pe", tr, reads=[xnk, "ident"], writes=PK(b))
                for cc in range(4):
                    c = half * 4 + cc
                    if c % 2 == 0:
                        P.op("act", lambda e, c=c, cc=cc, b=b: e.activation(
                            out=dst_fn(c), in_=PB(b)[:, cc * 128:(cc + 1) * 128], func=AF.Identity,
                            scale=Sc[:, c, col:col + 1], bias=Sh[:, c, col:col + 1]),
                            reads=["mod"], writes=PK(b) + [dst_keys[c]])
                    else:
                        P.op("dve", lambda e, c=c, cc=cc, b=b: e.tensor_scalar(
                            out=dst_fn(c), in0=PB(b)[:, cc * 128:(cc + 1) * 128],
                            scalar1=Sc[:, c, col:col + 1], scalar2=Sh[:, c, col:col + 1], op0=ALU.mult, op1=ALU.add),
                            reads=["mod"], writes=PK(b) + [dst_keys[c]])

        def norm_transpose(xt_ap, xt_key, dst_fn, dst_keys, Sc, Sh, col, banks, tmp, tag, inplace=False):
            xnk = norm_p1(xt_ap, xt_key, tmp, inplace)
            norm_p2(xnk, dst_fn, dst_keys, Sc, Sh, col, banks, tmp)

        m0 = A.mark()
        cm_t = A.alloc([4, 128], F32)
        colc = A.alloc([6], F32)
        rowc = A.alloc([2, 128], F32)
        lgp = A.alloc([8], F32)
        lgb = A.alloc([16], F32)
        cfm = A.alloc([KC, 2], F32)
        A.release(m0)
        DT = A.alloc([8, 128], F32)
        kw = A.alloc([8, 2], F32)
        ckw = A.alloc([2, 8, 2], F32)
        QW = A.alloc([2, 128], F32)
        GL = A.alloc([8], F32)
        rowc = A.alloc([2, 128], F32)
        lgp = A.alloc([8], F32)
        hT = A.alloc([KC, SEQ], BF16)
        hcT = A.alloc([KC, CTX], BF16)
        m_after_persist = A.mark()
        cm_t = A.alloc([4, 128], F32)
        colc = A.alloc([6], F32)
        lgb = A.alloc([16], F32)
        cfm = A.alloc([KC, 2], F32)
        tmpA = A.alloc([128], F32)
        tmpB = A.alloc([128], F32)
        arg16 = A.alloc([16], F32)
        argc = A.alloc([2, 8, 2], F32)
        wbuf0 = A.alloc([KC, 1024], BF16)
        modfm = A.alloc([4, KC, 2], F32)

        P.dma("sp", cm_t, cmat, writes=["cmat"])
        P.dma("sp", colc, colc_d, writes=["colc"])
        P.dma("sp", rowc, rowc_d, writes=["rowc"])
        P.dma("sp", lgp, lgt_pair, writes=["lgp"])
        P.dma("sp", lgb, lgt_bc, writes=["lgb"])
        P.dma("sp", cfm, c_fm, writes=["cfm"])

        for t_, k_ in ((lgp, "lgp"), (lgb, "lgb")):
            P.op("act", lambda e, t_=t_: e.activation(out=t_, in_=t_, func=AF.Exp, scale=-1.0), writes=[k_])
            P.op("act", lambda e, t_=t_: e.activation(out=t_, in_=t_, func=AF.Ln, bias=1.0), writes=[k_])
            P.op("dve", lambda e, t_=t_: e.tensor_scalar(out=t_, in0=t_, scalar1=-1.0, scalar2=None, op0=ALU.mult), writes=[k_])
        for h in range(8):
            P.op("act", lambda e, h=h: e.activation(out=tmpA, in_=cm_t[:, 0, :], func=AF.Exp, scale=lgb[:, 2 * h:2 * h + 1]),
                 reads=["cmat", "lgb"], writes=["tmpA"])
            P.op("act", lambda e, h=h: e.activation(out=tmpB, in_=cm_t[:, 1, :], func=AF.Exp, scale=lgb[:, 2 * h + 1:2 * h + 2]),
                 reads=["cmat", "lgb"], writes=["tmpB"])
            P.op("dve", lambda e: e.tensor_tensor(out=tmpA, in0=tmpA, in1=cm_t[:, 2, :], op=ALU.mult), reads=["cmat"], writes=["tmpA"])
            P.op("dve", lambda e: e.tensor_tensor(out=tmpB, in0=tmpB, in1=cm_t[:, 3, :], op=ALU.mult), reads=["cmat"], writes=["tmpB"])
            P.op("dve", lambda e, h=h: e.tensor_tensor(out=DT[:, h, :], in0=tmpA, in1=tmpB, op=ALU.add),
                 reads=["tmpA", "tmpB"], writes=["DT"])
        lgb3 = lgb.rearrange("p (h d) -> p h d", d=2)
        arg3 = arg16.rearrange("p (h d) -> p h d", d=2)
        for d_ in range(2):
            P.op("dve", lambda e, d_=d_: e.tensor_scalar(out=arg3[:, :, d_], in0=lgb3[:, :, d_], scalar1=colc[:, d_:d_ + 1], scalar2=None, op0=ALU.mult),
                 reads=["lgb", "colc"], writes=["arg16"])
        P.op("act", lambda e: e.activation(out=arg16, in_=arg16, func=AF.Exp), writes=["arg16"])
        P.op("dve", lambda e: e.tensor_scalar(out=kw.rearrange("p h d -> p (h d)"), in0=arg16, scalar1=0.125, scalar2=None, op0=ALU.mult),
             reads=["arg16"], writes=["kw"])
        for ct in range(2):
            for d_ in range(2):
                cc_ = 2 + ct if d_ == 0 else 4 + ct
                P.op("dve", lambda e, ct=ct, d_=d_, cc_=cc_: e.tensor_scalar(out=argc[:, ct, :, d_], in0=lgb3[:, :, d_], scalar1=colc[:, cc_:cc_ + 1], scalar2=None, op0=ALU.mult),
                     reads=["lgb", "colc"], writes=["argc"])
        P.op("act", lambda e: e.activation(out=argc, in_=argc, func=AF.Exp), writes=["argc"])
        P.op("dve", lambda e: e.tensor_scalar(out=ckw, in0=argc, scalar1=0.125, scalar2=None, op0=ALU.mult), reads=["argc"], writes=["ckw"])
        P.op("act", lambda e: e.activation(out=GL, in_=lgp, func=AF.Exp, scale=128.0), reads=["lgp"], writes=["GL"])

        P.op("act", lambda e: e.activation(out=scb, in_=cfm, func=AF.Silu), reads=["cfm"], writes=["scb"])
        P.op("dve", lambda e: e.tensor_copy(out=sc_rep, in_=scb[:, :, 0:1].to_broadcast([128, KC, 128])), reads=["scb"], writes=["sc_rep"])

        def load_w(dst, src_rows_cols, key, nk=KC):
            P.dma("pool", dst, src_rows_cols.rearrange("(k p) n -> p k n", p=128), writes=[key])

        wbuf1 = A.alloc([KC, 1024], BF16)
        wbufs = {0: (wbuf1, "wbuf1"), 1: (wbuf0, "wbuf0"), 3: (wbuf1, "wbuf1"), 4: (wbuf0, "wbuf0")}

        def ada_load(j):
            wb_, key_ = wbufs[j]
            load_w(wb_, w_ada[:, j * D:(j + 1) * D], key_)

        def ada_mm(mi, j):
            wb_, key_ = wbufs[j]

            def mm(e):
                for cc in range(8):
                    for k in range(KC):
                        ins = e.matmul(PB(0)[:, cc * 2:cc * 2 + 2], lhsT=wb_[:, k, cc * 128:(cc + 1) * 128], rhs=scb[:, k, :],
                                       start=(k == 0), stop=(k == KC - 1))
                return ins
            P.op("pe", mm, reads=[key_, "scb"], writes=PK(0))
            P.op("dve", lambda e: e.tensor_tensor(
                out=modfm[:, mi, :, :], in0=PB(0)[:, 0:16].rearrange("p (c t) -> p c t", t=2),
                in1=badafm[:, j * 8:(j + 1) * 8].unsqueeze(2).to_broadcast([128, KC, 2]), op=ALU.add),
                reads=["badafm"], writes=PK(0) + [("modfm", mi)])

        def ada_fin(Sx, SHx, mi_sh, mi_sc, gi):
            P.op("dve", lambda e: e.scalar_tensor_tensor(
                out=Sx, in0=modfm[:, mi_sc, :, :], scalar=1.0, in1=gpre[:, gi, :].unsqueeze(2).to_broadcast([128, KC, 2]),
                op0=ALU.add, op1=ALU.mult), reads=[("modfm", mi_sc), "gpre"], writes=["mod"])
            P.op("dve", lambda e: e.tensor_copy(out=SHx, in_=modfm[:, mi_sh, :, :]), reads=[("modfm", mi_sh)], writes=["mod"])

        ada_load(1)
        ada_load(0)
        ada_mm(1, 1)
        ada_mm(0, 0)
        ada_fin(S1, SH1, 0, 1, 0)
        ada_load(4)
        ada_load(3)

        if upto == "p0":
            raise _Stop()
        NBUF1 = 3
        xts = [A.alloc([D], F32) for _ in range(NBUF1)]
        tmps = []
        for i in range(NBUF1):
            tmps.append(dict(junk=A.alloc([D], BF16), ss=A.alloc([1], F32), rstd=A.alloc([1], F32), xn=A.alloc([D], F32), key=("nt", i)))

        def p1_load(i):
            bi = i % NBUF1
            src = x[i * 128:(i + 1) * 128, :] if i < NT else ctx[(i - NT) * 128:(i - NT + 1) * 128, :]
            P.dma("sp", xts[bi], src, writes=[("xt", bi)])

        def p1_front(i):
            bi = i % NBUF1
            return norm_p1(xts[bi], ("xt", bi), tmps[bi])

        def p1_back(i, xnk):
            bi = i % NBUF1
            if i < NT:
                dst_fn = (lambda c: hT[:, c, i * 128:(i + 1) * 128])
                dkeys = [("hT", c, i) for c in range(KC)]
                col = 0
            else:
                dst_fn = (lambda c: hcT[:, c, (i - NT) * 128:(i - NT + 1) * 128])
                dkeys = [("hcT", c, i - NT) for c in range(KC)]
                col = 1
            pbk = (2 * (i % 2), 2 * (i % 2) + 1)
            norm_p2(xnk, dst_fn, dkeys, S1, SH1, col, pbk, tmps[bi])
        NTT = NT + 2
        p1_load(0)
        p1_load(1)
        xk_prev = p1_front(0)
        for i in range(NTT):
            if i + 2 < NTT:
                p1_load(i + 2)
            xk_next = p1_front(i + 1) if i + 1 < NTT else None
            p1_back(i, xk_prev)
            xk_prev = xk_next
        ada_mm(3, 4)
        ada_mm(2, 3)
        ada_fin(S2, SH2, 2, 3, 1)
        if "hT" in dbg:
            P.dma("sp", dbg["hT"], hT, reads=[("hT", c, i) for c in range(KC) for i in range(NT)])
        P.barrier(scr)
        A.release(m_after_persist)
        if upto == "p1":
            raise _Stop()

        maskB = A.alloc([NTYPE, 128], BF16)
        P.dma("pool", maskB, maskB_d, writes=["maskB"])
        wb = A.alloc([KC, 7, 128], BF16)
        BT = A.alloc([2, NTYPE, 128], BF16)
        rpst = A.alloc([NTYPE, 128], BF16)
        slabQ = A.alloc([SEQ], BF16)
        slabK = A.alloc([SEQ], BF16)
        rv = A.alloc([NT, 128], BF16)
        nva = A.alloc([NT, 2, 65], BF16)
        srg = A.alloc([NT, 128], BF16)
        DS = A.alloc([NT, 2, 64], F32)
        Rb = A.alloc([NT, 2, 64], BF16)
        BLK = 4
        rtmp = [A.alloc([BLK // 2, 4, 32], F32) for _ in range(4)]
        cs_t = [A.alloc([2, BLK, 32], F32) for _ in range(2)]
        qk_tm = [A.alloc([BLK, 256], BF16) for _ in range(2)]
        Vfb = [A.alloc([BLK, 2, 2, 64], BF16) for _ in range(2)]
        crk = A.alloc([2, 128], BF16)
        cVfb = A.alloc([2, 2, 2, 64], BF16)
        cnva = A.alloc([2, 2, 65], BF16)
        cnkT = A.alloc([CTX], BF16)
        PT = [A.alloc([7, 128], BF16) for _ in range(2)]
        SDT = [A.alloc([2, 4, 128], BF16) for _ in range(2)]
        QfbT = [A.alloc([2, 4, 128], BF16) for _ in range(2)]
        sqA = [A.alloc([512], F32) for _ in range(2)]
        msA = [A.alloc([8], F32) for _ in range(2)]
        onA = [A.alloc([512], F32) for _ in range(2)]
        ytile = [A.alloc([4, 128], BF16) for _ in range(2)]
        ystage = [A.alloc([512], BF16) for _ in range(2)]
        rc = A.alloc([2], F32)

        qm = [[A.alloc([128], BF16) for _ in range(2)] for _ in range(2)]
        for hh_ in range(2):
            for par_ in range(2):
                P.op("pool", lambda e, hh_=hh_, par_=par_: e.memset(qm[hh_][par_], 0.0), writes=[("qm", hh_, par_)])
        P.op("pool", lambda e: e.memset(nva, 1.0), writes=["nva_init"])
        P.op("pool", lambda e: e.memset(cnva, 1.0), writes=["cnva_init"])

        def pair_body(hp):
            hk = ("hp", hp)
            for d_ in range(2):
                P.op("act", lambda e, d_=d_: e.activation(out=QW[:, d_, :], in_=rowc[:, d_, :], func=AF.Exp,
                                                           scale=lgp[:, hp * 2 + d_:hp * 2 + d_ + 1]),
                     writes=["QW"])
            for s in range(7):
                c0 = s * 512 + hp * 128
                P.dma("pool", wb[:, :, s, :], w_in[:, c0:c0 + 128].rearrange("(k p) n -> p k n", p=128), writes=[("wb", s)])
            wbk = [("wb", s) for s in range(7)]
            for hh in range(2):
                P.dma("pool", rpst, rpbB[2 * hp + hh], writes=["rpst"])
                P.op("dve", lambda e, hh=hh: e.tensor_tensor(out=BT[:, hh, :, :], in0=rpst, in1=maskB, op=ALU.add),
                     reads=["rpst", "maskB"], writes=[("BT", hh)])
            for ct in range(2):
                def mm(e, ct=ct):
                    for k in range(KC):
                        ins = e.matmul(PB(6)[:, 0:256], lhsT=hcT[:, k, ct * 128:(ct + 1) * 128], rhs=wb[:, k, 1:3, :].rearrange("p s n -> p (s n)"),
                                       start=(k == 0), stop=(k == KC - 1))
                    for k in range(KC):
                        ins = e.matmul(PB(6)[:, 256:384], lhsT=hcT[:, k, ct * 128:(ct + 1) * 128], rhs=wb[:, k, 6, :],
                                       start=(k == 0), stop=(k == KC - 1))
                    return ins
                P.op("pe", mm, reads=wbk + ["hcT"], writes=PK(6))
                P.op("act", lambda e, ct=ct: e.copy(out=crk[:, ct, :], in_=PB(6)[:, 0:128]), writes=PK(6) + [("crk", ct)])
                for hh in range(2):
                    for d_ in range(2):
                        P.op("dve", lambda e, ct=ct, hh=hh, d_=d_: e.tensor_scalar(
                            out=cVfb[:, ct, hh, d_, :], in0=PB(6)[:, 128 + hh * 64:128 + (hh + 1) * 64],
                            scalar1=ckw[:, ct, 2 * hp + hh, d_:d_ + 1], scalar2=None, op0=ALU.mult),
                            reads=["ckw"], writes=PK(6) + [("cVfb", ct)])
                P.op("dve", lambda e, ct=ct: e.tensor_copy(out=cnva[:, ct, :, 0:64], in_=PB(6)[:, 256:384].rearrange("p (h d) -> p h d", d=64)),
                     reads=["cnva_init"], writes=PK(6) + [("cnva", ct)])

            def mm(e):
                for k in range(KC):
                    ins = e.matmul(PB(7)[:, 0:CTX], lhsT=wb[:, k, 5, :], rhs=hcT[:, k, :], start=(k == 0), stop=(k == KC - 1))
                return ins
            P.op("pe", mm, reads=wbk + ["hcT"], writes=PK(7))
            P.op("act", lambda e: e.copy(out=cnkT, in_=PB(7)[:, 0:CTX]), writes=PK(7) + ["cnkT"])

            def mm(e):
                for hh in range(2):
                    for ct in range(2):
                        ins = e.matmul(PB(6)[hh * 64:(hh + 1) * 64, 0:128], lhsT=crk[:, ct, hh * 64:(hh + 1) * 64],
                                       rhs=cVfb[:, ct, hh, :, :].rearrange("p a b -> p (a b)"), start=(ct == 0), stop=(ct == 1))
                return ins
            P.op("pe", mm, reads=[("crk", 0), ("crk", 1), ("cVfb", 0), ("cVfb", 1)], writes=PK(6))
            P.op("dve", lambda e: e.tensor_copy(out=DS[:, 0, 0, :], in_=PB(6)[:, 0:64]), writes=PK(6) + [("DS", 0, 0)])
            P.op("dve", lambda e: e.tensor_copy(out=DS[:, NT - 1, 1, :], in_=PB(6)[:, 64:128]), writes=PK(6) + [("DS", NT - 1, 1)])

            if upto == "p2ctx":
                raise _Stop()
            def frontA(b0):
                bi = (b0 // BLK) % 2
                qk_ = qk_tm[bi]
                cst = cs_t[bi]
                P.dma("sp", cst[:, 0, :, :], cos_d[:, b0:b0 + BLK, :], writes=[("cs", bi, 0)])
                P.dma("sp", cst[:, 1, :, :], sin_d[:, b0:b0 + BLK, :], writes=[("cs", bi, 1)])
                q5 = qk_.rearrange("p b (g t f) -> p b g t f", g=4, t=2)
                for half in range(2):
                    hb = [2 + 2 * half, 3 + 2 * half]
                    for ii2 in range(2):
                        ii = half * 2 + ii2
                        i = b0 + ii
                        pb = hb[ii2]

                        def mm(e, i=i, pb=pb):
                            for k in range(KC):
                                ins = e.matmul(PB(pb)[:, 0:512], lhsT=hT[:, k, i * 128:(i + 1) * 128], rhs=wb[:, k, 0:4, :].rearrange("p s n -> p (s n)"),
                                               start=(k == 0), stop=(k == KC - 1))
                            return ins
                        P.op("pe", mm, reads=wbk, writes=PK(pb))
                        P.op("dve", lambda e, i=i, pb=pb: e.tensor_copy(out=rv[:, i, :], in_=PB(pb)[:, 256:384]),
                             writes=PK(pb) + [("rv", i)])
                        P.op("act", lambda e, i=i, pb=pb: e.activation(out=srg[:, i, :], in_=PB(pb)[:, 384:512], func=AF.Silu),
                             writes=PK(pb) + [("srg", i)])
                    s5 = ps[:, hb[0]:hb[0] + 2, 0:256].rearrange("p b (g t f) -> p b g t f", g=4, t=2)
                    cosb = cst[:, 0, 2 * half:2 * half + 2, :].unsqueeze(2).to_broadcast([128, 2, 4, 32])
                    sinb = cst[:, 1, 2 * half:2 * half + 2, :].unsqueeze(2).to_broadcast([128, 2, 4, 32])
                    PKH = PK(*hb)
                    qh = q5[:, 2 * half:2 * half + 2]
                    P.op("dve", lambda e, s5=s5, cosb=cosb: e.tensor_tensor(out=rtmp[0], in0=s5[:, :, :, 0, :], in1=cosb, op=ALU.mult),
                         reads=[("cs", bi, 0)], writes=PKH + [("rtmp", 0)])
                    P.op("dve", lambda e, s5=s5, sinb=sinb: e.tensor_tensor(out=rtmp[1], in0=s5[:, :, :, 1, :], in1=sinb, op=ALU.mult),
                         reads=[("cs", bi, 1)], writes=PKH + [("rtmp", 1)])
                    P.op("dve", lambda e, s5=s5, sinb=sinb: e.tensor_tensor(out=rtmp[2], in0=s5[:, :, :, 0, :], in1=sinb, op=ALU.mult),
                         reads=[("cs", bi, 1)], writes=PKH + [("rtmp", 2)])
                    P.op("dve", lambda e, s5=s5, cosb=cosb: e.tensor_tensor(out=rtmp[3], in0=s5[:, :, :, 1, :], in1=cosb, op=ALU.mult),
                         reads=[("cs", bi, 0)], writes=PKH + [("rtmp", 3)])
                    P.op("pool", lambda e, qh=qh: e.tensor_tensor(out=qh[:, :, :, 0, :], in0=rtmp[0], in1=rtmp[1], op=ALU.subtract),
                         reads=[("rtmp", 0), ("rtmp", 1)], writes=[("qk", bi, half, 0)])
                    P.op("pool", lambda e, qh=qh: e.tensor_tensor(out=qh[:, :, :, 1, :], in0=rtmp[2], in1=rtmp[3], op=ALU.add),
                         reads=[("rtmp", 2), ("rtmp", 3)], writes=[("qk", bi, half, 1)])

            def blockA(b0):
                bi = (b0 // BLK) % 2
                qk_, vf_ = qk_tm[bi], Vfb[bi]
                qkk = [("qk", bi, half, w_) for half in range(2) for w_ in range(2)]
                for hh in range(2):
                    for d_ in range(2):
                        P.op("act", lambda e, hh=hh, d_=d_, vf_=vf_: e.activation(
                            out=vf_[:, :, hh, d_, :], in_=rv[:, b0:b0 + BLK, hh * 64:(hh + 1) * 64], func=AF.Copy,
                            scale=kw[:, 2 * hp + hh, d_:d_ + 1]),
                            reads=[("rv", b0 + ii) for ii in range(BLK)] + ["kw"], writes=[("Vfb", bi, hh, d_)])
                vfk = [("Vfb", bi, hh, d_) for hh in range(2) for d_ in range(2)]
                for which, slab, sk in ((0, slabQ, "slabQ"), (1, slabK, "slabK")):
                    pbt = 6 + which
                    pbv = PB(pbt).bitcast(BF16)

                    def tr(e, which=which, pbv=pbv, qk_=qk_):
                        for ii in range(BLK):
                            ins = e.transpose(out=pbv[:, ii * 128:(ii + 1) * 128], in_=qk_[:, ii, which * 128:(which + 1) * 128], identity=identb)
                        return ins
                    P.op("pe", tr, reads=qkk + ["identb"], writes=PK(pbt))
                    if which == 0:
                        P.op("act", lambda e, pbv=pbv, slab=slab: e.copy(out=slab[:, b0 * 128:(b0 + BLK) * 128], in_=pbv[:, 0:BLK * 128]),
                             writes=PK(pbt) + [(sk, b0 // BLK)])
                    else:
                        P.op("dve", lambda e, pbv=pbv, slab=slab: e.tensor_copy(out=slab[:, b0 * 128:(b0 + BLK) * 128], in_=pbv[:, 0:BLK * 128]),
                             writes=PK(pbt) + [(sk, b0 // BLK)])
                pbd = (b0 // BLK) % 2

                def mm(e, pbd=pbd, qk_=qk_, vf_=vf_):
                    for ii in range(BLK):
                        for hh in range(2):
                            ins = e.matmul(PB(pbd)[hh * 64:(hh + 1) * 64, ii * 128:(ii + 1) * 128],
                                           lhsT=qk_[:, ii, 128 + hh * 64:128 + (hh + 1) * 64],
                                           rhs=vf_[:, ii, hh, :, :].rearrange("p a b -> p (a b)"), start=True, stop=True)
                    return ins
                P.op("pe", mm, reads=qkk + vfk, writes=PK(pbd))
                pv = PB(pbd).rearrange("p (b d f) -> p b d f", d=2, f=64)
                lo, hi = b0, min(b0 + BLK, NT - 1)
                if hi > lo:
                    P.op("act", lambda e, lo=lo, hi=hi, pv=pv: e.copy(out=DS[:, lo + 1:hi + 1, 0, :], in_=pv[:, lo - b0:hi - b0, 0, :]),
                         writes=PK(pbd) + [("DS", c + 1, 0) for c in range(lo, hi)])
                lo2, hi2 = max(b0, 1), b0 + BLK
                if hi2 > lo2:
                    P.op("dve", lambda e, lo2=lo2, hi2=hi2, pv=pv: e.tensor_copy(out=DS[:, lo2 - 1:hi2 - 1, 1, :], in_=pv[:, lo2 - b0:hi2 - b0, 1, :]),
                         writes=PK(pbd) + [("DS", c - 1, 1) for c in range(lo2, hi2)])
            frontA(0)
            for b0 in range(0, NT, BLK):
                if b0 + BLK < NT:
                    frontA(b0 + BLK)
                blockA(b0)
            if upto == "p2a":
                raise _Stop()
            def nvproj(nb):
                pbp = 6 + (nb % 2)

                def mm(e):
                    for ii in range(4):
                        i = nb * 4 + ii
                        for k in range(KC):
                            ins = e.matmul(PB(pbp)[:, ii * 128:(ii + 1) * 128], lhsT=hT[:, k, i * 128:(i + 1) * 128], rhs=wb[:, k, 6, :],
                                           start=(k == 0), stop=(k == KC - 1))
                    return ins
                P.op("pe", mm, reads=wbk, writes=PK(pbp))
                P.op("act", lambda e: e.copy(out=nva[:, nb * 4:(nb + 1) * 4, :, 0:64], in_=PB(pbp).rearrange("p (i h d) -> p i h d", h=2, d=64)),
                     reads=["nva_init"], writes=PK(pbp) + [("nva", nb)])
            for nb in range(NB):
                nvproj(nb)
            for c in range(NT - 1):
                P.op("dve", lambda e, c=c: e.scalar_tensor_tensor(out=DS[:, c + 1, 0, :], in0=DS[:, c, 0, :], scalar=GL[:, 2 * hp:2 * hp + 1],
                                                                  in1=DS[:, c + 1, 0, :], op0=ALU.mult, op1=ALU.add),
                     reads=[("DS", c, 0), "GL"], writes=[("DS", c + 1, 0)])
            for c in range(NT - 1, 0, -1):
                P.op("dve", lambda e, c=c: e.scalar_tensor_tensor(out=DS[:, c - 1, 1, :], in0=DS[:, c, 1, :], scalar=GL[:, 2 * hp + 1:2 * hp + 2],
                                                                  in1=DS[:, c - 1, 1, :], op0=ALU.mult, op1=ALU.add),
                     reads=[("DS", c, 1), "GL"], writes=[("DS", c - 1, 1)])
            P.op("dve", lambda e: e.tensor_copy(out=Rb, in_=DS), reads=[("DS", c, d_) for c in range(NT) for d_ in range(2)], writes=["Rb"])
            if f"Rb{hp}" in dbg:
                P.dma("sp", dbg[f"Rb{hp}"], Rb, reads=["Rb"])

            if upto == "p2scan":
                raise _Stop()
            def blockB(g0):
                gi = (g0 // 4) % 2
                pbo = [2 + gi, 4 + gi]
                yt = ytile[gi]
                qf = QfbT[gi]
                sd = SDT[gi]
                sq, ms, on = sqA[gi], msA[gi], onA[gi]
                P.op("pool", lambda e, qf=qf: e.tensor_tensor(
                    out=qf, in0=slabQ[:, g0 * 128:(g0 + 4) * 128].rearrange("p (c i) -> p c i", c=4).unsqueeze(1).to_broadcast([128, 2, 4, 128]),
                    in1=QW.unsqueeze(2).to_broadcast([128, 2, 4, 128]), op=ALU.mult),
                    reads=[("slabQ", g0 // BLK), "QW"], writes=[("QfbT", gi)])
                for hh in range(2):
                    def mm(e, hh=hh):
                        for cc in range(4):
                            c = g0 + cc
                            ins = e.matmul(PB(hh)[:, cc * 128:(cc + 1) * 128], lhsT=slabK[hh * 64:(hh + 1) * 64, c * 128:(c + 1) * 128],
                                           rhs=slabQ[hh * 64:(hh + 1) * 64, c * 128:(c + 1) * 128], start=True, stop=True)
                        return ins
                    P.op("pe", mm, reads=[("slabQ", g0 // BLK), ("slabK", g0 // BLK)], writes=PK(hh))
                    P.op("dve", lambda e, hh=hh, sd=sd: e.tensor_tensor(
                        out=sd[:, hh, :, :], in0=PB(hh).rearrange("p (c i) -> p c i", c=4),
                        in1=DT[:, 2 * hp + hh, :].unsqueeze(1).to_broadcast([128, 4, 128]), op=ALU.mult),
                        reads=["DT"], writes=PK(hh) + [("SDT", gi, hh)])
                for hh in range(2):
                    def mm(e, hh=hh, sd=sd, qf=qf):
                        for cc in range(4):
                            c = g0 + cc
                            o_ = PB(pbo[hh])[:, cc * 64:(cc + 1) * 64]
                            e.matmul(o_, lhsT=sd[:, hh, cc, :], rhs=rv[:, c, hh * 64:(hh + 1) * 64], start=True, stop=False)
                            e.matmul(o_, lhsT=qf[hh * 64:(hh + 1) * 64, 0, cc, :], rhs=Rb[hh * 64:(hh + 1) * 64, c, 0, :], start=False, stop=False)
                            ins = e.matmul(o_, lhsT=qf[hh * 64:(hh + 1) * 64, 1, cc, :], rhs=Rb[hh * 64:(hh + 1) * 64, c, 1, :], start=False, stop=True)
                        return ins
                    P.op("pe", mm, reads=[("SDT", gi, hh), ("QfbT", gi), "Rb"] + [("rv", g0 + cc) for cc in range(4)], writes=PK(pbo[hh]))
                for hh in range(2):
                    P.op("act", lambda e, hh=hh: e.activation(out=sq[:, hh * 256:(hh + 1) * 256], in_=PB(pbo[hh])[:, 0:256], func=AF.Square),
                         writes=PK(pbo[hh]) + [("sq", gi, hh)])
                P.op("dve", lambda e: e.tensor_reduce(out=ms, in_=sq.rearrange("p (g f) -> p g f", f=64), axis=AX.X, op=ALU.add),
                     reads=[("sq", gi, 0), ("sq", gi, 1)], writes=[("ms", gi)])
                P.op("act", lambda e: e.activation(out=ms, in_=ms, func=AF.Sqrt, scale=1.0 / 64, bias=EPS), writes=[("ms", gi)])
                P.op("dve", lambda e: e.reciprocal(out=ms, in_=ms), writes=[("ms", gi)])
                for hh in range(2):
                    P.op("dve", lambda e, hh=hh: e.tensor_tensor(
                        out=on[:, hh * 256:(hh + 1) * 256].rearrange("p (g f) -> p g f", f=64),
                        in0=PB(pbo[hh])[:, 0:256].rearrange("p (g f) -> p g f", f=64),
                        in1=ms[:, hh * 4:(hh + 1) * 4].unsqueeze(2).to_broadcast([128, 4, 64]), op=ALU.mult),
                        reads=[("ms", gi)], writes=PK(pbo[hh]) + [("on", gi, hh)])
                    P.op("pool", lambda e, hh=hh: e.tensor_tensor(
                        out=yt[:, :, hh * 64:(hh + 1) * 64], in0=on[:, hh * 256:(hh + 1) * 256].rearrange("p (c f) -> p c f", f=64),
                        in1=srg[:, g0:g0 + 4, hh * 64:(hh + 1) * 64], op=ALU.mult),
                        reads=[("on", gi, hh)] + [("srg", g0 + cc) for cc in range(4)],
                        writes=[("ytile", gi, "h", hh)] + ([("ytile", gi)] + [("ytile", gi, cc) for cc in range(4)] if hh == 1 else []))
                pbt = 6 + gi
                pbv = PB(pbt).bitcast(BF16)

                def tr(e, pbv=pbv, yt=yt):
                    for cc in range(4):
                        ins = e.transpose(out=pbv[:, cc * 128:(cc + 1) * 128], in_=yt[:, cc, :], identity=identb)
                    return ins
                P.op("pe", tr, reads=[("ytile", gi), "identb", ("ytile", gi, "h", 0), ("ytile", gi, "h", 1)] + [("ytile", gi, cc) for cc in range(4)], writes=PK(pbt))
                P.op("act", lambda e, pbv=pbv, gi=gi: e.copy(out=ystage[gi], in_=pbv[:, 0:512]), writes=PK(pbt) + [("ystage", gi)])
                P.dma("sp", yT_d[hp, :, g0 * 128:(g0 + 4) * 128], ystage[gi], reads=[("ystage", gi)], writes=[("yT", 0, hp, g0 // 4)])
            for g0 in range(0, NT, 4):
                blockB(g0)

            if upto == "p2b":
                raise _Stop()
            def naproj(nb):
                for which, slab, sk, slot in ((0, slabQ, "slabQ", 4), (1, slabK, "slabK", 5)):
                    pbp = 4 + which

                    def mm(e, nb=nb, slot=slot, pbp=pbp):
                        for k in range(KC):
                            ins = e.matmul(PB(pbp)[:, 0:512], lhsT=wb[:, k, slot, :], rhs=hT[:, k, nb * 512:(nb + 1) * 512],
                                           start=(k == 0), stop=(k == KC - 1))
                        return ins
                    P.op("pe", mm, reads=wbk, writes=PK(pbp))
                    if which == 0:
                        P.op("act", lambda e, nb=nb, pbp=pbp: e.activation(out=slabQ[:, nb * 512:(nb + 1) * 512], in_=PB(pbp), func=AF.Copy, scale=0.125),
                             writes=PK(pbp) + [("slabQ", nb)])
                    else:
                        P.op("dve", lambda e, nb=nb, pbp=pbp: e.tensor_copy(out=slabK[:, nb * 512:(nb + 1) * 512], in_=PB(pbp)),
                             writes=PK(pbp) + [("slabK", nb)])
            for nb in range(NB):
                naproj(nb)
            if upto == "p2np":
                raise _Stop()
            def na_front(t, hh):
                lst = per_t[t]
                pi = hh
                pA, pB_ = 2 * pi, 2 * pi + 1
                pt_ = PT[pi]
                nloc = len(lst)
                assert nloc <= 5
                qmb = qm[hh][t % 2]
                P.op("pool", lambda e: e.tensor_copy(out=qmb[hh * 64:(hh + 1) * 64, :], in_=slabQ[hh * 64:(hh + 1) * 64, t * 128:(t + 1) * 128]),
                     reads=[("slabQ", t // 4)], writes=[("qm", hh, t % 2)])

                def mm(e):
                    for m, (u, ty) in enumerate(lst):
                        o_ = (PB(pA)[:, m * 128:(m + 1) * 128] if m < 4 else PB(pB_)[:, 0:128])
                        e.matmul(o_, lhsT=slabK[:, u * 128:(u + 1) * 128], rhs=qmb, start=True, stop=False)
                        ins = e.matmul(o_, lhsT=identb, rhs=BT[:, hh, ty, :], start=False, stop=True)
                    for ct in range(2):
                        ins = e.matmul(PB(pB_)[:, (1 + ct) * 128:(2 + ct) * 128], lhsT=cnkT[:, ct * 128:(ct + 1) * 128],
                                       rhs=qmb, start=True, stop=True)
                    return ins
                kblocks = sorted(set(u // 4 for (u, _) in lst))
                P.op("pe", mm, reads=[("qm", hh, t % 2), "cnkT", ("BT", hh), "identb"] + [("slabK", kb) for kb in kblocks],
                     writes=PK(pA, pB_))
                na4 = min(nloc, 4)
                P.op("act", lambda e: e.activation(out=pt_[:, 0:na4, :], in_=PB(pA)[:, 0:na4 * 128].rearrange("p (m q) -> p m q", q=128), func=AF.Exp),
                     writes=PK(pA) + [("PT", pi, 0)])
                lo_ = 0 if nloc == 5 else 1
                P.op("act", lambda e: e.activation(out=pt_[:, 4 + lo_:7, :], in_=PB(pB_)[:, lo_ * 128:3 * 128].rearrange("p (m q) -> p m q", q=128), func=AF.Exp),
                     writes=PK(pB_) + [("PT", pi, 1)])

            def na_back(t, hh):
                lst = per_t[t]
                pi = hh
                pt_ = PT[pi]
                gi = (t // 4) % 2
                yt = ytile[gi]
                pbo = 4 + (t % 2)
                kblocks = sorted(set(u // 4 for (u, _) in lst))

                def mm(e):
                    o_ = PB(pbo)[:, hh * 66:hh * 66 + 65]
                    for m, (u, ty) in enumerate(lst):
                        slot = m if m < 4 else 4
                        e.matmul(o_, lhsT=pt_[:, slot, :], rhs=nva[:, u, hh, :], start=(m == 0), stop=False)
                    for ct in range(2):
                        ins = e.matmul(o_, lhsT=pt_[:, 5 + ct, :], rhs=cnva[:, ct, hh, :], start=False, stop=(ct == 1))
                    return ins
                P.op("pe", mm, reads=[("PT", pi, 0), ("PT", pi, 1), ("cnva", 0), ("cnva", 1)] + [("nva", kb) for kb in kblocks],
                     writes=PK(pbo))
                if hh == 0:
                    return
                ov = PB(pbo)[:, 0:132].rearrange("p (h f) -> p h f", f=66)
                P.op("dve", lambda e: e.reciprocal(out=rc, in_=ov[:, :, 64]), writes=PK(pbo) + ["rc"])
                P.op("dve", lambda e: e.tensor_tensor(
                    out=yt[:, t % 4, :].rearrange("p (h d) -> p h d", d=64), in0=ov[:, :, 0:64],
                    in1=rc.unsqueeze(2).to_broadcast([128, 2, 64]), op=ALU.mult),
                    reads=["rc"], writes=PK(pbo) + [("ytile", gi, t % 4)])
                if t % 4 == 3:
                    g0 = t - 3
                    pbt = 6 + gi
                    pbv = PB(pbt).bitcast(BF16)

                    def tr(e):
                        for cc in range(4):
                            ins = e.transpose(out=pbv[:, cc * 128:(cc + 1) * 128], in_=yt[:, cc, :], identity=identb)
                        return ins
                    P.op("pe", tr, reads=[("ytile", gi, cc) for cc in range(4)] + [("ytile", gi), "identb"], writes=PK(pbt))
                    P.op("act", lambda e: e.copy(out=ystage[gi], in_=pbv[:, 0:512]), writes=PK(pbt) + [("ystage", gi)])
                    P.dma("sp", yT_d[4 + hp, :, g0 * 128:(g0 + 4) * 128], ystage[gi], reads=[("ystage", gi)], writes=[("yT", 1, hp, g0 // 4)])
            if hp == 0:
                P.dma("pool", wgb, w_in[:, 3584:5632], writes=["wgb"])
                P.dma("pool", wrob, w_ro, writes=["wrob"])
                P.dma("pool", wnob, w_no, writes=["wnob"])
                P.dma("pool", wob, w_o, writes=["wob"])
                P.dma("pool", wadab[0], w_ada[:, 2 * D:3 * D], writes=["wadab0"])
                P.dma("pool", wadab[1], w_ada[:, 5 * D:6 * D], writes=["wadab1"])
            if hp == 1:
                P.dma("pool", w1b.rearrange("r (a c) -> (r a) c", a=2), w_ff1.rearrange("r (a c) -> (r a) c", a=2), writes=["w1b"])
            if hp == 2:
                P.dma("pool", w2b, w_ff2, writes=["w2b"])
            units = [(t, hh) for t in range(NT) for hh in range(2)]
            for k_, (t_, hh_) in enumerate(units):
                na_front(t_, hh_)
                if k_ >= 1:
                    na_back(*units[k_ - 1])
            na_back(*units[-1])
        for hp in range(4):
            pair_body(hp)
        P.barrier(scr)
        A.release(m_after_persist)
        A.release(m0)

        if upto == "p2":
            raise _Stop()
        GT = [A.alloc([D], F32) for _ in range(2)]
        m3 = A.mark()
        wbufg = A.alloc([KC, 1024], BF16)
        bb = A.alloc([D], F32)
        gb = A.alloc([D], F32)
        for gi_, j in enumerate((2, 5)):
            P.dma("sp", wbufg, wadab[gi_].rearrange("(k p) n -> p k n", p=128), writes=["wbufg"])
            P.dma("sp", bb, b_ada[0:1, j * D:(j + 1) * D].partition_broadcast(128), writes=["bb"])
            P.dma("sp", gb, gpost[gi_:gi_ + 1, :].partition_broadcast(128), writes=["gb"])

            def mm(e):
                for half in range(2):
                    for k in range(KC):
                        ins = e.matmul(PB(half)[:, 0:512], lhsT=sc_rep[:, k, :], rhs=wbufg[:, k, half * 512:(half + 1) * 512],
                                       start=(k == 0), stop=(k == KC - 1))
                return ins
            P.op("pe", mm, reads=["wbufg", "sc_rep"], writes=PK(0, 1))
            P.op("dve", lambda e, gi_=gi_: e.tensor_tensor(out=GT[gi_].rearrange("p (b n) -> p b n", b=2), in0=ps[:, 0:2, :], in1=bb.rearrange("p (b n) -> p b n", b=2), op=ALU.add),
                 reads=["bb"], writes=PK(0, 1) + [("GT", gi_)])
            P.op("dve", lambda e, gi_=gi_: e.tensor_tensor(out=GT[gi_], in0=GT[gi_], in1=gb, op=ALU.mult), reads=["gb"], writes=[("GT", gi_)])
        P.barrier(scr)
        A.release(m3)

        if upto == "p3p":
            raise _Stop()
        Wg = A.alloc([KC, 2048], BF16)
        Wro = A.alloc([4, D], BF16)
        Wno = A.alloc([4, D], BF16)
        Wo = A.alloc([KC, D], BF16)
        def load3a_weights():
            for q4 in range(4):
                P.dma("sp", Wg[:, :, q4 * 512:(q4 + 1) * 512], wgb[:, q4 * 512:(q4 + 1) * 512].rearrange("(k p) n -> p k n", p=128), writes=[("Wg", q4)])
            P.dma("sp", Wro, wrob.rearrange("(k p) n -> p k n", p=128), writes=["Wro"])
            P.dma("sp", Wno, wnob.rearrange("(k p) n -> p k n", p=128), writes=["Wno"])
            for q2 in range(2):
                P.dma("sp", Wo[:, :, q2 * 512:(q2 + 1) * 512], wob[:, q2 * 512:(q2 + 1) * 512].rearrange("(k p) n -> p k n", p=128), writes=[("Wo", q2)])
        xbA = [A.alloc([4, D], F32) for _ in range(2)]
        junkA = A.alloc([D], BF16)
        tmpA3 = [dict(junk=junkA, junk_key="junkA", ss=A.alloc([1], F32), rstd=A.alloc([1], F32), xn=A.alloc([D], F32), key=("nt3", i_)) for i_ in range(2)]
        hTb = A.alloc([KC, 512], BF16)
        yTbA = [A.alloc([8, 512], BF16) for _ in range(2)]
        sgT = A.alloc([16, 512], F32)
        z1A = [A.alloc([512], F32) for _ in range(2)]
        z2A = [A.alloc([512], F32) for _ in range(2)]
        zT = A.alloc([KC, 512], BF16)
        ssyA = [A.alloc([1], F32) for _ in range(2)]
        rsyA = [A.alloc([1], F32) for _ in range(2)]
        tyA = [A.alloc([D], F32) for _ in range(2)]
        Wgk = [("Wg", q4) for q4 in range(4)]

        def load3a(nb):
            xb = xbA[nb % 2]
            for tt in range(4):
                P.dma("sp", xb[:, tt, :], x[(nb * 4 + tt) * 128:(nb * 4 + tt + 1) * 128, :], writes=[("xbA", nb % 2, tt)])
            P.dma("sp", yTbA[nb % 2], yT_d[:, :, nb * 512:(nb + 1) * 512].rearrange("a p n -> p a n"), writes=[("yTb", nb % 2)])

        def n3a_p1(nb, tt):
            xb = xbA[nb % 2]
            return norm_p1(xb[:, tt, :], ("xbA", nb % 2, tt), tmpA3[tt % 2])

        def n3a_p2(nb, tt, xnk):
            norm_p2(xnk, (lambda c: hTb[:, c, tt * 128:(tt + 1) * 128]), [("hTb", c, tt) for c in range(KC)], S1, SH1, 0, (6, 7), tmpA3[tt % 2])

        def norm3a(nb):
            for tt in range(4):
                n3a_p2(nb, tt, n3a_p1(nb, tt))

        def blk3a(nb):
            xb = xbA[nb % 2]
            yTb = yTbA[nb % 2]
            if nb + 1 < NB:
                load3a(nb + 1)
            hkeys = [("hTb", c, tt) for c in range(KC) for tt in range(4)]
            xnks = {}
            for g in range(16):
                pb = g % 2

                def mm(e, g=g, pb=pb):
                    for k in range(KC):
                        ins = e.matmul(PB(pb), lhsT=Wg[:, k, g * 128:(g + 1) * 128], rhs=hTb[:, k, :], start=(k == 0), stop=(k == KC - 1))
                    return ins
                P.op("pe", mm, reads=hkeys + [("Wg", g // 4)], writes=PK(pb))
                P.op("act", lambda e, g=g, pb=pb: e.activation(out=sgT[:, g, :], in_=PB(pb), func=AF.Sigmoid), writes=PK(pb) + [("sgT", g)])
                if nb + 1 < NB and g in (2, 6):
                    xnks[g // 4] = n3a_p1(nb + 1, g // 4)
            for fc in range(KC):
                pa, pbb = 2 + fc % 2, 4 + fc % 2
                z1, z2 = z1A[fc % 2], z2A[fc % 2]

                def mm(e, fc=fc, pa=pa):
                    for k in range(4):
                        ins = e.matmul(PB(pa), lhsT=Wro[:, k, fc * 128:(fc + 1) * 128], rhs=yTb[:, k, :], start=(k == 0), stop=(k == 3))
                    return ins
                P.op("pe", mm, reads=[("yTb", nb % 2), "Wro"], writes=PK(pa))

                def mm(e, fc=fc, pbb=pbb):
                    for k in range(4):
                        ins = e.matmul(PB(pbb), lhsT=Wno[:, k, fc * 128:(fc + 1) * 128], rhs=yTb[:, 4 + k, :], start=(k == 0), stop=(k == 3))
                    return ins
                P.op("pe", mm, reads=[("yTb", nb % 2), "Wno"], writes=PK(pbb))
                P.op("dve", lambda e, fc=fc, pa=pa, z1=z1: e.tensor_tensor(out=z1, in0=PB(pa), in1=sgT[:, fc, :], op=ALU.mult),
                     reads=[("sgT", fc)], writes=PK(pa) + [("z1", fc % 2)])
                P.op("dve", lambda e, fc=fc, pbb=pbb, z2=z2: e.tensor_tensor(out=z2, in0=PB(pbb), in1=sgT[:, 8 + fc, :], op=ALU.mult),
                     reads=[("sgT", 8 + fc)], writes=PK(pbb) + [("z2", fc % 2)])
                P.op("pool", lambda e, fc=fc, z1=z1, z2=z2: e.tensor_tensor(out=zT[:, fc, :], in0=z1, in1=z2, op=ALU.add),
                     reads=[("z1", fc % 2), ("z2", fc % 2)], writes=[("zT", fc)])
                if nb + 1 < NB and fc % 2 == 1:
                    tt_ = fc // 2
                    n3a_p2(nb + 1, tt_, xnks[tt_])
                    if tt_ + 2 < 4:
                        xnks[tt_ + 2] = n3a_p1(nb + 1, tt_ + 2)
            for tt in range(4):
                i = nb * 4 + tt
                py0 = 2 * (tt % 2)
                ssy, rsy, ty = ssyA[tt % 2], rsyA[tt % 2], tyA[tt % 2]

                def mm(e, tt=tt, py0=py0):
                    for half in range(2):
                        for k in range(KC):
                            ins = e.matmul(PB(py0 + half), lhsT=zT[:, k, tt * 128:(tt + 1) * 128], rhs=Wo[:, k, half * 512:(half + 1) * 512],
                                           start=(k == 0), stop=(k == KC - 1))
                    return ins
                P.op("pe", mm, reads=[("zT", fc) for fc in range(KC)] + [("Wo", 0), ("Wo", 1)], writes=PK(py0, py0 + 1))
                jk = tmpA3[tt % 2]
                P.op("act", lambda e, py0=py0, jk=jk, ssy=ssy: e.activation(out=jk["junk"].rearrange("p (b n) -> p b n", b=2), in_=ps[:, py0:py0 + 2, :], func=AF.Square, accum_out=ssy),
                     writes=PK(py0, py0 + 1) + ["junkA", ("ssy", tt % 2)])
                P.op("act", lambda e, ssy=ssy, rsy=rsy: e.activation(out=rsy, in_=ssy, func=AF.Sqrt, scale=1.0 / D, bias=EPS),
                     reads=[("ssy", tt % 2)], writes=[("rsy", tt % 2)])
                P.op("dve", lambda e, rsy=rsy: e.reciprocal(out=rsy, in_=rsy), writes=[("rsy", tt % 2)])
                P.op("dve", lambda e, py0=py0, rsy=rsy, ty=ty: e.scalar_tensor_tensor(
                    out=ty.rearrange("p (b n) -> p b n", b=2), in0=ps[:, py0:py0 + 2, :], scalar=rsy[:, 0:1],
                    in1=GT[0].rearrange("p (b n) -> p b n", b=2), op0=ALU.mult, op1=ALU.mult),
                    reads=[("rsy", tt % 2), ("GT", 0)], writes=PK(py0, py0 + 1) + [("ty", tt % 2)])
                P.op("pool", lambda e, tt=tt, ty=ty, xb=xb: e.tensor_tensor(out=ty, in0=ty, in1=xb[:, tt, :], op=ALU.add),
                     reads=[("xbA", nb % 2, tt)], writes=[("ty", tt % 2)])
                P.dma("sp", out[i * 128:(i + 1) * 128, :], ty, reads=[("ty", tt % 2)], writes=[("x1d", i)])
        load3a(0)
        load3a_weights()
        norm3a(0)
        for nb in range(NB):
            blk3a(nb)
        P.barrier(scr)
        A.release(m3)

        if upto == "p3a":
            raise _Stop()
        W1 = A.alloc([KC, 4 * D], BF16)
        W2 = A.alloc([32, D], BF16)
        def load3b_weights():
            for q8 in range(8):
                P.dma("sp", W1[:, :, q8 * 512:(q8 + 1) * 512], w1b[:, q8 * 512:(q8 + 1) * 512].rearrange("(k p) n -> p k n", p=128), writes=[("W1", q8)])
            for q8 in range(8):
                P.dma("sp", W2[:, q8 * 4:(q8 + 1) * 4, :], w2b[q8 * 512:(q8 + 1) * 512, :].rearrange("(k p) n -> p k n", p=128), writes=[("W2", q8)])
        xtB = [A.alloc([D], F32) for _ in range(2)]
        xrB = A.alloc([D], F32)
        rl = [A.alloc([512], F32) for _ in range(2)]
        h2TB = [A.alloc([KC, 512], BF16) for _ in range(2)]
        uT = A.alloc([32, 512], BF16)
        ssB = [A.alloc([1], F32) for _ in range(2)]
        rstdB = [A.alloc([1], F32) for _ in range(2)]
        ssyB = [A.alloc([1], F32) for _ in range(2)]
        rsyB = [A.alloc([1], F32) for _ in range(2)]
        tmpB3 = [dict(junk=rl[i_].bitcast(BF16), junk_key=("rl", i_), ss=ssB[i_], rstd=rstdB[i_], xn=xtB[i_], key=("nt4", i_)) for i_ in range(2)]
        W2k = [("W2", q8) for q8 in range(8)]

        def n3b_p1(nb, tt):
            i = nb * 4 + tt
            bi = i % 2
            P.dma("sp", xtB[bi], out[i * 128:(i + 1) * 128, :], reads=[("x1d", i)], writes=[("xtB", bi)])
            return norm_p1(xtB[bi], ("xtB", bi), tmpB3[bi], inplace=True)

        def n3b_p2(nb, tt, xnk):
            i = nb * 4 + tt
            h2T = h2TB[nb % 2]
            norm_p2(xnk, (lambda c: h2T[:, c, tt * 128:(tt + 1) * 128]), [("h2T", nb % 2, c, tt) for c in range(KC)], S2, SH2, 0, (6, 7), tmpB3[i % 2])

        def blk3b(nb):
            h2T = h2TB[nb % 2]
            hkeys = [("h2T", nb % 2, c, tt) for c in range(KC) for tt in range(4)]
            nxt = nb + 1 < NB
            xnks = {}
            for j in range(32):
                pb = j % 2

                def mm(e, j=j, pb=pb):
                    for k in range(KC):
                        ins = e.matmul(PB(pb), lhsT=W1[:, k, j * 128:(j + 1) * 128], rhs=h2T[:, k, :], start=(k == 0), stop=(k == KC - 1))
                    return ins
                P.op("pe", mm, reads=hkeys + [("W1", j // 4)], writes=PK(pb))
                P.op("act", lambda e, pb=pb: e.activation(out=rl[pb], in_=PB(pb), func=AF.Relu), writes=PK(pb) + [("rl", pb)])
                P.op("dve" if j % 2 == 0 else "pool", lambda e, j=j, pb=pb: e.tensor_tensor(out=uT[:, j, :], in0=rl[pb], in1=rl[pb], op=ALU.mult),
                     reads=[("rl", pb)], writes=[("uT", j)])
                if nxt:
                    if j == 3:
                        xnks[0] = n3b_p1(nb + 1, 0)
                    elif j == 7:
                        xnks[1] = n3b_p1(nb + 1, 1)
                    elif j == 15:
                        n3b_p2(nb + 1, 0, xnks[0])
                        xnks[2] = n3b_p1(nb + 1, 2)
                    elif j == 21:
                        n3b_p2(nb + 1, 1, xnks[1])
                        xnks[3] = n3b_p1(nb + 1, 3)
                    elif j == 27:
                        n3b_p2(nb + 1, 2, xnks[2])
                    elif j == 31:
                        n3b_p2(nb + 1, 3, xnks[3])
            for tt in range(4):
                i = nb * 4 + tt
                pbm = 2 + 2 * (tt % 2)
                ssy, rsy = ssyB[tt % 2], rsyB[tt % 2]
                P.dma("sp", xrB, out[i * 128:(i + 1) * 128, :], reads=[("x1d", i)], writes=["xrB"])

                def mm(e, tt=tt, pbm=pbm):
                    for half in range(2):
                        for j in range(32):
                            ins = e.matmul(PB(pbm + half), lhsT=uT[:, j, tt * 128:(tt + 1) * 128], rhs=W2[:, j, half * 512:(half + 1) * 512],
                                           start=(j == 0), stop=(j == 31))
                    return ins
                P.op("pe", mm, reads=[("uT", j) for j in range(32)] + W2k, writes=PK(pbm, pbm + 1))
                jb = tt % 2
                P.op("act", lambda e, pbm=pbm, jb=jb, ssy=ssy: e.activation(out=rl[jb].bitcast(BF16).rearrange("p (b n) -> p b n", b=2), in_=ps[:, pbm:pbm + 2, :], func=AF.Square, accum_out=ssy),
                     writes=PK(pbm, pbm + 1) + [("rl", jb), ("ssyB", tt % 2)])
                P.op("act", lambda e, ssy=ssy, rsy=rsy: e.activation(out=rsy, in_=ssy, func=AF.Sqrt, scale=1.0 / D, bias=EPS),
                     reads=[("ssyB", tt % 2)], writes=[("rsyB", tt % 2)])
                P.op("dve", lambda e, rsy=rsy: e.reciprocal(out=rsy, in_=rsy), writes=[("rsyB", tt % 2)])
                P.op("dve", lambda e, pbm=pbm, rsy=rsy: e.scalar_tensor_tensor(
                    out=ps[:, pbm:pbm + 2, :], in0=ps[:, pbm:pbm + 2, :], scalar=rsy[:, 0:1],
                    in1=GT[1].rearrange("p (b n) -> p b n", b=2), op0=ALU.mult, op1=ALU.mult),
                    reads=[("rsyB", tt % 2), ("GT", 1)], writes=PK(pbm, pbm + 1))
                P.op("dve", lambda e, pbm=pbm: e.tensor_tensor(out=xrB.rearrange("p (b n) -> p b n", b=2), in0=ps[:, pbm:pbm + 2, :],
                                                               in1=xrB.rearrange("p (b n) -> p b n", b=2), op=ALU.add),
                     writes=PK(pbm, pbm + 1) + ["xrB"])
                P.dma("sp", out[i * 128:(i + 1) * 128, :], xrB, reads=["xrB"], writes=[("outd", i)])
        for tt in range(4):
            n3b_p2(0, tt, n3b_p1(0, tt))
        load3b_weights()
        for nb in range(NB):
            blk3b(nb)
    try:
        body()
    except _Stop:
        pass
    info = P.emit()
    info["arena_peak"] = A.peak
    cmp_.__exit__(None, None, None)
    cm.__exit__(None, None, None)
    return nc, info, types


def prep_inputs(inputs, SEQ, types):
    NT = SEQ // 128
    f = lambda a: np.ascontiguousarray(np.asarray(a, dtype=np.float32))
    x = f(inputs["x"]); c = f(inputs["c"]); ctx = f(inputs["ctx"]); c_ctx = f(inputs["c_ctx"])
    B = x.shape[0]
    w_ada = f(inputs["w_ada"][0]); b_ada = f(inputs["b_ada"][0])
    shared = dict(
        w_ada=w_ada,
        bada_fm=np.ascontiguousarray(b_ada.reshape(48, 128).T),
        b_ada=b_ada.reshape(1, -1),
        gpre_fm=np.ascontiguousarray(np.stack([f(inputs["norm_pre_mix"][0]).reshape(KC, 128).T,
                                               f(inputs["norm_pre_ffn"][0]).reshape(KC, 128).T], axis=1)),
        gpost=np.ascontiguousarray(np.stack([f(inputs["norm_post_mix"][0]), f(inputs["norm_post_ffn"][0])], axis=0)),
        w_in=f(inputs["w_in"][0]), w_ro=f(inputs["w_ret_out"][0]), w_no=f(inputs["w_na_out"][0]),
        w_o=f(inputs["w_o"][0]), w_ff1=f(inputs["w_ff1"][0]), w_ff2=f(inputs["w_ff2"][0]),
    )
    lg = f(inputs["ret_decay_logit"][0])
    lgt_pair = np.zeros((128, 8), np.float32)
    def pair_body(hp):
        for d_ in range(2):
            lgt_pair[0:64, hp * 2 + d_] = lg[d_, 2 * hp]
            lgt_pair[64:128, hp * 2 + d_] = lg[d_, 2 * hp + 1]
    for hp in range(4):
        pair_body(hp)
    lgt_bc = np.zeros((128, 16), np.float32)
    for h in range(8):
        for d_ in range(2):
            lgt_bc[:, 2 * h + d_] = lg[d_, h]
    shared["lgt_pair"] = lgt_pair
    shared["lgt_bc"] = lgt_bc
    shared["ident"] = np.eye(128, dtype=np.float32)
    j = np.arange(128)[:, None].astype(np.float32)
    i = np.arange(128)[None, :].astype(np.float32)
    cmat = np.stack([np.maximum(i - j, 0), np.maximum(j - i, 0), (i >= j) * 0.125, (j > i) * 0.125], axis=1).astype(np.float32)
    shared["cmat"] = np.ascontiguousarray(cmat)
    jj = np.arange(128, dtype=np.float32)
    shared["colc"] = np.ascontiguousarray(np.stack([127 - jj, jj, 255 - jj, 127 - jj, jj, 128 + jj], axis=1))
    ii = np.arange(128, dtype=np.float32)
    shared["rowc"] = np.ascontiguousarray(np.broadcast_to(np.stack([ii + 1, 128 - ii], axis=0)[None], (128, 2, 128)).astype(np.float32))
    cos, sin = rope_tables(SEQ)
    shared["cos_tm"] = np.ascontiguousarray(cos.reshape(NT, 128, 32).transpose(1, 0, 2))
    shared["sin_tm"] = np.ascontiguousarray(sin.reshape(NT, 128, 32).transpose(1, 0, 2))
    mask, idr, idc = na_consts(types)
    rpb = f(inputs["na_rpb"][0])
    rpbB = rpb[:, idr, idc]
    shared["rpbB"] = np.ascontiguousarray(rpbB.transpose(0, 2, 1, 3))
    shared["maskB"] = np.ascontiguousarray(mask.transpose(1, 0, 2))
    in_maps = []
    for b in range(B):
        m = dict(shared)
        m["x"] = x[b]
        m["ctx"] = ctx[b]
        m["c_fm"] = np.ascontiguousarray(np.stack([c[b].reshape(KC, 128).T, c_ctx.reshape(KC, 128).T], axis=2))
        in_maps.append(m)
    return in_maps


_CACHE = {}


def kernel(**inputs):
    x = inputs["x"]
    B, SEQ, _ = x.shape
    if SEQ not in _CACHE:
        _CACHE[SEQ] = build(SEQ)
    nc, info, types = _CACHE[SEQ]
    in_maps = prep_inputs(inputs, SEQ, types)
    res = run_bass_kernel_spmd(nc, in_maps, core_ids=list(range(B)))
    return np.stack([np.asarray(r["out"], dtype=np.float32) for r in res.results], axis=0)
```

```python
import numpy as np
import ml_dtypes
import concourse.bass as bass
import concourse.mybir as mybir
from concourse.bass_utils import run_bass_kernel_spmd

F32 = mybir.dt.float32
BF16 = mybir.dt.bfloat16
U8 = mybir.dt.uint8
AF = mybir.ActivationFunctionType
ALU = mybir.AluOpType
AX = mybir.AxisListType

D = 1024
KC = 8
CTX = 256
GRID_W = 64
EPS = 1e-6
CH = 4096


class Prog:
    ENGS = ("pe", "act", "dve", "pool", "sp")

    def __init__(self, nc, n_dma_sems=16):
        self.nc = nc
        self.ops = []
        self.last_w = {}
        self.readers = {}
        self.n_dma_sems = n_dma_sems
        self.pending = {e: set() for e in self.ENGS}
        self.bar_start = 0
        self.nbar = 0

    def op(self, eng, fn, reads=(), writes=(), dma=False):
        oid = len(self.ops)
        deps = set()
        for k in list(reads) + list(writes):
            if k in self.last_w:
                deps.add(self.last_w[k])
        for k in writes:
            for r in self.readers.get(k, ()):
                deps.add(r)
        deps |= self.pending[eng]
        self.pending[eng] = set()
        deps.discard(oid)
        self.ops.append(dict(eng=eng, fn=fn, deps=deps, dma=dma, has_dep=False))
        for k in reads:
            self.readers.setdefault(k, []).append(oid)
        for k in writes:
            self.last_w[k] = oid
            self.readers[k] = []
        return oid

    def dma(self, q, out, in_, reads=(), writes=(), **kw):
        def fn(e):
            return e.dma_start(out=out, in_=in_, **kw)
        return self.op(q, fn, reads, writes, dma=True)

    def barrier(self, scratch):
        n = self.nbar
        self.nbar += 1
        dmas = [i for i in range(self.bar_start, len(self.ops)) if self.ops[i]["dma"]]
        marks = []
        marks.append(self.op("act", lambda e: e.copy(out=scratch["act"], in_=scratch["act"]), writes=[("bar", n, "act")]))
        marks.append(self.op("dve", lambda e: e.memset(scratch["dve"], 0.0), writes=[("bar", n, "dve")]))
        marks.append(self.op("pool", lambda e: e.memset(scratch["pool"], 0.0), writes=[("bar", n, "pool")]))
        for e in self.ENGS:
            self.pending[e] = set(marks) | set(dmas)
        self.last_w = {}
        self.readers = {}
        self.bar_start = len(self.ops)

    def emit(self, final_wait_eng="sp"):
        nc = self.nc
        ops = self.ops
        for i, o in enumerate(ops):
            keep = set()
            for d in o["deps"]:
                od = ops[d]
                if (not od["dma"]) and od["eng"] == o["eng"] and o["eng"] == "pe" and not o["dma"]:
                    continue
                keep.add(d)
            o["deps"] = keep
            for d in keep:
                ops[d]["has_dep"] = True
        tail = [i for i, o in enumerate(ops) if o["dma"] and not o["has_dep"]]
        for i in tail:
            ops[i]["has_dep"] = True
        cnt = {e: 0 for e in self.ENGS}
        for o in ops:
            if not o["dma"] and o["has_dep"]:
                o["seq"] = cnt[o["eng"]]
                cnt[o["eng"]] += 1
        sems = {}
        for e in self.ENGS:
            n = (cnt[e] + CH - 1) // CH
            sems[e] = [nc.alloc_semaphore(name=f"s_{e}_{j}") for j in range(n)]
        dsems = [nc.alloc_semaphore(name=f"s_dma_{j}") for j in range(self.n_dma_sems)]
        dcount = [0] * self.n_dma_sems
        dnext = 0
        waited = {e: {} for e in self.ENGS}

        def plan_wait(o, e, sem, val):
            key = id(sem)
            if waited[e].get(key, 0) >= val:
                return
            waited[e][key] = val
            o["waits"].append((sem, val))

        for i, o in enumerate(ops):
            e = o["eng"]
            o["waits"] = []
            for d in sorted(o["deps"]):
                od = ops[d]
                if od["dma"]:
                    plan_wait(o, e, od["dsem"], od["dval"])
                else:
                    s = od["seq"]
                    plan_wait(o, e, sems[od["eng"]][s // CH], s % CH + 1)
            if o["dma"] and e == "pool":
                sw = nc.alloc_semaphore(name=f"s_swdma_{i}")
                o["dsem"] = sw
                o["dval"] = 16
                o["inc"] = (sw, 16)
            elif o["dma"]:
                j = dnext
                dnext = (dnext + 1) % self.n_dma_sems
                if dcount[j] > 0:
                    plan_wait(o, e, dsems[j], dcount[j])
                dcount[j] += 16
                o["dsem"] = dsems[j]
                o["dval"] = dcount[j]
                o["inc"] = (dsems[j], 16)
            elif o["has_dep"]:
                s = o["seq"]
                o["inc"] = (sems[e][s // CH], 1)
            else:
                o["inc"] = None
        final_waits = []
        fo = dict(waits=final_waits)
        for i in tail:
            plan_wait(fo, final_wait_eng, ops[i]["dsem"], ops[i]["dval"])

        def run_engine(ename, eng):
            for o in ops:
                if o["eng"] != ename:
                    continue
                for (sem, val) in o["waits"]:
                    eng.wait_ge(sem, val)
                ins = o["fn"](eng)
                if o["inc"] is not None:
                    ins.then_inc(o["inc"][0], o["inc"][1])
            if ename == final_wait_eng:
                for (sem, val) in final_waits:
                    eng.wait_ge(sem, val)

        with nc.Block() as block:
            @block.sync
            def _(eng):
                run_engine("sp", eng)

            @block.tensor
            def _(eng):
                run_engine("pe", eng)

            @block.scalar
            def _(eng):
                run_engine("act", eng)

            @block.vector
            def _(eng):
                run_engine("dve", eng)

            @block.gpsimd
            def _(eng):
                run_engine("pool", eng)
        return dict(n_ops=len(ops), cnt=cnt)


class Arena:
    def __init__(self, ap_u8, size):
        self.ap = ap_u8
        self.size = size
        self.off = 0
        self.peak = 0

    def alloc(self, shape, dtype):
        esz = {F32: 4, BF16: 2}[dtype]
        n = int(np.prod(shape))
        nbytes = (n * esz + 63) // 64 * 64
        assert self.off + nbytes <= self.size, f"arena overflow {self.off}+{nbytes}>{self.size}"
        v = self.ap[:, self.off:self.off + n * esz].bitcast(dtype)
        self.off += nbytes
        self.peak = max(self.peak, self.off)
        if len(shape) == 1:
            return v
        names = " ".join(f"d{i}" for i in range(len(shape)))
        kw = {f"d{i}": int(s) for i, s in enumerate(shape)}
        return v.rearrange(f"p ({names}) -> p {names}", **kw)

    def mark(self):
        return self.off

    def release(self, m):
        self.off = m


def na_structure(rows):
    T = rows // 2
    types = {}
    per_t = []
    for t in range(T):
        lst = []
        for u in range(T):
            vis = []
            anyv = False
            for kr in range(2):
                for qr in range(2):
                    r = 2 * t + qr
                    r0 = min(max(r - 4, 0), rows - 8)
                    v = r0 <= 2 * u + kr < r0 + 8
                    vis.append(v)
                    anyv = anyv or v
            if not anyv:
                continue
            key = (u - t, tuple(vis))
            if key not in types:
                types[key] = len(types)
            lst.append((u, types[key]))
        per_t.append(lst)
    return per_t, types


def na_consts(types):
    nt = len(types)
    mask = np.zeros((nt, 128, 128), np.float32)
    idr = np.zeros((nt, 128, 128), np.int64)
    idc = np.zeros((nt, 128, 128), np.int64)
    kc = np.arange(64)[:, None]
    qc = np.arange(64)[None, :]
    c0 = np.clip(qc - 8, 0, 48)
    colok = (kc >= c0) & (kc < c0 + 16)
    dc = np.clip(kc - qc + 15, 0, 30)
    for (delta, vis), ti in types.items():
        for kr in range(2):
            for qr in range(2):
                v = vis[kr * 2 + qr]
                dr = int(np.clip(2 * delta + kr - qr + 7, 0, 14))
                blk = np.where(colok & v, 0.0, -30000.0).astype(np.float32)
                mask[ti, kr * 64:(kr + 1) * 64, qr * 64:(qr + 1) * 64] = blk
                idr[ti, kr * 64:(kr + 1) * 64, qr * 64:(qr + 1) * 64] = dr
                idc[ti, kr * 64:(kr + 1) * 64, qr * 64:(qr + 1) * 64] = dc
    return mask, idr, idc


def rope_tables(n):
    pos = np.arange(n)
    row = (pos // GRID_W).astype(np.float32)
    col = (pos % GRID_W).astype(np.float32)
    inv = (10000.0 ** (-np.arange(0, 32, 2, dtype=np.float32) / 32)).astype(np.float32)
    ang = np.concatenate([row[:, None] * inv, col[:, None] * inv], axis=-1).astype(np.float32)
    return np.cos(ang).astype(np.float32), np.sin(ang).astype(np.float32)


class _Stop(Exception):
    pass


def build(SEQ, debug=(), upto=None):
    NT = SEQ // 128
    ROWS = SEQ // 64
    NB = SEQ // 512
    per_t, types = na_structure(ROWS)
    NTYPE = len(types)
    nc = bass.Bass("TRN2", target_bir_lowering=False)

    def din(name, shape, dt=F32):
        return nc.dram_tensor(name, list(shape), dt, kind="ExternalInput").ap()

    x = din("x", [SEQ, D])
    ctx = din("ctx", [CTX, D])
    c_fm = din("c_fm", [128, KC, 2])
    w_ada = din("w_ada", [D, 6 * D])
    bada_fm = din("bada_fm", [128, 48])
    b_ada = din("b_ada", [1, 6 * D])
    gpre_fm = din("gpre_fm", [128, 2, KC])
    gpost = din("gpost", [2, D])
    w_in = din("w_in", [D, 5632])
    w_ro = din("w_ro", [512, D])
    w_no = din("w_no", [512, D])
    w_o = din("w_o", [D, D])
    w_ff1 = din("w_ff1", [D, 4 * D])
    w_ff2 = din("w_ff2", [4 * D, D])
    lgt_pair = din("lgt_pair", [128, 8])
    lgt_bc = din("lgt_bc", [128, 16])
    ident_d = din("ident", [128, 128])
    cmat = din("cmat", [128, 4, 128])
    colc_d = din("colc", [128, 6])
    rowc_d = din("rowc", [128, 2, 128])
    cos_d = din("cos_tm", [128, NT, 32])
    sin_d = din("sin_tm", [128, NT, 32])
    rpbB = din("rpbB", [8, 128, NTYPE, 128])
    maskB_d = din("maskB", [128, NTYPE, 128])
    out = nc.dram_tensor("out", [SEQ, D], F32, kind="ExternalOutput").ap()
    yT_d = nc.dram_tensor("yT_scratch", [8, 128, SEQ], BF16, kind="Internal").ap()
    wgb = nc.dram_tensor("wg_bf", [D, 2048], BF16, kind="Internal").ap()
    wrob = nc.dram_tensor("wro_bf", [512, D], BF16, kind="Internal").ap()
    wnob = nc.dram_tensor("wno_bf", [512, D], BF16, kind="Internal").ap()
    wob = nc.dram_tensor("wo_bf", [D, D], BF16, kind="Internal").ap()
    w1b = nc.dram_tensor("w1_bf", [D, 4 * D], BF16, kind="Internal").ap()
    w2b = nc.dram_tensor("w2_bf", [4 * D, D], BF16, kind="Internal").ap()
    wadab = nc.dram_tensor("wada_bf", [2, D, D], BF16, kind="Internal").ap()
    dbg = {}
    for name, shape, dt in debug:
        dbg[name] = nc.dram_tensor(name, list(shape), dt, kind="ExternalOutput").ap()

    P = Prog(nc)
    ARENA_BYTES = 207 * 1024
    cm = nc.sbuf_tensor("arena", [128, ARENA_BYTES], U8)
    arena_h = cm.__enter__()
    A = Arena(arena_h, ARENA_BYTES)
    cmp_ = nc.psum_tensor("ps", [128, 8, 512], F32)
    ps = cmp_.__enter__()

    def PB(b):
        return ps[:, b, :]

    def PK(*bs):
        return [("ps", b) for b in bs]

    def body():
        ident = A.alloc([128], F32)
        identb = A.alloc([128], BF16)
        scr = {e: A.alloc([16], F32) for e in ("act", "dve", "pool")}
        S1 = A.alloc([KC, 2], F32)
        SH1 = A.alloc([KC, 2], F32)
        S2 = A.alloc([KC, 2], F32)
        SH2 = A.alloc([KC, 2], F32)
        scb = A.alloc([KC, 2], BF16)
        sc_rep = A.alloc([KC, 128], BF16)
        gpre = A.alloc([2, KC], F32)
        badafm = A.alloc([48], F32)

        P.dma("sp", ident, ident_d, writes=["ident"])
        P.op("dve", lambda e: e.tensor_copy(out=identb, in_=ident), reads=["ident"], writes=["identb"])
        for e_ in ("act", "dve", "pool"):
            pass
        P.op("dve", lambda e: e.memset(scr["dve"], 0.0), writes=["scr_dve"])
        P.op("pool", lambda e: e.memset(scr["pool"], 0.0), writes=["scr_pool"])
        P.op("dve", lambda e: e.memset(scr["act"], 0.0), writes=["scr_act"])
        P.dma("sp", gpre, gpre_fm, writes=["gpre"])
        P.dma("sp", badafm, bada_fm, writes=["badafm"])

        m_phase2 = None

        def norm_p1(xt_ap, xt_key, tmp, inplace=False):
            junk, ss, rstd, xn = tmp["junk"], tmp["ss"], tmp["rstd"], tmp["xn"]
            tk = tmp["key"]
            jkey = tmp.get("junk_key", (tk, "junk"))
            P.op("act", lambda e: e.activation(out=junk, in_=xt_ap, func=AF.Square, accum_out=ss),
                 reads=[xt_key], writes=[jkey, (tk, "ss")])
            P.op("act", lambda e: e.activation(out=rstd, in_=ss, func=AF.Sqrt, scale=1.0 / D, bias=EPS),
                 reads=[(tk, "ss")], writes=[(tk, "rstd")])
            P.op("dve", lambda e: e.reciprocal(out=rstd, in_=rstd), writes=[(tk, "rstd")])
            xnk = xt_key if inplace else (tk, "xn")
            P.op("dve", lambda e: e.tensor_scalar(out=xn, in0=xt_ap, scalar1=rstd[:, 0:1], scalar2=None, op0=ALU.mult),
                 reads=[(tk, "rstd")] + ([] if inplace else [xt_key]), writes=[xnk])
            return xnk

        def norm_p2(xnk, dst_fn, dst_keys, Sc, Sh, col, banks, tmp):
            xn = tmp["xn"]
            for half in range(2):
                b = banks[half]

                def tr(e, half=half, b=b):
                    for cc in range(4):
                        c = half * 4 + cc
                        ins = e.transpose(out=PB(b)[:, cc * 128:(cc + 1) * 128], in_=xn[:, c * 128:(c + 1) * 128], identity=ident)
                    return ins
                P.op("pe", tr, reads=[xnk, "ident"], writes=PK(b))
                for cc in range(4):
                    c = half * 4 + cc
                    if c % 2 == 0:
                        P.op("act", lambda e, c=c, cc=cc, b=b: e.activation(
                            out=dst_fn(c), in_=PB(b)[:, cc * 128:(cc + 1) * 128], func=AF.Identity,
                            scale=Sc[:, c, col:col + 1], bias=Sh[:, c, col:col + 1]),
                            reads=["mod"], writes=PK(b) + [dst_keys[c]])
                    else:
                        P.op("dve", lambda e, c=c, cc=cc, b=b: e.tensor_scalar(
                            out=dst_fn(c), in0=PB(b)[:, cc * 128:(cc + 1) * 128],
                            scalar1=Sc[:, c, col:col + 1], scalar2=Sh[:, c, col:col + 1], op0=ALU.mult, op1=ALU.add),
                            reads=["mod"], writes=PK(b) + [dst_keys[c]])

        def norm_transpose(xt_ap, xt_key, dst_fn, dst_keys, Sc, Sh, col, banks, tmp, tag, inplace=False):
            xnk = norm_p1(xt_ap, xt_key, tmp, inplace)
            norm_p2(xnk, dst_fn, dst_keys, Sc, Sh, col, banks, tmp)

        m0 = A.mark()
        cm_t = A.alloc([4, 128], F32)
        colc = A.alloc([6], F32)
        rowc = A.alloc([2, 128], F32)
        lgp = A.alloc([8], F32)
        lgb = A.alloc([16], F32)
        cfm = A.alloc([KC, 2], F32)
        A.release(m0)
        DT = A.alloc([8, 128], F32)
        kw = A.alloc([8, 2], F32)
        ckw = A.alloc([2, 8, 2], F32)
        QW = A.alloc([2, 128], F32)
        GL = A.alloc([8], F32)
        rowc = A.alloc([2, 128], F32)
        lgp = A.alloc([8], F32)
        hT = A.alloc([KC, SEQ], BF16)
        hcT = A.alloc([KC, CTX], BF16)
        m_after_persist = A.mark()
        cm_t = A.alloc([4, 128], F32)
        colc = A.alloc([6], F32)
        lgb = A.alloc([16], F32)
        cfm = A.alloc([KC, 2], F32)
        tmpA = A.alloc([128], F32)
        tmpB = A.alloc([128], F32)
        arg16 = A.alloc([16], F32)
        argc = A.alloc([2, 8, 2], F32)
        wbuf0 = A.alloc([KC, 1024], BF16)
        modfm = A.alloc([4, KC, 2], F32)

        P.dma("sp", cm_t, cmat, writes=["cmat"])
        P.dma("sp", colc, colc_d, writes=["colc"])
        P.dma("sp", rowc, rowc_d, writes=["rowc"])
        P.dma("sp", lgp, lgt_pair, writes=["lgp"])
        P.dma("sp", lgb, lgt_bc, writes=["lgb"])
        P.dma("sp", cfm, c_fm, writes=["cfm"])

        for t_, k_ in ((lgp, "lgp"), (lgb, "lgb")):
            P.op("act", lambda e, t_=t_: e.activation(out=t_, in_=t_, func=AF.Exp, scale=-1.0), writes=[k_])
            P.op("act", lambda e, t_=t_: e.activation(out=t_, in_=t_, func=AF.Ln, bias=1.0), writes=[k_])
            P.op("dve", lambda e, t_=t_: e.tensor_scalar(out=t_, in0=t_, scalar1=-1.0, scalar2=None, op0=ALU.mult), writes=[k_])
        for h in range(8):
            P.op("act", lambda e, h=h: e.activation(out=tmpA, in_=cm_t[:, 0, :], func=AF.Exp, scale=lgb[:, 2 * h:2 * h + 1]),
                 reads=["cmat", "lgb"], writes=["tmpA"])
            P.op("act", lambda e, h=h: e.activation(out=tmpB, in_=cm_t[:, 1, :], func=AF.Exp, scale=lgb[:, 2 * h + 1:2 * h + 2]),
                 reads=["cmat", "lgb"], writes=["tmpB"])
            P.op("dve", lambda e: e.tensor_tensor(out=tmpA, in0=tmpA, in1=cm_t[:, 2, :], op=ALU.mult), reads=["cmat"], writes=["tmpA"])
            P.op("dve", lambda e: e.tensor_tensor(out=tmpB, in0=tmpB, in1=cm_t[:, 3, :], op=ALU.mult), reads=["cmat"], writes=["tmpB"])
            P.op("dve", lambda e, h=h: e.tensor_tensor(out=DT[:, h, :], in0=tmpA, in1=tmpB, op=ALU.add),
                 reads=["tmpA", "tmpB"], writes=["DT"])
        lgb3 = lgb.rearrange("p (h d) -> p h d", d=2)
        arg3 = arg16.rearrange("p (h d) -> p h d", d=2)
        for d_ in range(2):
            P.op("dve", lambda e, d_=d_: e.tensor_scalar(out=arg3[:, :, d_], in0=lgb3[:, :, d_], scalar1=colc[:, d_:d_ + 1], scalar2=None, op0=ALU.mult),
                 reads=["lgb", "colc"], writes=["arg16"])
        P.op("act", lambda e: e.activation(out=arg16, in_=arg16, func=AF.Exp), writes=["arg16"])
        P.op("dve", lambda e: e.tensor_scalar(out=kw.rearrange("p h d -> p (h d)"), in0=arg16, scalar1=0.125, scalar2=None, op0=ALU.mult),
             reads=["arg16"], writes=["kw"])
        for ct in range(2):
            for d_ in range(2):
                cc_ = 2 + ct if d_ == 0 else 4 + ct
                P.op("dve", lambda e, ct=ct, d_=d_, cc_=cc_: e.tensor_scalar(out=argc[:, ct, :, d_], in0=lgb3[:, :, d_], scalar1=colc[:, cc_:cc_ + 1], scalar2=None, op0=ALU.mult),
                     reads=["lgb", "colc"], writes=["argc"])
        P.op("act", lambda e: e.activation(out=argc, in_=argc, func=AF.Exp), writes=["argc"])
        P.op("dve", lambda e: e.tensor_scalar(out=ckw, in0=argc, scalar1=0.125, scalar2=None, op0=ALU.mult), reads=["argc"], writes=["ckw"])
        P.op("act", lambda e: e.activation(out=GL, in_=lgp, func=AF.Exp, scale=128.0), reads=["lgp"], writes=["GL"])

        P.op("act", lambda e: e.activation(out=scb, in_=cfm, func=AF.Silu), reads=["cfm"], writes=["scb"])
        P.op("dve", lambda e: e.tensor_copy(out=sc_rep, in_=scb[:, :, 0:1].to_broadcast([128, KC, 128])), reads=["scb"], writes=["sc_rep"])

        def load_w(dst, src_rows_cols, key, nk=KC):
            P.dma("pool", dst, src_rows_cols.rearrange("(k p) n -> p k n", p=128), writes=[key])

        wbuf1 = A.alloc([KC, 1024], BF16)
        wbufs = {0: (wbuf1, "wbuf1"), 1: (wbuf0, "wbuf0"), 3: (wbuf1, "wbuf1"), 4: (wbuf0, "wbuf0")}

        def ada_load(j):
            wb_, key_ = wbufs[j]
            load_w(wb_, w_ada[:, j * D:(j + 1) * D], key_)

        def ada_mm(mi, j):
            wb_, key_ = wbufs[j]

            def mm(e):
                for cc in range(8):
                    for k in range(KC):
                        ins = e.matmul(PB(0)[:, cc * 2:cc * 2 + 2], lhsT=wb_[:, k, cc * 128:(cc + 1) * 128], rhs=scb[:, k, :],
                                       start=(k == 0), stop=(k == KC - 1))
                return ins
            P.op("pe", mm, reads=[key_, "scb"], writes=PK(0))
            P.op("dve", lambda e: e.tensor_tensor(
                out=modfm[:, mi, :, :], in0=PB(0)[:, 0:16].rearrange("p (c t) -> p c t", t=2),
                in1=badafm[:, j * 8:(j + 1) * 8].unsqueeze(2).to_broadcast([128, KC, 2]), op=ALU.add),
                reads=["badafm"], writes=PK(0) + [("modfm", mi)])

        def ada_fin(Sx, SHx, mi_sh, mi_sc, gi):
            P.op("dve", lambda e: e.scalar_tensor_tensor(
                out=Sx, in0=modfm[:, mi_sc, :, :], scalar=1.0, in1=gpre[:, gi, :].unsqueeze(2).to_broadcast([128, KC, 2]),
                op0=ALU.add, op1=ALU.mult), reads=[("modfm", mi_sc), "gpre"], writes=["mod"])
            P.op("dve", lambda e: e.tensor_copy(out=SHx, in_=modfm[:, mi_sh, :, :]), reads=[("modfm", mi_sh)], writes=["mod"])

        ada_load(1)
        ada_load(0)
        ada_mm(1, 1)
        ada_mm(0, 0)
        ada_fin(S1, SH1, 0, 1, 0)
        ada_load(4)
        ada_load(3)

        if upto == "p0":
            raise _Stop()
        NBUF1 = 3
        xts = [A.alloc([D], F32) for _ in range(NBUF1)]
        tmps = []
        for i in range(NBUF1):
            tmps.append(dict(junk=A.alloc([D], BF16), ss=A.alloc([1], F32), rstd=A.alloc([1], F32), xn=A.alloc([D], F32), key=("nt", i)))

        def p1_load(i):
            bi = i % NBUF1
            src = x[i * 128:(i + 1) * 128, :] if i < NT else ctx[(i - NT) * 128:(i - NT + 1) * 128, :]
            P.dma("sp", xts[bi], src, writes=[("xt", bi)])

        def p1_front(i):
            bi = i % NBUF1
            return norm_p1(xts[bi], ("xt", bi), tmps[bi])

        def p1_back(i, xnk):
            bi = i % NBUF1
            if i < NT:
                dst_fn = (lambda c: hT[:, c, i * 128:(i + 1) * 128])
                dkeys = [("hT", c, i) for c in range(KC)]
                col = 0
            else:
                dst_fn = (lambda c: hcT[:, c, (i - NT) * 128:(i - NT + 1) * 128])
                dkeys = [("hcT", c, i - NT) for c in range(KC)]
                col = 1
            pbk = (2 * (i % 2), 2 * (i % 2) + 1)
            norm_p2(xnk, dst_fn, dkeys, S1, SH1, col, pbk, tmps[bi])
        NTT = NT + 2
        p1_load(0)
        p1_load(1)
        xk_prev = p1_front(0)
        for i in range(NTT):
            if i + 2 < NTT:
                p1_load(i + 2)
            xk_next = p1_front(i + 1) if i + 1 < NTT else None
            p1_back(i, xk_prev)
            xk_prev = xk_next
        ada_mm(3, 4)
        ada_mm(2, 3)
        ada_fin(S2, SH2, 2, 3, 1)
        if "hT" in dbg:
            P.dma("sp", dbg["hT"], hT, reads=[("hT", c, i) for c in range(KC) for i in range(NT)])
        P.barrier(scr)
        A.release(m_after_persist)
        if upto == "p1":
            raise _Stop()

        maskB = A.alloc([NTYPE, 128], BF16)
        P.dma("pool", maskB, maskB_d, writes=["maskB"])
        wb = A.alloc([KC, 7, 128], BF16)
        BT = A.alloc([2, NTYPE, 128], BF16)
        rpst = A.alloc([NTYPE, 128], BF16)
        slabQ = A.alloc([SEQ], BF16)
        slabK = A.alloc([SEQ], BF16)
        rv = A.alloc([NT, 128], BF16)
        nva = A.alloc([NT, 2, 65], BF16)
        srg = A.alloc([NT, 128], BF16)
        DS = A.alloc([NT, 2, 64], F32)
        Rb = A.alloc([NT, 2, 64], BF16)
        BLK = 4
        rtmp = [A.alloc([BLK // 2, 4, 32], F32) for _ in range(4)]
        cs_t = [A.alloc([2, BLK, 32], F32) for _ in range(2)]
        qk_tm = [A.alloc([BLK, 256], BF16) for _ in range(2)]
        Vfb = [A.alloc([BLK, 2, 2, 64], BF16) for _ in range(2)]
        crk = A.alloc([2, 128], BF16)
        cVfb = A.alloc([2, 2, 2, 64], BF16)
        cnva = A.alloc([2, 2, 65], BF16)
        cnkT = A.alloc([CTX], BF16)
        PT = [A.alloc([7, 128], BF16) for _ in range(2)]
        SDT = [A.alloc([2, 4, 128], BF16) for _ in range(2)]
        QfbT = [A.alloc([2, 4, 128], BF16) for _ in range(2)]
        sqA = [A.alloc([512], F32) for _ in range(2)]
        msA = [A.alloc([8], F32) for _ in range(2)]
        onA = [A.alloc([512], F32) for _ in range(2)]
        ytile = [A.alloc([4, 128], BF16) for _ in range(2)]
        ystage = [A.alloc([512], BF16) for _ in range(2)]
        rc = A.alloc([2], F32)

        qm = [[A.alloc([128], BF16) for _ in range(2)] for _ in range(2)]
        for hh_ in range(2):
            for par_ in range(2):
                P.op("pool", lambda e, hh_=hh_, par_=par_: e.memset(qm[hh_][par_], 0.0), writes=[("qm", hh_, par_)])
        P.op("pool", lambda e: e.memset(nva, 1.0), writes=["nva_init"])
        P.op("pool", lambda e: e.memset(cnva, 1.0), writes=["cnva_init"])

        def pair_body(hp):
            hk = ("hp", hp)
            for d_ in range(2):
                P.op("act", lambda e, d_=d_: e.activation(out=QW[:, d_, :], in_=rowc[:, d_, :], func=AF.Exp,
                                                           scale=lgp[:, hp * 2 + d_:hp * 2 + d_ + 1]),
                     writes=["QW"])
            def load_wb(hp_):
                for s in range(7):
                    c0 = s * 512 + hp_ * 128
                    P.dma("pool", wb[:, :, s, :], w_in[:, c0:c0 + 128].rearrange("(k p) n -> p k n", p=128), writes=[("wb", s)])
            if hp == 0:
                load_wb(0)
            wbk = [("wb", s) for s in range(7)]
            for hh in range(2):
                P.dma("pool", rpst, rpbB[2 * hp + hh], writes=["rpst"])
                P.op("dve", lambda e, hh=hh: e.tensor_tensor(out=BT[:, hh, :, :], in0=rpst, in1=maskB, op=ALU.add),
                     reads=["rpst", "maskB"], writes=[("BT", hh)])
            for ct in range(2):
                def mm(e, ct=ct):
                    for k in range(KC):
                        ins = e.matmul(PB(6)[:, 0:256], lhsT=hcT[:, k, ct * 128:(ct + 1) * 128], rhs=wb[:, k, 1:3, :].rearrange("p s n -> p (s n)"),
                                       start=(k == 0), stop=(k == KC - 1))
                    for k in range(KC):
                        ins = e.matmul(PB(6)[:, 256:384], lhsT=hcT[:, k, ct * 128:(ct + 1) * 128], rhs=wb[:, k, 6, :],
                                       start=(k == 0), stop=(k == KC - 1))
                    return ins
                P.op("pe", mm, reads=wbk + ["hcT"], writes=PK(6))
                P.op("act", lambda e, ct=ct: e.copy(out=crk[:, ct, :], in_=PB(6)[:, 0:128]), writes=PK(6) + [("crk", ct)])
                for hh in range(2):
                    for d_ in range(2):
                        P.op("dve", lambda e, ct=ct, hh=hh, d_=d_: e.tensor_scalar(
                            out=cVfb[:, ct, hh, d_, :], in0=PB(6)[:, 128 + hh * 64:128 + (hh + 1) * 64],
                            scalar1=ckw[:, ct, 2 * hp + hh, d_:d_ + 1], scalar2=None, op0=ALU.mult),
                            reads=["ckw"], writes=PK(6) + [("cVfb", ct)])
                P.op("dve", lambda e, ct=ct: e.tensor_copy(out=cnva[:, ct, :, 0:64], in_=PB(6)[:, 256:384].rearrange("p (h d) -> p h d", d=64)),
                     reads=["cnva_init"], writes=PK(6) + [("cnva", ct)])

            def mm(e):
                for k in range(KC):
                    ins = e.matmul(PB(7)[:, 0:CTX], lhsT=wb[:, k, 5, :], rhs=hcT[:, k, :], start=(k == 0), stop=(k == KC - 1))
                return ins
            P.op("pe", mm, reads=wbk + ["hcT"], writes=PK(7))
            P.op("act", lambda e: e.copy(out=cnkT, in_=PB(7)[:, 0:CTX]), writes=PK(7) + ["cnkT"])

            def mm(e):
                for hh in range(2):
                    for ct in range(2):
                        ins = e.matmul(PB(6)[hh * 64:(hh + 1) * 64, 0:128], lhsT=crk[:, ct, hh * 64:(hh + 1) * 64],
                                       rhs=cVfb[:, ct, hh, :, :].rearrange("p a b -> p (a b)"), start=(ct == 0), stop=(ct == 1))
                return ins
            P.op("pe", mm, reads=[("crk", 0), ("crk", 1), ("cVfb", 0), ("cVfb", 1)], writes=PK(6))
            P.op("dve", lambda e: e.tensor_copy(out=DS[:, 0, 0, :], in_=PB(6)[:, 0:64]), writes=PK(6) + [("DS", 0, 0)])
            P.op("dve", lambda e: e.tensor_copy(out=DS[:, NT - 1, 1, :], in_=PB(6)[:, 64:128]), writes=PK(6) + [("DS", NT - 1, 1)])

            if upto == "p2ctx":
                raise _Stop()
            def frontA(b0):
                bi = (b0 // BLK) % 2
                qk_ = qk_tm[bi]
                cst = cs_t[bi]
                P.dma("sp", cst[:, 0, :, :], cos_d[:, b0:b0 + BLK, :], writes=[("cs", bi, 0)])
                P.dma("sp", cst[:, 1, :, :], sin_d[:, b0:b0 + BLK, :], writes=[("cs", bi, 1)])
                q5 = qk_.rearrange("p b (g t f) -> p b g t f", g=4, t=2)
                for half in range(2):
                    hb = [2 + 2 * half, 3 + 2 * half]
                    for ii2 in range(2):
                        ii = half * 2 + ii2
                        i = b0 + ii
                        pb = hb[ii2]

                        def mm(e, i=i, pb=pb):
                            for k in range(KC):
                                ins = e.matmul(PB(pb)[:, 0:512], lhsT=hT[:, k, i * 128:(i + 1) * 128], rhs=wb[:, k, 0:4, :].rearrange("p s n -> p (s n)"),
                                               start=(k == 0), stop=(k == KC - 1))
                            return ins
                        P.op("pe", mm, reads=wbk, writes=PK(pb))
                        P.op("dve", lambda e, i=i, pb=pb: e.tensor_copy(out=rv[:, i, :], in_=PB(pb)[:, 256:384]),
                             writes=PK(pb) + [("rv", i)])
                        P.op("act", lambda e, i=i, pb=pb: e.activation(out=srg[:, i, :], in_=PB(pb)[:, 384:512], func=AF.Silu),
                             writes=PK(pb) + [("srg", i)])
                    s5 = ps[:, hb[0]:hb[0] + 2, 0:256].rearrange("p b (g t f) -> p b g t f", g=4, t=2)
                    cosb = cst[:, 0, 2 * half:2 * half + 2, :].unsqueeze(2).to_broadcast([128, 2, 4, 32])
                    sinb = cst[:, 1, 2 * half:2 * half + 2, :].unsqueeze(2).to_broadcast([128, 2, 4, 32])
                    PKH = PK(*hb)
                    qh = q5[:, 2 * half:2 * half + 2]
                    P.op("dve", lambda e, s5=s5, cosb=cosb: e.tensor_tensor(out=rtmp[0], in0=s5[:, :, :, 0, :], in1=cosb, op=ALU.mult),
                         reads=[("cs", bi, 0)], writes=PKH + [("rtmp", 0)])
                    P.op("dve", lambda e, s5=s5, sinb=sinb: e.tensor_tensor(out=rtmp[1], in0=s5[:, :, :, 1, :], in1=sinb, op=ALU.mult),
                         reads=[("cs", bi, 1)], writes=PKH + [("rtmp", 1)])
                    P.op("dve", lambda e, s5=s5, sinb=sinb: e.tensor_tensor(out=rtmp[2], in0=s5[:, :, :, 0, :], in1=sinb, op=ALU.mult),
                         reads=[("cs", bi, 1)], writes=PKH + [("rtmp", 2)])
                    P.op("dve", lambda e, s5=s5, cosb=cosb: e.tensor_tensor(out=rtmp[3], in0=s5[:, :, :, 1, :], in1=cosb, op=ALU.mult),
                         reads=[("cs", bi, 0)], writes=PKH + [("rtmp", 3)])
                    P.op("pool", lambda e, qh=qh: e.tensor_tensor(out=qh[:, :, :, 0, :], in0=rtmp[0], in1=rtmp[1], op=ALU.subtract),
                         reads=[("rtmp", 0), ("rtmp", 1)], writes=[("qk", bi, half, 0)])
                    P.op("pool", lambda e, qh=qh: e.tensor_tensor(out=qh[:, :, :, 1, :], in0=rtmp[2], in1=rtmp[3], op=ALU.add),
                         reads=[("rtmp", 2), ("rtmp", 3)], writes=[("qk", bi, half, 1)])

            def blockA(b0):
                bi = (b0 // BLK) % 2
                qk_, vf_ = qk_tm[bi], Vfb[bi]
                qkk = [("qk", bi, half, w_) for half in range(2) for w_ in range(2)]
                for hh in range(2):
                    for d_ in range(2):
                        P.op("act", lambda e, hh=hh, d_=d_, vf_=vf_: e.activation(
                            out=vf_[:, :, hh, d_, :], in_=rv[:, b0:b0 + BLK, hh * 64:(hh + 1) * 64], func=AF.Copy,
                            scale=kw[:, 2 * hp + hh, d_:d_ + 1]),
                            reads=[("rv", b0 + ii) for ii in range(BLK)] + ["kw"], writes=[("Vfb", bi, hh, d_)])
                vfk = [("Vfb", bi, hh, d_) for hh in range(2) for d_ in range(2)]
                for which, slab, sk in ((0, slabQ, "slabQ"), (1, slabK, "slabK")):
                    pbt = 6 + which
                    pbv = PB(pbt).bitcast(BF16)

                    def tr(e, which=which, pbv=pbv, qk_=qk_):
                        for ii in range(BLK):
                            ins = e.transpose(out=pbv[:, ii * 128:(ii + 1) * 128], in_=qk_[:, ii, which * 128:(which + 1) * 128], identity=identb)
                        return ins
                    P.op("pe", tr, reads=qkk + ["identb"], writes=PK(pbt))
                    if which == 0:
                        P.op("act", lambda e, pbv=pbv, slab=slab: e.copy(out=slab[:, b0 * 128:(b0 + BLK) * 128], in_=pbv[:, 0:BLK * 128]),
                             writes=PK(pbt) + [(sk, b0 // BLK)])
                    else:
                        P.op("dve", lambda e, pbv=pbv, slab=slab: e.tensor_copy(out=slab[:, b0 * 128:(b0 + BLK) * 128], in_=pbv[:, 0:BLK * 128]),
                             writes=PK(pbt) + [(sk, b0 // BLK)])
                pbd = (b0 // BLK) % 2

                def mm(e, pbd=pbd, qk_=qk_, vf_=vf_):
                    for ii in range(BLK):
                        for hh in range(2):
                            ins = e.matmul(PB(pbd)[hh * 64:(hh + 1) * 64, ii * 128:(ii + 1) * 128],
                                           lhsT=qk_[:, ii, 128 + hh * 64:128 + (hh + 1) * 64],
                                           rhs=vf_[:, ii, hh, :, :].rearrange("p a b -> p (a b)"), start=True, stop=True)
                    return ins
                P.op("pe", mm, reads=qkk + vfk, writes=PK(pbd))
                pv = PB(pbd).rearrange("p (b d f) -> p b d f", d=2, f=64)
                lo, hi = b0, min(b0 + BLK, NT - 1)
                if hi > lo:
                    P.op("act", lambda e, lo=lo, hi=hi, pv=pv: e.copy(out=DS[:, lo + 1:hi + 1, 0, :], in_=pv[:, lo - b0:hi - b0, 0, :]),
                         writes=PK(pbd) + [("DS", c + 1, 0) for c in range(lo, hi)])
                lo2, hi2 = max(b0, 1), b0 + BLK
                if hi2 > lo2:
                    P.op("dve", lambda e, lo2=lo2, hi2=hi2, pv=pv: e.tensor_copy(out=DS[:, lo2 - 1:hi2 - 1, 1, :], in_=pv[:, lo2 - b0:hi2 - b0, 1, :]),
                         writes=PK(pbd) + [("DS", c - 1, 1) for c in range(lo2, hi2)])
            frontA(0)
            for b0 in range(0, NT, BLK):
                if b0 + BLK < NT:
                    frontA(b0 + BLK)
                blockA(b0)
            if upto == "p2a":
                raise _Stop()
            def nvproj(nb):
                pbp = 6 + (nb % 2)

                def mm(e):
                    for ii in range(4):
                        i = nb * 4 + ii
                        for k in range(KC):
                            ins = e.matmul(PB(pbp)[:, ii * 128:(ii + 1) * 128], lhsT=hT[:, k, i * 128:(i + 1) * 128], rhs=wb[:, k, 6, :],
                                           start=(k == 0), stop=(k == KC - 1))
                    return ins
                P.op("pe", mm, reads=wbk, writes=PK(pbp))
                P.op("act", lambda e: e.copy(out=nva[:, nb * 4:(nb + 1) * 4, :, 0:64], in_=PB(pbp).rearrange("p (i h d) -> p i h d", h=2, d=64)),
                     reads=["nva_init"], writes=PK(pbp) + [("nva", nb)])
            for nb in range(NB):
                nvproj(nb)
            for c in range(NT - 1):
                P.op("dve", lambda e, c=c: e.scalar_tensor_tensor(out=DS[:, c + 1, 0, :], in0=DS[:, c, 0, :], scalar=GL[:, 2 * hp:2 * hp + 1],
                                                                  in1=DS[:, c + 1, 0, :], op0=ALU.mult, op1=ALU.add),
                     reads=[("DS", c, 0), "GL"], writes=[("DS", c + 1, 0)])
            for c in range(NT - 1, 0, -1):
                P.op("dve", lambda e, c=c: e.scalar_tensor_tensor(out=DS[:, c - 1, 1, :], in0=DS[:, c, 1, :], scalar=GL[:, 2 * hp + 1:2 * hp + 2],
                                                                  in1=DS[:, c - 1, 1, :], op0=ALU.mult, op1=ALU.add),
                     reads=[("DS", c, 1), "GL"], writes=[("DS", c - 1, 1)])
            P.op("dve", lambda e: e.tensor_copy(out=Rb, in_=DS), reads=[("DS", c, d_) for c in range(NT) for d_ in range(2)], writes=["Rb"])
            if f"Rb{hp}" in dbg:
                P.dma("sp", dbg[f"Rb{hp}"], Rb, reads=["Rb"])

            if upto == "p2scan":
                raise _Stop()
            def blockB(g0):
                gi = (g0 // 4) % 2
                pbo = [2 + gi, 4 + gi]
                yt = ytile[gi]
                qf = QfbT[gi]
                sd = SDT[gi]
                sq, ms, on = sqA[gi], msA[gi], onA[gi]
                P.op("pool", lambda e, qf=qf: e.tensor_tensor(
                    out=qf, in0=slabQ[:, g0 * 128:(g0 + 4) * 128].rearrange("p (c i) -> p c i", c=4).unsqueeze(1).to_broadcast([128, 2, 4, 128]),
                    in1=QW.unsqueeze(2).to_broadcast([128, 2, 4, 128]), op=ALU.mult),
                    reads=[("slabQ", g0 // BLK), "QW"], writes=[("QfbT", gi)])
                for hh in range(2):
                    def mm(e, hh=hh):
                        for cc in range(4):
                            c = g0 + cc
                            ins = e.matmul(PB(hh)[:, cc * 128:(cc + 1) * 128], lhsT=slabK[hh * 64:(hh + 1) * 64, c * 128:(c + 1) * 128],
                                           rhs=slabQ[hh * 64:(hh + 1) * 64, c * 128:(c + 1) * 128], start=True, stop=True)
                        return ins
                    P.op("pe", mm, reads=[("slabQ", g0 // BLK), ("slabK", g0 // BLK)], writes=PK(hh))
                    P.op("dve", lambda e, hh=hh, sd=sd: e.tensor_tensor(
                        out=sd[:, hh, :, :], in0=PB(hh).rearrange("p (c i) -> p c i", c=4),
                        in1=DT[:, 2 * hp + hh, :].unsqueeze(1).to_broadcast([128, 4, 128]), op=ALU.mult),
                        reads=["DT"], writes=PK(hh) + [("SDT", gi, hh)])
                for hh in range(2):
                    def mm(e, hh=hh, sd=sd, qf=qf):
                        for cc in range(4):
                            c = g0 + cc
                            o_ = PB(pbo[hh])[:, cc * 64:(cc + 1) * 64]
                            e.matmul(o_, lhsT=sd[:, hh, cc, :], rhs=rv[:, c, hh * 64:(hh + 1) * 64], start=True, stop=False)
                            e.matmul(o_, lhsT=qf[hh * 64:(hh + 1) * 64, 0, cc, :], rhs=Rb[hh * 64:(hh + 1) * 64, c, 0, :], start=False, stop=False)
                            ins = e.matmul(o_, lhsT=qf[hh * 64:(hh + 1) * 64, 1, cc, :], rhs=Rb[hh * 64:(hh + 1) * 64, c, 1, :], start=False, stop=True)
                        return ins
                    P.op("pe", mm, reads=[("SDT", gi, hh), ("QfbT", gi), "Rb"] + [("rv", g0 + cc) for cc in range(4)], writes=PK(pbo[hh]))
            def backB(g0):
                gi = (g0 // 4) % 2
                pbo = [2 + gi, 4 + gi]
                yt = ytile[gi]
                sq, ms, on = sqA[gi], msA[gi], onA[gi]
                for hh in range(2):
                    P.op("act", lambda e, hh=hh: e.activation(out=sq[:, hh * 256:(hh + 1) * 256], in_=PB(pbo[hh])[:, 0:256], func=AF.Square),
                         writes=PK(pbo[hh]) + [("sq", gi, hh)])
                P.op("dve", lambda e: e.tensor_reduce(out=ms, in_=sq.rearrange("p (g f) -> p g f", f=64), axis=AX.X, op=ALU.add),
                     reads=[("sq", gi, 0), ("sq", gi, 1)], writes=[("ms", gi)])
                P.op("act", lambda e: e.activation(out=ms, in_=ms, func=AF.Sqrt, scale=1.0 / 64, bias=EPS), writes=[("ms", gi)])
                P.op("dve", lambda e: e.reciprocal(out=ms, in_=ms), writes=[("ms", gi)])
                for hh in range(2):
                    P.op("dve", lambda e, hh=hh: e.tensor_tensor(
                        out=on[:, hh * 256:(hh + 1) * 256].rearrange("p (g f) -> p g f", f=64),
                        in0=PB(pbo[hh])[:, 0:256].rearrange("p (g f) -> p g f", f=64),
                        in1=ms[:, hh * 4:(hh + 1) * 4].unsqueeze(2).to_broadcast([128, 4, 64]), op=ALU.mult),
                        reads=[("ms", gi)], writes=PK(pbo[hh]) + [("on", gi, hh)])
                    P.op("pool", lambda e, hh=hh: e.tensor_tensor(
                        out=yt[:, :, hh * 64:(hh + 1) * 64], in0=on[:, hh * 256:(hh + 1) * 256].rearrange("p (c f) -> p c f", f=64),
                        in1=srg[:, g0:g0 + 4, hh * 64:(hh + 1) * 64], op=ALU.mult),
                        reads=[("on", gi, hh)] + [("srg", g0 + cc) for cc in range(4)],
                        writes=[("ytile", gi, "h", hh)] + ([("ytile", gi)] + [("ytile", gi, cc) for cc in range(4)] if hh == 1 else []))
                pbt = 6 + gi
                pbv = PB(pbt).bitcast(BF16)

                def tr(e, pbv=pbv, yt=yt):
                    for cc in range(4):
                        ins = e.transpose(out=pbv[:, cc * 128:(cc + 1) * 128], in_=yt[:, cc, :], identity=identb)
                    return ins
                P.op("pe", tr, reads=[("ytile", gi), "identb", ("ytile", gi, "h", 0), ("ytile", gi, "h", 1)] + [("ytile", gi, cc) for cc in range(4)], writes=PK(pbt))
                P.op("act", lambda e, pbv=pbv, gi=gi: e.copy(out=ystage[gi], in_=pbv[:, 0:512]), writes=PK(pbt) + [("ystage", gi)])
                P.dma("sp", yT_d[hp, :, g0 * 128:(g0 + 4) * 128], ystage[gi], reads=[("ystage", gi)], writes=[("yT", 0, hp, g0 // 4)])
            blockB(0)
            for g0 in range(0, NT, 4):
                if g0 + 4 < NT:
                    blockB(g0 + 4)
                backB(g0)

            if upto == "p2b":
                raise _Stop()
            def naproj(nb):
                for which, slab, sk, slot in ((0, slabQ, "slabQ", 4), (1, slabK, "slabK", 5)):
                    pbp = 4 + which

                    def mm(e, nb=nb, slot=slot, pbp=pbp):
                        for k in range(KC):
                            ins = e.matmul(PB(pbp)[:, 0:512], lhsT=wb[:, k, slot, :], rhs=hT[:, k, nb * 512:(nb + 1) * 512],
                                           start=(k == 0), stop=(k == KC - 1))
                        return ins
                    P.op("pe", mm, reads=wbk, writes=PK(pbp))
                    if which == 0:
                        P.op("act", lambda e, nb=nb, pbp=pbp: e.activation(out=slabQ[:, nb * 512:(nb + 1) * 512], in_=PB(pbp), func=AF.Copy, scale=0.125),
                             writes=PK(pbp) + [("slabQ", nb)])
                    else:
                        P.op("dve", lambda e, nb=nb, pbp=pbp: e.tensor_copy(out=slabK[:, nb * 512:(nb + 1) * 512], in_=PB(pbp)),
                             writes=PK(pbp) + [("slabK", nb)])
            for nb in range(NB):
                naproj(nb)
            if hp + 1 < 4:
                load_wb(hp + 1)
            if upto == "p2np":
                raise _Stop()
            def na_front(t, hh):
                lst = per_t[t]
                pi = hh
                pA, pB_ = 2 * pi, 2 * pi + 1
                pt_ = PT[pi]
                nloc = len(lst)
                assert nloc <= 5
                qmb = qm[hh][t % 2]
                P.op("pool", lambda e: e.tensor_copy(out=qmb[hh * 64:(hh + 1) * 64, :], in_=slabQ[hh * 64:(hh + 1) * 64, t * 128:(t + 1) * 128]),
                     reads=[("slabQ", t // 4)], writes=[("qm", hh, t % 2)])

                def mm(e):
                    for m, (u, ty) in enumerate(lst):
                        o_ = (PB(pA)[:, m * 128:(m + 1) * 128] if m < 4 else PB(pB_)[:, 0:128])
                        e.matmul(o_, lhsT=slabK[:, u * 128:(u + 1) * 128], rhs=qmb, start=True, stop=False)
                        ins = e.matmul(o_, lhsT=identb, rhs=BT[:, hh, ty, :], start=False, stop=True)
                    for ct in range(2):
                        ins = e.matmul(PB(pB_)[:, (1 + ct) * 128:(2 + ct) * 128], lhsT=cnkT[:, ct * 128:(ct + 1) * 128],
                                       rhs=qmb, start=True, stop=True)
                    return ins
                kblocks = sorted(set(u // 4 for (u, _) in lst))
                P.op("pe", mm, reads=[("qm", hh, t % 2), "cnkT", ("BT", hh), "identb"] + [("slabK", kb) for kb in kblocks],
                     writes=PK(pA, pB_))
                na4 = min(nloc, 4)
                P.op("act", lambda e: e.activation(out=pt_[:, 0:na4, :], in_=PB(pA)[:, 0:na4 * 128].rearrange("p (m q) -> p m q", q=128), func=AF.Exp),
                     writes=PK(pA) + [("PT", pi, 0)])
                lo_ = 0 if nloc == 5 else 1
                P.op("act", lambda e: e.activation(out=pt_[:, 4 + lo_:7, :], in_=PB(pB_)[:, lo_ * 128:3 * 128].rearrange("p (m q) -> p m q", q=128), func=AF.Exp),
                     writes=PK(pB_) + [("PT", pi, 1)])

            def na_back(t, hh):
                lst = per_t[t]
                pi = hh
                pt_ = PT[pi]
                gi = (t // 4) % 2
                yt = ytile[gi]
                pbo = 4 + (t % 2)
                kblocks = sorted(set(u // 4 for (u, _) in lst))

                def mm(e):
                    o_ = PB(pbo)[:, hh * 66:hh * 66 + 65]
                    for m, (u, ty) in enumerate(lst):
                        slot = m if m < 4 else 4
                        e.matmul(o_, lhsT=pt_[:, slot, :], rhs=nva[:, u, hh, :], start=(m == 0), stop=False)
                    for ct in range(2):
                        ins = e.matmul(o_, lhsT=pt_[:, 5 + ct, :], rhs=cnva[:, ct, hh, :], start=False, stop=(ct == 1))
                    return ins
                P.op("pe", mm, reads=[("PT", pi, 0), ("PT", pi, 1), ("cnva", 0), ("cnva", 1)] + [("nva", kb) for kb in kblocks],
                     writes=PK(pbo))
                if hh == 0:
                    return
                ov = PB(pbo)[:, 0:132].rearrange("p (h f) -> p h f", f=66)
                P.op("dve", lambda e: e.reciprocal(out=rc, in_=ov[:, :, 64]), writes=PK(pbo) + ["rc"])
                P.op("dve", lambda e: e.tensor_tensor(
                    out=yt[:, t % 4, :].rearrange("p (h d) -> p h d", d=64), in0=ov[:, :, 0:64],
                    in1=rc.unsqueeze(2).to_broadcast([128, 2, 64]), op=ALU.mult),
                    reads=["rc"], writes=PK(pbo) + [("ytile", gi, t % 4)])
                if t % 4 == 3:
                    g0 = t - 3
                    pbt = 6 + gi
                    pbv = PB(pbt).bitcast(BF16)

                    def tr(e):
                        for cc in range(4):
                            ins = e.transpose(out=pbv[:, cc * 128:(cc + 1) * 128], in_=yt[:, cc, :], identity=identb)
                        return ins
                    P.op("pe", tr, reads=[("ytile", gi, cc) for cc in range(4)] + [("ytile", gi), "identb"], writes=PK(pbt))
                    P.op("act", lambda e: e.copy(out=ystage[gi], in_=pbv[:, 0:512]), writes=PK(pbt) + [("ystage", gi)])
                    P.dma("sp", yT_d[4 + hp, :, g0 * 128:(g0 + 4) * 128], ystage[gi], reads=[("ystage", gi)], writes=[("yT", 1, hp, g0 // 4)])
            if hp == 0:
                P.dma("pool", wgb, w_in[:, 3584:5632], writes=["wgb"])
                P.dma("pool", wrob, w_ro, writes=["wrob"])
                P.dma("pool", wnob, w_no, writes=["wnob"])
                P.dma("pool", wob, w_o, writes=["wob"])
                P.dma("pool", wadab[0], w_ada[:, 2 * D:3 * D], writes=["wadab0"])
                P.dma("pool", wadab[1], w_ada[:, 5 * D:6 * D], writes=["wadab1"])
            if hp == 1:
                P.dma("pool", w1b.rearrange("r (a c) -> (r a) c", a=2), w_ff1.rearrange("r (a c) -> (r a) c", a=2), writes=["w1b"])
            if hp == 2:
                P.dma("pool", w2b, w_ff2, writes=["w2b"])
            units = [(t, hh) for t in range(NT) for hh in range(2)]
            for k_, (t_, hh_) in enumerate(units):
                na_front(t_, hh_)
                if k_ >= 1:
                    na_back(*units[k_ - 1])
            na_back(*units[-1])
        for hp in range(4):
            pair_body(hp)
        P.barrier(scr)
        A.release(m_after_persist)
        A.release(m0)

        if upto == "p2":
            raise _Stop()
        GT = [A.alloc([D], F32) for _ in range(2)]
        m3 = A.mark()
        wbufg = A.alloc([KC, 1024], BF16)
        bb = A.alloc([D], F32)
        gb = A.alloc([D], F32)
        for gi_, j in enumerate((2, 5)):
            P.dma("sp", wbufg, wadab[gi_].rearrange("(k p) n -> p k n", p=128), writes=["wbufg"])
            P.dma("sp", bb, b_ada[0:1, j * D:(j + 1) * D].partition_broadcast(128), writes=["bb"])
            P.dma("sp", gb, gpost[gi_:gi_ + 1, :].partition_broadcast(128), writes=["gb"])

            def mm(e):
                for half in range(2):
                    for k in range(KC):
                        ins = e.matmul(PB(half)[:, 0:512], lhsT=sc_rep[:, k, :], rhs=wbufg[:, k, half * 512:(half + 1) * 512],
                                       start=(k == 0), stop=(k == KC - 1))
                return ins
            P.op("pe", mm, reads=["wbufg", "sc_rep"], writes=PK(0, 1))
            P.op("dve", lambda e, gi_=gi_: e.tensor_tensor(out=GT[gi_].rearrange("p (b n) -> p b n", b=2), in0=ps[:, 0:2, :], in1=bb.rearrange("p (b n) -> p b n", b=2), op=ALU.add),
                 reads=["bb"], writes=PK(0, 1) + [("GT", gi_)])
            P.op("dve", lambda e, gi_=gi_: e.tensor_tensor(out=GT[gi_], in0=GT[gi_], in1=gb, op=ALU.mult), reads=["gb"], writes=[("GT", gi_)])
        P.barrier(scr)
        A.release(m3)

        if upto == "p3p":
            raise _Stop()
        Wg = A.alloc([KC, 2048], BF16)
        Wro = A.alloc([4, D], BF16)
        Wno = A.alloc([4, D], BF16)
        Wo = A.alloc([KC, D], BF16)
        def load3a_weights():
            for q4 in range(4):
                P.dma("sp", Wg[:, :, q4 * 512:(q4 + 1) * 512], wgb[:, q4 * 512:(q4 + 1) * 512].rearrange("(k p) n -> p k n", p=128), writes=[("Wg", q4)])
            P.dma("sp", Wro, wrob.rearrange("(k p) n -> p k n", p=128), writes=["Wro"])
            P.dma("sp", Wno, wnob.rearrange("(k p) n -> p k n", p=128), writes=["Wno"])
            for q2 in range(2):
                P.dma("sp", Wo[:, :, q2 * 512:(q2 + 1) * 512], wob[:, q2 * 512:(q2 + 1) * 512].rearrange("(k p) n -> p k n", p=128), writes=[("Wo", q2)])
        xbA = [A.alloc([4, D], F32) for _ in range(2)]
        junkA = A.alloc([D], BF16)
        tmpA3 = [dict(junk=junkA, junk_key="junkA", ss=A.alloc([1], F32), rstd=A.alloc([1], F32), xn=A.alloc([D], F32), key=("nt3", i_)) for i_ in range(2)]
        hTb = A.alloc([KC, 512], BF16)
        yTbA = [A.alloc([8, 512], BF16) for _ in range(2)]
        sgT = A.alloc([16, 512], F32)
        z1A = [A.alloc([512], F32) for _ in range(2)]
        z2A = [A.alloc([512], F32) for _ in range(2)]
        zT = A.alloc([KC, 512], BF16)
        ssyA = [A.alloc([1], F32) for _ in range(2)]
        rsyA = [A.alloc([1], F32) for _ in range(2)]
        tyA = [A.alloc([D], F32) for _ in range(2)]
        Wgk = [("Wg", q4) for q4 in range(4)]

        def load3a(nb):
            xb = xbA[nb % 2]
            for tt in range(4):
                P.dma("sp", xb[:, tt, :], x[(nb * 4 + tt) * 128:(nb * 4 + tt + 1) * 128, :], writes=[("xbA", nb % 2, tt)])
            P.dma("sp", yTbA[nb % 2], yT_d[:, :, nb * 512:(nb + 1) * 512].rearrange("a p n -> p a n"), writes=[("yTb", nb % 2)])

        def n3a_p1(nb, tt):
            xb = xbA[nb % 2]
            return norm_p1(xb[:, tt, :], ("xbA", nb % 2, tt), tmpA3[tt % 2])

        def n3a_p2(nb, tt, xnk):
            norm_p2(xnk, (lambda c: hTb[:, c, tt * 128:(tt + 1) * 128]), [("hTb", c, tt) for c in range(KC)], S1, SH1, 0, (6, 7), tmpA3[tt % 2])

        def norm3a(nb):
            for tt in range(4):
                n3a_p2(nb, tt, n3a_p1(nb, tt))

        def blk3a(nb):
            xb = xbA[nb % 2]
            yTb = yTbA[nb % 2]
            if nb + 1 < NB:
                load3a(nb + 1)
            hkeys = [("hTb", c, tt) for c in range(KC) for tt in range(4)]
            xnks = {}
            for g in range(16):
                pb = g % 2

                def mm(e, g=g, pb=pb):
                    for k in range(KC):
                        ins = e.matmul(PB(pb), lhsT=Wg[:, k, g * 128:(g + 1) * 128], rhs=hTb[:, k, :], start=(k == 0), stop=(k == KC - 1))
                    return ins
                P.op("pe", mm, reads=hkeys + [("Wg", g // 4)], writes=PK(pb))
                P.op("act", lambda e, g=g, pb=pb: e.activation(out=sgT[:, g, :], in_=PB(pb), func=AF.Sigmoid), writes=PK(pb) + [("sgT", g)])
                if nb + 1 < NB and g in (2, 6):
                    xnks[g // 4] = n3a_p1(nb + 1, g // 4)
            for fc in range(KC):
                pa, pbb = 2 + fc % 2, 4 + fc % 2
                z1, z2 = z1A[fc % 2], z2A[fc % 2]

                def mm(e, fc=fc, pa=pa):
                    for k in range(4):
                        ins = e.matmul(PB(pa), lhsT=Wro[:, k, fc * 128:(fc + 1) * 128], rhs=yTb[:, k, :], start=(k == 0), stop=(k == 3))
                    return ins
                P.op("pe", mm, reads=[("yTb", nb % 2), "Wro"], writes=PK(pa))

                def mm(e, fc=fc, pbb=pbb):
                    for k in range(4):
                        ins = e.matmul(PB(pbb), lhsT=Wno[:, k, fc * 128:(fc + 1) * 128], rhs=yTb[:, 4 + k, :], start=(k == 0), stop=(k == 3))
                    return ins
                P.op("pe", mm, reads=[("yTb", nb % 2), "Wno"], writes=PK(pbb))
                P.op("dve", lambda e, fc=fc, pa=pa, z1=z1: e.tensor_tensor(out=z1, in0=PB(pa), in1=sgT[:, fc, :], op=ALU.mult),
                     reads=[("sgT", fc)], writes=PK(pa) + [("z1", fc % 2)])
                P.op("dve", lambda e, fc=fc, pbb=pbb, z2=z2: e.tensor_tensor(out=z2, in0=PB(pbb), in1=sgT[:, 8 + fc, :], op=ALU.mult),
                     reads=[("sgT", 8 + fc)], writes=PK(pbb) + [("z2", fc % 2)])
                P.op("pool", lambda e, fc=fc, z1=z1, z2=z2: e.tensor_tensor(out=zT[:, fc, :], in0=z1, in1=z2, op=ALU.add),
                     reads=[("z1", fc % 2), ("z2", fc % 2)], writes=[("zT", fc)])
                if nb + 1 < NB and fc % 2 == 1:
                    tt_ = fc // 2
                    n3a_p2(nb + 1, tt_, xnks[tt_])
                    if tt_ + 2 < 4:
                        xnks[tt_ + 2] = n3a_p1(nb + 1, tt_ + 2)
            for tt in range(4):
                i = nb * 4 + tt
                py0 = 6 - 2 * (tt % 2)
                ssy, rsy, ty = ssyA[tt % 2], rsyA[tt % 2], tyA[tt % 2]

                def mm(e, tt=tt, py0=py0):
                    for half in range(2):
                        for k in range(KC):
                            ins = e.matmul(PB(py0 + half), lhsT=zT[:, k, tt * 128:(tt + 1) * 128], rhs=Wo[:, k, half * 512:(half + 1) * 512],
                                           start=(k == 0), stop=(k == KC - 1))
                    return ins
                P.op("pe", mm, reads=[("zT", fc) for fc in range(KC)] + [("Wo", 0), ("Wo", 1)], writes=PK(py0, py0 + 1))
                jk = tmpA3[tt % 2]
                P.op("act", lambda e, py0=py0, jk=jk, ssy=ssy: e.activation(out=jk["junk"].rearrange("p (b n) -> p b n", b=2), in_=ps[:, py0:py0 + 2, :], func=AF.Square, accum_out=ssy),
                     writes=PK(py0, py0 + 1) + ["junkA", ("ssy", tt % 2)])
                P.op("act", lambda e, ssy=ssy, rsy=rsy: e.activation(out=rsy, in_=ssy, func=AF.Sqrt, scale=1.0 / D, bias=EPS),
                     reads=[("ssy", tt % 2)], writes=[("rsy", tt % 2)])
                P.op("dve", lambda e, rsy=rsy: e.reciprocal(out=rsy, in_=rsy), writes=[("rsy", tt % 2)])
                P.op("dve", lambda e, py0=py0, rsy=rsy, ty=ty: e.scalar_tensor_tensor(
                    out=ty.rearrange("p (b n) -> p b n", b=2), in0=ps[:, py0:py0 + 2, :], scalar=rsy[:, 0:1],
                    in1=GT[0].rearrange("p (b n) -> p b n", b=2), op0=ALU.mult, op1=ALU.mult),
                    reads=[("rsy", tt % 2), ("GT", 0)], writes=PK(py0, py0 + 1) + [("ty", tt % 2)])
                P.op("pool", lambda e, tt=tt, ty=ty, xb=xb: e.tensor_tensor(out=ty, in0=ty, in1=xb[:, tt, :], op=ALU.add),
                     reads=[("xbA", nb % 2, tt)], writes=[("ty", tt % 2)])
                P.dma("sp", out[i * 128:(i + 1) * 128, :], ty, reads=[("ty", tt % 2)], writes=[("x1d", i)])
        load3a(0)
        load3a_weights()
        norm3a(0)
        for nb in range(NB):
            blk3a(nb)
        P.barrier(scr)
        A.release(m3)

        if upto == "p3a":
            raise _Stop()
        W1 = A.alloc([KC, 4 * D], BF16)
        W2 = A.alloc([32, D], BF16)
        def load3b_weights():
            for q8 in range(8):
                P.dma("sp", W1[:, :, q8 * 512:(q8 + 1) * 512], w1b[:, q8 * 512:(q8 + 1) * 512].rearrange("(k p) n -> p k n", p=128), writes=[("W1", q8)])
            for q8 in range(8):
                P.dma("sp", W2[:, q8 * 4:(q8 + 1) * 4, :], w2b[q8 * 512:(q8 + 1) * 512, :].rearrange("(k p) n -> p k n", p=128), writes=[("W2", q8)])
        xtB = [A.alloc([D], F32) for _ in range(2)]
        xrB = A.alloc([D], F32)
        rl = [A.alloc([512], F32) for _ in range(2)]
        h2TB = [A.alloc([KC, 512], BF16) for _ in range(2)]
        uT = A.alloc([32, 512], BF16)
        ssB = [A.alloc([1], F32) for _ in range(2)]
        rstdB = [A.alloc([1], F32) for _ in range(2)]
        ssyB = [A.alloc([1], F32) for _ in range(2)]
        rsyB = [A.alloc([1], F32) for _ in range(2)]
        tmpB3 = [dict(junk=rl[i_].bitcast(BF16), junk_key=("rl", i_), ss=ssB[i_], rstd=rstdB[i_], xn=xtB[i_], key=("nt4", i_)) for i_ in range(2)]
        W2k = [("W2", q8) for q8 in range(8)]

        def n3b_p1(nb, tt):
            i = nb * 4 + tt
            bi = i % 2
            P.dma("sp", xtB[bi], out[i * 128:(i + 1) * 128, :], reads=[("x1d", i)], writes=[("xtB", bi)])
            return norm_p1(xtB[bi], ("xtB", bi), tmpB3[bi], inplace=True)

        def n3b_p2(nb, tt, xnk):
            i = nb * 4 + tt
            h2T = h2TB[nb % 2]
            norm_p2(xnk, (lambda c: h2T[:, c, tt * 128:(tt + 1) * 128]), [("h2T", nb % 2, c, tt) for c in range(KC)], S2, SH2, 0, (6, 7), tmpB3[i % 2])

        def blk3b(nb):
            h2T = h2TB[nb % 2]
            hkeys = [("h2T", nb % 2, c, tt) for c in range(KC) for tt in range(4)]
            nxt = nb + 1 < NB
            xnks = {}
            for j in range(32):
                pb = j % 2

                def mm(e, j=j, pb=pb):
                    for k in range(KC):
                        ins = e.matmul(PB(pb), lhsT=W1[:, k, j * 128:(j + 1) * 128], rhs=h2T[:, k, :], start=(k == 0), stop=(k == KC - 1))
                    return ins
                P.op("pe", mm, reads=hkeys + [("W1", j // 4)], writes=PK(pb))
                P.op("act", lambda e, pb=pb: e.activation(out=rl[pb], in_=PB(pb), func=AF.Relu), writes=PK(pb) + [("rl", pb)])
                P.op("dve" if j % 2 == 0 else "pool", lambda e, j=j, pb=pb: e.tensor_tensor(out=uT[:, j, :], in0=rl[pb], in1=rl[pb], op=ALU.mult),
                     reads=[("rl", pb)], writes=[("uT", j)])
                if nxt:
                    if j == 3:
                        xnks[0] = n3b_p1(nb + 1, 0)
                    elif j == 7:
                        xnks[1] = n3b_p1(nb + 1, 1)
                    elif j == 15:
                        n3b_p2(nb + 1, 0, xnks[0])
                        xnks[2] = n3b_p1(nb + 1, 2)
                    elif j == 21:
                        n3b_p2(nb + 1, 1, xnks[1])
                        xnks[3] = n3b_p1(nb + 1, 3)
                    elif j == 27:
                        n3b_p2(nb + 1, 2, xnks[2])
                    elif j == 31:
                        n3b_p2(nb + 1, 3, xnks[3])
            for tt in range(4):
                i = nb * 4 + tt
                pbm = 2 + 2 * (tt % 2)
                ssy, rsy = ssyB[tt % 2], rsyB[tt % 2]
                P.dma("sp", xrB, out[i * 128:(i + 1) * 128, :], reads=[("x1d", i)], writes=["xrB"])

                def mm(e, tt=tt, pbm=pbm):
                    for half in range(2):
                        for j in range(32):
                            ins = e.matmul(PB(pbm + half), lhsT=uT[:, j, tt * 128:(tt + 1) * 128], rhs=W2[:, j, half * 512:(half + 1) * 512],
                                           start=(j == 0), stop=(j == 31))
                    return ins
                P.op("pe", mm, reads=[("uT", j) for j in range(32)] + W2k, writes=PK(pbm, pbm + 1))
                jb = tt % 2
                P.op("act", lambda e, pbm=pbm, jb=jb, ssy=ssy: e.activation(out=rl[jb].bitcast(BF16).rearrange("p (b n) -> p b n", b=2), in_=ps[:, pbm:pbm + 2, :], func=AF.Square, accum_out=ssy),
                     writes=PK(pbm, pbm + 1) + [("rl", jb), ("ssyB", tt % 2)])
                P.op("act", lambda e, ssy=ssy, rsy=rsy: e.activation(out=rsy, in_=ssy, func=AF.Sqrt, scale=1.0 / D, bias=EPS),
                     reads=[("ssyB", tt % 2)], writes=[("rsyB", tt % 2)])
                P.op("dve", lambda e, rsy=rsy: e.reciprocal(out=rsy, in_=rsy), writes=[("rsyB", tt % 2)])
                P.op("dve", lambda e, pbm=pbm, rsy=rsy: e.scalar_tensor_tensor(
                    out=ps[:, pbm:pbm + 2, :], in0=ps[:, pbm:pbm + 2, :], scalar=rsy[:, 0:1],
                    in1=GT[1].rearrange("p (b n) -> p b n", b=2), op0=ALU.mult, op1=ALU.mult),
                    reads=[("rsyB", tt % 2), ("GT", 1)], writes=PK(pbm, pbm + 1))
                P.op("dve", lambda e, pbm=pbm: e.tensor_tensor(out=xrB.rearrange("p (b n) -> p b n", b=2), in0=ps[:, pbm:pbm + 2, :],
                                                               in1=xrB.rearrange("p (b n) -> p b n", b=2), op=ALU.add),
                     writes=PK(pbm, pbm + 1) + ["xrB"])
                P.dma("sp", out[i * 128:(i + 1) * 128, :], xrB, reads=["xrB"], writes=[("outd", i)])
        for tt in range(4):
            n3b_p2(0, tt, n3b_p1(0, tt))
        load3b_weights()
        for nb in range(NB):
            blk3b(nb)
    try:
        body()
    except _Stop:
        pass
    info = P.emit()
    info["arena_peak"] = A.peak
    cmp_.__exit__(None, None, None)
    cm.__exit__(None, None, None)
    return nc, info, types


def prep_inputs(inputs, SEQ, types):
    NT = SEQ // 128
    f = lambda a: np.ascontiguousarray(np.asarray(a, dtype=np.float32))
    x = f(inputs["x"]); c = f(inputs["c"]); ctx = f(inputs["ctx"]); c_ctx = f(inputs["c_ctx"])
    B = x.shape[0]
    w_ada = f(inputs["w_ada"][0]); b_ada = f(inputs["b_ada"][0])
    shared = dict(
        w_ada=w_ada,
        bada_fm=np.ascontiguousarray(b_ada.reshape(48, 128).T),
        b_ada=b_ada.reshape(1, -1),
        gpre_fm=np.ascontiguousarray(np.stack([f(inputs["norm_pre_mix"][0]).reshape(KC, 128).T,
                                               f(inputs["norm_pre_ffn"][0]).reshape(KC, 128).T], axis=1)),
        gpost=np.ascontiguousarray(np.stack([f(inputs["norm_post_mix"][0]), f(inputs["norm_post_ffn"][0])], axis=0)),
        w_in=f(inputs["w_in"][0]), w_ro=f(inputs["w_ret_out"][0]), w_no=f(inputs["w_na_out"][0]),
        w_o=f(inputs["w_o"][0]), w_ff1=f(inputs["w_ff1"][0]), w_ff2=f(inputs["w_ff2"][0]),
    )
    lg = f(inputs["ret_decay_logit"][0])
    lgt_pair = np.zeros((128, 8), np.float32)
    def pair_body(hp):
        for d_ in range(2):
            lgt_pair[0:64, hp * 2 + d_] = lg[d_, 2 * hp]
            lgt_pair[64:128, hp * 2 + d_] = lg[d_, 2 * hp + 1]
    for hp in range(4):
        pair_body(hp)
    lgt_bc = np.zeros((128, 16), np.float32)
    for h in range(8):
        for d_ in range(2):
            lgt_bc[:, 2 * h + d_] = lg[d_, h]
    shared["lgt_pair"] = lgt_pair
    shared["lgt_bc"] = lgt_bc
    shared["ident"] = np.eye(128, dtype=np.float32)
    j = np.arange(128)[:, None].astype(np.float32)
    i = np.arange(128)[None, :].astype(np.float32)
    cmat = np.stack([np.maximum(i - j, 0), np.maximum(j - i, 0), (i >= j) * 0.125, (j > i) * 0.125], axis=1).astype(np.float32)
    shared["cmat"] = np.ascontiguousarray(cmat)
    jj = np.arange(128, dtype=np.float32)
    shared["colc"] = np.ascontiguousarray(np.stack([127 - jj, jj, 255 - jj, 127 - jj, jj, 128 + jj], axis=1))
    ii = np.arange(128, dtype=np.float32)
    shared["rowc"] = np.ascontiguousarray(np.broadcast_to(np.stack([ii + 1, 128 - ii], axis=0)[None], (128, 2, 128)).astype(np.float32))
    cos, sin = rope_tables(SEQ)
    shared["cos_tm"] = np.ascontiguousarray(cos.reshape(NT, 128, 32).transpose(1, 0, 2))
    shared["sin_tm"] = np.ascontiguousarray(sin.reshape(NT, 128, 32).transpose(1, 0, 2))
    mask, idr, idc = na_consts(types)
    rpb = f(inputs["na_rpb"][0])
    rpbB = rpb[:, idr, idc]
    shared["rpbB"] = np.ascontiguousarray(rpbB.transpose(0, 2, 1, 3))
    shared["maskB"] = np.ascontiguousarray(mask.transpose(1, 0, 2))
    in_maps = []
    for b in range(B):
        m = dict(shared)
        m["x"] = x[b]
        m["ctx"] = ctx[b]
        m["c_fm"] = np.ascontiguousarray(np.stack([c[b].reshape(KC, 128).T, c_ctx.reshape(KC, 128).T], axis=2))
        in_maps.append(m)
    return in_maps


_CACHE = {}


def kernel(**inputs):
    x = inputs["x"]
    B, SEQ, _ = x.shape
    if SEQ not in _CACHE:
        _CACHE[SEQ] = build(SEQ)
    nc, info, types = _CACHE[SEQ]
    in_maps = prep_inputs(inputs, SEQ, types)
    res = run_bass_kernel_spmd(nc, in_maps, core_ids=list(range(B)))
    return np.stack([np.asarray(r["out"], dtype=np.float32) for r in res.results], axis=0)
```

```python
import numpy as np
import ml_dtypes
import concourse.bass as bass
import concourse.mybir as mybir
from concourse.bass_utils import run_bass_kernel_spmd

F32 = mybir.dt.float32
BF16 = mybir.dt.bfloat16
U8 = mybir.dt.uint8
AF = mybir.ActivationFunctionType
ALU = mybir.AluOpType
AX = mybir.AxisListType

D = 1024
KC = 8
CTX = 256
GRID_W = 64
EPS = 1e-6
CH = 4096


class Prog:
    ENGS = ("pe", "act", "dve", "pool", "sp")

    def __init__(self, nc, n_dma_sems=16):
        self.nc = nc
        self.ops = []
        self.last_w = {}
        self.readers = {}
        self.n_dma_sems = n_dma_sems
        self.pending = {e: set() for e in self.ENGS}
        self.bar_start = 0
        self.nbar = 0

    def op(self, eng, fn, reads=(), writes=(), dma=False):
        oid = len(self.ops)
        deps = set()
        for k in list(reads) + list(writes):
            if k in self.last_w:
                deps.add(self.last_w[k])
        for k in writes:
            for r in self.readers.get(k, ()):
                deps.add(r)
        deps |= self.pending[eng]
        self.pending[eng] = set()
        deps.discard(oid)
        self.ops.append(dict(eng=eng, fn=fn, deps=deps, dma=dma, has_dep=False))
        for k in reads:
            self.readers.setdefault(k, []).append(oid)
        for k in writes:
            self.last_w[k] = oid
            self.readers[k] = []
        return oid

    def dma(self, q, out, in_, reads=(), writes=(), **kw):
        def fn(e):
            return e.dma_start(out=out, in_=in_, **kw)
        return self.op(q, fn, reads, writes, dma=True)

    def barrier(self, scratch):
        n = self.nbar
        self.nbar += 1
        dmas = [i for i in range(self.bar_start, len(self.ops)) if self.ops[i]["dma"]]
        marks = []
        marks.append(self.op("act", lambda e: e.copy(out=scratch["act"], in_=scratch["act"]), writes=[("bar", n, "act")]))
        marks.append(self.op("dve", lambda e: e.memset(scratch["dve"], 0.0), writes=[("bar", n, "dve")]))
        marks.append(self.op("pool", lambda e: e.memset(scratch["pool"], 0.0), writes=[("bar", n, "pool")]))
        for e in self.ENGS:
            self.pending[e] = set(marks) | set(dmas)
        self.last_w = {}
        self.readers = {}
        self.bar_start = len(self.ops)

    def emit(self, final_wait_eng="sp"):
        nc = self.nc
        ops = self.ops
        for i, o in enumerate(ops):
            keep = set()
            for d in o["deps"]:
                od = ops[d]
                if (not od["dma"]) and od["eng"] == o["eng"] and o["eng"] == "pe" and not o["dma"]:
                    continue
                keep.add(d)
            o["deps"] = keep
            for d in keep:
                ops[d]["has_dep"] = True
        tail = [i for i, o in enumerate(ops) if o["dma"] and not o["has_dep"]]
        for i in tail:
            ops[i]["has_dep"] = True
        cnt = {e: 0 for e in self.ENGS}
        for o in ops:
            if not o["dma"] and o["has_dep"]:
                o["seq"] = cnt[o["eng"]]
                cnt[o["eng"]] += 1
        sems = {}
        for e in self.ENGS:
            n = (cnt[e] + CH - 1) // CH
            sems[e] = [nc.alloc_semaphore(name=f"s_{e}_{j}") for j in range(n)]
        dsems = [nc.alloc_semaphore(name=f"s_dma_{j}") for j in range(self.n_dma_sems)]
        dcount = [0] * self.n_dma_sems
        dnext = 0
        waited = {e: {} for e in self.ENGS}

        def plan_wait(o, e, sem, val):
            key = id(sem)
            if waited[e].get(key, 0) >= val:
                return
            waited[e][key] = val
            o["waits"].append((sem, val))

        for i, o in enumerate(ops):
            e = o["eng"]
            o["waits"] = []
            for d in sorted(o["deps"]):
                od = ops[d]
                if od["dma"]:
                    plan_wait(o, e, od["dsem"], od["dval"])
                else:
                    s = od["seq"]
                    plan_wait(o, e, sems[od["eng"]][s // CH], s % CH + 1)
            if o["dma"] and e == "pool":
                sw = nc.alloc_semaphore(name=f"s_swdma_{i}")
                o["dsem"] = sw
                o["dval"] = 16
                o["inc"] = (sw, 16)
            elif o["dma"]:
                j = dnext
                dnext = (dnext + 1) % self.n_dma_sems
                if dcount[j] > 0:
                    plan_wait(o, e, dsems[j], dcount[j])
                dcount[j] += 16
                o["dsem"] = dsems[j]
                o["dval"] = dcount[j]
                o["inc"] = (dsems[j], 16)
            elif o["has_dep"]:
                s = o["seq"]
                o["inc"] = (sems[e][s // CH], 1)
            else:
                o["inc"] = None
        final_waits = []
        fo = dict(waits=final_waits)
        for i in tail:
            plan_wait(fo, final_wait_eng, ops[i]["dsem"], ops[i]["dval"])

        def run_engine(ename, eng):
            for o in ops:
                if o["eng"] != ename:
                    continue
                for (sem, val) in o["waits"]:
                    eng.wait_ge(sem, val)
                ins = o["fn"](eng)
                if o["inc"] is not None:
                    ins.then_inc(o["inc"][0], o["inc"][1])
            if ename == final_wait_eng:
                for (sem, val) in final_waits:
                    eng.wait_ge(sem, val)

        with nc.Block() as block:
            @block.sync
            def _(eng):
                run_engine("sp", eng)

            @block.tensor
            def _(eng):
                run_engine("pe", eng)

            @block.scalar
            def _(eng):
                run_engine("act", eng)

            @block.vector
            def _(eng):
                run_engine("dve", eng)

            @block.gpsimd
            def _(eng):
                run_engine("pool", eng)
        return dict(n_ops=len(ops), cnt=cnt)


class Arena:
    def __init__(self, ap_u8, size):
        self.ap = ap_u8
        self.size = size
        self.off = 0
        self.peak = 0

    def alloc(self, shape, dtype):
        esz = {F32: 4, BF16: 2}[dtype]
        n = int(np.prod(shape))
        nbytes = (n * esz + 63) // 64 * 64
        assert self.off + nbytes <= self.size, f"arena overflow {self.off}+{nbytes}>{self.size}"
        v = self.ap[:, self.off:self.off + n * esz].bitcast(dtype)
        self.off += nbytes
        self.peak = max(self.peak, self.off)
        if len(shape) == 1:
            return v
        names = " ".join(f"d{i}" for i in range(len(shape)))
        kw = {f"d{i}": int(s) for i, s in enumerate(shape)}
        return v.rearrange(f"p ({names}) -> p {names}", **kw)

    def mark(self):
        return self.off

    def release(self, m):
        self.off = m


def na_structure(rows):
    T = rows // 2
    types = {}
    per_t = []
    for t in range(T):
        lst = []
        for u in range(T):
            vis = []
            anyv = False
            for kr in range(2):
                for qr in range(2):
                    r = 2 * t + qr
                    r0 = min(max(r - 4, 0), rows - 8)
                    v = r0 <= 2 * u + kr < r0 + 8
                    vis.append(v)
                    anyv = anyv or v
            if not anyv:
                continue
            key = (u - t, tuple(vis))
            if key not in types:
                types[key] = len(types)
            lst.append((u, types[key]))
        per_t.append(lst)
    return per_t, types


def na_consts(types):
    nt = len(types)
    mask = np.zeros((nt, 128, 128), np.float32)
    idr = np.zeros((nt, 128, 128), np.int64)
    idc = np.zeros((nt, 128, 128), np.int64)
    kc = np.arange(64)[:, None]
    qc = np.arange(64)[None, :]
    c0 = np.clip(qc - 8, 0, 48)
    colok = (kc >= c0) & (kc < c0 + 16)
    dc = np.clip(kc - qc + 15, 0, 30)
    for (delta, vis), ti in types.items():
        for kr in range(2):
            for qr in range(2):
                v = vis[kr * 2 + qr]
                dr = int(np.clip(2 * delta + kr - qr + 7, 0, 14))
                blk = np.where(colok & v, 0.0, -30000.0).astype(np.float32)
                mask[ti, kr * 64:(kr + 1) * 64, qr * 64:(qr + 1) * 64] = blk
                idr[ti, kr * 64:(kr + 1) * 64, qr * 64:(qr + 1) * 64] = dr
                idc[ti, kr * 64:(kr + 1) * 64, qr * 64:(qr + 1) * 64] = dc
    return mask, idr, idc


def rope_tables(n):
    pos = np.arange(n)
    row = (pos // GRID_W).astype(np.float32)
    col = (pos % GRID_W).astype(np.float32)
    inv = (10000.0 ** (-np.arange(0, 32, 2, dtype=np.float32) / 32)).astype(np.float32)
    ang = np.concatenate([row[:, None] * inv, col[:, None] * inv], axis=-1).astype(np.float32)
    return np.cos(ang).astype(np.float32), np.sin(ang).astype(np.float32)


class _Stop(Exception):
    pass


def build(SEQ, debug=(), upto=None):
    NT = SEQ // 128
    ROWS = SEQ // 64
    NB = SEQ // 512
    per_t, types = na_structure(ROWS)
    NTYPE = len(types)
    nc = bass.Bass("TRN2", target_bir_lowering=False)

    def din(name, shape, dt=F32):
        return nc.dram_tensor(name, list(shape), dt, kind="ExternalInput").ap()

    x = din("x", [SEQ, D])
    ctx = din("ctx", [CTX, D])
    c_fm = din("c_fm", [128, KC, 2])
    w_ada = din("w_ada", [D, 6 * D])
    bada_fm = din("bada_fm", [128, 48])
    b_ada = din("b_ada", [1, 6 * D])
    gpre_fm = din("gpre_fm", [128, 2, KC])
    gpost = din("gpost", [2, D])
    w_in = din("w_in", [D, 5632])
    w_ro = din("w_ro", [512, D])
    w_no = din("w_no", [512, D])
    w_o = din("w_o", [D, D])
    w_ff1 = din("w_ff1", [D, 4 * D])
    w_ff2 = din("w_ff2", [4 * D, D])
    lgt_pair = din("lgt_pair", [128, 8])
    lgt_bc = din("lgt_bc", [128, 16])
    ident_d = din("ident", [128, 128])
    cmat = din("cmat", [128, 4, 128])
    colc_d = din("colc", [128, 6])
    rowc_d = din("rowc", [128, 2, 128])
    cos_d = din("cos_tm", [128, NT, 32])
    sin_d = din("sin_tm", [128, NT, 32])
    rpbB = din("rpbB", [8, 128, NTYPE, 128])
    maskB_d = din("maskB", [128, NTYPE, 128])
    out = nc.dram_tensor("out", [SEQ, D], F32, kind="ExternalOutput").ap()
    yT_d = nc.dram_tensor("yT_scratch", [8, 128, SEQ], BF16, kind="Internal").ap()
    wgb = nc.dram_tensor("wg_bf", [D, 2048], BF16, kind="Internal").ap()
    wrob = nc.dram_tensor("wro_bf", [512, D], BF16, kind="Internal").ap()
    wnob = nc.dram_tensor("wno_bf", [512, D], BF16, kind="Internal").ap()
    wob = nc.dram_tensor("wo_bf", [D, D], BF16, kind="Internal").ap()
    w1b = nc.dram_tensor("w1_bf", [D, 4 * D], BF16, kind="Internal").ap()
    w2b = nc.dram_tensor("w2_bf", [4 * D, D], BF16, kind="Internal").ap()
    wadab = nc.dram_tensor("wada_bf", [2, D, D], BF16, kind="Internal").ap()
    dbg = {}
    for name, shape, dt in debug:
        dbg[name] = nc.dram_tensor(name, list(shape), dt, kind="ExternalOutput").ap()

    P = Prog(nc)
    ARENA_BYTES = 207 * 1024
    cm = nc.sbuf_tensor("arena", [128, ARENA_BYTES], U8)
    arena_h = cm.__enter__()
    A = Arena(arena_h, ARENA_BYTES)
    cmp_ = nc.psum_tensor("ps", [128, 8, 512], F32)
    ps = cmp_.__enter__()

    def PB(b):
        return ps[:, b, :]

    def PK(*bs):
        return [("ps", b) for b in bs]

    def body():
        ident = A.alloc([128], F32)
        identb = A.alloc([128], BF16)
        scr = {e: A.alloc([16], F32) for e in ("act", "dve", "pool")}
        S1 = A.alloc([KC, 2], F32)
        SH1 = A.alloc([KC, 2], F32)
        S2 = A.alloc([KC, 2], F32)
        SH2 = A.alloc([KC, 2], F32)
        scb = A.alloc([KC, 2], BF16)
        sc_rep = A.alloc([KC, 128], BF16)
        gpre = A.alloc([2, KC], F32)
        badafm = A.alloc([48], F32)

        P.dma("sp", ident, ident_d, writes=["ident"])
        P.op("dve", lambda e: e.tensor_copy(out=identb, in_=ident), reads=["ident"], writes=["identb"])
        for e_ in ("act", "dve", "pool"):
            pass
        P.op("dve", lambda e: e.memset(scr["dve"], 0.0), writes=["scr_dve"])
        P.op("pool", lambda e: e.memset(scr["pool"], 0.0), writes=["scr_pool"])
        P.op("dve", lambda e: e.memset(scr["act"], 0.0), writes=["scr_act"])
        P.dma("sp", gpre, gpre_fm, writes=["gpre"])
        P.dma("sp", badafm, bada_fm, writes=["badafm"])

        m_phase2 = None

        def norm_p1(xt_ap, xt_key, tmp, inplace=False):
            junk, ss, rstd, xn = tmp["junk"], tmp["ss"], tmp["rstd"], tmp["xn"]
            tk = tmp["key"]
            jkey = tmp.get("junk_key", (tk, "junk"))
            P.op("act", lambda e: e.activation(out=junk, in_=xt_ap, func=AF.Square, accum_out=ss),
                 reads=[xt_key], writes=[jkey, (tk, "ss")])
            P.op("act", lambda e: e.activation(out=rstd, in_=ss, func=AF.Sqrt, scale=1.0 / D, bias=EPS),
                 reads=[(tk, "ss")], writes=[(tk, "rstd")])
            P.op("dve", lambda e: e.reciprocal(out=rstd, in_=rstd), writes=[(tk, "rstd")])
            xnk = xt_key if inplace else (tk, "xn")
            P.op("dve", lambda e: e.tensor_scalar(out=xn, in0=xt_ap, scalar1=rstd[:, 0:1], scalar2=None, op0=ALU.mult),
                 reads=[(tk, "rstd")] + ([] if inplace else [xt_key]), writes=[xnk])
            return xnk

        def norm_p2(xnk, dst_fn, dst_keys, Sc, Sh, col, banks, tmp):
            xn = tmp["xn"]
            for half in range(2):
                b = banks[half]

                def tr(e, half=half, b=b):
                    for cc in range(4):
                        c = half * 4 + cc
                        ins = e.transpose(out=PB(b)[:, cc * 128:(cc + 1) * 128], in_=xn[:, c * 128:(c + 1) * 128], identity=ident)
                    return ins
                P.op("pe", tr, reads=[xnk, "ident"], writes=PK(b))
                for cc in range(4):
                    c = half * 4 + cc
                    if half == 0:
                        P.op("act", lambda e, c=c, cc=cc, b=b: e.activation(
                            out=dst_fn(c), in_=PB(b)[:, cc * 128:(cc + 1) * 128], func=AF.Identity,
                            scale=Sc[:, c, col:col + 1], bias=Sh[:, c, col:col + 1]),
                            reads=["mod"] + PK(b), writes=[dst_keys[c]])
                    else:
                        P.op("dve", lambda e, c=c, cc=cc, b=b: e.tensor_scalar(
                            out=dst_fn(c), in0=PB(b)[:, cc * 128:(cc + 1) * 128],
                            scalar1=Sc[:, c, col:col + 1], scalar2=Sh[:, c, col:col + 1], op0=ALU.mult, op1=ALU.add),
                            reads=["mod"] + PK(b), writes=[dst_keys[c]])

        def norm_transpose(xt_ap, xt_key, dst_fn, dst_keys, Sc, Sh, col, banks, tmp, tag, inplace=False):
            xnk = norm_p1(xt_ap, xt_key, tmp, inplace)
            norm_p2(xnk, dst_fn, dst_keys, Sc, Sh, col, banks, tmp)

        m0 = A.mark()
        cm_t = A.alloc([4, 128], F32)
        colc = A.alloc([6], F32)
        rowc = A.alloc([2, 128], F32)
        lgp = A.alloc([8], F32)
        lgb = A.alloc([16], F32)
        cfm = A.alloc([KC, 2], F32)
        A.release(m0)
        DT = A.alloc([8, 128], F32)
        kw = A.alloc([8, 2], F32)
        ckw = A.alloc([2, 8, 2], F32)
        QW = A.alloc([2, 128], F32)
        GL = A.alloc([8], F32)
        rowc = A.alloc([2, 128], F32)
        lgp = A.alloc([8], F32)
        hT = A.alloc([KC, SEQ], BF16)
        hcT = A.alloc([KC, CTX], BF16)
        m_after_persist = A.mark()
        cm_t = A.alloc([4, 128], F32)
        colc = A.alloc([6], F32)
        lgb = A.alloc([16], F32)
        cfm = A.alloc([KC, 2], F32)
        tmpA = A.alloc([128], F32)
        tmpB = A.alloc([128], F32)
        arg16 = A.alloc([16], F32)
        argc = A.alloc([2, 8, 2], F32)
        wbuf0 = A.alloc([KC, 1024], BF16)
        modfm = A.alloc([4, KC, 2], F32)

        P.dma("sp", cm_t, cmat, writes=["cmat"])
        P.dma("sp", colc, colc_d, writes=["colc"])
        P.dma("sp", rowc, rowc_d, writes=["rowc"])
        P.dma("sp", lgp, lgt_pair, writes=["lgp"])
        P.dma("sp", lgb, lgt_bc, writes=["lgb"])
        P.dma("sp", cfm, c_fm, writes=["cfm"])

        for t_, k_ in ((lgp, "lgp"), (lgb, "lgb")):
            P.op("act", lambda e, t_=t_: e.activation(out=t_, in_=t_, func=AF.Exp, scale=-1.0), writes=[k_])
            P.op("act", lambda e, t_=t_: e.activation(out=t_, in_=t_, func=AF.Ln, bias=1.0), writes=[k_])
            P.op("dve", lambda e, t_=t_: e.tensor_scalar(out=t_, in0=t_, scalar1=-1.0, scalar2=None, op0=ALU.mult), writes=[k_])
        for h in range(8):
            P.op("act", lambda e, h=h: e.activation(out=tmpA, in_=cm_t[:, 0, :], func=AF.Exp, scale=lgb[:, 2 * h:2 * h + 1]),
                 reads=["cmat", "lgb"], writes=["tmpA"])
            P.op("act", lambda e, h=h: e.activation(out=tmpB, in_=cm_t[:, 1, :], func=AF.Exp, scale=lgb[:, 2 * h + 1:2 * h + 2]),
                 reads=["cmat", "lgb"], writes=["tmpB"])
            P.op("dve", lambda e: e.tensor_tensor(out=tmpA, in0=tmpA, in1=cm_t[:, 2, :], op=ALU.mult), reads=["cmat"], writes=["tmpA"])
            P.op("dve", lambda e: e.tensor_tensor(out=tmpB, in0=tmpB, in1=cm_t[:, 3, :], op=ALU.mult), reads=["cmat"], writes=["tmpB"])
            P.op("dve", lambda e, h=h: e.tensor_tensor(out=DT[:, h, :], in0=tmpA, in1=tmpB, op=ALU.add),
                 reads=["tmpA", "tmpB"], writes=["DT"])
        lgb3 = lgb.rearrange("p (h d) -> p h d", d=2)
        arg3 = arg16.rearrange("p (h d) -> p h d", d=2)
        for d_ in range(2):
            P.op("dve", lambda e, d_=d_: e.tensor_scalar(out=arg3[:, :, d_], in0=lgb3[:, :, d_], scalar1=colc[:, d_:d_ + 1], scalar2=None, op0=ALU.mult),
                 reads=["lgb", "colc"], writes=["arg16"])
        P.op("act", lambda e: e.activation(out=arg16, in_=arg16, func=AF.Exp), writes=["arg16"])
        P.op("dve", lambda e: e.tensor_scalar(out=kw.rearrange("p h d -> p (h d)"), in0=arg16, scalar1=0.125, scalar2=None, op0=ALU.mult),
             reads=["arg16"], writes=["kw"])
        for ct in range(2):
            for d_ in range(2):
                cc_ = 2 + ct if d_ == 0 else 4 + ct
                P.op("dve", lambda e, ct=ct, d_=d_, cc_=cc_: e.tensor_scalar(out=argc[:, ct, :, d_], in0=lgb3[:, :, d_], scalar1=colc[:, cc_:cc_ + 1], scalar2=None, op0=ALU.mult),
                     reads=["lgb", "colc"], writes=["argc"])
        P.op("act", lambda e: e.activation(out=argc, in_=argc, func=AF.Exp), writes=["argc"])
        P.op("dve", lambda e: e.tensor_scalar(out=ckw, in0=argc, scalar1=0.125, scalar2=None, op0=ALU.mult), reads=["argc"], writes=["ckw"])
        P.op("act", lambda e: e.activation(out=GL, in_=lgp, func=AF.Exp, scale=128.0), reads=["lgp"], writes=["GL"])

        P.op("act", lambda e: e.activation(out=scb, in_=cfm, func=AF.Silu), reads=["cfm"], writes=["scb"])
        P.op("dve", lambda e: e.tensor_copy(out=sc_rep, in_=scb[:, :, 0:1].to_broadcast([128, KC, 128])), reads=["scb"], writes=["sc_rep"])

        def load_w(dst, src_rows_cols, key, nk=KC):
            P.dma("pool", dst, src_rows_cols.rearrange("(k p) n -> p k n", p=128), writes=[key])

        wbuf1 = A.alloc([KC, 1024], BF16)
        wbufs = {0: (wbuf1, "wbuf1"), 1: (wbuf0, "wbuf0"), 3: (wbuf1, "wbuf1"), 4: (wbuf0, "wbuf0")}

        def ada_load(j):
            wb_, key_ = wbufs[j]
            load_w(wb_, w_ada[:, j * D:(j + 1) * D], key_)

        def ada_mm(mi, j):
            wb_, key_ = wbufs[j]

            def mm(e):
                for cc in range(8):
                    for k in range(KC):
                        ins = e.matmul(PB(0)[:, cc * 2:cc * 2 + 2], lhsT=wb_[:, k, cc * 128:(cc + 1) * 128], rhs=scb[:, k, :],
                                       start=(k == 0), stop=(k == KC - 1))
                return ins
            P.op("pe", mm, reads=[key_, "scb"], writes=PK(0))
            P.op("dve", lambda e: e.tensor_tensor(
                out=modfm[:, mi, :, :], in0=PB(0)[:, 0:16].rearrange("p (c t) -> p c t", t=2),
                in1=badafm[:, j * 8:(j + 1) * 8].unsqueeze(2).to_broadcast([128, KC, 2]), op=ALU.add),
                reads=["badafm"], writes=PK(0) + [("modfm", mi)])

        def ada_fin(Sx, SHx, mi_sh, mi_sc, gi):
            P.op("dve", lambda e: e.scalar_tensor_tensor(
                out=Sx, in0=modfm[:, mi_sc, :, :], scalar=1.0, in1=gpre[:, gi, :].unsqueeze(2).to_broadcast([128, KC, 2]),
                op0=ALU.add, op1=ALU.mult), reads=[("modfm", mi_sc), "gpre"], writes=["mod"])
            P.op("dve", lambda e: e.tensor_copy(out=SHx, in_=modfm[:, mi_sh, :, :]), reads=[("modfm", mi_sh)], writes=["mod"])

        ada_load(1)
        ada_load(0)
        ada_mm(1, 1)
        ada_mm(0, 0)
        ada_fin(S1, SH1, 0, 1, 0)
        ada_load(4)
        ada_load(3)

        if upto == "p0":
            raise _Stop()
        NBUF1 = 3
        xts = [A.alloc([D], F32) for _ in range(NBUF1)]
        tmps = []
        for i in range(NBUF1):
            tmps.append(dict(junk=A.alloc([D], BF16), ss=A.alloc([1], F32), rstd=A.alloc([1], F32), xn=A.alloc([D], F32), key=("nt", i)))

        def p1_load(i):
            bi = i % NBUF1
            src = x[i * 128:(i + 1) * 128, :] if i < NT else ctx[(i - NT) * 128:(i - NT + 1) * 128, :]
            P.dma("sp", xts[bi], src, writes=[("xt", bi)])

        def p1_front(i):
            bi = i % NBUF1
            return norm_p1(xts[bi], ("xt", bi), tmps[bi])

        def p1_back(i, xnk):
            bi = i % NBUF1
            if i < NT:
                dst_fn = (lambda c: hT[:, c, i * 128:(i + 1) * 128])
                dkeys = [("hT", c, i) for c in range(KC)]
                col = 0
            else:
                dst_fn = (lambda c: hcT[:, c, (i - NT) * 128:(i - NT + 1) * 128])
                dkeys = [("hcT", c, i - NT) for c in range(KC)]
                col = 1
            pbk = (2 * (i % 2), 2 * (i % 2) + 1)
            norm_p2(xnk, dst_fn, dkeys, S1, SH1, col, pbk, tmps[bi])
        NTT = NT + 2
        p1_load(0)
        p1_load(1)
        xk_prev = p1_front(0)
        for i in range(NTT):
            if i + 2 < NTT:
                p1_load(i + 2)
            xk_next = p1_front(i + 1) if i + 1 < NTT else None
            p1_back(i, xk_prev)
            xk_prev = xk_next
        ada_mm(3, 4)
        ada_mm(2, 3)
        ada_fin(S2, SH2, 2, 3, 1)
        if "hT" in dbg:
            P.dma("sp", dbg["hT"], hT, reads=[("hT", c, i) for c in range(KC) for i in range(NT)])
        P.barrier(scr)
        A.release(m_after_persist)
        if upto == "p1":
            raise _Stop()

        maskB = A.alloc([NTYPE, 128], BF16)
        P.dma("pool", maskB, maskB_d, writes=["maskB"])
        wb = A.alloc([KC, 7, 128], BF16)
        BT = A.alloc([2, NTYPE, 128], BF16)
        rpst = A.alloc([NTYPE, 128], BF16)
        slabQ = A.alloc([SEQ], BF16)
        slabK = A.alloc([SEQ], BF16)
        rv = A.alloc([NT, 128], BF16)
        nva = A.alloc([NT, 2, 65], BF16)
        srg = A.alloc([NT, 128], BF16)
        DS = A.alloc([NT, 2, 64], F32)
        Rb = A.alloc([NT, 2, 64], BF16)
        BLK = 4
        rtmp = [A.alloc([BLK // 2, 4, 32], F32) for _ in range(4)]
        cs_t = [A.alloc([2, BLK, 32], F32) for _ in range(2)]
        qk_tm = [A.alloc([BLK, 256], BF16) for _ in range(2)]
        Vfb = [A.alloc([BLK, 2, 2, 64], BF16) for _ in range(2)]
        crk = A.alloc([2, 128], BF16)
        cVfb = A.alloc([2, 2, 2, 64], BF16)
        cnva = A.alloc([2, 2, 65], BF16)
        cnkT = A.alloc([CTX], BF16)
        PT = [A.alloc([7, 128], BF16) for _ in range(2)]
        SDT = [A.alloc([2, 4, 128], BF16) for _ in range(2)]
        QfbT = [A.alloc([2, 4, 128], BF16) for _ in range(2)]
        sqA = [A.alloc([512], F32) for _ in range(2)]
        msA = [A.alloc([8], F32) for _ in range(2)]
        onA = [A.alloc([512], F32) for _ in range(2)]
        ytile = [A.alloc([4, 128], BF16) for _ in range(2)]
        ystage = [A.alloc([512], BF16) for _ in range(2)]
        rc = A.alloc([2], F32)

        qm = [[A.alloc([128], BF16) for _ in range(2)] for _ in range(2)]
        for hh_ in range(2):
            for par_ in range(2):
                P.op("pool", lambda e, hh_=hh_, par_=par_: e.memset(qm[hh_][par_], 0.0), writes=[("qm", hh_, par_)])
        P.op("pool", lambda e: e.memset(nva, 1.0), writes=["nva_init"])
        P.op("pool", lambda e: e.memset(cnva, 1.0), writes=["cnva_init"])

        def pair_body(hp):
            hk = ("hp", hp)
            for d_ in range(2):
                P.op("act", lambda e, d_=d_: e.activation(out=QW[:, d_, :], in_=rowc[:, d_, :], func=AF.Exp,
                                                           scale=lgp[:, hp * 2 + d_:hp * 2 + d_ + 1]),
                     writes=["QW"])
            def load_wb(hp_):
                for s in range(7):
                    c0 = s * 512 + hp_ * 128
                    P.dma("pool", wb[:, :, s, :], w_in[:, c0:c0 + 128].rearrange("(k p) n -> p k n", p=128), writes=[("wb", s)])
            if hp == 0:
                load_wb(0)
            wbk = [("wb", s) for s in range(7)]
            for hh in range(2):
                P.dma("pool", rpst, rpbB[2 * hp + hh], writes=["rpst"])
                P.op("dve", lambda e, hh=hh: e.tensor_tensor(out=BT[:, hh, :, :], in0=rpst, in1=maskB, op=ALU.add),
                     reads=["rpst", "maskB"], writes=[("BT", hh)])
            for ct in range(2):
                def mm(e, ct=ct):
                    for k in range(KC):
                        ins = e.matmul(PB(6)[:, 0:256], lhsT=hcT[:, k, ct * 128:(ct + 1) * 128], rhs=wb[:, k, 1:3, :].rearrange("p s n -> p (s n)"),
                                       start=(k == 0), stop=(k == KC - 1))
                    for k in range(KC):
                        ins = e.matmul(PB(6)[:, 256:384], lhsT=hcT[:, k, ct * 128:(ct + 1) * 128], rhs=wb[:, k, 6, :],
                                       start=(k == 0), stop=(k == KC - 1))
                    return ins
                P.op("pe", mm, reads=wbk + ["hcT"], writes=PK(6))
                P.op("act", lambda e, ct=ct: e.copy(out=crk[:, ct, :], in_=PB(6)[:, 0:128]), writes=PK(6) + [("crk", ct)])
                for hh in range(2):
                    for d_ in range(2):
                        P.op("dve", lambda e, ct=ct, hh=hh, d_=d_: e.tensor_scalar(
                            out=cVfb[:, ct, hh, d_, :], in0=PB(6)[:, 128 + hh * 64:128 + (hh + 1) * 64],
                            scalar1=ckw[:, ct, 2 * hp + hh, d_:d_ + 1], scalar2=None, op0=ALU.mult),
                            reads=["ckw"], writes=PK(6) + [("cVfb", ct)])
                P.op("dve", lambda e, ct=ct: e.tensor_copy(out=cnva[:, ct, :, 0:64], in_=PB(6)[:, 256:384].rearrange("p (h d) -> p h d", d=64)),
                     reads=["cnva_init"], writes=PK(6) + [("cnva", ct)])

            def mm(e):
                for k in range(KC):
                    ins = e.matmul(PB(7)[:, 0:CTX], lhsT=wb[:, k, 5, :], rhs=hcT[:, k, :], start=(k == 0), stop=(k == KC - 1))
                return ins
            P.op("pe", mm, reads=wbk + ["hcT"], writes=PK(7))
            P.op("act", lambda e: e.copy(out=cnkT, in_=PB(7)[:, 0:CTX]), writes=PK(7) + ["cnkT"])

            def mm(e):
                for hh in range(2):
                    for ct in range(2):
                        ins = e.matmul(PB(6)[hh * 64:(hh + 1) * 64, 0:128], lhsT=crk[:, ct, hh * 64:(hh + 1) * 64],
                                       rhs=cVfb[:, ct, hh, :, :].rearrange("p a b -> p (a b)"), start=(ct == 0), stop=(ct == 1))
                return ins
            P.op("pe", mm, reads=[("crk", 0), ("crk", 1), ("cVfb", 0), ("cVfb", 1)], writes=PK(6))
            P.op("dve", lambda e: e.tensor_copy(out=DS[:, 0, 0, :], in_=PB(6)[:, 0:64]), writes=PK(6) + [("DS", 0, 0)])
            P.op("dve", lambda e: e.tensor_copy(out=DS[:, NT - 1, 1, :], in_=PB(6)[:, 64:128]), writes=PK(6) + [("DS", NT - 1, 1)])

            if upto == "p2ctx":
                raise _Stop()
            def frontA(b0):
                bi = (b0 // BLK) % 2
                qk_ = qk_tm[bi]
                cst = cs_t[bi]
                P.dma("sp", cst[:, 0, :, :], cos_d[:, b0:b0 + BLK, :], writes=[("cs", bi, 0)])
                P.dma("sp", cst[:, 1, :, :], sin_d[:, b0:b0 + BLK, :], writes=[("cs", bi, 1)])
                q5 = qk_.rearrange("p b (g t f) -> p b g t f", g=4, t=2)
                for half in range(2):
                    hb = [2 + 2 * half, 3 + 2 * half]
                    for ii2 in range(2):
                        ii = half * 2 + ii2
                        i = b0 + ii
                        pb = hb[ii2]

                        def mm(e, i=i, pb=pb):
                            for k in range(KC):
                                ins = e.matmul(PB(pb)[:, 0:512], lhsT=hT[:, k, i * 128:(i + 1) * 128], rhs=wb[:, k, 0:4, :].rearrange("p s n -> p (s n)"),
                                               start=(k == 0), stop=(k == KC - 1))
                            return ins
                        P.op("pe", mm, reads=wbk, writes=PK(pb))
                        P.op("dve", lambda e, i=i, pb=pb: e.tensor_copy(out=rv[:, i, :], in_=PB(pb)[:, 256:384]),
                             writes=PK(pb) + [("rv", i)])
                        P.op("act", lambda e, i=i, pb=pb: e.activation(out=srg[:, i, :], in_=PB(pb)[:, 384:512], func=AF.Silu),
                             writes=PK(pb) + [("srg", i)])
                    s5 = ps[:, hb[0]:hb[0] + 2, 0:256].rearrange("p b (g t f) -> p b g t f", g=4, t=2)
                    cosb = cst[:, 0, 2 * half:2 * half + 2, :].unsqueeze(2).to_broadcast([128, 2, 4, 32])
                    sinb = cst[:, 1, 2 * half:2 * half + 2, :].unsqueeze(2).to_broadcast([128, 2, 4, 32])
                    PKH = PK(*hb)
                    qh = q5[:, 2 * half:2 * half + 2]
                    P.op("dve", lambda e, s5=s5, cosb=cosb: e.tensor_tensor(out=rtmp[0], in0=s5[:, :, :, 0, :], in1=cosb, op=ALU.mult),
                         reads=[("cs", bi, 0)], writes=PKH + [("rtmp", 0)])
                    P.op("dve", lambda e, s5=s5, sinb=sinb: e.tensor_tensor(out=rtmp[1], in0=s5[:, :, :, 1, :], in1=sinb, op=ALU.mult),
                         reads=[("cs", bi, 1)], writes=PKH + [("rtmp", 1)])
                    P.op("dve", lambda e, s5=s5, sinb=sinb: e.tensor_tensor(out=rtmp[2], in0=s5[:, :, :, 0, :], in1=sinb, op=ALU.mult),
                         reads=[("cs", bi, 1)], writes=PKH + [("rtmp", 2)])
                    P.op("dve", lambda e, s5=s5, cosb=cosb: e.tensor_tensor(out=rtmp[3], in0=s5[:, :, :, 1, :], in1=cosb, op=ALU.mult),
                         reads=[("cs", bi, 0)], writes=PKH + [("rtmp", 3)])
                    P.op("pool", lambda e, qh=qh: e.tensor_tensor(out=qh[:, :, :, 0, :], in0=rtmp[0], in1=rtmp[1], op=ALU.subtract),
                         reads=[("rtmp", 0), ("rtmp", 1)], writes=[("qk", bi, half, 0)])
                    P.op("pool", lambda e, qh=qh: e.tensor_tensor(out=qh[:, :, :, 1, :], in0=rtmp[2], in1=rtmp[3], op=ALU.add),
                         reads=[("rtmp", 2), ("rtmp", 3)], writes=[("qk", bi, half, 1)])

            def blockA(b0):
                bi = (b0 // BLK) % 2
                qk_, vf_ = qk_tm[bi], Vfb[bi]
                qkk = [("qk", bi, half, w_) for half in range(2) for w_ in range(2)]
                for hh in range(2):
                    for d_ in range(2):
                        P.op("act", lambda e, hh=hh, d_=d_, vf_=vf_: e.activation(
                            out=vf_[:, :, hh, d_, :], in_=rv[:, b0:b0 + BLK, hh * 64:(hh + 1) * 64], func=AF.Copy,
                            scale=kw[:, 2 * hp + hh, d_:d_ + 1]),
                            reads=[("rv", b0 + ii) for ii in range(BLK)] + ["kw"], writes=[("Vfb", bi, hh, d_)])
                vfk = [("Vfb", bi, hh, d_) for hh in range(2) for d_ in range(2)]
                for which, slab, sk in ((0, slabQ, "slabQ"), (1, slabK, "slabK")):
                    pbt = 6 + which
                    pbv = PB(pbt).bitcast(BF16)

                    def tr(e, which=which, pbv=pbv, qk_=qk_):
                        for ii in range(BLK):
                            ins = e.transpose(out=pbv[:, ii * 128:(ii + 1) * 128], in_=qk_[:, ii, which * 128:(which + 1) * 128], identity=identb)
                        return ins
                    P.op("pe", tr, reads=qkk + ["identb"], writes=PK(pbt))
                    if which == 0:
                        P.op("act", lambda e, pbv=pbv, slab=slab: e.copy(out=slab[:, b0 * 128:(b0 + BLK) * 128], in_=pbv[:, 0:BLK * 128]),
                             writes=PK(pbt) + [(sk, b0 // BLK)])
                    else:
                        P.op("dve", lambda e, pbv=pbv, slab=slab: e.tensor_copy(out=slab[:, b0 * 128:(b0 + BLK) * 128], in_=pbv[:, 0:BLK * 128]),
                             writes=PK(pbt) + [(sk, b0 // BLK)])
                pbd = (b0 // BLK) % 2

                def mm(e, pbd=pbd, qk_=qk_, vf_=vf_):
                    for ii in range(BLK):
                        for hh in range(2):
                            ins = e.matmul(PB(pbd)[hh * 64:(hh + 1) * 64, ii * 128:(ii + 1) * 128],
                                           lhsT=qk_[:, ii, 128 + hh * 64:128 + (hh + 1) * 64],
                                           rhs=vf_[:, ii, hh, :, :].rearrange("p a b -> p (a b)"), start=True, stop=True)
                    return ins
                P.op("pe", mm, reads=qkk + vfk, writes=PK(pbd))
                pv = PB(pbd).rearrange("p (b d f) -> p b d f", d=2, f=64)
                lo, hi = b0, min(b0 + BLK, NT - 1)
                if hi > lo:
                    P.op("act", lambda e, lo=lo, hi=hi, pv=pv: e.copy(out=DS[:, lo + 1:hi + 1, 0, :], in_=pv[:, lo - b0:hi - b0, 0, :]),
                         writes=PK(pbd) + [("DS", c + 1, 0) for c in range(lo, hi)])
                lo2, hi2 = max(b0, 1), b0 + BLK
                if hi2 > lo2:
                    P.op("dve", lambda e, lo2=lo2, hi2=hi2, pv=pv: e.tensor_copy(out=DS[:, lo2 - 1:hi2 - 1, 1, :], in_=pv[:, lo2 - b0:hi2 - b0, 1, :]),
                         writes=PK(pbd) + [("DS", c - 1, 1) for c in range(lo2, hi2)])
            frontA(0)
            for b0 in range(0, NT, BLK):
                if b0 + BLK < NT:
                    frontA(b0 + BLK)
                blockA(b0)
            if upto == "p2a":
                raise _Stop()
            def nvproj(nb):
                pbp = 6 + (nb % 2)

                def mm(e):
                    for ii in range(4):
                        i = nb * 4 + ii
                        for k in range(KC):
                            ins = e.matmul(PB(pbp)[:, ii * 128:(ii + 1) * 128], lhsT=hT[:, k, i * 128:(i + 1) * 128], rhs=wb[:, k, 6, :],
                                           start=(k == 0), stop=(k == KC - 1))
                    return ins
                P.op("pe", mm, reads=wbk, writes=PK(pbp))
                P.op("act", lambda e: e.copy(out=nva[:, nb * 4:(nb + 1) * 4, :, 0:64], in_=PB(pbp).rearrange("p (i h d) -> p i h d", h=2, d=64)),
                     reads=["nva_init"], writes=PK(pbp) + [("nva", nb)])
            for nb in range(NB):
                nvproj(nb)
            for c in range(NT - 1):
                P.op("dve", lambda e, c=c: e.scalar_tensor_tensor(out=DS[:, c + 1, 0, :], in0=DS[:, c, 0, :], scalar=GL[:, 2 * hp:2 * hp + 1],
                                                                  in1=DS[:, c + 1, 0, :], op0=ALU.mult, op1=ALU.add),
                     reads=[("DS", c, 0), "GL"], writes=[("DS", c + 1, 0)])
            for c in range(NT - 1, 0, -1):
                P.op("dve", lambda e, c=c: e.scalar_tensor_tensor(out=DS[:, c - 1, 1, :], in0=DS[:, c, 1, :], scalar=GL[:, 2 * hp + 1:2 * hp + 2],
                                                                  in1=DS[:, c - 1, 1, :], op0=ALU.mult, op1=ALU.add),
                     reads=[("DS", c, 1), "GL"], writes=[("DS", c - 1, 1)])
            P.op("dve", lambda e: e.tensor_copy(out=Rb, in_=DS), reads=[("DS", c, d_) for c in range(NT) for d_ in range(2)], writes=["Rb"])
            if f"Rb{hp}" in dbg:
                P.dma("sp", dbg[f"Rb{hp}"], Rb, reads=["Rb"])

            if upto == "p2scan":
                raise _Stop()
            def blockB(g0):
                gi = (g0 // 4) % 2
                pbo = [2 + gi, 4 + gi]
                yt = ytile[gi]
                qf = QfbT[gi]
                sd = SDT[gi]
                sq, ms, on = sqA[gi], msA[gi], onA[gi]
                P.op("pool", lambda e, qf=qf: e.tensor_tensor(
                    out=qf, in0=slabQ[:, g0 * 128:(g0 + 4) * 128].rearrange("p (c i) -> p c i", c=4).unsqueeze(1).to_broadcast([128, 2, 4, 128]),
                    in1=QW.unsqueeze(2).to_broadcast([128, 2, 4, 128]), op=ALU.mult),
                    reads=[("slabQ", g0 // BLK), "QW"], writes=[("QfbT", gi)])
                for hh in range(2):
                    def mm(e, hh=hh):
                        for cc in range(4):
                            c = g0 + cc
                            ins = e.matmul(PB(hh)[:, cc * 128:(cc + 1) * 128], lhsT=slabK[hh * 64:(hh + 1) * 64, c * 128:(c + 1) * 128],
                                           rhs=slabQ[hh * 64:(hh + 1) * 64, c * 128:(c + 1) * 128], start=True, stop=True)
                        return ins
                    P.op("pe", mm, reads=[("slabQ", g0 // BLK), ("slabK", g0 // BLK)], writes=PK(hh))
                    P.op("dve", lambda e, hh=hh, sd=sd: e.tensor_tensor(
                        out=sd[:, hh, :, :], in0=PB(hh).rearrange("p (c i) -> p c i", c=4),
                        in1=DT[:, 2 * hp + hh, :].unsqueeze(1).to_broadcast([128, 4, 128]), op=ALU.mult),
                        reads=["DT"], writes=PK(hh) + [("SDT", gi, hh)])
                for hh in range(2):
                    def mm(e, hh=hh, sd=sd, qf=qf):
                        for cc in range(4):
                            c = g0 + cc
                            o_ = PB(pbo[hh])[:, cc * 64:(cc + 1) * 64]
                            e.matmul(o_, lhsT=sd[:, hh, cc, :], rhs=rv[:, c, hh * 64:(hh + 1) * 64], start=True, stop=False)
                            e.matmul(o_, lhsT=qf[hh * 64:(hh + 1) * 64, 0, cc, :], rhs=Rb[hh * 64:(hh + 1) * 64, c, 0, :], start=False, stop=False)
                            ins = e.matmul(o_, lhsT=qf[hh * 64:(hh + 1) * 64, 1, cc, :], rhs=Rb[hh * 64:(hh + 1) * 64, c, 1, :], start=False, stop=True)
                        return ins
                    P.op("pe", mm, reads=[("SDT", gi, hh), ("QfbT", gi), "Rb"] + [("rv", g0 + cc) for cc in range(4)], writes=PK(pbo[hh]))
            def backB(g0):
                gi = (g0 // 4) % 2
                pbo = [2 + gi, 4 + gi]
                yt = ytile[gi]
                sq, ms, on = sqA[gi], msA[gi], onA[gi]
                for hh in range(2):
                    P.op("act", lambda e, hh=hh: e.activation(out=sq[:, hh * 256:(hh + 1) * 256], in_=PB(pbo[hh])[:, 0:256], func=AF.Square),
                         writes=PK(pbo[hh]) + [("sq", gi, hh)])
                P.op("dve", lambda e: e.tensor_reduce(out=ms, in_=sq.rearrange("p (g f) -> p g f", f=64), axis=AX.X, op=ALU.add),
                     reads=[("sq", gi, 0), ("sq", gi, 1)], writes=[("ms", gi)])
                P.op("act", lambda e: e.activation(out=ms, in_=ms, func=AF.Sqrt, scale=1.0 / 64, bias=EPS), writes=[("ms", gi)])
                P.op("dve", lambda e: e.reciprocal(out=ms, in_=ms), writes=[("ms", gi)])
                for hh in range(2):
                    P.op("dve", lambda e, hh=hh: e.tensor_tensor(
                        out=on[:, hh * 256:(hh + 1) * 256].rearrange("p (g f) -> p g f", f=64),
                        in0=PB(pbo[hh])[:, 0:256].rearrange("p (g f) -> p g f", f=64),
                        in1=ms[:, hh * 4:(hh + 1) * 4].unsqueeze(2).to_broadcast([128, 4, 64]), op=ALU.mult),
                        reads=[("ms", gi)], writes=PK(pbo[hh]) + [("on", gi, hh)])
                    P.op("pool", lambda e, hh=hh: e.tensor_tensor(
                        out=yt[:, :, hh * 64:(hh + 1) * 64], in0=on[:, hh * 256:(hh + 1) * 256].rearrange("p (c f) -> p c f", f=64),
                        in1=srg[:, g0:g0 + 4, hh * 64:(hh + 1) * 64], op=ALU.mult),
                        reads=[("on", gi, hh)] + [("srg", g0 + cc) for cc in range(4)],
                        writes=[("ytile", gi, "h", hh)] + ([("ytile", gi)] + [("ytile", gi, cc) for cc in range(4)] if hh == 1 else []))
                pbt = 6 + gi
                pbv = PB(pbt).bitcast(BF16)

                def tr(e, pbv=pbv, yt=yt):
                    for cc in range(4):
                        ins = e.transpose(out=pbv[:, cc * 128:(cc + 1) * 128], in_=yt[:, cc, :], identity=identb)
                    return ins
                P.op("pe", tr, reads=[("ytile", gi), "identb", ("ytile", gi, "h", 0), ("ytile", gi, "h", 1)] + [("ytile", gi, cc) for cc in range(4)], writes=PK(pbt))
                P.op("act", lambda e, pbv=pbv, gi=gi: e.copy(out=ystage[gi], in_=pbv[:, 0:512]), writes=PK(pbt) + [("ystage", gi)])
                P.dma("sp", yT_d[hp, :, g0 * 128:(g0 + 4) * 128], ystage[gi], reads=[("ystage", gi)], writes=[("yT", 0, hp, g0 // 4)])
            blockB(0)
            for g0 in range(0, NT, 4):
                if g0 + 4 < NT:
                    blockB(g0 + 4)
                backB(g0)

            if upto == "p2b":
                raise _Stop()
            def naproj(nb):
                for which, slab, sk, slot in ((0, slabQ, "slabQ", 4), (1, slabK, "slabK", 5)):
                    pbp = 4 + which

                    def mm(e, nb=nb, slot=slot, pbp=pbp):
                        for k in range(KC):
                            ins = e.matmul(PB(pbp)[:, 0:512], lhsT=wb[:, k, slot, :], rhs=hT[:, k, nb * 512:(nb + 1) * 512],
                                           start=(k == 0), stop=(k == KC - 1))
                        return ins
                    P.op("pe", mm, reads=wbk, writes=PK(pbp))
                    if which == 0:
                        P.op("act", lambda e, nb=nb, pbp=pbp: e.activation(out=slabQ[:, nb * 512:(nb + 1) * 512], in_=PB(pbp), func=AF.Copy, scale=0.125),
                             writes=PK(pbp) + [("slabQ", nb)])
                    else:
                        P.op("dve", lambda e, nb=nb, pbp=pbp: e.tensor_copy(out=slabK[:, nb * 512:(nb + 1) * 512], in_=PB(pbp)),
                             writes=PK(pbp) + [("slabK", nb)])
            for nb in range(NB):
                naproj(nb)
            if hp + 1 < 4:
                load_wb(hp + 1)
            if upto == "p2np":
                raise _Stop()
            def na_front(t, hh):
                lst = per_t[t]
                pi = hh
                pA, pB_ = 2 * pi, 2 * pi + 1
                pt_ = PT[pi]
                nloc = len(lst)
                assert nloc <= 5
                qmb = qm[hh][t % 2]
                P.op("pool", lambda e: e.tensor_copy(out=qmb[hh * 64:(hh + 1) * 64, :], in_=slabQ[hh * 64:(hh + 1) * 64, t * 128:(t + 1) * 128]),
                     reads=[("slabQ", t // 4)], writes=[("qm", hh, t % 2)])

                def mm(e):
                    for m, (u, ty) in enumerate(lst):
                        o_ = (PB(pA)[:, m * 128:(m + 1) * 128] if m < 4 else PB(pB_)[:, 0:128])
                        e.matmul(o_, lhsT=slabK[:, u * 128:(u + 1) * 128], rhs=qmb, start=True, stop=False)
                        ins = e.matmul(o_, lhsT=identb, rhs=BT[:, hh, ty, :], start=False, stop=True)
                    for ct in range(2):
                        ins = e.matmul(PB(pB_)[:, (1 + ct) * 128:(2 + ct) * 128], lhsT=cnkT[:, ct * 128:(ct + 1) * 128],
                                       rhs=qmb, start=True, stop=True)
                    return ins
                kblocks = sorted(set(u // 4 for (u, _) in lst))
                P.op("pe", mm, reads=[("qm", hh, t % 2), "cnkT", ("BT", hh), "identb"] + [("slabK", kb) for kb in kblocks],
                     writes=PK(pA, pB_))
                na4 = min(nloc, 4)
                P.op("act", lambda e: e.activation(out=pt_[:, 0:na4, :], in_=PB(pA)[:, 0:na4 * 128].rearrange("p (m q) -> p m q", q=128), func=AF.Exp),
                     writes=PK(pA) + [("PT", pi, 0)])
                lo_ = 0 if nloc == 5 else 1
                P.op("act", lambda e: e.activation(out=pt_[:, 4 + lo_:7, :], in_=PB(pB_)[:, lo_ * 128:3 * 128].rearrange("p (m q) -> p m q", q=128), func=AF.Exp),
                     writes=PK(pB_) + [("PT", pi, 1)])

            def na_back(t, hh):
                lst = per_t[t]
                pi = hh
                pt_ = PT[pi]
                gi = (t // 4) % 2
                yt = ytile[gi]
                pbo = 4 + (t % 2)
                kblocks = sorted(set(u // 4 for (u, _) in lst))

                def mm(e):
                    o_ = PB(pbo)[:, hh * 66:hh * 66 + 65]
                    for m, (u, ty) in enumerate(lst):
                        slot = m if m < 4 else 4
                        e.matmul(o_, lhsT=pt_[:, slot, :], rhs=nva[:, u, hh, :], start=(m == 0), stop=False)
                    for ct in range(2):
                        ins = e.matmul(o_, lhsT=pt_[:, 5 + ct, :], rhs=cnva[:, ct, hh, :], start=False, stop=(ct == 1))
                    return ins
                P.op("pe", mm, reads=[("PT", pi, 0), ("PT", pi, 1), ("cnva", 0), ("cnva", 1)] + [("nva", kb) for kb in kblocks],
                     writes=PK(pbo))
                if hh == 0:
                    return
                ov = PB(pbo)[:, 0:132].rearrange("p (h f) -> p h f", f=66)
                P.op("dve", lambda e: e.reciprocal(out=rc, in_=ov[:, :, 64]), writes=PK(pbo) + ["rc"])
                P.op("dve", lambda e: e.tensor_tensor(
                    out=yt[:, t % 4, :].rearrange("p (h d) -> p h d", d=64), in0=ov[:, :, 0:64],
                    in1=rc.unsqueeze(2).to_broadcast([128, 2, 64]), op=ALU.mult),
                    reads=["rc"], writes=PK(pbo) + [("ytile", gi, t % 4)])
                if t % 4 == 3:
                    g0 = t - 3
                    pbt = 6 + gi
                    pbv = PB(pbt).bitcast(BF16)

                    def tr(e):
                        for cc in range(4):
                            ins = e.transpose(out=pbv[:, cc * 128:(cc + 1) * 128], in_=yt[:, cc, :], identity=identb)
                        return ins
                    P.op("pe", tr, reads=[("ytile", gi, cc) for cc in range(4)] + [("ytile", gi), "identb"], writes=PK(pbt))
                    P.op("act", lambda e: e.copy(out=ystage[gi], in_=pbv[:, 0:512]), writes=PK(pbt) + [("ystage", gi)])
                    P.dma("sp", yT_d[4 + hp, :, g0 * 128:(g0 + 4) * 128], ystage[gi], reads=[("ystage", gi)], writes=[("yT", 1, hp, g0 // 4)])
            if hp == 0:
                P.dma("pool", wgb, w_in[:, 3584:5632], writes=["wgb"])
                P.dma("pool", wrob, w_ro, writes=["wrob"])
                P.dma("pool", wnob, w_no, writes=["wnob"])
                P.dma("pool", wob, w_o, writes=["wob"])
                P.dma("pool", wadab[0], w_ada[:, 2 * D:3 * D], writes=["wadab0"])
                P.dma("pool", wadab[1], w_ada[:, 5 * D:6 * D], writes=["wadab1"])
            if hp == 1:
                P.dma("pool", w1b.rearrange("r (a c) -> (r a) c", a=2), w_ff1.rearrange("r (a c) -> (r a) c", a=2), writes=["w1b"])
            if hp == 2:
                P.dma("pool", w2b, w_ff2, writes=["w2b"])
            units = [(t, hh) for t in range(NT) for hh in range(2)]
            for k_, (t_, hh_) in enumerate(units):
                na_front(t_, hh_)
                if k_ >= 1:
                    na_back(*units[k_ - 1])
            na_back(*units[-1])
        for hp in range(4):
            pair_body(hp)
        P.barrier(scr)
        A.release(m_after_persist)
        A.release(m0)

        if upto == "p2":
            raise _Stop()
        GT = [A.alloc([D], F32) for _ in range(2)]
        m3 = A.mark()
        wbufg = A.alloc([KC, 1024], BF16)
        bb = A.alloc([D], F32)
        gb = A.alloc([D], F32)
        for gi_, j in enumerate((2, 5)):
            P.dma("sp", wbufg, wadab[gi_].rearrange("(k p) n -> p k n", p=128), writes=["wbufg"])
            P.dma("sp", bb, b_ada[0:1, j * D:(j + 1) * D].partition_broadcast(128), writes=["bb"])
            P.dma("sp", gb, gpost[gi_:gi_ + 1, :].partition_broadcast(128), writes=["gb"])

            def mm(e):
                for half in range(2):
                    for k in range(KC):
                        ins = e.matmul(PB(half)[:, 0:512], lhsT=sc_rep[:, k, :], rhs=wbufg[:, k, half * 512:(half + 1) * 512],
                                       start=(k == 0), stop=(k == KC - 1))
                return ins
            P.op("pe", mm, reads=["wbufg", "sc_rep"], writes=PK(0, 1))
            P.op("dve", lambda e, gi_=gi_: e.tensor_tensor(out=GT[gi_].rearrange("p (b n) -> p b n", b=2), in0=ps[:, 0:2, :], in1=bb.rearrange("p (b n) -> p b n", b=2), op=ALU.add),
                 reads=["bb"], writes=PK(0, 1) + [("GT", gi_)])
            P.op("dve", lambda e, gi_=gi_: e.tensor_tensor(out=GT[gi_], in0=GT[gi_], in1=gb, op=ALU.mult), reads=["gb"], writes=[("GT", gi_)])
        P.barrier(scr)
        A.release(m3)

        if upto == "p3p":
            raise _Stop()
        Wg = A.alloc([KC, 2048], BF16)
        Wro = A.alloc([4, D], BF16)
        Wno = A.alloc([4, D], BF16)
        Wo = A.alloc([KC, D], BF16)
        def load3a_weights():
            for q4 in range(4):
                P.dma("sp", Wg[:, :, q4 * 512:(q4 + 1) * 512], wgb[:, q4 * 512:(q4 + 1) * 512].rearrange("(k p) n -> p k n", p=128), writes=[("Wg", q4)])
            P.dma("sp", Wro, wrob.rearrange("(k p) n -> p k n", p=128), writes=["Wro"])
            P.dma("sp", Wno, wnob.rearrange("(k p) n -> p k n", p=128), writes=["Wno"])
            for q2 in range(2):
                P.dma("sp", Wo[:, :, q2 * 512:(q2 + 1) * 512], wob[:, q2 * 512:(q2 + 1) * 512].rearrange("(k p) n -> p k n", p=128), writes=[("Wo", q2)])
        xbA = [A.alloc([4, D], F32) for _ in range(2)]
        junkA = A.alloc([D], BF16)
        tmpA3 = [dict(junk=junkA, junk_key="junkA", ss=A.alloc([1], F32), rstd=A.alloc([1], F32), xn=A.alloc([D], F32), key=("nt3", i_)) for i_ in range(2)]
        hTb = A.alloc([KC, 512], BF16)
        yTbA = [A.alloc([8, 512], BF16) for _ in range(2)]
        sgT = A.alloc([16, 512], F32)
        z1A = [A.alloc([512], F32) for _ in range(2)]
        z2A = [A.alloc([512], F32) for _ in range(2)]
        zT = A.alloc([KC, 512], BF16)
        ssyA = [A.alloc([1], F32) for _ in range(2)]
        rsyA = [A.alloc([1], F32) for _ in range(2)]
        tyA = [A.alloc([D], F32) for _ in range(2)]
        Wgk = [("Wg", q4) for q4 in range(4)]

        def load3a(nb):
            xb = xbA[nb % 2]
            for tt in range(4):
                P.dma("sp", xb[:, tt, :], x[(nb * 4 + tt) * 128:(nb * 4 + tt + 1) * 128, :], writes=[("xbA", nb % 2, tt)])
            P.dma("sp", yTbA[nb % 2], yT_d[:, :, nb * 512:(nb + 1) * 512].rearrange("a p n -> p a n"), writes=[("yTb", nb % 2)])

        def n3a_p1(nb, tt):
            xb = xbA[nb % 2]
            return norm_p1(xb[:, tt, :], ("xbA", nb % 2, tt), tmpA3[tt % 2])

        def n3a_p2(nb, tt, xnk):
            norm_p2(xnk, (lambda c: hTb[:, c, tt * 128:(tt + 1) * 128]), [("hTb", c, tt) for c in range(KC)], S1, SH1, 0, (6, 7), tmpA3[tt % 2])

        def norm3a(nb):
            for tt in range(4):
                n3a_p2(nb, tt, n3a_p1(nb, tt))

        def blk3a(nb):
            xb = xbA[nb % 2]
            yTb = yTbA[nb % 2]
            if nb + 1 < NB:
                load3a(nb + 1)
            hkeys = [("hTb", c, tt) for c in range(KC) for tt in range(4)]
            xnks = {}
            for g in range(16):
                pb = g % 2

                def mm(e, g=g, pb=pb):
                    for k in range(KC):
                        ins = e.matmul(PB(pb), lhsT=Wg[:, k, g * 128:(g + 1) * 128], rhs=hTb[:, k, :], start=(k == 0), stop=(k == KC - 1))
                    return ins
                P.op("pe", mm, reads=hkeys + [("Wg", g // 4)], writes=PK(pb))
                P.op("act", lambda e, g=g, pb=pb: e.activation(out=sgT[:, g, :], in_=PB(pb), func=AF.Sigmoid), writes=PK(pb) + [("sgT", g)])
                if nb + 1 < NB and g in (2, 6):
                    xnks[g // 4] = n3a_p1(nb + 1, g // 4)
            for fc in range(KC):
                pa, pbb = 2 + fc % 2, 4 + fc % 2
                z1, z2 = z1A[fc % 2], z2A[fc % 2]

                def mm(e, fc=fc, pa=pa):
                    for k in range(4):
                        ins = e.matmul(PB(pa), lhsT=Wro[:, k, fc * 128:(fc + 1) * 128], rhs=yTb[:, k, :], start=(k == 0), stop=(k == 3))
                    return ins
                P.op("pe", mm, reads=[("yTb", nb % 2), "Wro"], writes=PK(pa))

                def mm(e, fc=fc, pbb=pbb):
                    for k in range(4):
                        ins = e.matmul(PB(pbb), lhsT=Wno[:, k, fc * 128:(fc + 1) * 128], rhs=yTb[:, 4 + k, :], start=(k == 0), stop=(k == 3))
                    return ins
                P.op("pe", mm, reads=[("yTb", nb % 2), "Wno"], writes=PK(pbb))
                P.op("dve", lambda e, fc=fc, pa=pa, z1=z1: e.tensor_tensor(out=z1, in0=PB(pa), in1=sgT[:, fc, :], op=ALU.mult),
                     reads=[("sgT", fc)], writes=PK(pa) + [("z1", fc % 2)])
                P.op("dve", lambda e, fc=fc, pbb=pbb, z2=z2: e.tensor_tensor(out=z2, in0=PB(pbb), in1=sgT[:, 8 + fc, :], op=ALU.mult),
                     reads=[("sgT", 8 + fc)], writes=PK(pbb) + [("z2", fc % 2)])
                P.op("pool", lambda e, fc=fc, z1=z1, z2=z2: e.tensor_tensor(out=zT[:, fc, :], in0=z1, in1=z2, op=ALU.add),
                     reads=[("z1", fc % 2), ("z2", fc % 2)], writes=[("zT", fc)])
                if nb + 1 < NB and fc % 2 == 1:
                    tt_ = fc // 2
                    n3a_p2(nb + 1, tt_, xnks[tt_])
                    if tt_ + 2 < 4:
                        xnks[tt_ + 2] = n3a_p1(nb + 1, tt_ + 2)
            for tt in range(4):
                i = nb * 4 + tt
                py0 = 6 - 2 * (tt % 2)
                ssy, rsy, ty = ssyA[tt % 2], rsyA[tt % 2], tyA[tt % 2]

                def mm(e, tt=tt, py0=py0):
                    for half in range(2):
                        for k in range(KC):
                            ins = e.matmul(PB(py0 + half), lhsT=zT[:, k, tt * 128:(tt + 1) * 128], rhs=Wo[:, k, half * 512:(half + 1) * 512],
                                           start=(k == 0), stop=(k == KC - 1))
                    return ins
                P.op("pe", mm, reads=[("zT", fc) for fc in range(KC)] + [("Wo", 0), ("Wo", 1)], writes=PK(py0, py0 + 1))
                jk = tmpA3[tt % 2]
                P.op("act", lambda e, py0=py0, jk=jk, ssy=ssy: e.activation(out=jk["junk"].rearrange("p (b n) -> p b n", b=2), in_=ps[:, py0:py0 + 2, :], func=AF.Square, accum_out=ssy),
                     writes=PK(py0, py0 + 1) + ["junkA", ("ssy", tt % 2)])
                P.op("act", lambda e, ssy=ssy, rsy=rsy: e.activation(out=rsy, in_=ssy, func=AF.Sqrt, scale=1.0 / D, bias=EPS),
                     reads=[("ssy", tt % 2)], writes=[("rsy", tt % 2)])
                P.op("dve", lambda e, rsy=rsy: e.reciprocal(out=rsy, in_=rsy), writes=[("rsy", tt % 2)])
                P.op("dve", lambda e, py0=py0, rsy=rsy, ty=ty: e.scalar_tensor_tensor(
                    out=ty.rearrange("p (b n) -> p b n", b=2), in0=ps[:, py0:py0 + 2, :], scalar=rsy[:, 0:1],
                    in1=GT[0].rearrange("p (b n) -> p b n", b=2), op0=ALU.mult, op1=ALU.mult),
                    reads=[("rsy", tt % 2), ("GT", 0)], writes=PK(py0, py0 + 1) + [("ty", tt % 2)])
                P.op("pool", lambda e, tt=tt, ty=ty, xb=xb: e.tensor_tensor(out=ty, in0=ty, in1=xb[:, tt, :], op=ALU.add),
                     reads=[("xbA", nb % 2, tt)], writes=[("ty", tt % 2)])
                P.dma("sp", out[i * 128:(i + 1) * 128, :], ty, reads=[("ty", tt % 2)], writes=[("x1d", i)])
        load3a(0)
        load3a_weights()
        norm3a(0)
        for nb in range(NB):
            blk3a(nb)
        P.barrier(scr)
        A.release(m3)

        if upto == "p3a":
            raise _Stop()
        W1 = A.alloc([KC, 4 * D], BF16)
        W2 = A.alloc([32, D], BF16)
        def load3b_weights():
            for q8 in range(8):
                P.dma("sp", W1[:, :, q8 * 512:(q8 + 1) * 512], w1b[:, q8 * 512:(q8 + 1) * 512].rearrange("(k p) n -> p k n", p=128), writes=[("W1", q8)])
            for q8 in range(8):
                P.dma("sp", W2[:, q8 * 4:(q8 + 1) * 4, :], w2b[q8 * 512:(q8 + 1) * 512, :].rearrange("(k p) n -> p k n", p=128), writes=[("W2", q8)])
        xtB = [A.alloc([D], F32) for _ in range(2)]
        xrB = A.alloc([D], F32)
        rl = [A.alloc([512], F32) for _ in range(2)]
        h2TB = [A.alloc([KC, 512], BF16) for _ in range(2)]
        uT = A.alloc([32, 512], BF16)
        ssB = [A.alloc([1], F32) for _ in range(2)]
        rstdB = [A.alloc([1], F32) for _ in range(2)]
        ssyB = [A.alloc([1], F32) for _ in range(2)]
        rsyB = [A.alloc([1], F32) for _ in range(2)]
        tmpB3 = [dict(junk=rl[i_].bitcast(BF16), junk_key=("rl", i_), ss=ssB[i_], rstd=rstdB[i_], xn=xtB[i_], key=("nt4", i_)) for i_ in range(2)]
        W2k = [("W2", q8) for q8 in range(8)]

        def n3b_p1(nb, tt):
            i = nb * 4 + tt
            bi = i % 2
            P.dma("sp", xtB[bi], out[i * 128:(i + 1) * 128, :], reads=[("x1d", i)], writes=[("xtB", bi)])
            return norm_p1(xtB[bi], ("xtB", bi), tmpB3[bi], inplace=True)

        def n3b_p2(nb, tt, xnk):
            i = nb * 4 + tt
            h2T = h2TB[nb % 2]
            norm_p2(xnk, (lambda c: h2T[:, c, tt * 128:(tt + 1) * 128]), [("h2T", nb % 2, c, tt) for c in range(KC)], S2, SH2, 0, (6, 7), tmpB3[i % 2])

        def blk3b(nb):
            h2T = h2TB[nb % 2]
            hkeys = [("h2T", nb % 2, c, tt) for c in range(KC) for tt in range(4)]
            nxt = nb + 1 < NB
            xnks = {}
            for j in range(32):
                pb = j % 2

                def mm(e, j=j, pb=pb):
                    for k in range(KC):
                        ins = e.matmul(PB(pb), lhsT=W1[:, k, j * 128:(j + 1) * 128], rhs=h2T[:, k, :], start=(k == 0), stop=(k == KC - 1))
                    return ins
                P.op("pe", mm, reads=hkeys + [("W1", j // 4)], writes=PK(pb))
                P.op("act", lambda e, pb=pb: e.activation(out=rl[pb], in_=PB(pb), func=AF.Relu), writes=PK(pb) + [("rl", pb)])
                P.op("dve" if j % 2 == 0 else "pool", lambda e, j=j, pb=pb: e.tensor_tensor(out=uT[:, j, :], in0=rl[pb], in1=rl[pb], op=ALU.mult),
                     reads=[("rl", pb)], writes=[("uT", j)])
                if nxt:
                    if j == 3:
                        xnks[0] = n3b_p1(nb + 1, 0)
                    elif j == 7:
                        xnks[1] = n3b_p1(nb + 1, 1)
                    elif j == 15:
                        n3b_p2(nb + 1, 0, xnks[0])
                        xnks[2] = n3b_p1(nb + 1, 2)
                    elif j == 21:
                        n3b_p2(nb + 1, 1, xnks[1])
                        xnks[3] = n3b_p1(nb + 1, 3)
                    elif j == 27:
                        n3b_p2(nb + 1, 2, xnks[2])
                    elif j == 31:
                        n3b_p2(nb + 1, 3, xnks[3])
            for tt in range(4):
                i = nb * 4 + tt
                pbm = 2 + 2 * (tt % 2)
                ssy, rsy = ssyB[tt % 2], rsyB[tt % 2]
                P.dma("sp", xrB, out[i * 128:(i + 1) * 128, :], reads=[("x1d", i)], writes=["xrB"])

                def mm(e, tt=tt, pbm=pbm):
                    for half in range(2):
                        for j in range(32):
                            ins = e.matmul(PB(pbm + half), lhsT=uT[:, j, tt * 128:(tt + 1) * 128], rhs=W2[:, j, half * 512:(half + 1) * 512],
                                           start=(j == 0), stop=(j == 31))
                    return ins
                P.op("pe", mm, reads=[("uT", j) for j in range(32)] + W2k, writes=PK(pbm, pbm + 1))
                jb = tt % 2
                P.op("act", lambda e, pbm=pbm, jb=jb, ssy=ssy: e.activation(out=rl[jb].bitcast(BF16).rearrange("p (b n) -> p b n", b=2), in_=ps[:, pbm:pbm + 2, :], func=AF.Square, accum_out=ssy),
                     writes=PK(pbm, pbm + 1) + [("rl", jb), ("ssyB", tt % 2)])
                P.op("act", lambda e, ssy=ssy, rsy=rsy: e.activation(out=rsy, in_=ssy, func=AF.Sqrt, scale=1.0 / D, bias=EPS),
                     reads=[("ssyB", tt % 2)], writes=[("rsyB", tt % 2)])
                P.op("dve", lambda e, rsy=rsy: e.reciprocal(out=rsy, in_=rsy), writes=[("rsyB", tt % 2)])
                P.op("dve", lambda e, pbm=pbm, rsy=rsy: e.scalar_tensor_tensor(
                    out=ps[:, pbm:pbm + 2, :], in0=ps[:, pbm:pbm + 2, :], scalar=rsy[:, 0:1],
                    in1=GT[1].rearrange("p (b n) -> p b n", b=2), op0=ALU.mult, op1=ALU.mult),
                    reads=[("rsyB", tt % 2), ("GT", 1)], writes=PK(pbm, pbm + 1))
                P.op("dve", lambda e, pbm=pbm: e.tensor_tensor(out=xrB.rearrange("p (b n) -> p b n", b=2), in0=ps[:, pbm:pbm + 2, :],
                                                               in1=xrB.rearrange("p (b n) -> p b n", b=2), op=ALU.add),
                     writes=PK(pbm, pbm + 1) + ["xrB"])
                P.dma("sp", out[i * 128:(i + 1) * 128, :], xrB, reads=["xrB"], writes=[("outd", i)])
        for tt in range(4):
            n3b_p2(0, tt, n3b_p1(0, tt))
        load3b_weights()
        for nb in range(NB):
            blk3b(nb)
    try:
        body()
    except _Stop:
        pass
    info = P.emit()
    info["arena_peak"] = A.peak
    cmp_.__exit__(None, None, None)
    cm.__exit__(None, None, None)
    return nc, info, types


def prep_inputs(inputs, SEQ, types):
    NT = SEQ // 128
    f = lambda a: np.ascontiguousarray(np.asarray(a, dtype=np.float32))
    x = f(inputs["x"]); c = f(inputs["c"]); ctx = f(inputs["ctx"]); c_ctx = f(inputs["c_ctx"])
    B = x.shape[0]
    w_ada = f(inputs["w_ada"][0]); b_ada = f(inputs["b_ada"][0])
    shared = dict(
        w_ada=w_ada,
        bada_fm=np.ascontiguousarray(b_ada.reshape(48, 128).T),
        b_ada=b_ada.reshape(1, -1),
        gpre_fm=np.ascontiguousarray(np.stack([f(inputs["norm_pre_mix"][0]).reshape(KC, 128).T,
                                               f(inputs["norm_pre_ffn"][0]).reshape(KC, 128).T], axis=1)),
        gpost=np.ascontiguousarray(np.stack([f(inputs["norm_post_mix"][0]), f(inputs["norm_post_ffn"][0])], axis=0)),
        w_in=f(inputs["w_in"][0]), w_ro=f(inputs["w_ret_out"][0]), w_no=f(inputs["w_na_out"][0]),
        w_o=f(inputs["w_o"][0]), w_ff1=f(inputs["w_ff1"][0]), w_ff2=f(inputs["w_ff2"][0]),
    )
    lg = f(inputs["ret_decay_logit"][0])
    lgt_pair = np.zeros((128, 8), np.float32)
    def pair_body(hp):
        for d_ in range(2):
            lgt_pair[0:64, hp * 2 + d_] = lg[d_, 2 * hp]
            lgt_pair[64:128, hp * 2 + d_] = lg[d_, 2 * hp + 1]
    for hp in range(4):
        pair_body(hp)
    lgt_bc = np.zeros((128, 16), np.float32)
    for h in range(8):
        for d_ in range(2):
            lgt_bc[:, 2 * h + d_] = lg[d_, h]
    shared["lgt_pair"] = lgt_pair
    shared["lgt_bc"] = lgt_bc
    shared["ident"] = np.eye(128, dtype=np.float32)
    j = np.arange(128)[:, None].astype(np.float32)
    i = np.arange(128)[None, :].astype(np.float32)
    cmat = np.stack([np.maximum(i - j, 0), np.maximum(j - i, 0), (i >= j) * 0.125, (j > i) * 0.125], axis=1).astype(np.float32)
    shared["cmat"] = np.ascontiguousarray(cmat)
    jj = np.arange(128, dtype=np.float32)
    shared["colc"] = np.ascontiguousarray(np.stack([127 - jj, jj, 255 - jj, 127 - jj, jj, 128 + jj], axis=1))
    ii = np.arange(128, dtype=np.float32)
    shared["rowc"] = np.ascontiguousarray(np.broadcast_to(np.stack([ii + 1, 128 - ii], axis=0)[None], (128, 2, 128)).astype(np.float32))
    cos, sin = rope_tables(SEQ)
    shared["cos_tm"] = np.ascontiguousarray(cos.reshape(NT, 128, 32).transpose(1, 0, 2))
    shared["sin_tm"] = np.ascontiguousarray(sin.reshape(NT, 128, 32).transpose(1, 0, 2))
    mask, idr, idc = na_consts(types)
    rpb = f(inputs["na_rpb"][0])
    rpbB = rpb[:, idr, idc]
    shared["rpbB"] = np.ascontiguousarray(rpbB.transpose(0, 2, 1, 3))
    shared["maskB"] = np.ascontiguousarray(mask.transpose(1, 0, 2))
    in_maps = []
    for b in range(B):
        m = dict(shared)
        m["x"] = x[b]
        m["ctx"] = ctx[b]
        m["c_fm"] = np.ascontiguousarray(np.stack([c[b].reshape(KC, 128).T, c_ctx.reshape(KC, 128).T], axis=2))
        in_maps.append(m)
    return in_maps


_CACHE = {}


def kernel(**inputs):
    x = inputs["x"]
    B, SEQ, _ = x.shape
    if SEQ not in _CACHE:
        _CACHE[SEQ] = build(SEQ)
    nc, info, types = _CACHE[SEQ]
    in_maps = prep_inputs(inputs, SEQ, types)
    res = run_bass_kernel_spmd(nc, in_maps, core_ids=list(range(B)))
    return np.stack([np.asarray(r["out"], dtype=np.float32) for r in res.results], axis=0)
```

```python
import numpy as np
import ml_dtypes
import concourse.bass as bass
import concourse.mybir as mybir
from concourse.bass_utils import run_bass_kernel_spmd

F32 = mybir.dt.float32
BF16 = mybir.dt.bfloat16
U8 = mybir.dt.uint8
AF = mybir.ActivationFunctionType
ALU = mybir.AluOpType
AX = mybir.AxisListType

D = 1024
KC = 8
CTX = 256
GRID_W = 64
EPS = 1e-6
CH = 4096


class Prog:
    ENGS = ("pe", "act", "dve", "pool", "sp")

    def __init__(self, nc, n_dma_sems=16):
        self.nc = nc
        self.ops = []
        self.last_w = {}
        self.readers = {}
        self.n_dma_sems = n_dma_sems
        self.pending = {e: set() for e in self.ENGS}
        self.bar_start = 0
        self.nbar = 0

    def op(self, eng, fn, reads=(), writes=(), dma=False):
        oid = len(self.ops)
        deps = set()
        for k in list(reads) + list(writes):
            if k in self.last_w:
                deps.add(self.last_w[k])
        for k in writes:
            for r in self.readers.get(k, ()):
                deps.add(r)
        deps |= self.pending[eng]
        self.pending[eng] = set()
        deps.discard(oid)
        self.ops.append(dict(eng=eng, fn=fn, deps=deps, dma=dma, has_dep=False))
        for k in reads:
            self.readers.setdefault(k, []).append(oid)
        for k in writes:
            self.last_w[k] = oid
            self.readers[k] = []
        return oid

    def dma(self, q, out, in_, reads=(), writes=(), **kw):
        def fn(e):
            return e.dma_start(out=out, in_=in_, **kw)
        return self.op(q, fn, reads, writes, dma=True)

    def barrier(self, scratch):
        n = self.nbar
        self.nbar += 1
        dmas = [i for i in range(self.bar_start, len(self.ops)) if self.ops[i]["dma"]]
        marks = []
        marks.append(self.op("act", lambda e: e.copy(out=scratch["act"], in_=scratch["act"]), writes=[("bar", n, "act")]))
        marks.append(self.op("dve", lambda e: e.memset(scratch["dve"], 0.0), writes=[("bar", n, "dve")]))
        marks.append(self.op("pool", lambda e: e.memset(scratch["pool"], 0.0), writes=[("bar", n, "pool")]))
        for e in self.ENGS:
            self.pending[e] = set(marks) | set(dmas)
        self.last_w = {}
        self.readers = {}
        self.bar_start = len(self.ops)

    def emit(self, final_wait_eng="sp"):
        nc = self.nc
        ops = self.ops
        for i, o in enumerate(ops):
            keep = set()
            for d in o["deps"]:
                od = ops[d]
                if (not od["dma"]) and od["eng"] == o["eng"] and o["eng"] == "pe" and not o["dma"]:
                    continue
                keep.add(d)
            o["deps"] = keep
            for d in keep:
                ops[d]["has_dep"] = True
        tail = [i for i, o in enumerate(ops) if o["dma"] and not o["has_dep"]]
        for i in tail:
            ops[i]["has_dep"] = True
        cnt = {e: 0 for e in self.ENGS}
        for o in ops:
            if not o["dma"] and o["has_dep"]:
                o["seq"] = cnt[o["eng"]]
                cnt[o["eng"]] += 1
        sems = {}
        for e in self.ENGS:
            n = (cnt[e] + CH - 1) // CH
            sems[e] = [nc.alloc_semaphore(name=f"s_{e}_{j}") for j in range(n)]
        dsems = [nc.alloc_semaphore(name=f"s_dma_{j}") for j in range(self.n_dma_sems)]
        dcount = [0] * self.n_dma_sems
        dnext = 0
        waited = {e: {} for e in self.ENGS}

        def plan_wait(o, e, sem, val):
            key = id(sem)
            if waited[e].get(key, 0) >= val:
                return
            waited[e][key] = val
            o["waits"].append((sem, val))

        for i, o in enumerate(ops):
            e = o["eng"]
            o["waits"] = []
            for d in sorted(o["deps"]):
                od = ops[d]
                if od["dma"]:
                    plan_wait(o, e, od["dsem"], od["dval"])
                else:
                    s = od["seq"]
                    plan_wait(o, e, sems[od["eng"]][s // CH], s % CH + 1)
            if o["dma"] and e == "pool":
                sw = nc.alloc_semaphore(name=f"s_swdma_{i}")
                o["dsem"] = sw
                o["dval"] = 16
                o["inc"] = (sw, 16)
            elif o["dma"]:
                j = dnext
                dnext = (dnext + 1) % self.n_dma_sems
                if dcount[j] > 0:
                    plan_wait(o, e, dsems[j], dcount[j])
                dcount[j] += 16
                o["dsem"] = dsems[j]
                o["dval"] = dcount[j]
                o["inc"] = (dsems[j], 16)
            elif o["has_dep"]:
                s = o["seq"]
                o["inc"] = (sems[e][s // CH], 1)
            else:
                o["inc"] = None
        final_waits = []
        fo = dict(waits=final_waits)
        for i in tail:
            plan_wait(fo, final_wait_eng, ops[i]["dsem"], ops[i]["dval"])

        def run_engine(ename, eng):
            for o in ops:
                if o["eng"] != ename:
                    continue
                for (sem, val) in o["waits"]:
                    eng.wait_ge(sem, val)
                ins = o["fn"](eng)
                if o["inc"] is not None:
                    ins.then_inc(o["inc"][0], o["inc"][1])
            if ename == final_wait_eng:
                for (sem, val) in final_waits:
                    eng.wait_ge(sem, val)

        with nc.Block() as block:
            @block.sync
            def _(eng):
                run_engine("sp", eng)

            @block.tensor
            def _(eng):
                run_engine("pe", eng)

            @block.scalar
            def _(eng):
                run_engine("act", eng)

            @block.vector
            def _(eng):
                run_engine("dve", eng)

            @block.gpsimd
            def _(eng):
                run_engine("pool", eng)
        return dict(n_ops=len(ops), cnt=cnt)


class Arena:
    def __init__(self, ap_u8, size):
        self.ap = ap_u8
        self.size = size
        self.off = 0
        self.peak = 0

    def alloc(self, shape, dtype):
        esz = {F32: 4, BF16: 2}[dtype]
        n = int(np.prod(shape))
        nbytes = (n * esz + 63) // 64 * 64
        assert self.off + nbytes <= self.size, f"arena overflow {self.off}+{nbytes}>{self.size}"
        v = self.ap[:, self.off:self.off + n * esz].bitcast(dtype)
        self.off += nbytes
        self.peak = max(self.peak, self.off)
        if len(shape) == 1:
            return v
        names = " ".join(f"d{i}" for i in range(len(shape)))
        kw = {f"d{i}": int(s) for i, s in enumerate(shape)}
        return v.rearrange(f"p ({names}) -> p {names}", **kw)

    def mark(self):
        return self.off

    def release(self, m):
        self.off = m


def na_structure(rows):
    T = rows // 2
    types = {}
    per_t = []
    for t in range(T):
        lst = []
        for u in range(T):
            vis = []
            anyv = False
            for kr in range(2):
                for qr in range(2):
                    r = 2 * t + qr
                    r0 = min(max(r - 4, 0), rows - 8)
                    v = r0 <= 2 * u + kr < r0 + 8
                    vis.append(v)
                    anyv = anyv or v
            if not anyv:
                continue
            key = (u - t, tuple(vis))
            if key not in types:
                types[key] = len(types)
            lst.append((u, types[key]))
        per_t.append(lst)
    return per_t, types


def na_consts(types):
    nt = len(types)
    mask = np.zeros((nt, 128, 128), np.float32)
    idr = np.zeros((nt, 128, 128), np.int64)
    idc = np.zeros((nt, 128, 128), np.int64)
    kc = np.arange(64)[:, None]
    qc = np.arange(64)[None, :]
    c0 = np.clip(qc - 8, 0, 48)
    colok = (kc >= c0) & (kc < c0 + 16)
    dc = np.clip(kc - qc + 15, 0, 30)
    for (delta, vis), ti in types.items():
        for kr in range(2):
            for qr in range(2):
                v = vis[kr * 2 + qr]
                dr = int(np.clip(2 * delta + kr - qr + 7, 0, 14))
                blk = np.where(colok & v, 0.0, -30000.0).astype(np.float32)
                mask[ti, kr * 64:(kr + 1) * 64, qr * 64:(qr + 1) * 64] = blk
                idr[ti, kr * 64:(kr + 1) * 64, qr * 64:(qr + 1) * 64] = dr
                idc[ti, kr * 64:(kr + 1) * 64, qr * 64:(qr + 1) * 64] = dc
    return mask, idr, idc


def rope_tables(n):
    pos = np.arange(n)
    row = (pos // GRID_W).astype(np.float32)
    col = (pos % GRID_W).astype(np.float32)
    inv = (10000.0 ** (-np.arange(0, 32, 2, dtype=np.float32) / 32)).astype(np.float32)
    ang = np.concatenate([row[:, None] * inv, col[:, None] * inv], axis=-1).astype(np.float32)
    return np.cos(ang).astype(np.float32), np.sin(ang).astype(np.float32)


class _Stop(Exception):
    pass


def build(SEQ, debug=(), upto=None):
    NT = SEQ // 128
    ROWS = SEQ // 64
    NB = SEQ // 512
    per_t, types = na_structure(ROWS)
    NTYPE = len(types)
    nc = bass.Bass("TRN2", target_bir_lowering=False)

    def din(name, shape, dt=F32):
        return nc.dram_tensor(name, list(shape), dt, kind="ExternalInput").ap()

    x = din("x", [SEQ, D])
    ctx = din("ctx", [CTX, D])
    c_fm = din("c_fm", [128, KC, 2])
    w_ada = din("w_ada", [D, 6 * D])
    bada_fm = din("bada_fm", [128, 48])
    b_ada = din("b_ada", [1, 6 * D])
    gpre_fm = din("gpre_fm", [128, 2, KC])
    gpost = din("gpost", [2, D])
    w_in = din("w_in", [D, 5632])
    w_ro = din("w_ro", [512, D])
    w_no = din("w_no", [512, D])
    w_o = din("w_o", [D, D])
    w_ff1 = din("w_ff1", [D, 4 * D])
    w_ff2 = din("w_ff2", [4 * D, D])
    lgt_pair = din("lgt_pair", [128, 8])
    lgt_bc = din("lgt_bc", [128, 16])
    ident_d = din("ident", [128, 128])
    cmat = din("cmat", [128, 4, 128])
    colc_d = din("colc", [128, 6])
    rowc_d = din("rowc", [128, 2, 128])
    cos_d = din("cos_tm", [128, NT, 32])
    sin_d = din("sin_tm", [128, NT, 32])
    rpbB = din("rpbB", [8, 128, NTYPE, 128])
    maskB_d = din("maskB", [128, NTYPE, 128])
    out = nc.dram_tensor("out", [SEQ, D], F32, kind="ExternalOutput").ap()
    yT_d = nc.dram_tensor("yT_scratch", [8, 128, SEQ], BF16, kind="Internal").ap()
    wgb = nc.dram_tensor("wg_bf", [D, 2048], BF16, kind="Internal").ap()
    wrob = nc.dram_tensor("wro_bf", [512, D], BF16, kind="Internal").ap()
    wnob = nc.dram_tensor("wno_bf", [512, D], BF16, kind="Internal").ap()
    wob = nc.dram_tensor("wo_bf", [D, D], BF16, kind="Internal").ap()
    w1b = nc.dram_tensor("w1_bf", [D, 4 * D], BF16, kind="Internal").ap()
    w2b = nc.dram_tensor("w2_bf", [4 * D, D], BF16, kind="Internal").ap()
    gt_d = nc.dram_tensor("gt_scratch", [2, 128, D], F32, kind="Internal").ap()
    dbg = {}
    for name, shape, dt in debug:
        dbg[name] = nc.dram_tensor(name, list(shape), dt, kind="ExternalOutput").ap()

    P = Prog(nc)
    ARENA_BYTES = 207 * 1024
    cm = nc.sbuf_tensor("arena", [128, ARENA_BYTES], U8)
    arena_h = cm.__enter__()
    A = Arena(arena_h, ARENA_BYTES)
    cmp_ = nc.psum_tensor("ps", [128, 8, 512], F32)
    ps = cmp_.__enter__()

    def PB(b):
        return ps[:, b, :]

    def PK(*bs):
        return [("ps", b) for b in bs]

    def body():
        ident = A.alloc([128], F32)
        identb = A.alloc([128], BF16)
        scr = {e: A.alloc([16], F32) for e in ("act", "dve", "pool")}
        S1 = A.alloc([KC, 2], F32)
        SH1 = A.alloc([KC, 2], F32)
        S2 = A.alloc([KC, 2], F32)
        SH2 = A.alloc([KC, 2], F32)
        scb = A.alloc([KC, 2], BF16)
        sc_rep = A.alloc([KC, 128], BF16)
        gpre = A.alloc([2, KC], F32)
        badafm = A.alloc([48], F32)

        P.dma("sp", ident, ident_d, writes=["ident"])
        P.op("dve", lambda e: e.tensor_copy(out=identb, in_=ident), reads=["ident"], writes=["identb"])
        for e_ in ("act", "dve", "pool"):
            pass
        P.op("dve", lambda e: e.memset(scr["dve"], 0.0), writes=["scr_dve"])
        P.op("pool", lambda e: e.memset(scr["pool"], 0.0), writes=["scr_pool"])
        P.op("dve", lambda e: e.memset(scr["act"], 0.0), writes=["scr_act"])
        P.dma("sp", gpre, gpre_fm, writes=["gpre"])
        P.dma("sp", badafm, bada_fm, writes=["badafm"])

        m_phase2 = None

        def norm_p1(xt_ap, xt_key, tmp, inplace=False):
            junk, ss, rstd, xn = tmp["junk"], tmp["ss"], tmp["rstd"], tmp["xn"]
            tk = tmp["key"]
            jkey = tmp.get("junk_key", (tk, "junk"))
            P.op("act", lambda e: e.activation(out=junk, in_=xt_ap, func=AF.Square, accum_out=ss),
                 reads=[xt_key], writes=[jkey, (tk, "ss")])
            P.op("act", lambda e: e.activation(out=rstd, in_=ss, func=AF.Sqrt, scale=1.0 / D, bias=EPS),
                 reads=[(tk, "ss")], writes=[(tk, "rstd")])
            P.op("dve", lambda e: e.reciprocal(out=rstd, in_=rstd), writes=[(tk, "rstd")])
            xnk = xt_key if inplace else (tk, "xn")
            P.op("dve", lambda e: e.tensor_scalar(out=xn, in0=xt_ap, scalar1=rstd[:, 0:1], scalar2=None, op0=ALU.mult),
                 reads=[(tk, "rstd")] + ([] if inplace else [xt_key]), writes=[xnk])
            return xnk

        def norm_p2(xnk, dst_fn, dst_keys, Sc, Sh, col, banks, tmp):
            xn = tmp["xn"]
            for half in range(2):
                b = banks[half]

                def tr(e, half=half, b=b):
                    for cc in range(4):
                        c = half * 4 + cc
                        ins = e.transpose(out=PB(b)[:, cc * 128:(cc + 1) * 128], in_=xn[:, c * 128:(c + 1) * 128], identity=ident)
                    return ins
                P.op("pe", tr, reads=[xnk, "ident"], writes=PK(b))
                for cc in range(4):
                    c = half * 4 + cc
                    if half == 0:
                        P.op("act", lambda e, c=c, cc=cc, b=b: e.activation(
                            out=dst_fn(c), in_=PB(b)[:, cc * 128:(cc + 1) * 128], func=AF.Identity,
                            scale=Sc[:, c, col:col + 1], bias=Sh[:, c, col:col + 1]),
                            reads=["mod"] + PK(b), writes=[dst_keys[c]])
                    else:
                        P.op("dve", lambda e, c=c, cc=cc, b=b: e.tensor_scalar(
                            out=dst_fn(c), in0=PB(b)[:, cc * 128:(cc + 1) * 128],
                            scalar1=Sc[:, c, col:col + 1], scalar2=Sh[:, c, col:col + 1], op0=ALU.mult, op1=ALU.add),
                            reads=["mod"] + PK(b), writes=[dst_keys[c]])

        def norm_transpose(xt_ap, xt_key, dst_fn, dst_keys, Sc, Sh, col, banks, tmp, tag, inplace=False):
            xnk = norm_p1(xt_ap, xt_key, tmp, inplace)
            norm_p2(xnk, dst_fn, dst_keys, Sc, Sh, col, banks, tmp)

        m0 = A.mark()
        cm_t = A.alloc([4, 128], F32)
        colc = A.alloc([6], F32)
        rowc = A.alloc([2, 128], F32)
        lgp = A.alloc([8], F32)
        lgb = A.alloc([16], F32)
        cfm = A.alloc([KC, 2], F32)
        A.release(m0)
        DT = A.alloc([8, 128], F32)
        kw = A.alloc([8, 2], F32)
        ckw = A.alloc([2, 8, 2], F32)
        QW = A.alloc([2, 128], F32)
        GL = A.alloc([8], F32)
        rowc = A.alloc([2, 128], F32)
        lgp = A.alloc([8], F32)
        hT = A.alloc([KC, SEQ], BF16)
        hcT = A.alloc([KC, CTX], BF16)
        m_after_persist = A.mark()
        cm_t = A.alloc([4, 128], F32)
        colc = A.alloc([6], F32)
        lgb = A.alloc([16], F32)
        cfm = A.alloc([KC, 2], F32)
        tmpA = A.alloc([128], F32)
        tmpB = A.alloc([128], F32)
        arg16 = A.alloc([16], F32)
        argc = A.alloc([2, 8, 2], F32)
        wbuf0 = A.alloc([KC, 1024], BF16)
        modfm = A.alloc([4, KC, 2], F32)

        P.dma("sp", cm_t, cmat, writes=["cmat"])
        P.dma("sp", colc, colc_d, writes=["colc"])
        P.dma("sp", rowc, rowc_d, writes=["rowc"])
        P.dma("sp", lgp, lgt_pair, writes=["lgp"])
        P.dma("sp", lgb, lgt_bc, writes=["lgb"])
        P.dma("sp", cfm, c_fm, writes=["cfm"])

        for t_, k_ in ((lgp, "lgp"), (lgb, "lgb")):
            P.op("act", lambda e, t_=t_: e.activation(out=t_, in_=t_, func=AF.Exp, scale=-1.0), writes=[k_])
            P.op("act", lambda e, t_=t_: e.activation(out=t_, in_=t_, func=AF.Ln, bias=1.0), writes=[k_])
            P.op("dve", lambda e, t_=t_: e.tensor_scalar(out=t_, in0=t_, scalar1=-1.0, scalar2=None, op0=ALU.mult), writes=[k_])
        for h in range(8):
            P.op("act", lambda e, h=h: e.activation(out=tmpA, in_=cm_t[:, 0, :], func=AF.Exp, scale=lgb[:, 2 * h:2 * h + 1]),
                 reads=["cmat", "lgb"], writes=["tmpA"])
            P.op("act", lambda e, h=h: e.activation(out=tmpB, in_=cm_t[:, 1, :], func=AF.Exp, scale=lgb[:, 2 * h + 1:2 * h + 2]),
                 reads=["cmat", "lgb"], writes=["tmpB"])
            P.op("dve", lambda e: e.tensor_tensor(out=tmpA, in0=tmpA, in1=cm_t[:, 2, :], op=ALU.mult), reads=["cmat"], writes=["tmpA"])
            P.op("dve", lambda e: e.tensor_tensor(out=tmpB, in0=tmpB, in1=cm_t[:, 3, :], op=ALU.mult), reads=["cmat"], writes=["tmpB"])
            P.op("dve", lambda e, h=h: e.tensor_tensor(out=DT[:, h, :], in0=tmpA, in1=tmpB, op=ALU.add),
                 reads=["tmpA", "tmpB"], writes=["DT"])
        lgb3 = lgb.rearrange("p (h d) -> p h d", d=2)
        arg3 = arg16.rearrange("p (h d) -> p h d", d=2)
        for d_ in range(2):
            P.op("dve", lambda e, d_=d_: e.tensor_scalar(out=arg3[:, :, d_], in0=lgb3[:, :, d_], scalar1=colc[:, d_:d_ + 1], scalar2=None, op0=ALU.mult),
                 reads=["lgb", "colc"], writes=["arg16"])
        P.op("act", lambda e: e.activation(out=arg16, in_=arg16, func=AF.Exp), writes=["arg16"])
        P.op("dve", lambda e: e.tensor_scalar(out=kw.rearrange("p h d -> p (h d)"), in0=arg16, scalar1=0.125, scalar2=None, op0=ALU.mult),
             reads=["arg16"], writes=["kw"])
        for ct in range(2):
            for d_ in range(2):
                cc_ = 2 + ct if d_ == 0 else 4 + ct
                P.op("dve", lambda e, ct=ct, d_=d_, cc_=cc_: e.tensor_scalar(out=argc[:, ct, :, d_], in0=lgb3[:, :, d_], scalar1=colc[:, cc_:cc_ + 1], scalar2=None, op0=ALU.mult),
                     reads=["lgb", "colc"], writes=["argc"])
        P.op("act", lambda e: e.activation(out=argc, in_=argc, func=AF.Exp), writes=["argc"])
        P.op("dve", lambda e: e.tensor_scalar(out=ckw, in0=argc, scalar1=0.125, scalar2=None, op0=ALU.mult), reads=["argc"], writes=["ckw"])
        P.op("act", lambda e: e.activation(out=GL, in_=lgp, func=AF.Exp, scale=128.0), reads=["lgp"], writes=["GL"])

        P.op("act", lambda e: e.activation(out=scb, in_=cfm, func=AF.Silu), reads=["cfm"], writes=["scb"])
        P.op("dve", lambda e: e.tensor_copy(out=sc_rep, in_=scb[:, :, 0:1].to_broadcast([128, KC, 128])), reads=["scb"], writes=["sc_rep"])

        def load_w(dst, src_rows_cols, key, nk=KC):
            P.dma("pool", dst, src_rows_cols.rearrange("(k p) n -> p k n", p=128), writes=[key])

        wbuf1 = A.alloc([KC, 1024], BF16)
        wbufs = {0: (wbuf1, "wbuf1"), 1: (wbuf0, "wbuf0"), 3: (wbuf1, "wbuf1"), 4: (wbuf0, "wbuf0")}

        def ada_load(j):
            wb_, key_ = wbufs[j]
            load_w(wb_, w_ada[:, j * D:(j + 1) * D], key_)

        def ada_mm(mi, j):
            wb_, key_ = wbufs[j]

            def mm(e):
                for cc in range(8):
                    for k in range(KC):
                        ins = e.matmul(PB(0)[:, cc * 2:cc * 2 + 2], lhsT=wb_[:, k, cc * 128:(cc + 1) * 128], rhs=scb[:, k, :],
                                       start=(k == 0), stop=(k == KC - 1))
                return ins
            P.op("pe", mm, reads=[key_, "scb"], writes=PK(0))
            P.op("dve", lambda e: e.tensor_tensor(
                out=modfm[:, mi, :, :], in0=PB(0)[:, 0:16].rearrange("p (c t) -> p c t", t=2),
                in1=badafm[:, j * 8:(j + 1) * 8].unsqueeze(2).to_broadcast([128, KC, 2]), op=ALU.add),
                reads=["badafm"], writes=PK(0) + [("modfm", mi)])

        def ada_fin(Sx, SHx, mi_sh, mi_sc, gi):
            P.op("dve", lambda e: e.scalar_tensor_tensor(
                out=Sx, in0=modfm[:, mi_sc, :, :], scalar=1.0, in1=gpre[:, gi, :].unsqueeze(2).to_broadcast([128, KC, 2]),
                op0=ALU.add, op1=ALU.mult), reads=[("modfm", mi_sc), "gpre"], writes=["mod"])
            P.op("dve", lambda e: e.tensor_copy(out=SHx, in_=modfm[:, mi_sh, :, :]), reads=[("modfm", mi_sh)], writes=["mod"])

        ada_load(1)
        ada_load(0)
        ada_mm(1, 1)
        ada_mm(0, 0)
        ada_fin(S1, SH1, 0, 1, 0)
        ada_load(4)
        ada_load(3)
        wbufG = [A.alloc([KC, 1024], BF16) for _ in range(2)]
        bbG = [A.alloc([D], F32) for _ in range(2)]
        gbG = [A.alloc([D], F32) for _ in range(2)]
        gtG = [A.alloc([D], F32) for _ in range(2)]
        for gi_, j in enumerate((2, 5)):
            load_w(wbufG[gi_], w_ada[:, j * D:(j + 1) * D], ("wbufG", gi_))
            P.dma("sp", bbG[gi_], b_ada[0:1, j * D:(j + 1) * D].partition_broadcast(128), writes=[("bbG", gi_)])
            P.dma("sp", gbG[gi_], gpost[gi_:gi_ + 1, :].partition_broadcast(128), writes=[("gbG", gi_)])

        if upto == "p0":
            raise _Stop()
        NBUF1 = 3
        xts = [A.alloc([D], F32) for _ in range(NBUF1)]
        tmps = []
        for i in range(NBUF1):
            tmps.append(dict(junk=A.alloc([D], BF16), ss=A.alloc([1], F32), rstd=A.alloc([1], F32), xn=A.alloc([D], F32), key=("nt", i)))

        def p1_load(i):
            bi = i % NBUF1
            src = x[i * 128:(i + 1) * 128, :] if i < NT else ctx[(i - NT) * 128:(i - NT + 1) * 128, :]
            P.dma("sp", xts[bi], src, writes=[("xt", bi)])

        def p1_front(i):
            bi = i % NBUF1
            return norm_p1(xts[bi], ("xt", bi), tmps[bi])

        def p1_back(i, xnk):
            bi = i % NBUF1
            if i < NT:
                dst_fn = (lambda c: hT[:, c, i * 128:(i + 1) * 128])
                dkeys = [("hT", c, i) for c in range(KC)]
                col = 0
            else:
                dst_fn = (lambda c: hcT[:, c, (i - NT) * 128:(i - NT + 1) * 128])
                dkeys = [("hcT", c, i - NT) for c in range(KC)]
                col = 1
            pbk = (2 * (i % 2), 2 * (i % 2) + 1)
            norm_p2(xnk, dst_fn, dkeys, S1, SH1, col, pbk, tmps[bi])
        NTT = NT + 2
        p1_load(0)
        p1_load(1)
        xk_prev = p1_front(0)
        for i in range(NTT):
            if i + 2 < NTT:
                p1_load(i + 2)
            xk_next = p1_front(i + 1) if i + 1 < NTT else None
            p1_back(i, xk_prev)
            xk_prev = xk_next
        ada_mm(3, 4)
        ada_mm(2, 3)
        ada_fin(S2, SH2, 2, 3, 1)
        for gi_ in range(2):
            def mm(e, gi_=gi_):
                for half in range(2):
                    for k in range(KC):
                        ins = e.matmul(PB(half)[:, 0:512], lhsT=sc_rep[:, k, :], rhs=wbufG[gi_][:, k, half * 512:(half + 1) * 512],
                                       start=(k == 0), stop=(k == KC - 1))
                return ins
            P.op("pe", mm, reads=[("wbufG", gi_), "sc_rep"], writes=PK(0, 1))
            P.op("dve", lambda e, gi_=gi_: e.tensor_tensor(out=gtG[gi_].rearrange("p (b n) -> p b n", b=2), in0=ps[:, 0:2, :],
                                                           in1=bbG[gi_].rearrange("p (b n) -> p b n", b=2), op=ALU.add),
                 reads=[("bbG", gi_)], writes=PK(0, 1) + [("gtG", gi_)])
            P.op("dve", lambda e, gi_=gi_: e.tensor_tensor(out=gtG[gi_], in0=gtG[gi_], in1=gbG[gi_], op=ALU.mult), reads=[("gbG", gi_)], writes=[("gtG", gi_)])
            P.dma("sp", gt_d[gi_], gtG[gi_], reads=[("gtG", gi_)], writes=[("gt_d", gi_)])
        if "hT" in dbg:
            P.dma("sp", dbg["hT"], hT, reads=[("hT", c, i) for c in range(KC) for i in range(NT)])
        P.barrier(scr)
        A.release(m_after_persist)
        if upto == "p1":
            raise _Stop()

        maskB = A.alloc([NTYPE, 128], BF16)
        P.dma("pool", maskB, maskB_d, writes=["maskB"])
        wb = A.alloc([KC, 7, 128], BF16)
        BT = A.alloc([2, NTYPE, 128], BF16)
        rpst = A.alloc([NTYPE, 128], BF16)
        slabQ = A.alloc([SEQ], BF16)
        slabK = A.alloc([SEQ], BF16)
        rv = A.alloc([NT, 128], BF16)
        nva = A.alloc([NT, 2, 65], BF16)
        srg = A.alloc([NT, 128], BF16)
        DS = A.alloc([NT, 2, 64], F32)
        Rb = A.alloc([NT, 2, 64], BF16)
        BLK = 4
        rtmp = [A.alloc([BLK // 2, 4, 32], F32) for _ in range(4)]
        cs_t = [A.alloc([2, BLK, 32], F32) for _ in range(2)]
        qk_tm = [A.alloc([BLK, 256], BF16) for _ in range(2)]
        Vfb = [A.alloc([BLK, 2, 2, 64], BF16) for _ in range(2)]
        crk = A.alloc([2, 128], BF16)
        cVfb = A.alloc([2, 2, 2, 64], BF16)
        cnva = A.alloc([2, 2, 65], BF16)
        cnkT = A.alloc([CTX], BF16)
        PT = [A.alloc([7, 128], BF16) for _ in range(2)]
        SDT = [A.alloc([2, 4, 128], BF16) for _ in range(2)]
        QfbT = [A.alloc([2, 4, 128], BF16) for _ in range(2)]
        sqA = [A.alloc([512], F32) for _ in range(2)]
        msA = [A.alloc([8], F32) for _ in range(2)]
        onA = [A.alloc([512], F32) for _ in range(2)]
        ytile = [A.alloc([4, 128], BF16) for _ in range(2)]
        ystage = [A.alloc([512], BF16) for _ in range(2)]
        rc = A.alloc([2], F32)

        qm = [[A.alloc([128], BF16) for _ in range(2)] for _ in range(2)]
        for hh_ in range(2):
            for par_ in range(2):
                P.op("pool", lambda e, hh_=hh_, par_=par_: e.memset(qm[hh_][par_], 0.0), writes=[("qm", hh_, par_)])
        P.op("pool", lambda e: e.memset(nva, 1.0), writes=["nva_init"])
        P.op("pool", lambda e: e.memset(cnva, 1.0), writes=["cnva_init"])

        def pair_body(hp):
            hk = ("hp", hp)
            for d_ in range(2):
                P.op("act", lambda e, d_=d_: e.activation(out=QW[:, d_, :], in_=rowc[:, d_, :], func=AF.Exp,
                                                           scale=lgp[:, hp * 2 + d_:hp * 2 + d_ + 1]),
                     writes=["QW"])
            def load_wb(hp_):
                for s in range(7):
                    c0 = s * 512 + hp_ * 128
                    P.dma("pool", wb[:, :, s, :], w_in[:, c0:c0 + 128].rearrange("(k p) n -> p k n", p=128), writes=[("wb", s)])
            if hp == 0:
                load_wb(0)
            wbk = [("wb", s) for s in range(7)]
            for hh in range(2):
                P.dma("pool", rpst, rpbB[2 * hp + hh], writes=["rpst"])
                P.op("dve", lambda e, hh=hh: e.tensor_tensor(out=BT[:, hh, :, :], in0=rpst, in1=maskB, op=ALU.add),
                     reads=["rpst", "maskB"], writes=[("BT", hh)])
            for ct in range(2):
                def mm(e, ct=ct):
                    for k in range(KC):
                        ins = e.matmul(PB(6)[:, 0:256], lhsT=hcT[:, k, ct * 128:(ct + 1) * 128], rhs=wb[:, k, 1:3, :].rearrange("p s n -> p (s n)"),
                                       start=(k == 0), stop=(k == KC - 1))
                    for k in range(KC):
                        ins = e.matmul(PB(6)[:, 256:384], lhsT=hcT[:, k, ct * 128:(ct + 1) * 128], rhs=wb[:, k, 6, :],
                                       start=(k == 0), stop=(k == KC - 1))
                    return ins
                P.op("pe", mm, reads=wbk + ["hcT"], writes=PK(6))
                P.op("act", lambda e, ct=ct: e.copy(out=crk[:, ct, :], in_=PB(6)[:, 0:128]), writes=PK(6) + [("crk", ct)])
                for hh in range(2):
                    for d_ in range(2):
                        P.op("dve", lambda e, ct=ct, hh=hh, d_=d_: e.tensor_scalar(
                            out=cVfb[:, ct, hh, d_, :], in0=PB(6)[:, 128 + hh * 64:128 + (hh + 1) * 64],
                            scalar1=ckw[:, ct, 2 * hp + hh, d_:d_ + 1], scalar2=None, op0=ALU.mult),
                            reads=["ckw"], writes=PK(6) + [("cVfb", ct)])
                P.op("dve", lambda e, ct=ct: e.tensor_copy(out=cnva[:, ct, :, 0:64], in_=PB(6)[:, 256:384].rearrange("p (h d) -> p h d", d=64)),
                     reads=["cnva_init"], writes=PK(6) + [("cnva", ct)])

            def mm(e):
                for k in range(KC):
                    ins = e.matmul(PB(7)[:, 0:CTX], lhsT=wb[:, k, 5, :], rhs=hcT[:, k, :], start=(k == 0), stop=(k == KC - 1))
                return ins
            P.op("pe", mm, reads=wbk + ["hcT"], writes=PK(7))
            P.op("act", lambda e: e.copy(out=cnkT, in_=PB(7)[:, 0:CTX]), writes=PK(7) + ["cnkT"])

            def mm(e):
                for hh in range(2):
                    for ct in range(2):
                        ins = e.matmul(PB(6)[hh * 64:(hh + 1) * 64, 0:128], lhsT=crk[:, ct, hh * 64:(hh + 1) * 64],
                                       rhs=cVfb[:, ct, hh, :, :].rearrange("p a b -> p (a b)"), start=(ct == 0), stop=(ct == 1))
                return ins
            P.op("pe", mm, reads=[("crk", 0), ("crk", 1), ("cVfb", 0), ("cVfb", 1)], writes=PK(6))
            P.op("dve", lambda e: e.tensor_copy(out=DS[:, 0, 0, :], in_=PB(6)[:, 0:64]), writes=PK(6) + [("DS", 0, 0)])
            P.op("dve", lambda e: e.tensor_copy(out=DS[:, NT - 1, 1, :], in_=PB(6)[:, 64:128]), writes=PK(6) + [("DS", NT - 1, 1)])

            if upto == "p2ctx":
                raise _Stop()
            def frontA(b0):
                bi = (b0 // BLK) % 2
                qk_ = qk_tm[bi]
                cst = cs_t[bi]
                P.dma("sp", cst[:, 0, :, :], cos_d[:, b0:b0 + BLK, :], writes=[("cs", bi, 0)])
                P.dma("sp", cst[:, 1, :, :], sin_d[:, b0:b0 + BLK, :], writes=[("cs", bi, 1)])
                q5 = qk_.rearrange("p b (g t f) -> p b g t f", g=4, t=2)
                for half in range(2):
                    hb = [2 + 2 * half, 3 + 2 * half]
                    for ii2 in range(2):
                        ii = half * 2 + ii2
                        i = b0 + ii
                        pb = hb[ii2]

                        def mm(e, i=i, pb=pb):
                            for k in range(KC):
                                ins = e.matmul(PB(pb)[:, 0:512], lhsT=hT[:, k, i * 128:(i + 1) * 128], rhs=wb[:, k, 0:4, :].rearrange("p s n -> p (s n)"),
                                               start=(k == 0), stop=(k == KC - 1))
                            return ins
                        P.op("pe", mm, reads=wbk, writes=PK(pb))
                        P.op("dve", lambda e, i=i, pb=pb: e.tensor_copy(out=rv[:, i, :], in_=PB(pb)[:, 256:384]),
                             writes=PK(pb) + [("rv", i)])
                        P.op("act", lambda e, i=i, pb=pb: e.activation(out=srg[:, i, :], in_=PB(pb)[:, 384:512], func=AF.Silu),
                             writes=PK(pb) + [("srg", i)])
                    s5 = ps[:, hb[0]:hb[0] + 2, 0:256].rearrange("p b (g t f) -> p b g t f", g=4, t=2)
                    cosb = cst[:, 0, 2 * half:2 * half + 2, :].unsqueeze(2).to_broadcast([128, 2, 4, 32])
                    sinb = cst[:, 1, 2 * half:2 * half + 2, :].unsqueeze(2).to_broadcast([128, 2, 4, 32])
                    PKH = PK(*hb)
                    qh = q5[:, 2 * half:2 * half + 2]
                    P.op("dve", lambda e, s5=s5, cosb=cosb: e.tensor_tensor(out=rtmp[0], in0=s5[:, :, :, 0, :], in1=cosb, op=ALU.mult),
                         reads=[("cs", bi, 0)], writes=PKH + [("rtmp", 0)])
                    P.op("dve", lambda e, s5=s5, sinb=sinb: e.tensor_tensor(out=rtmp[1], in0=s5[:, :, :, 1, :], in1=sinb, op=ALU.mult),
                         reads=[("cs", bi, 1)], writes=PKH + [("rtmp", 1)])
                    P.op("dve", lambda e, s5=s5, sinb=sinb: e.tensor_tensor(out=rtmp[2], in0=s5[:, :, :, 0, :], in1=sinb, op=ALU.mult),
                         reads=[("cs", bi, 1)], writes=PKH + [("rtmp", 2)])
                    P.op("dve", lambda e, s5=s5, cosb=cosb: e.tensor_tensor(out=rtmp[3], in0=s5[:, :, :, 1, :], in1=cosb, op=ALU.mult),
                         reads=[("cs", bi, 0)], writes=PKH + [("rtmp", 3)])
                    P.op("pool", lambda e, qh=qh: e.tensor_tensor(out=qh[:, :, :, 0, :], in0=rtmp[0], in1=rtmp[1], op=ALU.subtract),
                         reads=[("rtmp", 0), ("rtmp", 1)], writes=[("qk", bi, half, 0)])
                    P.op("pool", lambda e, qh=qh: e.tensor_tensor(out=qh[:, :, :, 1, :], in0=rtmp[2], in1=rtmp[3], op=ALU.add),
                         reads=[("rtmp", 2), ("rtmp", 3)], writes=[("qk", bi, half, 1)])

            def blockA(b0):
                bi = (b0 // BLK) % 2
                qk_, vf_ = qk_tm[bi], Vfb[bi]
                qkk = [("qk", bi, half, w_) for half in range(2) for w_ in range(2)]
                for hh in range(2):
                    for d_ in range(2):
                        P.op("act", lambda e, hh=hh, d_=d_, vf_=vf_: e.activation(
                            out=vf_[:, :, hh, d_, :], in_=rv[:, b0:b0 + BLK, hh * 64:(hh + 1) * 64], func=AF.Copy,
                            scale=kw[:, 2 * hp + hh, d_:d_ + 1]),
                            reads=[("rv", b0 + ii) for ii in range(BLK)] + ["kw"], writes=[("Vfb", bi, hh, d_)])
                vfk = [("Vfb", bi, hh, d_) for hh in range(2) for d_ in range(2)]
                for which, slab, sk in ((0, slabQ, "slabQ"), (1, slabK, "slabK")):
                    pbt = 6 + which
                    pbv = PB(pbt).bitcast(BF16)

                    def tr(e, which=which, pbv=pbv, qk_=qk_):
                        for ii in range(BLK):
                            ins = e.transpose(out=pbv[:, ii * 128:(ii + 1) * 128], in_=qk_[:, ii, which * 128:(which + 1) * 128], identity=identb)
                        return ins
                    P.op("pe", tr, reads=qkk + ["identb"], writes=PK(pbt))
                    if which == 0:
                        P.op("act", lambda e, pbv=pbv, slab=slab: e.copy(out=slab[:, b0 * 128:(b0 + BLK) * 128], in_=pbv[:, 0:BLK * 128]),
                             writes=PK(pbt) + [(sk, b0 // BLK)])
                    else:
                        P.op("dve", lambda e, pbv=pbv, slab=slab: e.tensor_copy(out=slab[:, b0 * 128:(b0 + BLK) * 128], in_=pbv[:, 0:BLK * 128]),
                             writes=PK(pbt) + [(sk, b0 // BLK)])
                pbd = (b0 // BLK) % 2

                def mm(e, pbd=pbd, qk_=qk_, vf_=vf_):
                    for ii in range(BLK):
                        for hh in range(2):
                            ins = e.matmul(PB(pbd)[hh * 64:(hh + 1) * 64, ii * 128:(ii + 1) * 128],
                                           lhsT=qk_[:, ii, 128 + hh * 64:128 + (hh + 1) * 64],
                                           rhs=vf_[:, ii, hh, :, :].rearrange("p a b -> p (a b)"), start=True, stop=True)
                    return ins
                P.op("pe", mm, reads=qkk + vfk, writes=PK(pbd))
                pv = PB(pbd).rearrange("p (b d f) -> p b d f", d=2, f=64)
                lo, hi = b0, min(b0 + BLK, NT - 1)
                if hi > lo:
                    P.op("act", lambda e, lo=lo, hi=hi, pv=pv: e.copy(out=DS[:, lo + 1:hi + 1, 0, :], in_=pv[:, lo - b0:hi - b0, 0, :]),
                         writes=PK(pbd) + [("DS", c + 1, 0) for c in range(lo, hi)])
                lo2, hi2 = max(b0, 1), b0 + BLK
                if hi2 > lo2:
                    P.op("dve", lambda e, lo2=lo2, hi2=hi2, pv=pv: e.tensor_copy(out=DS[:, lo2 - 1:hi2 - 1, 1, :], in_=pv[:, lo2 - b0:hi2 - b0, 1, :]),
                         writes=PK(pbd) + [("DS", c - 1, 1) for c in range(lo2, hi2)])
            frontA(0)
            for b0 in range(0, NT, BLK):
                if b0 + BLK < NT:
                    frontA(b0 + BLK)
                blockA(b0)
            if upto == "p2a":
                raise _Stop()
            def nvproj(nb):
                pbp = 6 + (nb % 2)

                def mm(e):
                    for ii in range(4):
                        i = nb * 4 + ii
                        for k in range(KC):
                            ins = e.matmul(PB(pbp)[:, ii * 128:(ii + 1) * 128], lhsT=hT[:, k, i * 128:(i + 1) * 128], rhs=wb[:, k, 6, :],
                                           start=(k == 0), stop=(k == KC - 1))
                    return ins
                P.op("pe", mm, reads=wbk, writes=PK(pbp))
                P.op("act", lambda e: e.copy(out=nva[:, nb * 4:(nb + 1) * 4, :, 0:64], in_=PB(pbp).rearrange("p (i h d) -> p i h d", h=2, d=64)),
                     reads=["nva_init"], writes=PK(pbp) + [("nva", nb)])
            for nb in range(NB):
                nvproj(nb)
            for c in range(NT - 1):
                P.op("dve", lambda e, c=c: e.scalar_tensor_tensor(out=DS[:, c + 1, 0, :], in0=DS[:, c, 0, :], scalar=GL[:, 2 * hp:2 * hp + 1],
                                                                  in1=DS[:, c + 1, 0, :], op0=ALU.mult, op1=ALU.add),
                     reads=[("DS", c, 0), "GL"], writes=[("DS", c + 1, 0)])
            for c in range(NT - 1, 0, -1):
                P.op("dve", lambda e, c=c: e.scalar_tensor_tensor(out=DS[:, c - 1, 1, :], in0=DS[:, c, 1, :], scalar=GL[:, 2 * hp + 1:2 * hp + 2],
                                                                  in1=DS[:, c - 1, 1, :], op0=ALU.mult, op1=ALU.add),
                     reads=[("DS", c, 1), "GL"], writes=[("DS", c - 1, 1)])
            P.op("dve", lambda e: e.tensor_copy(out=Rb, in_=DS), reads=[("DS", c, d_) for c in range(NT) for d_ in range(2)], writes=["Rb"])
            if f"Rb{hp}" in dbg:
                P.dma("sp", dbg[f"Rb{hp}"], Rb, reads=["Rb"])

            if upto == "p2scan":
                raise _Stop()
            def blockB(g0):
                gi = (g0 // 4) % 2
                pbo = [2 + gi, 4 + gi]
                yt = ytile[gi]
                qf = QfbT[gi]
                sd = SDT[gi]
                sq, ms, on = sqA[gi], msA[gi], onA[gi]
                P.op("pool", lambda e, qf=qf: e.tensor_tensor(
                    out=qf, in0=slabQ[:, g0 * 128:(g0 + 4) * 128].rearrange("p (c i) -> p c i", c=4).unsqueeze(1).to_broadcast([128, 2, 4, 128]),
                    in1=QW.unsqueeze(2).to_broadcast([128, 2, 4, 128]), op=ALU.mult),
                    reads=[("slabQ", g0 // BLK), "QW"], writes=[("QfbT", gi)])
                for hh in range(2):
                    def mm(e, hh=hh):
                        for cc in range(4):
                            c = g0 + cc
                            ins = e.matmul(PB(hh)[:, cc * 128:(cc + 1) * 128], lhsT=slabK[hh * 64:(hh + 1) * 64, c * 128:(c + 1) * 128],
                                           rhs=slabQ[hh * 64:(hh + 1) * 64, c * 128:(c + 1) * 128], start=True, stop=True)
                        return ins
                    P.op("pe", mm, reads=[("slabQ", g0 // BLK), ("slabK", g0 // BLK)], writes=PK(hh))
                    P.op("dve", lambda e, hh=hh, sd=sd: e.tensor_tensor(
                        out=sd[:, hh, :, :], in0=PB(hh).rearrange("p (c i) -> p c i", c=4),
                        in1=DT[:, 2 * hp + hh, :].unsqueeze(1).to_broadcast([128, 4, 128]), op=ALU.mult),
                        reads=["DT"], writes=PK(hh) + [("SDT", gi, hh)])
                for hh in range(2):
                    def mm(e, hh=hh, sd=sd, qf=qf):
                        for cc in range(4):
                            c = g0 + cc
                            o_ = PB(pbo[hh])[:, cc * 64:(cc + 1) * 64]
                            e.matmul(o_, lhsT=sd[:, hh, cc, :], rhs=rv[:, c, hh * 64:(hh + 1) * 64], start=True, stop=False)
                            e.matmul(o_, lhsT=qf[hh * 64:(hh + 1) * 64, 0, cc, :], rhs=Rb[hh * 64:(hh + 1) * 64, c, 0, :], start=False, stop=False)
                            ins = e.matmul(o_, lhsT=qf[hh * 64:(hh + 1) * 64, 1, cc, :], rhs=Rb[hh * 64:(hh + 1) * 64, c, 1, :], start=False, stop=True)
                        return ins
                    P.op("pe", mm, reads=[("SDT", gi, hh), ("QfbT", gi), "Rb"] + [("rv", g0 + cc) for cc in range(4)], writes=PK(pbo[hh]))
            def backB(g0):
                gi = (g0 // 4) % 2
                pbo = [2 + gi, 4 + gi]
                yt = ytile[gi]
                sq, ms, on = sqA[gi], msA[gi], onA[gi]
                for hh in range(2):
                    P.op("act", lambda e, hh=hh: e.activation(out=sq[:, hh * 256:(hh + 1) * 256], in_=PB(pbo[hh])[:, 0:256], func=AF.Square),
                         writes=PK(pbo[hh]) + [("sq", gi, hh)])
                P.op("dve", lambda e: e.tensor_reduce(out=ms, in_=sq.rearrange("p (g f) -> p g f", f=64), axis=AX.X, op=ALU.add),
                     reads=[("sq", gi, 0), ("sq", gi, 1)], writes=[("ms", gi)])
                P.op("act", lambda e: e.activation(out=ms, in_=ms, func=AF.Sqrt, scale=1.0 / 64, bias=EPS), writes=[("ms", gi)])
                P.op("dve", lambda e: e.reciprocal(out=ms, in_=ms), writes=[("ms", gi)])
                for hh in range(2):
                    P.op("dve", lambda e, hh=hh: e.tensor_tensor(
                        out=on[:, hh * 256:(hh + 1) * 256].rearrange("p (g f) -> p g f", f=64),
                        in0=PB(pbo[hh])[:, 0:256].rearrange("p (g f) -> p g f", f=64),
                        in1=ms[:, hh * 4:(hh + 1) * 4].unsqueeze(2).to_broadcast([128, 4, 64]), op=ALU.mult),
                        reads=[("ms", gi)], writes=PK(pbo[hh]) + [("on", gi, hh)])
                    P.op("pool", lambda e, hh=hh: e.tensor_tensor(
                        out=yt[:, :, hh * 64:(hh + 1) * 64], in0=on[:, hh * 256:(hh + 1) * 256].rearrange("p (c f) -> p c f", f=64),
                        in1=srg[:, g0:g0 + 4, hh * 64:(hh + 1) * 64], op=ALU.mult),
                        reads=[("on", gi, hh)] + [("srg", g0 + cc) for cc in range(4)],
                        writes=[("ytile", gi, "h", hh)] + ([("ytile", gi)] + [("ytile", gi, cc) for cc in range(4)] if hh == 1 else []))
                pbt = 6 + gi
                pbv = PB(pbt).bitcast(BF16)

                def tr(e, pbv=pbv, yt=yt):
                    for cc in range(4):
                        ins = e.transpose(out=pbv[:, cc * 128:(cc + 1) * 128], in_=yt[:, cc, :], identity=identb)
                    return ins
                P.op("pe", tr, reads=[("ytile", gi), "identb", ("ytile", gi, "h", 0), ("ytile", gi, "h", 1)] + [("ytile", gi, cc) for cc in range(4)], writes=PK(pbt))
                P.op("act", lambda e, pbv=pbv, gi=gi: e.copy(out=ystage[gi], in_=pbv[:, 0:512]), writes=PK(pbt) + [("ystage", gi)])
                P.dma("sp", yT_d[hp, :, g0 * 128:(g0 + 4) * 128], ystage[gi], reads=[("ystage", gi)], writes=[("yT", 0, hp, g0 // 4)])
            blockB(0)
            for g0 in range(0, NT, 4):
                if g0 + 4 < NT:
                    blockB(g0 + 4)
                backB(g0)

            if upto == "p2b":
                raise _Stop()
            def naproj(nb):
                for which, slab, sk, slot in ((0, slabQ, "slabQ", 4), (1, slabK, "slabK", 5)):
                    pbp = 4 + which

                    def mm(e, nb=nb, slot=slot, pbp=pbp):
                        for k in range(KC):
                            ins = e.matmul(PB(pbp)[:, 0:512], lhsT=wb[:, k, slot, :], rhs=hT[:, k, nb * 512:(nb + 1) * 512],
                                           start=(k == 0), stop=(k == KC - 1))
                        return ins
                    P.op("pe", mm, reads=wbk, writes=PK(pbp))
                    if which == 0:
                        P.op("act", lambda e, nb=nb, pbp=pbp: e.activation(out=slabQ[:, nb * 512:(nb + 1) * 512], in_=PB(pbp), func=AF.Copy, scale=0.125),
                             writes=PK(pbp) + [("slabQ", nb)])
                    else:
                        P.op("dve", lambda e, nb=nb, pbp=pbp: e.tensor_copy(out=slabK[:, nb * 512:(nb + 1) * 512], in_=PB(pbp)),
                             writes=PK(pbp) + [("slabK", nb)])
            for nb in range(NB):
                naproj(nb)
            if hp + 1 < 4:
                load_wb(hp + 1)
            if upto == "p2np":
                raise _Stop()
            def na_front(t, hh):
                lst = per_t[t]
                pi = hh
                pA, pB_ = 2 * pi, 2 * pi + 1
                pt_ = PT[pi]
                nloc = len(lst)
                assert nloc <= 5
                qmb = qm[hh][t % 2]
                P.op("pool", lambda e: e.tensor_copy(out=qmb[hh * 64:(hh + 1) * 64, :], in_=slabQ[hh * 64:(hh + 1) * 64, t * 128:(t + 1) * 128]),
                     reads=[("slabQ", t // 4)], writes=[("qm", hh, t % 2)])

                def mm(e):
                    for m, (u, ty) in enumerate(lst):
                        o_ = (PB(pA)[:, m * 128:(m + 1) * 128] if m < 4 else PB(pB_)[:, 0:128])
                        e.matmul(o_, lhsT=slabK[:, u * 128:(u + 1) * 128], rhs=qmb, start=True, stop=False)
                        ins = e.matmul(o_, lhsT=identb, rhs=BT[:, hh, ty, :], start=False, stop=True)
                    for ct in range(2):
                        ins = e.matmul(PB(pB_)[:, (1 + ct) * 128:(2 + ct) * 128], lhsT=cnkT[:, ct * 128:(ct + 1) * 128],
                                       rhs=qmb, start=True, stop=True)
                    return ins
                kblocks = sorted(set(u // 4 for (u, _) in lst))
                P.op("pe", mm, reads=[("qm", hh, t % 2), "cnkT", ("BT", hh), "identb"] + [("slabK", kb) for kb in kblocks],
                     writes=PK(pA, pB_))
                na4 = min(nloc, 4)
                P.op("act", lambda e: e.activation(out=pt_[:, 0:na4, :], in_=PB(pA)[:, 0:na4 * 128].rearrange("p (m q) -> p m q", q=128), func=AF.Exp),
                     writes=PK(pA) + [("PT", pi, 0)])
                lo_ = 0 if nloc == 5 else 1
                P.op("act", lambda e: e.activation(out=pt_[:, 4 + lo_:7, :], in_=PB(pB_)[:, lo_ * 128:3 * 128].rearrange("p (m q) -> p m q", q=128), func=AF.Exp),
                     writes=PK(pB_) + [("PT", pi, 1)])

            def na_back(t, hh):
                lst = per_t[t]
                pi = hh
                pt_ = PT[pi]
                gi = (t // 4) % 2
                yt = ytile[gi]
                pbo = 4 + (t % 2)
                kblocks = sorted(set(u // 4 for (u, _) in lst))

                def mm(e):
                    o_ = PB(pbo)[:, hh * 66:hh * 66 + 65]
                    for m, (u, ty) in enumerate(lst):
                        slot = m if m < 4 else 4
                        e.matmul(o_, lhsT=pt_[:, slot, :], rhs=nva[:, u, hh, :], start=(m == 0), stop=False)
                    for ct in range(2):
                        ins = e.matmul(o_, lhsT=pt_[:, 5 + ct, :], rhs=cnva[:, ct, hh, :], start=False, stop=(ct == 1))
                    return ins
                P.op("pe", mm, reads=[("PT", pi, 0), ("PT", pi, 1), ("cnva", 0), ("cnva", 1)] + [("nva", kb) for kb in kblocks],
                     writes=PK(pbo))
                if hh == 0:
                    return
                ov = PB(pbo)[:, 0:132].rearrange("p (h f) -> p h f", f=66)
                P.op("dve", lambda e: e.reciprocal(out=rc, in_=ov[:, :, 64]), writes=PK(pbo) + ["rc"])
                P.op("dve", lambda e: e.tensor_tensor(
                    out=yt[:, t % 4, :].rearrange("p (h d) -> p h d", d=64), in0=ov[:, :, 0:64],
                    in1=rc.unsqueeze(2).to_broadcast([128, 2, 64]), op=ALU.mult),
                    reads=["rc"], writes=PK(pbo) + [("ytile", gi, t % 4)])
                if t % 4 == 3:
                    g0 = t - 3
                    pbt = 6 + gi
                    pbv = PB(pbt).bitcast(BF16)

                    def tr(e):
                        for cc in range(4):
                            ins = e.transpose(out=pbv[:, cc * 128:(cc + 1) * 128], in_=yt[:, cc, :], identity=identb)
                        return ins
                    P.op("pe", tr, reads=[("ytile", gi, cc) for cc in range(4)] + [("ytile", gi), "identb"], writes=PK(pbt))
                    P.op("act", lambda e: e.copy(out=ystage[gi], in_=pbv[:, 0:512]), writes=PK(pbt) + [("ystage", gi)])
                    P.dma("sp", yT_d[4 + hp, :, g0 * 128:(g0 + 4) * 128], ystage[gi], reads=[("ystage", gi)], writes=[("yT", 1, hp, g0 // 4)])
            if hp == 0:
                P.dma("pool", wgb, w_in[:, 3584:5632], writes=["wgb"])
                P.dma("pool", wrob, w_ro, writes=["wrob"])
                P.dma("pool", wnob, w_no, writes=["wnob"])
                P.dma("pool", wob, w_o, writes=["wob"])
            if hp == 1:
                P.dma("pool", w1b.rearrange("r (a c) -> (r a) c", a=2), w_ff1.rearrange("r (a c) -> (r a) c", a=2), writes=["w1b"])
            if hp == 2:
                P.dma("pool", w2b, w_ff2, writes=["w2b"])
            units = [(t, hh) for t in range(NT) for hh in range(2)]
            for k_, (t_, hh_) in enumerate(units):
                na_front(t_, hh_)
                if k_ >= 1:
                    na_back(*units[k_ - 1])
            na_back(*units[-1])
        for hp in range(4):
            pair_body(hp)
        P.barrier(scr)
        A.release(m_after_persist)
        A.release(m0)

        if upto == "p2":
            raise _Stop()
        GT = [A.alloc([D], F32) for _ in range(2)]
        m3 = A.mark()
        for gi_ in range(2):
            P.dma("sp", GT[gi_], gt_d[gi_], writes=[("GT", gi_)])

        if upto == "p3p":
            raise _Stop()
        Wg = A.alloc([KC, 2048], BF16)
        Wro = A.alloc([4, D], BF16)
        Wno = A.alloc([4, D], BF16)
        Wo = A.alloc([KC, D], BF16)
        def load3a_weights():
            for q4 in range(4):
                P.dma("sp", Wg[:, :, q4 * 512:(q4 + 1) * 512], wgb[:, q4 * 512:(q4 + 1) * 512].rearrange("(k p) n -> p k n", p=128), writes=[("Wg", q4)])
            P.dma("sp", Wro, wrob.rearrange("(k p) n -> p k n", p=128), writes=["Wro"])
            P.dma("sp", Wno, wnob.rearrange("(k p) n -> p k n", p=128), writes=["Wno"])
            for q2 in range(2):
                P.dma("sp", Wo[:, :, q2 * 512:(q2 + 1) * 512], wob[:, q2 * 512:(q2 + 1) * 512].rearrange("(k p) n -> p k n", p=128), writes=[("Wo", q2)])
        xbA = [A.alloc([4, D], F32) for _ in range(2)]
        junkA = A.alloc([D], BF16)
        tmpA3 = [dict(junk=junkA, junk_key="junkA", ss=A.alloc([1], F32), rstd=A.alloc([1], F32), xn=A.alloc([D], F32), key=("nt3", i_)) for i_ in range(2)]
        hTb = A.alloc([KC, 512], BF16)
        yTbA = [A.alloc([8, 512], BF16) for _ in range(2)]
        sgT = A.alloc([16, 512], F32)
        z1A = [A.alloc([512], F32) for _ in range(2)]
        z2A = [A.alloc([512], F32) for _ in range(2)]
        zT = A.alloc([KC, 512], BF16)
        ssyA = [A.alloc([1], F32) for _ in range(2)]
        rsyA = [A.alloc([1], F32) for _ in range(2)]
        tyA = [A.alloc([D], F32) for _ in range(2)]
        Wgk = [("Wg", q4) for q4 in range(4)]

        def load3a(nb):
            xb = xbA[nb % 2]
            for tt in range(4):
                P.dma("sp", xb[:, tt, :], x[(nb * 4 + tt) * 128:(nb * 4 + tt + 1) * 128, :], writes=[("xbA", nb % 2, tt)])
            P.dma("sp", yTbA[nb % 2], yT_d[:, :, nb * 512:(nb + 1) * 512].rearrange("a p n -> p a n"), writes=[("yTb", nb % 2)])

        def n3a_p1(nb, tt):
            xb = xbA[nb % 2]
            return norm_p1(xb[:, tt, :], ("xbA", nb % 2, tt), tmpA3[tt % 2])

        ss4A = [A.alloc([4], F32) for _ in range(2)]
        rs4A = [A.alloc([4], F32) for _ in range(2)]

        def n3a_sq(nb, tt):
            xb = xbA[nb % 2]
            P.op("act", lambda e: e.activation(out=junkA, in_=xb[:, tt, :], func=AF.Square, accum_out=ss4A[nb % 2][:, tt:tt + 1]),
                 reads=[("xbA", nb % 2, tt)], writes=["junkA", ("ss4", nb % 2, tt)])

        def n3a_rs(nb):
            ss4, rs4 = ss4A[nb % 2], rs4A[nb % 2]
            P.op("act", lambda e: e.activation(out=rs4, in_=ss4, func=AF.Sqrt, scale=1.0 / D, bias=EPS),
                 reads=[("ss4", nb % 2, tt) for tt in range(4)], writes=[("rs4", nb % 2)])
            P.op("dve", lambda e: e.reciprocal(out=rs4, in_=rs4), writes=[("rs4", nb % 2)])

        def n3a_xn(nb, tt):
            xb = xbA[nb % 2]
            tmp = tmpA3[tt % 2]
            xn = tmp["xn"]
            xnk = (tmp["key"], "xn")
            P.op("dve", lambda e: e.tensor_scalar(out=xn, in0=xb[:, tt, :], scalar1=rs4A[nb % 2][:, tt:tt + 1], scalar2=None, op0=ALU.mult),
                 reads=[("rs4", nb % 2), ("xbA", nb % 2, tt)], writes=[xnk])
            return xnk

        def n3a_p2(nb, tt, xnk):
            norm_p2(xnk, (lambda c: hTb[:, c, tt * 128:(tt + 1) * 128]), [("hTb", c, tt) for c in range(KC)], S1, SH1, 0, (6, 7), tmpA3[tt % 2])

        def norm3a(nb):
            for tt in range(4):
                n3a_p2(nb, tt, n3a_p1(nb, tt))

        def blk3a(nb):
            xb = xbA[nb % 2]
            yTb = yTbA[nb % 2]
            if nb + 1 < NB:
                load3a(nb + 1)
            hkeys = [("hTb", c, tt) for c in range(KC) for tt in range(4)]
            xnks = {}
            for g in range(16):
                pb = g % 2

                def mm(e, g=g, pb=pb):
                    for k in range(KC):
                        ins = e.matmul(PB(pb), lhsT=Wg[:, k, g * 128:(g + 1) * 128], rhs=hTb[:, k, :], start=(k == 0), stop=(k == KC - 1))
                    return ins
                P.op("pe", mm, reads=hkeys + [("Wg", g // 4)], writes=PK(pb))
                P.op("act", lambda e, g=g, pb=pb: e.activation(out=sgT[:, g, :], in_=PB(pb), func=AF.Sigmoid), writes=PK(pb) + [("sgT", g)])
                if nb + 1 < NB:
                    if g in (2, 4, 6, 8):
                        n3a_sq(nb + 1, (g - 2) // 2)
                    elif g == 10:
                        n3a_rs(nb + 1)
                    elif g == 12:
                        xnks[0] = n3a_xn(nb + 1, 0)
                        xnks[1] = n3a_xn(nb + 1, 1)
            for fc in range(KC):
                pa, pbb = 2 + fc % 2, 4 + fc % 2
                z1, z2 = z1A[fc % 2], z2A[fc % 2]

                def mm(e, fc=fc, pa=pa):
                    for k in range(4):
                        ins = e.matmul(PB(pa), lhsT=Wro[:, k, fc * 128:(fc + 1) * 128], rhs=yTb[:, k, :], start=(k == 0), stop=(k == 3))
                    return ins
                P.op("pe", mm, reads=[("yTb", nb % 2), "Wro"], writes=PK(pa))

                def mm(e, fc=fc, pbb=pbb):
                    for k in range(4):
                        ins = e.matmul(PB(pbb), lhsT=Wno[:, k, fc * 128:(fc + 1) * 128], rhs=yTb[:, 4 + k, :], start=(k == 0), stop=(k == 3))
                    return ins
                P.op("pe", mm, reads=[("yTb", nb % 2), "Wno"], writes=PK(pbb))
                P.op("dve", lambda e, fc=fc, pa=pa, z1=z1: e.tensor_tensor(out=z1, in0=PB(pa), in1=sgT[:, fc, :], op=ALU.mult),
                     reads=[("sgT", fc)], writes=PK(pa) + [("z1", fc % 2)])
                P.op("dve", lambda e, fc=fc, pbb=pbb, z2=z2: e.tensor_tensor(out=z2, in0=PB(pbb), in1=sgT[:, 8 + fc, :], op=ALU.mult),
                     reads=[("sgT", 8 + fc)], writes=PK(pbb) + [("z2", fc % 2)])
                P.op("pool", lambda e, fc=fc, z1=z1, z2=z2: e.tensor_tensor(out=zT[:, fc, :], in0=z1, in1=z2, op=ALU.add),
                     reads=[("z1", fc % 2), ("z2", fc % 2)], writes=[("zT", fc)])
                if nb + 1 < NB and fc % 2 == 1:
                    tt_ = fc // 2
                    n3a_p2(nb + 1, tt_, xnks[tt_])
                    if tt_ + 2 < 4:
                        xnks[tt_ + 2] = n3a_xn(nb + 1, tt_ + 2)
            for tt in range(4):
                i = nb * 4 + tt
                py0 = 6 - 2 * (tt % 2)
                ssy, rsy, ty = ssyA[tt % 2], rsyA[tt % 2], tyA[tt % 2]

                def mm(e, tt=tt, py0=py0):
                    for half in range(2):
                        for k in range(KC):
                            ins = e.matmul(PB(py0 + half), lhsT=zT[:, k, tt * 128:(tt + 1) * 128], rhs=Wo[:, k, half * 512:(half + 1) * 512],
                                           start=(k == 0), stop=(k == KC - 1))
                    return ins
                P.op("pe", mm, reads=[("zT", fc) for fc in range(KC)] + [("Wo", 0), ("Wo", 1)], writes=PK(py0, py0 + 1))
                jk = tmpA3[tt % 2]
                P.op("act", lambda e, py0=py0, jk=jk, ssy=ssy: e.activation(out=jk["junk"].rearrange("p (b n) -> p b n", b=2), in_=ps[:, py0:py0 + 2, :], func=AF.Square, accum_out=ssy),
                     writes=PK(py0, py0 + 1) + ["junkA", ("ssy", tt % 2)])
                P.op("act", lambda e, ssy=ssy, rsy=rsy: e.activation(out=rsy, in_=ssy, func=AF.Sqrt, scale=1.0 / D, bias=EPS),
                     reads=[("ssy", tt % 2)], writes=[("rsy", tt % 2)])
                P.op("dve", lambda e, rsy=rsy: e.reciprocal(out=rsy, in_=rsy), writes=[("rsy", tt % 2)])
                P.op("dve", lambda e, py0=py0, rsy=rsy, ty=ty: e.scalar_tensor_tensor(
                    out=ty.rearrange("p (b n) -> p b n", b=2), in0=ps[:, py0:py0 + 2, :], scalar=rsy[:, 0:1],
                    in1=GT[0].rearrange("p (b n) -> p b n", b=2), op0=ALU.mult, op1=ALU.mult),
                    reads=[("rsy", tt % 2), ("GT", 0)], writes=PK(py0, py0 + 1) + [("ty", tt % 2)])
                P.op("pool", lambda e, tt=tt, ty=ty, xb=xb: e.tensor_tensor(out=ty, in0=ty, in1=xb[:, tt, :], op=ALU.add),
                     reads=[("xbA", nb % 2, tt)], writes=[("ty", tt % 2)])
                P.dma("sp", out[i * 128:(i + 1) * 128, :], ty, reads=[("ty", tt % 2)], writes=[("x1d", i)])
        load3a(0)
        load3a_weights()
        norm3a(0)
        for nb in range(NB):
            blk3a(nb)
        P.barrier(scr)
        A.release(m3)

        if upto == "p3a":
            raise _Stop()
        W1 = A.alloc([KC, 4 * D], BF16)
        W2 = A.alloc([32, D], BF16)
        def load3b_weights():
            for q8 in range(8):
                P.dma("sp", W1[:, :, q8 * 512:(q8 + 1) * 512], w1b[:, q8 * 512:(q8 + 1) * 512].rearrange("(k p) n -> p k n", p=128), writes=[("W1", q8)])
            for q8 in range(8):
                P.dma("sp", W2[:, q8 * 4:(q8 + 1) * 4, :], w2b[q8 * 512:(q8 + 1) * 512, :].rearrange("(k p) n -> p k n", p=128), writes=[("W2", q8)])
        xtB = [A.alloc([D], F32) for _ in range(2)]
        xrB = A.alloc([D], F32)
        rl = [A.alloc([512], F32) for _ in range(2)]
        h2TB = [A.alloc([KC, 512], BF16) for _ in range(2)]
        uT = A.alloc([32, 512], BF16)
        ssB = [A.alloc([1], F32) for _ in range(2)]
        rstdB = [A.alloc([1], F32) for _ in range(2)]
        ssyB = [A.alloc([1], F32) for _ in range(2)]
        rsyB = [A.alloc([1], F32) for _ in range(2)]
        tmpB3 = [dict(junk=rl[i_].bitcast(BF16), junk_key=("rl", i_), ss=ssB[i_], rstd=rstdB[i_], xn=xtB[i_], key=("nt4", i_)) for i_ in range(2)]
        W2k = [("W2", q8) for q8 in range(8)]

        def n3b_p1(nb, tt):
            i = nb * 4 + tt
            bi = i % 2
            P.dma("sp", xtB[bi], out[i * 128:(i + 1) * 128, :], reads=[("x1d", i)], writes=[("xtB", bi)])
            return norm_p1(xtB[bi], ("xtB", bi), tmpB3[bi], inplace=True)

        def n3b_p2(nb, tt, xnk):
            i = nb * 4 + tt
            h2T = h2TB[nb % 2]
            norm_p2(xnk, (lambda c: h2T[:, c, tt * 128:(tt + 1) * 128]), [("h2T", nb % 2, c, tt) for c in range(KC)], S2, SH2, 0, (6, 7), tmpB3[i % 2])

        def blk3b(nb):
            h2T = h2TB[nb % 2]
            hkeys = [("h2T", nb % 2, c, tt) for c in range(KC) for tt in range(4)]
            nxt = nb + 1 < NB
            xnks = {}
            for j in range(32):
                pb = j % 2

                def mm(e, j=j, pb=pb):
                    for k in range(KC):
                        ins = e.matmul(PB(pb), lhsT=W1[:, k, j * 128:(j + 1) * 128], rhs=h2T[:, k, :], start=(k == 0), stop=(k == KC - 1))
                    return ins
                P.op("pe", mm, reads=hkeys + [("W1", j // 4)], writes=PK(pb))
                P.op("act", lambda e, pb=pb: e.activation(out=rl[pb], in_=PB(pb), func=AF.Relu), writes=PK(pb) + [("rl", pb)])
                P.op("dve" if j % 2 == 0 else "pool", lambda e, j=j, pb=pb: e.tensor_tensor(out=uT[:, j, :], in0=rl[pb], in1=rl[pb], op=ALU.mult),
                     reads=[("rl", pb)], writes=[("uT", j)])
                if nxt:
                    if j == 3:
                        xnks[0] = n3b_p1(nb + 1, 0)
                    elif j == 7:
                        xnks[1] = n3b_p1(nb + 1, 1)
                    elif j == 15:
                        n3b_p2(nb + 1, 0, xnks[0])
                        xnks[2] = n3b_p1(nb + 1, 2)
                    elif j == 21:
                        n3b_p2(nb + 1, 1, xnks[1])
                        xnks[3] = n3b_p1(nb + 1, 3)
                    elif j == 27:
                        n3b_p2(nb + 1, 2, xnks[2])
                    elif j == 31:
                        n3b_p2(nb + 1, 3, xnks[3])
            for tt in range(4):
                i = nb * 4 + tt
                pbm = 2 + 2 * (tt % 2)
                ssy, rsy = ssyB[tt % 2], rsyB[tt % 2]
                P.dma("sp", xrB, out[i * 128:(i + 1) * 128, :], reads=[("x1d", i)], writes=["xrB"])

                def mm(e, tt=tt, pbm=pbm):
                    for half in range(2):
                        for j in range(32):
                            ins = e.matmul(PB(pbm + half), lhsT=uT[:, j, tt * 128:(tt + 1) * 128], rhs=W2[:, j, half * 512:(half + 1) * 512],
                                           start=(j == 0), stop=(j == 31))
                    return ins
                P.op("pe", mm, reads=[("uT", j) for j in range(32)] + W2k, writes=PK(pbm, pbm + 1))
                jb = tt % 2
                P.op("act", lambda e, pbm=pbm, jb=jb, ssy=ssy: e.activation(out=rl[jb].bitcast(BF16).rearrange("p (b n) -> p b n", b=2), in_=ps[:, pbm:pbm + 2, :], func=AF.Square, accum_out=ssy),
                     writes=PK(pbm, pbm + 1) + [("rl", jb), ("ssyB", tt % 2)])
                P.op("act", lambda e, ssy=ssy, rsy=rsy: e.activation(out=rsy, in_=ssy, func=AF.Sqrt, scale=1.0 / D, bias=EPS),
                     reads=[("ssyB", tt % 2)], writes=[("rsyB", tt % 2)])
                P.op("dve", lambda e, rsy=rsy: e.reciprocal(out=rsy, in_=rsy), writes=[("rsyB", tt % 2)])
                P.op("dve", lambda e, pbm=pbm, rsy=rsy: e.scalar_tensor_tensor(
                    out=ps[:, pbm:pbm + 2, :], in0=ps[:, pbm:pbm + 2, :], scalar=rsy[:, 0:1],
                    in1=GT[1].rearrange("p (b n) -> p b n", b=2), op0=ALU.mult, op1=ALU.mult),
                    reads=[("rsyB", tt % 2), ("GT", 1)], writes=PK(pbm, pbm + 1))
                P.op("dve", lambda e, pbm=pbm: e.tensor_tensor(out=xrB.rearrange("p (b n) -> p b n", b=2), in0=ps[:, pbm:pbm + 2, :],
                                                               in1=xrB.rearrange("p (b n) -> p b n", b=2), op=ALU.add),
                     writes=PK(pbm, pbm + 1) + ["xrB"])
                P.dma("sp", out[i * 128:(i + 1) * 128, :], xrB, reads=["xrB"], writes=[("outd", i)])
        for tt in range(4):
            n3b_p2(0, tt, n3b_p1(0, tt))
        load3b_weights()
        for nb in range(NB):
            blk3b(nb)
    try:
        body()
    except _Stop:
        pass
    info = P.emit()
    info["arena_peak"] = A.peak
    cmp_.__exit__(None, None, None)
    cm.__exit__(None, None, None)
    return nc, info, types


def prep_inputs(inputs, SEQ, types):
    NT = SEQ // 128
    f = lambda a: np.ascontiguousarray(np.asarray(a, dtype=np.float32))
    x = f(inputs["x"]); c = f(inputs["c"]); ctx = f(inputs["ctx"]); c_ctx = f(inputs["c_ctx"])
    B = x.shape[0]
    w_ada = f(inputs["w_ada"][0]); b_ada = f(inputs["b_ada"][0])
    shared = dict(
        w_ada=w_ada,
        bada_fm=np.ascontiguousarray(b_ada.reshape(48, 128).T),
        b_ada=b_ada.reshape(1, -1),
        gpre_fm=np.ascontiguousarray(np.stack([f(inputs["norm_pre_mix"][0]).reshape(KC, 128).T,
                                               f(inputs["norm_pre_ffn"][0]).reshape(KC, 128).T], axis=1)),
        gpost=np.ascontiguousarray(np.stack([f(inputs["norm_post_mix"][0]), f(inputs["norm_post_ffn"][0])], axis=0)),
        w_in=f(inputs["w_in"][0]), w_ro=f(inputs["w_ret_out"][0]), w_no=f(inputs["w_na_out"][0]),
        w_o=f(inputs["w_o"][0]), w_ff1=f(inputs["w_ff1"][0]), w_ff2=f(inputs["w_ff2"][0]),
    )
    lg = f(inputs["ret_decay_logit"][0])
    lgt_pair = np.zeros((128, 8), np.float32)
    def pair_body(hp):
        for d_ in range(2):
            lgt_pair[0:64, hp * 2 + d_] = lg[d_, 2 * hp]
            lgt_pair[64:128, hp * 2 + d_] = lg[d_, 2 * hp + 1]
    for hp in range(4):
        pair_body(hp)
    lgt_bc = np.zeros((128, 16), np.float32)
    for h in range(8):
        for d_ in range(2):
            lgt_bc[:, 2 * h + d_] = lg[d_, h]
    shared["lgt_pair"] = lgt_pair
    shared["lgt_bc"] = lgt_bc
    shared["ident"] = np.eye(128, dtype=np.float32)
    j = np.arange(128)[:, None].astype(np.float32)
    i = np.arange(128)[None, :].astype(np.float32)
    cmat = np.stack([np.maximum(i - j, 0), np.maximum(j - i, 0), (i >= j) * 0.125, (j > i) * 0.125], axis=1).astype(np.float32)
    shared["cmat"] = np.ascontiguousarray(cmat)
    jj = np.arange(128, dtype=np.float32)
    shared["colc"] = np.ascontiguousarray(np.stack([127 - jj, jj, 255 - jj, 127 - jj, jj, 128 + jj], axis=1))
    ii = np.arange(128, dtype=np.float32)
    shared["rowc"] = np.ascontiguousarray(np.broadcast_to(np.stack([ii + 1, 128 - ii], axis=0)[None], (128, 2, 128)).astype(np.float32))
    cos, sin = rope_tables(SEQ)
    shared["cos_tm"] = np.ascontiguousarray(cos.reshape(NT, 128, 32).transpose(1, 0, 2))
    shared["sin_tm"] = np.ascontiguousarray(sin.reshape(NT, 128, 32).transpose(1, 0, 2))
    mask, idr, idc = na_consts(types)
    rpb = f(inputs["na_rpb"][0])
    rpbB = rpb[:, idr, idc]
    shared["rpbB"] = np.ascontiguousarray(rpbB.transpose(0, 2, 1, 3))
    shared["maskB"] = np.ascontiguousarray(mask.transpose(1, 0, 2))
    in_maps = []
    for b in range(B):
        m = dict(shared)
        m["x"] = x[b]
        m["ctx"] = ctx[b]
        m["c_fm"] = np.ascontiguousarray(np.stack([c[b].reshape(KC, 128).T, c_ctx.reshape(KC, 128).T], axis=2))
        in_maps.append(m)
    return in_maps


_CACHE = {}


def kernel(**inputs):
    x = inputs["x"]
    B, SEQ, _ = x.shape
    if SEQ not in _CACHE:
        _CACHE[SEQ] = build(SEQ)
    nc, info, types = _CACHE[SEQ]
    in_maps = prep_inputs(inputs, SEQ, types)
    res = run_bass_kernel_spmd(nc, in_maps, core_ids=list(range(B)))
    return np.stack([np.asarray(r["out"], dtype=np.float32) for r in res.results], axis=0)
```

```python
import numpy as np
import ml_dtypes
import concourse.bass as bass
import concourse.mybir as mybir
from concourse.bass_utils import run_bass_kernel_spmd

F32 = mybir.dt.float32
BF16 = mybir.dt.bfloat16
U8 = mybir.dt.uint8
AF = mybir.ActivationFunctionType
ALU = mybir.AluOpType
AX = mybir.AxisListType

D = 1024
KC = 8
CTX = 256
GRID_W = 64
EPS = 1e-6
CH = 4096


class Prog:
    ENGS = ("pe", "act", "dve", "pool", "sp")

    def __init__(self, nc, n_dma_sems=16):
        self.nc = nc
        self.ops = []
        self.last_w = {}
        self.readers = {}
        self.n_dma_sems = n_dma_sems
        self.pending = {e: set() for e in self.ENGS}
        self.bar_start = 0
        self.nbar = 0

    def op(self, eng, fn, reads=(), writes=(), dma=False):
        oid = len(self.ops)
        deps = set()
        for k in list(reads) + list(writes):
            if k in self.last_w:
                deps.add(self.last_w[k])
        for k in writes:
            for r in self.readers.get(k, ()):
                deps.add(r)
        deps |= self.pending[eng]
        self.pending[eng] = set()
        deps.discard(oid)
        self.ops.append(dict(eng=eng, fn=fn, deps=deps, dma=dma, has_dep=False))
        for k in reads:
            self.readers.setdefault(k, []).append(oid)
        for k in writes:
            self.last_w[k] = oid
            self.readers[k] = []
        return oid

    def dma(self, q, out, in_, reads=(), writes=(), **kw):
        def fn(e):
            return e.dma_start(out=out, in_=in_, **kw)
        return self.op(q, fn, reads, writes, dma=True)

    def barrier(self, scratch):
        n = self.nbar
        self.nbar += 1
        dmas = [i for i in range(self.bar_start, len(self.ops)) if self.ops[i]["dma"]]
        marks = []
        marks.append(self.op("act", lambda e: e.copy(out=scratch["act"], in_=scratch["act"]), writes=[("bar", n, "act")]))
        marks.append(self.op("dve", lambda e: e.memset(scratch["dve"], 0.0), writes=[("bar", n, "dve")]))
        marks.append(self.op("pool", lambda e: e.memset(scratch["pool"], 0.0), writes=[("bar", n, "pool")]))
        for e in self.ENGS:
            self.pending[e] = set(marks) | set(dmas)
        self.last_w = {}
        self.readers = {}
        self.bar_start = len(self.ops)

    def emit(self, final_wait_eng="sp"):
        nc = self.nc
        ops = self.ops
        for i, o in enumerate(ops):
            keep = set()
            for d in o["deps"]:
                od = ops[d]
                if (not od["dma"]) and od["eng"] == o["eng"] and o["eng"] == "pe" and not o["dma"]:
                    continue
                keep.add(d)
            o["deps"] = keep
            for d in keep:
                ops[d]["has_dep"] = True
        tail = [i for i, o in enumerate(ops) if o["dma"] and not o["has_dep"]]
        for i in tail:
            ops[i]["has_dep"] = True
        cnt = {e: 0 for e in self.ENGS}
        for o in ops:
            if not o["dma"] and o["has_dep"]:
                o["seq"] = cnt[o["eng"]]
                cnt[o["eng"]] += 1
        sems = {}
        for e in self.ENGS:
            n = (cnt[e] + CH - 1) // CH
            sems[e] = [nc.alloc_semaphore(name=f"s_{e}_{j}") for j in range(n)]
        dsems = [nc.alloc_semaphore(name=f"s_dma_{j}") for j in range(self.n_dma_sems)]
        dcount = [0] * self.n_dma_sems
        dnext = 0
        waited = {e: {} for e in self.ENGS}

        def plan_wait(o, e, sem, val):
            key = id(sem)
            if waited[e].get(key, 0) >= val:
                return
            waited[e][key] = val
            o["waits"].append((sem, val))

        for i, o in enumerate(ops):
            e = o["eng"]
            o["waits"] = []
            for d in sorted(o["deps"]):
                od = ops[d]
                if od["dma"]:
                    plan_wait(o, e, od["dsem"], od["dval"])
                else:
                    s = od["seq"]
                    plan_wait(o, e, sems[od["eng"]][s // CH], s % CH + 1)
            if o["dma"] and e == "pool":
                sw = nc.alloc_semaphore(name=f"s_swdma_{i}")
                o["dsem"] = sw
                o["dval"] = 16
                o["inc"] = (sw, 16)
            elif o["dma"]:
                j = dnext
                dnext = (dnext + 1) % self.n_dma_sems
                if dcount[j] > 0:
                    plan_wait(o, e, dsems[j], dcount[j])
                dcount[j] += 16
                o["dsem"] = dsems[j]
                o["dval"] = dcount[j]
                o["inc"] = (dsems[j], 16)
            elif o["has_dep"]:
                s = o["seq"]
                o["inc"] = (sems[e][s // CH], 1)
            else:
                o["inc"] = None
        final_waits = []
        fo = dict(waits=final_waits)
        for i in tail:
            plan_wait(fo, final_wait_eng, ops[i]["dsem"], ops[i]["dval"])

        def run_engine(ename, eng):
            for o in ops:
                if o["eng"] != ename:
                    continue
                for (sem, val) in o["waits"]:
                    eng.wait_ge(sem, val)
                ins = o["fn"](eng)
                if o["inc"] is not None:
                    ins.then_inc(o["inc"][0], o["inc"][1])
            if ename == final_wait_eng:
                for (sem, val) in final_waits:
                    eng.wait_ge(sem, val)

        with nc.Block() as block:
            @block.sync
            def _(eng):
                run_engine("sp", eng)

            @block.tensor
            def _(eng):
                run_engine("pe", eng)

            @block.scalar
            def _(eng):
                run_engine("act", eng)

            @block.vector
            def _(eng):
                run_engine("dve", eng)

            @block.gpsimd
            def _(eng):
                run_engine("pool", eng)
        return dict(n_ops=len(ops), cnt=cnt)


class Arena:
    def __init__(self, ap_u8, size):
        self.ap = ap_u8
        self.size = size
        self.off = 0
        self.peak = 0

    def alloc(self, shape, dtype):
        esz = {F32: 4, BF16: 2}[dtype]
        n = int(np.prod(shape))
        nbytes = (n * esz + 63) // 64 * 64
        assert self.off + nbytes <= self.size, f"arena overflow {self.off}+{nbytes}>{self.size}"
        v = self.ap[:, self.off:self.off + n * esz].bitcast(dtype)
        self.off += nbytes
        self.peak = max(self.peak, self.off)
        if len(shape) == 1:
            return v
        names = " ".join(f"d{i}" for i in range(len(shape)))
        kw = {f"d{i}": int(s) for i, s in enumerate(shape)}
        return v.rearrange(f"p ({names}) -> p {names}", **kw)

    def mark(self):
        return self.off

    def release(self, m):
        self.off = m


def na_structure(rows):
    T = rows // 2
    types = {}
    per_t = []
    for t in range(T):
        lst = []
        for u in range(T):
            vis = []
            anyv = False
            for kr in range(2):
                for qr in range(2):
                    r = 2 * t + qr
                    r0 = min(max(r - 4, 0), rows - 8)
                    v = r0 <= 2 * u + kr < r0 + 8
                    vis.append(v)
                    anyv = anyv or v
            if not anyv:
                continue
            key = (u - t, tuple(vis))
            if key not in types:
                types[key] = len(types)
            lst.append((u, types[key]))
        per_t.append(lst)
    return per_t, types


def na_consts(types):
    nt = len(types)
    mask = np.zeros((nt, 128, 128), np.float32)
    idr = np.zeros((nt, 128, 128), np.int64)
    idc = np.zeros((nt, 128, 128), np.int64)
    kc = np.arange(64)[:, None]
    qc = np.arange(64)[None, :]
    c0 = np.clip(qc - 8, 0, 48)
    colok = (kc >= c0) & (kc < c0 + 16)
    dc = np.clip(kc - qc + 15, 0, 30)
    for (delta, vis), ti in types.items():
        for kr in range(2):
            for qr in range(2):
                v = vis[kr * 2 + qr]
                dr = int(np.clip(2 * delta + kr - qr + 7, 0, 14))
                blk = np.where(colok & v, 0.0, -30000.0).astype(np.float32)
                mask[ti, kr * 64:(kr + 1) * 64, qr * 64:(qr + 1) * 64] = blk
                idr[ti, kr * 64:(kr + 1) * 64, qr * 64:(qr + 1) * 64] = dr
                idc[ti, kr * 64:(kr + 1) * 64, qr * 64:(qr + 1) * 64] = dc
    return mask, idr, idc


def rope_tables(n):
    pos = np.arange(n)
    row = (pos // GRID_W).astype(np.float32)
    col = (pos % GRID_W).astype(np.float32)
    inv = (10000.0 ** (-np.arange(0, 32, 2, dtype=np.float32) / 32)).astype(np.float32)
    ang = np.concatenate([row[:, None] * inv, col[:, None] * inv], axis=-1).astype(np.float32)
    return np.cos(ang).astype(np.float32), np.sin(ang).astype(np.float32)


class _Stop(Exception):
    pass


def build(SEQ, debug=(), upto=None):
    NT = SEQ // 128
    ROWS = SEQ // 64
    NB = SEQ // 512
    per_t, types = na_structure(ROWS)
    NTYPE = len(types)
    nc = bass.Bass("TRN2", target_bir_lowering=False)

    def din(name, shape, dt=F32):
        return nc.dram_tensor(name, list(shape), dt, kind="ExternalInput").ap()

    x = din("x", [SEQ, D])
    ctx = din("ctx", [CTX, D])
    c_fm = din("c_fm", [128, KC, 2])
    w_ada = din("w_ada", [D, 6 * D])
    bada_fm = din("bada_fm", [128, 48])
    b_ada = din("b_ada", [1, 6 * D])
    gpre_fm = din("gpre_fm", [128, 2, KC])
    gpost = din("gpost", [2, D])
    w_in = din("w_in", [D, 5632])
    w_ro = din("w_ro", [512, D])
    w_no = din("w_no", [512, D])
    w_o = din("w_o", [D, D])
    w_ff1 = din("w_ff1", [D, 4 * D])
    w_ff2 = din("w_ff2", [4 * D, D])
    lgt_pair = din("lgt_pair", [128, 8])
    lgt_bc = din("lgt_bc", [128, 16])
    ident_d = din("ident", [128, 128])
    cmat = din("cmat", [128, 4, 128])
    colc_d = din("colc", [128, 6])
    rowc_d = din("rowc", [128, 2, 128])
    cos_d = din("cos_tm", [128, NT, 32])
    sin_d = din("sin_tm", [128, NT, 32])
    rpbB = din("rpbB", [8, 128, NTYPE, 128])
    maskB_d = din("maskB", [128, NTYPE, 128])
    out = nc.dram_tensor("out", [SEQ, D], F32, kind="ExternalOutput").ap()
    yT_d = nc.dram_tensor("yT_scratch", [8, 128, SEQ], BF16, kind="Internal").ap()
    wgb = nc.dram_tensor("wg_bf", [D, 2048], BF16, kind="Internal").ap()
    wrob = nc.dram_tensor("wro_bf", [512, D], BF16, kind="Internal").ap()
    wnob = nc.dram_tensor("wno_bf", [512, D], BF16, kind="Internal").ap()
    wob = nc.dram_tensor("wo_bf", [D, D], BF16, kind="Internal").ap()
    w1b = nc.dram_tensor("w1_bf", [D, 4 * D], BF16, kind="Internal").ap()
    w2b = nc.dram_tensor("w2_bf", [4 * D, D], BF16, kind="Internal").ap()
    gt_d = nc.dram_tensor("gt_scratch", [2, 128, D], F32, kind="Internal").ap()
    dbg = {}
    for name, shape, dt in debug:
        dbg[name] = nc.dram_tensor(name, list(shape), dt, kind="ExternalOutput").ap()

    P = Prog(nc)
    ARENA_BYTES = 207 * 1024
    cm = nc.sbuf_tensor("arena", [128, ARENA_BYTES], U8)
    arena_h = cm.__enter__()
    A = Arena(arena_h, ARENA_BYTES)
    cmp_ = nc.psum_tensor("ps", [128, 8, 512], F32)
    ps = cmp_.__enter__()

    def PB(b):
        return ps[:, b, :]

    def PK(*bs):
        return [("ps", b) for b in bs]

    def body():
        ident = A.alloc([128], F32)
        identb = A.alloc([128], BF16)
        scr = {e: A.alloc([16], F32) for e in ("act", "dve", "pool")}
        S1 = A.alloc([KC, 2], F32)
        SH1 = A.alloc([KC, 2], F32)
        S2 = A.alloc([KC, 2], F32)
        SH2 = A.alloc([KC, 2], F32)
        scb = A.alloc([KC, 2], BF16)
        sc_rep = A.alloc([KC, 128], BF16)
        gpre = A.alloc([2, KC], F32)
        badafm = A.alloc([48], F32)

        P.dma("sp", ident, ident_d, writes=["ident"])
        P.op("dve", lambda e: e.tensor_copy(out=identb, in_=ident), reads=["ident"], writes=["identb"])
        for e_ in ("act", "dve", "pool"):
            pass
        P.op("dve", lambda e: e.memset(scr["dve"], 0.0), writes=["scr_dve"])
        P.op("pool", lambda e: e.memset(scr["pool"], 0.0), writes=["scr_pool"])
        P.op("dve", lambda e: e.memset(scr["act"], 0.0), writes=["scr_act"])
        P.dma("sp", gpre, gpre_fm, writes=["gpre"])
        P.dma("sp", badafm, bada_fm, writes=["badafm"])

        m_phase2 = None

        def norm_p1(xt_ap, xt_key, tmp, inplace=False):
            junk, ss, rstd, xn = tmp["junk"], tmp["ss"], tmp["rstd"], tmp["xn"]
            tk = tmp["key"]
            jkey = tmp.get("junk_key", (tk, "junk"))
            P.op("act", lambda e: e.activation(out=junk, in_=xt_ap, func=AF.Square, accum_out=ss),
                 reads=[xt_key], writes=[jkey, (tk, "ss")])
            P.op("act", lambda e: e.activation(out=rstd, in_=ss, func=AF.Sqrt, scale=1.0 / D, bias=EPS),
                 reads=[(tk, "ss")], writes=[(tk, "rstd")])
            P.op("dve", lambda e: e.reciprocal(out=rstd, in_=rstd), writes=[(tk, "rstd")])
            xnk = xt_key if inplace else (tk, "xn")
            P.op("dve", lambda e: e.tensor_scalar(out=xn, in0=xt_ap, scalar1=rstd[:, 0:1], scalar2=None, op0=ALU.mult),
                 reads=[(tk, "rstd")] + ([] if inplace else [xt_key]), writes=[xnk])
            return xnk

        def norm_p2(xnk, dst_fn, dst_keys, Sc, Sh, col, banks, tmp):
            xn = tmp["xn"]
            for half in range(2):
                b = banks[half]

                def tr(e, half=half, b=b):
                    for cc in range(4):
                        c = half * 4 + cc
                        ins = e.transpose(out=PB(b)[:, cc * 128:(cc + 1) * 128], in_=xn[:, c * 128:(c + 1) * 128], identity=ident)
                    return ins
                P.op("pe", tr, reads=[xnk, "ident"], writes=PK(b))
                for cc in range(4):
                    c = half * 4 + cc
                    if half == 0:
                        P.op("act", lambda e, c=c, cc=cc, b=b: e.activation(
                            out=dst_fn(c), in_=PB(b)[:, cc * 128:(cc + 1) * 128], func=AF.Identity,
                            scale=Sc[:, c, col:col + 1], bias=Sh[:, c, col:col + 1]),
                            reads=["mod"] + PK(b), writes=[dst_keys[c]])
                    else:
                        P.op("dve", lambda e, c=c, cc=cc, b=b: e.tensor_scalar(
                            out=dst_fn(c), in0=PB(b)[:, cc * 128:(cc + 1) * 128],
                            scalar1=Sc[:, c, col:col + 1], scalar2=Sh[:, c, col:col + 1], op0=ALU.mult, op1=ALU.add),
                            reads=["mod"] + PK(b), writes=[dst_keys[c]])

        def norm_transpose(xt_ap, xt_key, dst_fn, dst_keys, Sc, Sh, col, banks, tmp, tag, inplace=False):
            xnk = norm_p1(xt_ap, xt_key, tmp, inplace)
            norm_p2(xnk, dst_fn, dst_keys, Sc, Sh, col, banks, tmp)

        m0 = A.mark()
        cm_t = A.alloc([4, 128], F32)
        colc = A.alloc([6], F32)
        rowc = A.alloc([2, 128], F32)
        lgp = A.alloc([8], F32)
        lgb = A.alloc([16], F32)
        cfm = A.alloc([KC, 2], F32)
        A.release(m0)
        DT = A.alloc([8, 128], F32)
        kw = A.alloc([8, 2], F32)
        ckw = A.alloc([2, 8, 2], F32)
        QW = A.alloc([2, 128], F32)
        GL = A.alloc([8], F32)
        rowc = A.alloc([2, 128], F32)
        lgp = A.alloc([8], F32)
        hT = A.alloc([KC, SEQ], BF16)
        hcT = A.alloc([KC, CTX], BF16)
        m_after_persist = A.mark()
        cm_t = A.alloc([4, 128], F32)
        colc = A.alloc([6], F32)
        lgb = A.alloc([16], F32)
        cfm = A.alloc([KC, 2], F32)
        tmpA = A.alloc([128], F32)
        tmpB = A.alloc([128], F32)
        arg16 = A.alloc([16], F32)
        argc = A.alloc([2, 8, 2], F32)
        wbuf0 = A.alloc([KC, 1024], BF16)
        modfm = A.alloc([4, KC, 2], F32)

        P.dma("sp", cm_t, cmat, writes=["cmat"])
        P.dma("sp", colc, colc_d, writes=["colc"])
        P.dma("sp", rowc, rowc_d, writes=["rowc"])
        P.dma("sp", lgp, lgt_pair, writes=["lgp"])
        P.dma("sp", lgb, lgt_bc, writes=["lgb"])
        P.dma("sp", cfm, c_fm, writes=["cfm"])

        for t_, k_ in ((lgp, "lgp"), (lgb, "lgb")):
            P.op("act", lambda e, t_=t_: e.activation(out=t_, in_=t_, func=AF.Exp, scale=-1.0), writes=[k_])
            P.op("act", lambda e, t_=t_: e.activation(out=t_, in_=t_, func=AF.Ln, bias=1.0), writes=[k_])
            P.op("dve", lambda e, t_=t_: e.tensor_scalar(out=t_, in0=t_, scalar1=-1.0, scalar2=None, op0=ALU.mult), writes=[k_])
        for h in range(8):
            P.op("act", lambda e, h=h: e.activation(out=tmpA, in_=cm_t[:, 0, :], func=AF.Exp, scale=lgb[:, 2 * h:2 * h + 1]),
                 reads=["cmat", "lgb"], writes=["tmpA"])
            P.op("act", lambda e, h=h: e.activation(out=tmpB, in_=cm_t[:, 1, :], func=AF.Exp, scale=lgb[:, 2 * h + 1:2 * h + 2]),
                 reads=["cmat", "lgb"], writes=["tmpB"])
            P.op("dve", lambda e: e.tensor_tensor(out=tmpA, in0=tmpA, in1=cm_t[:, 2, :], op=ALU.mult), reads=["cmat"], writes=["tmpA"])
            P.op("dve", lambda e: e.tensor_tensor(out=tmpB, in0=tmpB, in1=cm_t[:, 3, :], op=ALU.mult), reads=["cmat"], writes=["tmpB"])
            P.op("dve", lambda e, h=h: e.tensor_tensor(out=DT[:, h, :], in0=tmpA, in1=tmpB, op=ALU.add),
                 reads=["tmpA", "tmpB"], writes=["DT"])
        lgb3 = lgb.rearrange("p (h d) -> p h d", d=2)
        arg3 = arg16.rearrange("p (h d) -> p h d", d=2)
        for d_ in range(2):
            P.op("dve", lambda e, d_=d_: e.tensor_scalar(out=arg3[:, :, d_], in0=lgb3[:, :, d_], scalar1=colc[:, d_:d_ + 1], scalar2=None, op0=ALU.mult),
                 reads=["lgb", "colc"], writes=["arg16"])
        P.op("act", lambda e: e.activation(out=arg16, in_=arg16, func=AF.Exp), writes=["arg16"])
        P.op("dve", lambda e: e.tensor_scalar(out=kw.rearrange("p h d -> p (h d)"), in0=arg16, scalar1=0.125, scalar2=None, op0=ALU.mult),
             reads=["arg16"], writes=["kw"])
        for ct in range(2):
            for d_ in range(2):
                cc_ = 2 + ct if d_ == 0 else 4 + ct
                P.op("dve", lambda e, ct=ct, d_=d_, cc_=cc_: e.tensor_scalar(out=argc[:, ct, :, d_], in0=lgb3[:, :, d_], scalar1=colc[:, cc_:cc_ + 1], scalar2=None, op0=ALU.mult),
                     reads=["lgb", "colc"], writes=["argc"])
        P.op("act", lambda e: e.activation(out=argc, in_=argc, func=AF.Exp), writes=["argc"])
        P.op("dve", lambda e: e.tensor_scalar(out=ckw, in0=argc, scalar1=0.125, scalar2=None, op0=ALU.mult), reads=["argc"], writes=["ckw"])
        P.op("act", lambda e: e.activation(out=GL, in_=lgp, func=AF.Exp, scale=128.0), reads=["lgp"], writes=["GL"])

        P.op("act", lambda e: e.activation(out=scb, in_=cfm, func=AF.Silu), reads=["cfm"], writes=["scb"])
        P.op("dve", lambda e: e.tensor_copy(out=sc_rep, in_=scb[:, :, 0:1].to_broadcast([128, KC, 128])), reads=["scb"], writes=["sc_rep"])

        def load_w(dst, src_rows_cols, key, nk=KC):
            P.dma("pool", dst, src_rows_cols.rearrange("(k p) n -> p k n", p=128), writes=[key])

        wbuf1 = A.alloc([KC, 1024], BF16)
        wbufs = {0: (wbuf1, "wbuf1"), 1: (wbuf0, "wbuf0"), 3: (wbuf1, "wbuf1"), 4: (wbuf0, "wbuf0")}

        def ada_load(j):
            wb_, key_ = wbufs[j]
            load_w(wb_, w_ada[:, j * D:(j + 1) * D], key_)

        def ada_mm(mi, j):
            wb_, key_ = wbufs[j]

            def mm(e):
                for cc in range(8):
                    for k in range(KC):
                        ins = e.matmul(PB(0)[:, cc * 2:cc * 2 + 2], lhsT=wb_[:, k, cc * 128:(cc + 1) * 128], rhs=scb[:, k, :],
                                       start=(k == 0), stop=(k == KC - 1))
                return ins
            P.op("pe", mm, reads=[key_, "scb"], writes=PK(0))
            P.op("dve", lambda e: e.tensor_tensor(
                out=modfm[:, mi, :, :], in0=PB(0)[:, 0:16].rearrange("p (c t) -> p c t", t=2),
                in1=badafm[:, j * 8:(j + 1) * 8].unsqueeze(2).to_broadcast([128, KC, 2]), op=ALU.add),
                reads=["badafm"], writes=PK(0) + [("modfm", mi)])

        def ada_fin(Sx, SHx, mi_sh, mi_sc, gi):
            P.op("dve", lambda e: e.scalar_tensor_tensor(
                out=Sx, in0=modfm[:, mi_sc, :, :], scalar=1.0, in1=gpre[:, gi, :].unsqueeze(2).to_broadcast([128, KC, 2]),
                op0=ALU.add, op1=ALU.mult), reads=[("modfm", mi_sc), "gpre"], writes=["mod"])
            P.op("dve", lambda e: e.tensor_copy(out=SHx, in_=modfm[:, mi_sh, :, :]), reads=[("modfm", mi_sh)], writes=["mod"])

        ada_load(1)
        ada_load(0)
        ada_mm(1, 1)
        ada_mm(0, 0)
        ada_fin(S1, SH1, 0, 1, 0)
        ada_load(4)
        ada_load(3)
        wbufG = [A.alloc([KC, 1024], BF16) for _ in range(2)]
        bbG = [A.alloc([D], F32) for _ in range(2)]
        gbG = [A.alloc([D], F32) for _ in range(2)]
        gtG = [A.alloc([D], F32) for _ in range(2)]
        for gi_, j in enumerate((2, 5)):
            load_w(wbufG[gi_], w_ada[:, j * D:(j + 1) * D], ("wbufG", gi_))
            P.dma("sp", bbG[gi_], b_ada[0:1, j * D:(j + 1) * D].partition_broadcast(128), writes=[("bbG", gi_)])
            P.dma("sp", gbG[gi_], gpost[gi_:gi_ + 1, :].partition_broadcast(128), writes=[("gbG", gi_)])

        if upto == "p0":
            raise _Stop()
        NBUF1 = 3
        xts = [A.alloc([D], F32) for _ in range(NBUF1)]
        tmps = []
        for i in range(NBUF1):
            tmps.append(dict(junk=A.alloc([D], BF16), ss=A.alloc([1], F32), rstd=A.alloc([1], F32), xn=A.alloc([D], F32), key=("nt", i)))

        def p1_load(i):
            bi = i % NBUF1
            src = x[i * 128:(i + 1) * 128, :] if i < NT else ctx[(i - NT) * 128:(i - NT + 1) * 128, :]
            P.dma("sp", xts[bi], src, writes=[("xt", bi)])

        def p1_front(i):
            bi = i % NBUF1
            return norm_p1(xts[bi], ("xt", bi), tmps[bi])

        def p1_back(i, xnk):
            bi = i % NBUF1
            if i < NT:
                dst_fn = (lambda c: hT[:, c, i * 128:(i + 1) * 128])
                dkeys = [("hT", c, i) for c in range(KC)]
                col = 0
            else:
                dst_fn = (lambda c: hcT[:, c, (i - NT) * 128:(i - NT + 1) * 128])
                dkeys = [("hcT", c, i - NT) for c in range(KC)]
                col = 1
            pbk = (2 * (i % 2), 2 * (i % 2) + 1)
            norm_p2(xnk, dst_fn, dkeys, S1, SH1, col, pbk, tmps[bi])
        NTT = NT + 2
        p1_load(0)
        p1_load(1)
        xk_prev = p1_front(0)
        for i in range(NTT):
            if i + 2 < NTT:
                p1_load(i + 2)
            xk_next = p1_front(i + 1) if i + 1 < NTT else None
            p1_back(i, xk_prev)
            xk_prev = xk_next
        ada_mm(3, 4)
        ada_mm(2, 3)
        ada_fin(S2, SH2, 2, 3, 1)
        for gi_ in range(2):
            def mm(e, gi_=gi_):
                for half in range(2):
                    for k in range(KC):
                        ins = e.matmul(PB(half)[:, 0:512], lhsT=sc_rep[:, k, :], rhs=wbufG[gi_][:, k, half * 512:(half + 1) * 512],
                                       start=(k == 0), stop=(k == KC - 1))
                return ins
            P.op("pe", mm, reads=[("wbufG", gi_), "sc_rep"], writes=PK(0, 1))
            P.op("dve", lambda e, gi_=gi_: e.tensor_tensor(out=gtG[gi_].rearrange("p (b n) -> p b n", b=2), in0=ps[:, 0:2, :],
                                                           in1=bbG[gi_].rearrange("p (b n) -> p b n", b=2), op=ALU.add),
                 reads=[("bbG", gi_)], writes=PK(0, 1) + [("gtG", gi_)])
            P.op("dve", lambda e, gi_=gi_: e.tensor_tensor(out=gtG[gi_], in0=gtG[gi_], in1=gbG[gi_], op=ALU.mult), reads=[("gbG", gi_)], writes=[("gtG", gi_)])
            P.dma("sp", gt_d[gi_], gtG[gi_], reads=[("gtG", gi_)], writes=[("gt_d", gi_)])
        if "hT" in dbg:
            P.dma("sp", dbg["hT"], hT, reads=[("hT", c, i) for c in range(KC) for i in range(NT)])
        P.barrier(scr)
        A.release(m_after_persist)
        if upto == "p1":
            raise _Stop()

        maskB = A.alloc([NTYPE, 128], BF16)
        P.dma("pool", maskB, maskB_d, writes=["maskB"])
        wb = A.alloc([KC, 7, 128], BF16)
        BT = A.alloc([2, NTYPE, 128], BF16)
        rpst = A.alloc([NTYPE, 128], BF16)
        slabQ = A.alloc([SEQ], BF16)
        slabK = A.alloc([SEQ], BF16)
        rv = A.alloc([NT, 128], BF16)
        nva = A.alloc([NT, 2, 65], BF16)
        srg = A.alloc([NT, 128], BF16)
        DS = A.alloc([NT, 2, 64], F32)
        Rb = A.alloc([NT, 2, 64], BF16)
        BLK = 4
        rtmp = [A.alloc([BLK // 2, 4, 32], F32) for _ in range(4)]
        cs_t = [A.alloc([2, BLK, 32], F32) for _ in range(2)]
        qk_tm = [A.alloc([BLK, 256], BF16) for _ in range(2)]
        Vfb = [A.alloc([BLK, 2, 2, 64], BF16) for _ in range(2)]
        crk = A.alloc([2, 128], BF16)
        cVfb = A.alloc([2, 2, 2, 64], BF16)
        cnva = A.alloc([2, 2, 65], BF16)
        cnkT = A.alloc([CTX], BF16)
        PT = [A.alloc([7, 128], BF16) for _ in range(2)]
        SDT = [A.alloc([2, 4, 128], BF16) for _ in range(2)]
        QfbT = [A.alloc([2, 4, 128], BF16) for _ in range(2)]
        sqA = [A.alloc([512], F32) for _ in range(2)]
        msA = [A.alloc([8], F32) for _ in range(2)]
        onA = [A.alloc([512], F32) for _ in range(2)]
        ytile = [A.alloc([4, 128], BF16) for _ in range(2)]
        ystage = [A.alloc([512], BF16) for _ in range(2)]
        rc = A.alloc([2], F32)

        qm = [[A.alloc([128], BF16) for _ in range(2)] for _ in range(2)]
        for hh_ in range(2):
            for par_ in range(2):
                P.op("pool", lambda e, hh_=hh_, par_=par_: e.memset(qm[hh_][par_], 0.0), writes=[("qm", hh_, par_)])
        P.op("pool", lambda e: e.memset(nva, 1.0), writes=["nva_init"])
        P.op("pool", lambda e: e.memset(cnva, 1.0), writes=["cnva_init"])

        def pair_body(hp):
            hk = ("hp", hp)
            for d_ in range(2):
                P.op("act", lambda e, d_=d_: e.activation(out=QW[:, d_, :], in_=rowc[:, d_, :], func=AF.Exp,
                                                           scale=lgp[:, hp * 2 + d_:hp * 2 + d_ + 1]),
                     writes=["QW"])
            def load_wb(hp_):
                for s in range(7):
                    c0 = s * 512 + hp_ * 128
                    P.dma("pool", wb[:, :, s, :], w_in[:, c0:c0 + 128].rearrange("(k p) n -> p k n", p=128), writes=[("wb", s)])
            if hp == 0:
                load_wb(0)
            wbk = [("wb", s) for s in range(7)]
            for hh in range(2):
                P.dma("pool", rpst, rpbB[2 * hp + hh], writes=["rpst"])
                P.op("dve", lambda e, hh=hh: e.tensor_tensor(out=BT[:, hh, :, :], in0=rpst, in1=maskB, op=ALU.add),
                     reads=["rpst", "maskB"], writes=[("BT", hh)])
            for ct in range(2):
                def mm(e, ct=ct):
                    for k in range(KC):
                        ins = e.matmul(PB(6)[:, 0:256], lhsT=hcT[:, k, ct * 128:(ct + 1) * 128], rhs=wb[:, k, 1:3, :].rearrange("p s n -> p (s n)"),
                                       start=(k == 0), stop=(k == KC - 1))
                    for k in range(KC):
                        ins = e.matmul(PB(6)[:, 256:384], lhsT=hcT[:, k, ct * 128:(ct + 1) * 128], rhs=wb[:, k, 6, :],
                                       start=(k == 0), stop=(k == KC - 1))
                    return ins
                P.op("pe", mm, reads=wbk + ["hcT"], writes=PK(6))
                P.op("act", lambda e, ct=ct: e.copy(out=crk[:, ct, :], in_=PB(6)[:, 0:128]), writes=PK(6) + [("crk", ct)])
                for hh in range(2):
                    for d_ in range(2):
                        P.op("dve", lambda e, ct=ct, hh=hh, d_=d_: e.tensor_scalar(
                            out=cVfb[:, ct, hh, d_, :], in0=PB(6)[:, 128 + hh * 64:128 + (hh + 1) * 64],
                            scalar1=ckw[:, ct, 2 * hp + hh, d_:d_ + 1], scalar2=None, op0=ALU.mult),
                            reads=["ckw"], writes=PK(6) + [("cVfb", ct)])
                P.op("dve", lambda e, ct=ct: e.tensor_copy(out=cnva[:, ct, :, 0:64], in_=PB(6)[:, 256:384].rearrange("p (h d) -> p h d", d=64)),
                     reads=["cnva_init"], writes=PK(6) + [("cnva", ct)])

            def mm(e):
                for k in range(KC):
                    ins = e.matmul(PB(7)[:, 0:CTX], lhsT=wb[:, k, 5, :], rhs=hcT[:, k, :], start=(k == 0), stop=(k == KC - 1))
                return ins
            P.op("pe", mm, reads=wbk + ["hcT"], writes=PK(7))
            P.op("act", lambda e: e.copy(out=cnkT, in_=PB(7)[:, 0:CTX]), writes=PK(7) + ["cnkT"])

            def mm(e):
                for hh in range(2):
                    for ct in range(2):
                        ins = e.matmul(PB(6)[hh * 64:(hh + 1) * 64, 0:128], lhsT=crk[:, ct, hh * 64:(hh + 1) * 64],
                                       rhs=cVfb[:, ct, hh, :, :].rearrange("p a b -> p (a b)"), start=(ct == 0), stop=(ct == 1))
                return ins
            P.op("pe", mm, reads=[("crk", 0), ("crk", 1), ("cVfb", 0), ("cVfb", 1)], writes=PK(6))
            P.op("dve", lambda e: e.tensor_copy(out=DS[:, 0, 0, :], in_=PB(6)[:, 0:64]), writes=PK(6) + [("DS", 0, 0)])
            P.op("dve", lambda e: e.tensor_copy(out=DS[:, NT - 1, 1, :], in_=PB(6)[:, 64:128]), writes=PK(6) + [("DS", NT - 1, 1)])

            if upto == "p2ctx":
                raise _Stop()
            def frontA(b0):
                bi = (b0 // BLK) % 2
                qk_ = qk_tm[bi]
                cst = cs_t[bi]
                P.dma("sp", cst[:, 0, :, :], cos_d[:, b0:b0 + BLK, :], writes=[("cs", bi, 0)])
                P.dma("sp", cst[:, 1, :, :], sin_d[:, b0:b0 + BLK, :], writes=[("cs", bi, 1)])
                q5 = qk_.rearrange("p b (g t f) -> p b g t f", g=4, t=2)
                for half in range(2):
                    hb = [2 + 2 * half, 3 + 2 * half]
                    for ii2 in range(2):
                        ii = half * 2 + ii2
                        i = b0 + ii
                        pb = hb[ii2]

                        def mm(e, i=i, pb=pb):
                            for k in range(KC):
                                ins = e.matmul(PB(pb)[:, 0:512], lhsT=hT[:, k, i * 128:(i + 1) * 128], rhs=wb[:, k, 0:4, :].rearrange("p s n -> p (s n)"),
                                               start=(k == 0), stop=(k == KC - 1))
                            return ins
                        P.op("pe", mm, reads=wbk, writes=PK(pb))
                        P.op("dve", lambda e, i=i, pb=pb: e.tensor_copy(out=rv[:, i, :], in_=PB(pb)[:, 256:384]),
                             writes=PK(pb) + [("rv", i)])
                        P.op("act", lambda e, i=i, pb=pb: e.activation(out=srg[:, i, :], in_=PB(pb)[:, 384:512], func=AF.Silu),
                             writes=PK(pb) + [("srg", i)])
                    s5 = ps[:, hb[0]:hb[0] + 2, 0:256].rearrange("p b (g t f) -> p b g t f", g=4, t=2)
                    cosb = cst[:, 0, 2 * half:2 * half + 2, :].unsqueeze(2).to_broadcast([128, 2, 4, 32])
                    sinb = cst[:, 1, 2 * half:2 * half + 2, :].unsqueeze(2).to_broadcast([128, 2, 4, 32])
                    PKH = PK(*hb)
                    qh = q5[:, 2 * half:2 * half + 2]
                    P.op("dve", lambda e, s5=s5, cosb=cosb: e.tensor_tensor(out=rtmp[0], in0=s5[:, :, :, 0, :], in1=cosb, op=ALU.mult),
                         reads=[("cs", bi, 0)], writes=PKH + [("rtmp", 0)])
                    P.op("dve", lambda e, s5=s5, sinb=sinb: e.tensor_tensor(out=rtmp[1], in0=s5[:, :, :, 1, :], in1=sinb, op=ALU.mult),
                         reads=[("cs", bi, 1)], writes=PKH + [("rtmp", 1)])
                    P.op("dve", lambda e, s5=s5, sinb=sinb: e.tensor_tensor(out=rtmp[2], in0=s5[:, :, :, 0, :], in1=sinb, op=ALU.mult),
                         reads=[("cs", bi, 1)], writes=PKH + [("rtmp", 2)])
                    P.op("dve", lambda e, s5=s5, cosb=cosb: e.tensor_tensor(out=rtmp[3], in0=s5[:, :, :, 1, :], in1=cosb, op=ALU.mult),
                         reads=[("cs", bi, 0)], writes=PKH + [("rtmp", 3)])
                    P.op("pool", lambda e, qh=qh: e.tensor_tensor(out=qh[:, :, :, 0, :], in0=rtmp[0], in1=rtmp[1], op=ALU.subtract),
                         reads=[("rtmp", 0), ("rtmp", 1)], writes=[("qk", bi, half, 0)])
                    P.op("pool", lambda e, qh=qh: e.tensor_tensor(out=qh[:, :, :, 1, :], in0=rtmp[2], in1=rtmp[3], op=ALU.add),
                         reads=[("rtmp", 2), ("rtmp", 3)], writes=[("qk", bi, half, 1)])

            def blockA(b0):
                bi = (b0 // BLK) % 2
                qk_, vf_ = qk_tm[bi], Vfb[bi]
                qkk = [("qk", bi, half, w_) for half in range(2) for w_ in range(2)]
                for hh in range(2):
                    for d_ in range(2):
                        P.op("act", lambda e, hh=hh, d_=d_, vf_=vf_: e.activation(
                            out=vf_[:, :, hh, d_, :], in_=rv[:, b0:b0 + BLK, hh * 64:(hh + 1) * 64], func=AF.Copy,
                            scale=kw[:, 2 * hp + hh, d_:d_ + 1]),
                            reads=[("rv", b0 + ii) for ii in range(BLK)] + ["kw"], writes=[("Vfb", bi, hh, d_)])
                vfk = [("Vfb", bi, hh, d_) for hh in range(2) for d_ in range(2)]
                for which, slab, sk in ((0, slabQ, "slabQ"), (1, slabK, "slabK")):
                    pbt = 6 + which
                    pbv = PB(pbt).bitcast(BF16)

                    def tr(e, which=which, pbv=pbv, qk_=qk_):
                        for ii in range(BLK):
                            ins = e.transpose(out=pbv[:, ii * 128:(ii + 1) * 128], in_=qk_[:, ii, which * 128:(which + 1) * 128], identity=identb)
                        return ins
                    P.op("pe", tr, reads=qkk + ["identb"], writes=PK(pbt))
                    if which == 0:
                        P.op("act", lambda e, pbv=pbv, slab=slab: e.copy(out=slab[:, b0 * 128:(b0 + BLK) * 128], in_=pbv[:, 0:BLK * 128]),
                             writes=PK(pbt) + [(sk, b0 // BLK)])
                    else:
                        P.op("dve", lambda e, pbv=pbv, slab=slab: e.tensor_copy(out=slab[:, b0 * 128:(b0 + BLK) * 128], in_=pbv[:, 0:BLK * 128]),
                             writes=PK(pbt) + [(sk, b0 // BLK)])
                pbd = (b0 // BLK) % 2

                def mm(e, pbd=pbd, qk_=qk_, vf_=vf_):
                    for ii in range(BLK):
                        for hh in range(2):
                            ins = e.matmul(PB(pbd)[hh * 64:(hh + 1) * 64, ii * 128:(ii + 1) * 128],
                                           lhsT=qk_[:, ii, 128 + hh * 64:128 + (hh + 1) * 64],
                                           rhs=vf_[:, ii, hh, :, :].rearrange("p a b -> p (a b)"), start=True, stop=True)
                    return ins
                P.op("pe", mm, reads=qkk + vfk, writes=PK(pbd))
                pv = PB(pbd).rearrange("p (b d f) -> p b d f", d=2, f=64)
                lo, hi = b0, min(b0 + BLK, NT - 1)
                if hi > lo:
                    P.op("act", lambda e, lo=lo, hi=hi, pv=pv: e.copy(out=DS[:, lo + 1:hi + 1, 0, :], in_=pv[:, lo - b0:hi - b0, 0, :]),
                         writes=PK(pbd) + [("DS", c + 1, 0) for c in range(lo, hi)])
                lo2, hi2 = max(b0, 1), b0 + BLK
                if hi2 > lo2:
                    P.op("dve", lambda e, lo2=lo2, hi2=hi2, pv=pv: e.tensor_copy(out=DS[:, lo2 - 1:hi2 - 1, 1, :], in_=pv[:, lo2 - b0:hi2 - b0, 1, :]),
                         writes=PK(pbd) + [("DS", c - 1, 1) for c in range(lo2, hi2)])
            frontA(0)
            for b0 in range(0, NT, BLK):
                if b0 + BLK < NT:
                    frontA(b0 + BLK)
                blockA(b0)
            if upto == "p2a":
                raise _Stop()
            def nvproj(nb):
                pbp = 6 + (nb % 2)

                def mm(e):
                    for ii in range(4):
                        i = nb * 4 + ii
                        for k in range(KC):
                            ins = e.matmul(PB(pbp)[:, ii * 128:(ii + 1) * 128], lhsT=hT[:, k, i * 128:(i + 1) * 128], rhs=wb[:, k, 6, :],
                                           start=(k == 0), stop=(k == KC - 1))
                    return ins
                P.op("pe", mm, reads=wbk, writes=PK(pbp))
                P.op("act", lambda e: e.copy(out=nva[:, nb * 4:(nb + 1) * 4, :, 0:64], in_=PB(pbp).rearrange("p (i h d) -> p i h d", h=2, d=64)),
                     reads=["nva_init"], writes=PK(pbp) + [("nva", nb)])
            for nb in range(NB):
                nvproj(nb)
            for s_ in range(NT - 1):
                c = s_
                P.op("dve", lambda e, c=c: e.scalar_tensor_tensor(out=DS[:, c + 1, 0, :], in0=DS[:, c, 0, :], scalar=GL[:, 2 * hp:2 * hp + 1],
                                                                  in1=DS[:, c + 1, 0, :], op0=ALU.mult, op1=ALU.add),
                     reads=[("DS", c, 0), "GL"], writes=[("DS", c + 1, 0)])
                c = NT - 1 - s_
                P.op("dve", lambda e, c=c: e.scalar_tensor_tensor(out=DS[:, c - 1, 1, :], in0=DS[:, c, 1, :], scalar=GL[:, 2 * hp + 1:2 * hp + 2],
                                                                  in1=DS[:, c - 1, 1, :], op0=ALU.mult, op1=ALU.add),
                     reads=[("DS", c, 1), "GL"], writes=[("DS", c - 1, 1)])
            P.op("dve", lambda e: e.tensor_copy(out=Rb, in_=DS), reads=[("DS", c, d_) for c in range(NT) for d_ in range(2)], writes=["Rb"])
            if f"Rb{hp}" in dbg:
                P.dma("sp", dbg[f"Rb{hp}"], Rb, reads=["Rb"])

            if upto == "p2scan":
                raise _Stop()
            def blockB(g0):
                gi = (g0 // 4) % 2
                pbo = [2 + gi, 4 + gi]
                yt = ytile[gi]
                qf = QfbT[gi]
                sd = SDT[gi]
                sq, ms, on = sqA[gi], msA[gi], onA[gi]
                P.op("pool", lambda e, qf=qf: e.tensor_tensor(
                    out=qf, in0=slabQ[:, g0 * 128:(g0 + 4) * 128].rearrange("p (c i) -> p c i", c=4).unsqueeze(1).to_broadcast([128, 2, 4, 128]),
                    in1=QW.unsqueeze(2).to_broadcast([128, 2, 4, 128]), op=ALU.mult),
                    reads=[("slabQ", g0 // BLK), "QW"], writes=[("QfbT", gi)])
                for hh in range(2):
                    def mm(e, hh=hh):
                        for cc in range(4):
                            c = g0 + cc
                            ins = e.matmul(PB(hh)[:, cc * 128:(cc + 1) * 128], lhsT=slabK[hh * 64:(hh + 1) * 64, c * 128:(c + 1) * 128],
                                           rhs=slabQ[hh * 64:(hh + 1) * 64, c * 128:(c + 1) * 128], start=True, stop=True)
                        return ins
                    P.op("pe", mm, reads=[("slabQ", g0 // BLK), ("slabK", g0 // BLK)], writes=PK(hh))
                    P.op("dve", lambda e, hh=hh, sd=sd: e.tensor_tensor(
                        out=sd[:, hh, :, :], in0=PB(hh).rearrange("p (c i) -> p c i", c=4),
                        in1=DT[:, 2 * hp + hh, :].unsqueeze(1).to_broadcast([128, 4, 128]), op=ALU.mult),
                        reads=["DT"], writes=PK(hh) + [("SDT", gi, hh)])
                for hh in range(2):
                    def mm(e, hh=hh, sd=sd, qf=qf):
                        for cc in range(4):
                            c = g0 + cc
                            o_ = PB(pbo[hh])[:, cc * 64:(cc + 1) * 64]
                            e.matmul(o_, lhsT=sd[:, hh, cc, :], rhs=rv[:, c, hh * 64:(hh + 1) * 64], start=True, stop=False)
                            e.matmul(o_, lhsT=qf[hh * 64:(hh + 1) * 64, 0, cc, :], rhs=Rb[hh * 64:(hh + 1) * 64, c, 0, :], start=False, stop=False)
                            ins = e.matmul(o_, lhsT=qf[hh * 64:(hh + 1) * 64, 1, cc, :], rhs=Rb[hh * 64:(hh + 1) * 64, c, 1, :], start=False, stop=True)
                        return ins
                    P.op("pe", mm, reads=[("SDT", gi, hh), ("QfbT", gi), "Rb"] + [("rv", g0 + cc) for cc in range(4)], writes=PK(pbo[hh]))
            def backB(g0):
                gi = (g0 // 4) % 2
                pbo = [2 + gi, 4 + gi]
                yt = ytile[gi]
                sq, ms, on = sqA[gi], msA[gi], onA[gi]
                for hh in range(2):
                    P.op("act", lambda e, hh=hh: e.activation(out=sq[:, hh * 256:(hh + 1) * 256], in_=PB(pbo[hh])[:, 0:256], func=AF.Square),
                         writes=PK(pbo[hh]) + [("sq", gi, hh)])
                P.op("dve", lambda e: e.tensor_reduce(out=ms, in_=sq.rearrange("p (g f) -> p g f", f=64), axis=AX.X, op=ALU.add),
                     reads=[("sq", gi, 0), ("sq", gi, 1)], writes=[("ms", gi)])
                P.op("act", lambda e: e.activation(out=ms, in_=ms, func=AF.Sqrt, scale=1.0 / 64, bias=EPS), writes=[("ms", gi)])
                P.op("dve", lambda e: e.reciprocal(out=ms, in_=ms), writes=[("ms", gi)])
                for hh in range(2):
                    P.op("dve", lambda e, hh=hh: e.tensor_tensor(
                        out=on[:, hh * 256:(hh + 1) * 256].rearrange("p (g f) -> p g f", f=64),
                        in0=PB(pbo[hh])[:, 0:256].rearrange("p (g f) -> p g f", f=64),
                        in1=ms[:, hh * 4:(hh + 1) * 4].unsqueeze(2).to_broadcast([128, 4, 64]), op=ALU.mult),
                        reads=[("ms", gi)], writes=PK(pbo[hh]) + [("on", gi, hh)])
                    P.op("pool", lambda e, hh=hh: e.tensor_tensor(
                        out=yt[:, :, hh * 64:(hh + 1) * 64], in0=on[:, hh * 256:(hh + 1) * 256].rearrange("p (c f) -> p c f", f=64),
                        in1=srg[:, g0:g0 + 4, hh * 64:(hh + 1) * 64], op=ALU.mult),
                        reads=[("on", gi, hh)] + [("srg", g0 + cc) for cc in range(4)],
                        writes=[("ytile", gi, "h", hh)] + ([("ytile", gi)] + [("ytile", gi, cc) for cc in range(4)] if hh == 1 else []))
                pbt = 6 + gi
                pbv = PB(pbt).bitcast(BF16)

                def tr(e, pbv=pbv, yt=yt):
                    for cc in range(4):
                        ins = e.transpose(out=pbv[:, cc * 128:(cc + 1) * 128], in_=yt[:, cc, :], identity=identb)
                    return ins
                P.op("pe", tr, reads=[("ytile", gi), "identb", ("ytile", gi, "h", 0), ("ytile", gi, "h", 1)] + [("ytile", gi, cc) for cc in range(4)], writes=PK(pbt))
                P.op("act", lambda e, pbv=pbv, gi=gi: e.copy(out=ystage[gi], in_=pbv[:, 0:512]), writes=PK(pbt) + [("ystage", gi)])
                P.dma("sp", yT_d[hp, :, g0 * 128:(g0 + 4) * 128], ystage[gi], reads=[("ystage", gi)], writes=[("yT", 0, hp, g0 // 4)])
            blockB(0)
            for g0 in range(0, NT, 4):
                if g0 + 4 < NT:
                    blockB(g0 + 4)
                backB(g0)

            if upto == "p2b":
                raise _Stop()
            def naproj(nb):
                for which, slab, sk, slot in ((0, slabQ, "slabQ", 4), (1, slabK, "slabK", 5)):
                    pbp = 4 + which

                    def mm(e, nb=nb, slot=slot, pbp=pbp):
                        for k in range(KC):
                            ins = e.matmul(PB(pbp)[:, 0:512], lhsT=wb[:, k, slot, :], rhs=hT[:, k, nb * 512:(nb + 1) * 512],
                                           start=(k == 0), stop=(k == KC - 1))
                        return ins
                    P.op("pe", mm, reads=wbk, writes=PK(pbp))
                    if which == 0:
                        P.op("act", lambda e, nb=nb, pbp=pbp: e.activation(out=slabQ[:, nb * 512:(nb + 1) * 512], in_=PB(pbp), func=AF.Copy, scale=0.125),
                             writes=PK(pbp) + [("slabQ", nb)])
                    else:
                        P.op("dve", lambda e, nb=nb, pbp=pbp: e.tensor_copy(out=slabK[:, nb * 512:(nb + 1) * 512], in_=PB(pbp)),
                             writes=PK(pbp) + [("slabK", nb)])
            for nb in range(NB):
                naproj(nb)
            if upto == "p2np":
                raise _Stop()
            def na_front(t, hh):
                lst = per_t[t]
                pi = hh
                pA, pB_ = 2 * pi, 2 * pi + 1
                pt_ = PT[pi]
                nloc = len(lst)
                assert nloc <= 5
                qmb = qm[hh][t % 2]
                P.op("pool", lambda e: e.tensor_copy(out=qmb[hh * 64:(hh + 1) * 64, :], in_=slabQ[hh * 64:(hh + 1) * 64, t * 128:(t + 1) * 128]),
                     reads=[("slabQ", t // 4)], writes=[("qm", hh, t % 2)])

                def mm(e):
                    for m, (u, ty) in enumerate(lst):
                        o_ = (PB(pA)[:, m * 128:(m + 1) * 128] if m < 4 else PB(pB_)[:, 0:128])
                        e.matmul(o_, lhsT=slabK[:, u * 128:(u + 1) * 128], rhs=qmb, start=True, stop=False)
                        ins = e.matmul(o_, lhsT=identb, rhs=BT[:, hh, ty, :], start=False, stop=True)
                    for ct in range(2):
                        ins = e.matmul(PB(pB_)[:, (1 + ct) * 128:(2 + ct) * 128], lhsT=cnkT[:, ct * 128:(ct + 1) * 128],
                                       rhs=qmb, start=True, stop=True)
                    return ins
                kblocks = sorted(set(u // 4 for (u, _) in lst))
                P.op("pe", mm, reads=[("qm", hh, t % 2), "cnkT", ("BT", hh), "identb"] + [("slabK", kb) for kb in kblocks],
                     writes=PK(pA, pB_))
                na4 = min(nloc, 4)
                P.op("act", lambda e: e.activation(out=pt_[:, 0:na4, :], in_=PB(pA)[:, 0:na4 * 128].rearrange("p (m q) -> p m q", q=128), func=AF.Exp),
                     writes=PK(pA) + [("PT", pi, 0)])
                lo_ = 0 if nloc == 5 else 1
                P.op("act", lambda e: e.activation(out=pt_[:, 4 + lo_:7, :], in_=PB(pB_)[:, lo_ * 128:3 * 128].rearrange("p (m q) -> p m q", q=128), func=AF.Exp),
                     writes=PK(pB_) + [("PT", pi, 1)])

            def na_back(t, hh):
                lst = per_t[t]
                pi = hh
                pt_ = PT[pi]
                gi = (t // 4) % 2
                yt = ytile[gi]
                pbo = 4 + (t % 2)
                kblocks = sorted(set(u // 4 for (u, _) in lst))

                def mm(e):
                    o_ = PB(pbo)[:, hh * 66:hh * 66 + 65]
                    for m, (u, ty) in enumerate(lst):
                        slot = m if m < 4 else 4
                        e.matmul(o_, lhsT=pt_[:, slot, :], rhs=nva[:, u, hh, :], start=(m == 0), stop=False)
                    for ct in range(2):
                        ins = e.matmul(o_, lhsT=pt_[:, 5 + ct, :], rhs=cnva[:, ct, hh, :], start=False, stop=(ct == 1))
                    return ins
                P.op("pe", mm, reads=[("PT", pi, 0), ("PT", pi, 1), ("cnva", 0), ("cnva", 1)] + [("nva", kb) for kb in kblocks],
                     writes=PK(pbo))
                if hh == 0:
                    return
                ov = PB(pbo)[:, 0:132].rearrange("p (h f) -> p h f", f=66)
                P.op("dve", lambda e: e.reciprocal(out=rc, in_=ov[:, :, 64]), writes=PK(pbo) + ["rc"])
                P.op("dve", lambda e: e.tensor_tensor(
                    out=yt[:, t % 4, :].rearrange("p (h d) -> p h d", d=64), in0=ov[:, :, 0:64],
                    in1=rc.unsqueeze(2).to_broadcast([128, 2, 64]), op=ALU.mult),
                    reads=["rc"], writes=PK(pbo) + [("ytile", gi, t % 4)])
                if t % 4 == 3:
                    g0 = t - 3
                    pbt = 6 + gi
                    pbv = PB(pbt).bitcast(BF16)

                    def tr(e):
                        for cc in range(4):
                            ins = e.transpose(out=pbv[:, cc * 128:(cc + 1) * 128], in_=yt[:, cc, :], identity=identb)
                        return ins
                    P.op("pe", tr, reads=[("ytile", gi, cc) for cc in range(4)] + [("ytile", gi), "identb"], writes=PK(pbt))
                    P.op("act", lambda e: e.copy(out=ystage[gi], in_=pbv[:, 0:512]), writes=PK(pbt) + [("ystage", gi)])
                    P.dma("sp", yT_d[4 + hp, :, g0 * 128:(g0 + 4) * 128], ystage[gi], reads=[("ystage", gi)], writes=[("yT", 1, hp, g0 // 4)])
            if hp == 0:
                P.dma("pool", wgb, w_in[:, 3584:5632], writes=["wgb"])
                P.dma("pool", wrob, w_ro, writes=["wrob"])
                P.dma("pool", wnob, w_no, writes=["wnob"])
                P.dma("pool", wob, w_o, writes=["wob"])
            if hp == 1:
                P.dma("pool", w1b.rearrange("r (a c) -> (r a) c", a=2), w_ff1.rearrange("r (a c) -> (r a) c", a=2), writes=["w1b"])
            if hp == 2:
                P.dma("pool", w2b, w_ff2, writes=["w2b"])
            units = [(t, hh) for t in range(NT) for hh in range(2)]
            for k_, (t_, hh_) in enumerate(units):
                if k_ == 10 and hp + 1 < 4:
                    load_wb(hp + 1)
                na_front(t_, hh_)
                if k_ >= 1:
                    na_back(*units[k_ - 1])
            na_back(*units[-1])
        for hp in range(4):
            pair_body(hp)
        P.barrier(scr)
        A.release(m_after_persist)
        A.release(m0)

        if upto == "p2":
            raise _Stop()
        GT = [A.alloc([D], F32) for _ in range(2)]
        m3 = A.mark()
        for gi_ in range(2):
            P.dma("sp", GT[gi_], gt_d[gi_], writes=[("GT", gi_)])

        if upto == "p3p":
            raise _Stop()
        Wg = A.alloc([KC, 2048], BF16)
        Wro = A.alloc([4, D], BF16)
        Wno = A.alloc([4, D], BF16)
        Wo = A.alloc([KC, D], BF16)
        def load3a_weights():
            for q4 in range(4):
                P.dma("sp", Wg[:, :, q4 * 512:(q4 + 1) * 512], wgb[:, q4 * 512:(q4 + 1) * 512].rearrange("(k p) n -> p k n", p=128), writes=[("Wg", q4)])
            P.dma("sp", Wro, wrob.rearrange("(k p) n -> p k n", p=128), writes=["Wro"])
            P.dma("sp", Wno, wnob.rearrange("(k p) n -> p k n", p=128), writes=["Wno"])
            for q2 in range(2):
                P.dma("sp", Wo[:, :, q2 * 512:(q2 + 1) * 512], wob[:, q2 * 512:(q2 + 1) * 512].rearrange("(k p) n -> p k n", p=128), writes=[("Wo", q2)])
        xbA = [A.alloc([4, D], F32) for _ in range(2)]
        junkA = A.alloc([D], BF16)
        tmpA3 = [dict(junk=junkA, junk_key="junkA", ss=A.alloc([1], F32), rstd=A.alloc([1], F32), xn=A.alloc([D], F32), key=("nt3", i_)) for i_ in range(2)]
        hTb = A.alloc([KC, 512], BF16)
        yTbA = [A.alloc([8, 512], BF16) for _ in range(2)]
        sgT = A.alloc([16, 512], F32)
        z1A = [A.alloc([512], F32) for _ in range(2)]
        z2A = [A.alloc([512], F32) for _ in range(2)]
        zT = A.alloc([KC, 512], BF16)
        ssyA = [A.alloc([1], F32) for _ in range(2)]
        rsyA = [A.alloc([1], F32) for _ in range(2)]
        tyA = [A.alloc([D], F32) for _ in range(2)]
        Wgk = [("Wg", q4) for q4 in range(4)]

        def load3a(nb):
            xb = xbA[nb % 2]
            for tt in range(4):
                P.dma("sp", xb[:, tt, :], x[(nb * 4 + tt) * 128:(nb * 4 + tt + 1) * 128, :], writes=[("xbA", nb % 2, tt)])
            P.dma("sp", yTbA[nb % 2], yT_d[:, :, nb * 512:(nb + 1) * 512].rearrange("a p n -> p a n"), writes=[("yTb", nb % 2)])

        def n3a_p1(nb, tt):
            xb = xbA[nb % 2]
            return norm_p1(xb[:, tt, :], ("xbA", nb % 2, tt), tmpA3[tt % 2])

        ss4A = [A.alloc([4], F32) for _ in range(2)]
        rs4A = [A.alloc([4], F32) for _ in range(2)]

        def n3a_sq(nb, tt):
            xb = xbA[nb % 2]
            P.op("act", lambda e: e.activation(out=junkA, in_=xb[:, tt, :], func=AF.Square, accum_out=ss4A[nb % 2][:, tt:tt + 1]),
                 reads=[("xbA", nb % 2, tt)], writes=["junkA", ("ss4", nb % 2, tt)])

        def n3a_rs(nb):
            ss4, rs4 = ss4A[nb % 2], rs4A[nb % 2]
            P.op("act", lambda e: e.activation(out=rs4, in_=ss4, func=AF.Sqrt, scale=1.0 / D, bias=EPS),
                 reads=[("ss4", nb % 2, tt) for tt in range(4)], writes=[("rs4", nb % 2)])
            P.op("dve", lambda e: e.reciprocal(out=rs4, in_=rs4), writes=[("rs4", nb % 2)])

        def n3a_xn(nb, tt):
            xb = xbA[nb % 2]
            tmp = tmpA3[tt % 2]
            xn = tmp["xn"]
            xnk = (tmp["key"], "xn")
            P.op("dve", lambda e: e.tensor_scalar(out=xn, in0=xb[:, tt, :], scalar1=rs4A[nb % 2][:, tt:tt + 1], scalar2=None, op0=ALU.mult),
                 reads=[("rs4", nb % 2), ("xbA", nb % 2, tt)], writes=[xnk])
            return xnk

        def n3a_p2(nb, tt, xnk):
            norm_p2(xnk, (lambda c: hTb[:, c, tt * 128:(tt + 1) * 128]), [("hTb", c, tt) for c in range(KC)], S1, SH1, 0, (6, 7), tmpA3[tt % 2])

        def norm3a(nb):
            for tt in range(4):
                n3a_p2(nb, tt, n3a_p1(nb, tt))

        def blk3a(nb):
            xb = xbA[nb % 2]
            yTb = yTbA[nb % 2]
            if nb + 1 < NB:
                load3a(nb + 1)
            hkeys = [("hTb", c, tt) for c in range(KC) for tt in range(4)]
            xnks = {}
            for g in range(16):
                pb = g % 2

                def mm(e, g=g, pb=pb):
                    for k in range(KC):
                        ins = e.matmul(PB(pb), lhsT=Wg[:, k, g * 128:(g + 1) * 128], rhs=hTb[:, k, :], start=(k == 0), stop=(k == KC - 1))
                    return ins
                P.op("pe", mm, reads=hkeys + [("Wg", g // 4)], writes=PK(pb))
                P.op("act", lambda e, g=g, pb=pb: e.activation(out=sgT[:, g, :], in_=PB(pb), func=AF.Sigmoid), writes=PK(pb) + [("sgT", g)])
                if nb + 1 < NB:
                    if g in (2, 4, 6, 8):
                        n3a_sq(nb + 1, (g - 2) // 2)
                    elif g == 10:
                        n3a_rs(nb + 1)
                    elif g == 12:
                        xnks[0] = n3a_xn(nb + 1, 0)
                        xnks[1] = n3a_xn(nb + 1, 1)
            for fc in range(KC):
                pa, pbb = 2 + fc % 2, 4 + fc % 2
                z1, z2 = z1A[fc % 2], z2A[fc % 2]

                def mm(e, fc=fc, pa=pa):
                    for k in range(4):
                        ins = e.matmul(PB(pa), lhsT=Wro[:, k, fc * 128:(fc + 1) * 128], rhs=yTb[:, k, :], start=(k == 0), stop=(k == 3))
                    return ins
                P.op("pe", mm, reads=[("yTb", nb % 2), "Wro"], writes=PK(pa))

                def mm(e, fc=fc, pbb=pbb):
                    for k in range(4):
                        ins = e.matmul(PB(pbb), lhsT=Wno[:, k, fc * 128:(fc + 1) * 128], rhs=yTb[:, 4 + k, :], start=(k == 0), stop=(k == 3))
                    return ins
                P.op("pe", mm, reads=[("yTb", nb % 2), "Wno"], writes=PK(pbb))
                P.op("dve", lambda e, fc=fc, pa=pa, z1=z1: e.tensor_tensor(out=z1, in0=PB(pa), in1=sgT[:, fc, :], op=ALU.mult),
                     reads=[("sgT", fc)], writes=PK(pa) + [("z1", fc % 2)])
                P.op("dve", lambda e, fc=fc, pbb=pbb, z2=z2: e.tensor_tensor(out=z2, in0=PB(pbb), in1=sgT[:, 8 + fc, :], op=ALU.mult),
                     reads=[("sgT", 8 + fc)], writes=PK(pbb) + [("z2", fc % 2)])
                P.op("pool", lambda e, fc=fc, z1=z1, z2=z2: e.tensor_tensor(out=zT[:, fc, :], in0=z1, in1=z2, op=ALU.add),
                     reads=[("z1", fc % 2), ("z2", fc % 2)], writes=[("zT", fc)])
                if nb + 1 < NB and fc % 2 == 1:
                    tt_ = fc // 2
                    n3a_p2(nb + 1, tt_, xnks[tt_])
                    if tt_ + 2 < 4:
                        xnks[tt_ + 2] = n3a_xn(nb + 1, tt_ + 2)
            for tt in range(4):
                i = nb * 4 + tt
                py0 = 6 - 2 * (tt % 2)
                ssy, rsy, ty = ssyA[tt % 2], rsyA[tt % 2], tyA[tt % 2]

                def mm(e, tt=tt, py0=py0):
                    for half in range(2):
                        for k in range(KC):
                            ins = e.matmul(PB(py0 + half), lhsT=zT[:, k, tt * 128:(tt + 1) * 128], rhs=Wo[:, k, half * 512:(half + 1) * 512],
                                           start=(k == 0), stop=(k == KC - 1))
                    return ins
                P.op("pe", mm, reads=[("zT", fc) for fc in range(KC)] + [("Wo", 0), ("Wo", 1)], writes=PK(py0, py0 + 1))
                jk = tmpA3[tt % 2]
                P.op("act", lambda e, py0=py0, jk=jk, ssy=ssy: e.activation(out=jk["junk"].rearrange("p (b n) -> p b n", b=2), in_=ps[:, py0:py0 + 2, :], func=AF.Square, accum_out=ssy),
                     writes=PK(py0, py0 + 1) + ["junkA", ("ssy", tt % 2)])
                P.op("act", lambda e, ssy=ssy, rsy=rsy: e.activation(out=rsy, in_=ssy, func=AF.Sqrt, scale=1.0 / D, bias=EPS),
                     reads=[("ssy", tt % 2)], writes=[("rsy", tt % 2)])
                P.op("dve", lambda e, rsy=rsy: e.reciprocal(out=rsy, in_=rsy), writes=[("rsy", tt % 2)])
                P.op("dve", lambda e, py0=py0, rsy=rsy, ty=ty: e.scalar_tensor_tensor(
                    out=ty.rearrange("p (b n) -> p b n", b=2), in0=ps[:, py0:py0 + 2, :], scalar=rsy[:, 0:1],
                    in1=GT[0].rearrange("p (b n) -> p b n", b=2), op0=ALU.mult, op1=ALU.mult),
                    reads=[("rsy", tt % 2), ("GT", 0)], writes=PK(py0, py0 + 1) + [("ty", tt % 2)])
                P.op("pool", lambda e, tt=tt, ty=ty, xb=xb: e.tensor_tensor(out=ty, in0=ty, in1=xb[:, tt, :], op=ALU.add),
                     reads=[("xbA", nb % 2, tt)], writes=[("ty", tt % 2)])
                P.dma("sp", out[i * 128:(i + 1) * 128, :], ty, reads=[("ty", tt % 2)], writes=[("x1d", i)])
        load3a(0)
        load3a_weights()
        norm3a(0)
        for nb in range(NB):
            blk3a(nb)
        P.barrier(scr)
        A.release(m3)

        if upto == "p3a":
            raise _Stop()
        W1 = A.alloc([KC, 4 * D], BF16)
        W2 = A.alloc([32, D], BF16)
        def load3b_weights():
            for q8 in range(8):
                P.dma("sp", W1[:, :, q8 * 512:(q8 + 1) * 512], w1b[:, q8 * 512:(q8 + 1) * 512].rearrange("(k p) n -> p k n", p=128), writes=[("W1", q8)])
            for q8 in range(8):
                P.dma("sp", W2[:, q8 * 4:(q8 + 1) * 4, :], w2b[q8 * 512:(q8 + 1) * 512, :].rearrange("(k p) n -> p k n", p=128), writes=[("W2", q8)])
        xtB = [A.alloc([D], F32) for _ in range(2)]
        xrB = A.alloc([D], F32)
        rl = [A.alloc([512], F32) for _ in range(2)]
        h2TB = [A.alloc([KC, 512], BF16) for _ in range(2)]
        uT = A.alloc([32, 512], BF16)
        ssB = [A.alloc([1], F32) for _ in range(2)]
        rstdB = [A.alloc([1], F32) for _ in range(2)]
        ssyB = [A.alloc([1], F32) for _ in range(2)]
        rsyB = [A.alloc([1], F32) for _ in range(2)]
        tmpB3 = [dict(junk=rl[i_].bitcast(BF16), junk_key=("rl", i_), ss=ssB[i_], rstd=rstdB[i_], xn=xtB[i_], key=("nt4", i_)) for i_ in range(2)]
        W2k = [("W2", q8) for q8 in range(8)]

        def n3b_p1(nb, tt):
            i = nb * 4 + tt
            bi = i % 2
            P.dma("sp", xtB[bi], out[i * 128:(i + 1) * 128, :], reads=[("x1d", i)], writes=[("xtB", bi)])
            return norm_p1(xtB[bi], ("xtB", bi), tmpB3[bi], inplace=True)

        def n3b_p2(nb, tt, xnk):
            i = nb * 4 + tt
            h2T = h2TB[nb % 2]
            norm_p2(xnk, (lambda c: h2T[:, c, tt * 128:(tt + 1) * 128]), [("h2T", nb % 2, c, tt) for c in range(KC)], S2, SH2, 0, (6, 7), tmpB3[i % 2])

        def blk3b(nb):
            h2T = h2TB[nb % 2]
            hkeys = [("h2T", nb % 2, c, tt) for c in range(KC) for tt in range(4)]
            nxt = nb + 1 < NB
            xnks = {}
            for j in range(32):
                pb = j % 2

                def mm(e, j=j, pb=pb):
                    for k in range(KC):
                        ins = e.matmul(PB(pb), lhsT=W1[:, k, j * 128:(j + 1) * 128], rhs=h2T[:, k, :], start=(k == 0), stop=(k == KC - 1))
                    return ins
                P.op("pe", mm, reads=hkeys + [("W1", j // 4)], writes=PK(pb))
                P.op("act", lambda e, pb=pb: e.activation(out=rl[pb], in_=PB(pb), func=AF.Relu), writes=PK(pb) + [("rl", pb)])
                P.op("dve" if j % 2 == 0 else "pool", lambda e, j=j, pb=pb: e.tensor_tensor(out=uT[:, j, :], in0=rl[pb], in1=rl[pb], op=ALU.mult),
                     reads=[("rl", pb)], writes=[("uT", j)])
                if nxt:
                    if j == 3:
                        xnks[0] = n3b_p1(nb + 1, 0)
                    elif j == 7:
                        xnks[1] = n3b_p1(nb + 1, 1)
                    elif j == 15:
                        n3b_p2(nb + 1, 0, xnks[0])
                        xnks[2] = n3b_p1(nb + 1, 2)
                    elif j == 21:
                        n3b_p2(nb + 1, 1, xnks[1])
                        xnks[3] = n3b_p1(nb + 1, 3)
                    elif j == 27:
                        n3b_p2(nb + 1, 2, xnks[2])
                    elif j == 31:
                        n3b_p2(nb + 1, 3, xnks[3])
            for tt in range(4):
                i = nb * 4 + tt
                pbm = 2 + 2 * (tt % 2)
                ssy, rsy = ssyB[tt % 2], rsyB[tt % 2]
                P.dma("sp", xrB, out[i * 128:(i + 1) * 128, :], reads=[("x1d", i)], writes=["xrB"])

                def mm(e, tt=tt, pbm=pbm):
                    for half in range(2):
                        for j in range(32):
                            ins = e.matmul(PB(pbm + half), lhsT=uT[:, j, tt * 128:(tt + 1) * 128], rhs=W2[:, j, half * 512:(half + 1) * 512],
                                           start=(j == 0), stop=(j == 31))
                    return ins
                P.op("pe", mm, reads=[("uT", j) for j in range(32)] + W2k, writes=PK(pbm, pbm + 1))
                jb = tt % 2
                P.op("act", lambda e, pbm=pbm, jb=jb, ssy=ssy: e.activation(out=rl[jb].bitcast(BF16).rearrange("p (b n) -> p b n", b=2), in_=ps[:, pbm:pbm + 2, :], func=AF.Square, accum_out=ssy),
                     writes=PK(pbm, pbm + 1) + [("rl", jb), ("ssyB", tt % 2)])
                P.op("act", lambda e, ssy=ssy, rsy=rsy: e.activation(out=rsy, in_=ssy, func=AF.Sqrt, scale=1.0 / D, bias=EPS),
                     reads=[("ssyB", tt % 2)], writes=[("rsyB", tt % 2)])
                P.op("dve", lambda e, rsy=rsy: e.reciprocal(out=rsy, in_=rsy), writes=[("rsyB", tt % 2)])
                P.op("dve", lambda e, pbm=pbm, rsy=rsy: e.scalar_tensor_tensor(
                    out=ps[:, pbm:pbm + 2, :], in0=ps[:, pbm:pbm + 2, :], scalar=rsy[:, 0:1],
                    in1=GT[1].rearrange("p (b n) -> p b n", b=2), op0=ALU.mult, op1=ALU.mult),
                    reads=[("rsyB", tt % 2), ("GT", 1)], writes=PK(pbm, pbm + 1))
                P.op("dve", lambda e, pbm=pbm: e.tensor_tensor(out=xrB.rearrange("p (b n) -> p b n", b=2), in0=ps[:, pbm:pbm + 2, :],
                                                               in1=xrB.rearrange("p (b n) -> p b n", b=2), op=ALU.add),
                     writes=PK(pbm, pbm + 1) + ["xrB"])
                P.dma("sp", out[i * 128:(i + 1) * 128, :], xrB, reads=["xrB"], writes=[("outd", i)])
        for tt in range(4):
            n3b_p2(0, tt, n3b_p1(0, tt))
        load3b_weights()
        for nb in range(NB):
            blk3b(nb)
    try:
        body()
    except _Stop:
        pass
    info = P.emit()
    info["arena_peak"] = A.peak
    cmp_.__exit__(None, None, None)
    cm.__exit__(None, None, None)
    return nc, info, types


def prep_inputs(inputs, SEQ, types):
    NT = SEQ // 128
    f = lambda a: np.ascontiguousarray(np.asarray(a, dtype=np.float32))
    x = f(inputs["x"]); c = f(inputs["c"]); ctx = f(inputs["ctx"]); c_ctx = f(inputs["c_ctx"])
    B = x.shape[0]
    w_ada = f(inputs["w_ada"][0]); b_ada = f(inputs["b_ada"][0])
    shared = dict(
        w_ada=w_ada,
        bada_fm=np.ascontiguousarray(b_ada.reshape(48, 128).T),
        b_ada=b_ada.reshape(1, -1),
        gpre_fm=np.ascontiguousarray(np.stack([f(inputs["norm_pre_mix"][0]).reshape(KC, 128).T,
                                               f(inputs["norm_pre_ffn"][0]).reshape(KC, 128).T], axis=1)),
        gpost=np.ascontiguousarray(np.stack([f(inputs["norm_post_mix"][0]), f(inputs["norm_post_ffn"][0])], axis=0)),
        w_in=f(inputs["w_in"][0]), w_ro=f(inputs["w_ret_out"][0]), w_no=f(inputs["w_na_out"][0]),
        w_o=f(inputs["w_o"][0]), w_ff1=f(inputs["w_ff1"][0]), w_ff2=f(inputs["w_ff2"][0]),
    )
    lg = f(inputs["ret_decay_logit"][0])
    lgt_pair = np.zeros((128, 8), np.float32)
    def pair_body(hp):
        for d_ in range(2):
            lgt_pair[0:64, hp * 2 + d_] = lg[d_, 2 * hp]
            lgt_pair[64:128, hp * 2 + d_] = lg[d_, 2 * hp + 1]
    for hp in range(4):
        pair_body(hp)
    lgt_bc = np.zeros((128, 16), np.float32)
    for h in range(8):
        for d_ in range(2):
            lgt_bc[:, 2 * h + d_] = lg[d_, h]
    shared["lgt_pair"] = lgt_pair
    shared["lgt_bc"] = lgt_bc
    shared["ident"] = np.eye(128, dtype=np.float32)
    j = np.arange(128)[:, None].astype(np.float32)
    i = np.arange(128)[None, :].astype(np.float32)
    cmat = np.stack([np.maximum(i - j, 0), np.maximum(j - i, 0), (i >= j) * 0.125, (j > i) * 0.125], axis=1).astype(np.float32)
    shared["cmat"] = np.ascontiguousarray(cmat)
    jj = np.arange(128, dtype=np.float32)
    shared["colc"] = np.ascontiguousarray(np.stack([127 - jj, jj, 255 - jj, 127 - jj, jj, 128 + jj], axis=1))
    ii = np.arange(128, dtype=np.float32)
    shared["rowc"] = np.ascontiguousarray(np.broadcast_to(np.stack([ii + 1, 128 - ii], axis=0)[None], (128, 2, 128)).astype(np.float32))
    cos, sin = rope_tables(SEQ)
    shared["cos_tm"] = np.ascontiguousarray(cos.reshape(NT, 128, 32).transpose(1, 0, 2))
    shared["sin_tm"] = np.ascontiguousarray(sin.reshape(NT, 128, 32).transpose(1, 0, 2))
    mask, idr, idc = na_consts(types)
    rpb = f(inputs["na_rpb"][0])
    rpbB = rpb[:, idr, idc]
    shared["rpbB"] = np.ascontiguousarray(rpbB.transpose(0, 2, 1, 3))
    shared["maskB"] = np.ascontiguousarray(mask.transpose(1, 0, 2))
    in_maps = []
    for b in range(B):
        m = dict(shared)
        m["x"] = x[b]
        m["ctx"] = ctx[b]
        m["c_fm"] = np.ascontiguousarray(np.stack([c[b].reshape(KC, 128).T, c_ctx.reshape(KC, 128).T], axis=2))
        in_maps.append(m)
    return in_maps


_CACHE = {}


def kernel(**inputs):
    x = inputs["x"]
    B, SEQ, _ = x.shape
    if SEQ not in _CACHE:
        _CACHE[SEQ] = build(SEQ)
    nc, info, types = _CACHE[SEQ]
    in_maps = prep_inputs(inputs, SEQ, types)
    res = run_bass_kernel_spmd(nc, in_maps, core_ids=list(range(B)))
    return np.stack([np.asarray(r["out"], dtype=np.float32) for r in res.results], axis=0)
```
